# Optimizing a Trainium2 kernel written in Bass

```python
import math
import jax, jax.numpy as jnp
from jax import lax
import numpy as np

D_MODEL = 2048
BATCH = 4
SEQ = 4096
DEPTH = 2

N_META = 16
BLOCK = 128
PAD = BLOCK - N_META
MIX_W = D_MODEL
GROUP_W = MIX_W // 4
SSM_W = GROUP_W
SSM_H = 16
SSM_G = SSM_W // SSM_H
SSM_P = 64
ML_HEADS = 4
ML_DH = GROUP_W // ML_HEADS
ML_CONV = 4
FX_HEADS = 4
FX_DH = GROUP_W // FX_HEADS
POOL_WINDOWS = (2, 4, 8, 16)
POOL_GW = GROUP_W // len(POOL_WINDOWS)
D_FF = 4 * D_MODEL
EPS = 1e-6
NEG = -1e30
IN_SPLIT = (SSM_W, GROUP_W, GROUP_W, GROUP_W, GROUP_W, ML_HEADS, ML_HEADS, GROUP_W, GROUP_W, GROUP_W, FX_HEADS, GROUP_W)
IN_COLS = sum(IN_SPLIT)

kernel_name = "hymba_s5_mlstm_fox_pool_hybrid"

F32 = jnp.float32


def rmsnorm(x, g):
    xf = x.astype(F32)
    y = xf * lax.rsqrt(jnp.mean(xf * xf, axis=-1, keepdims=True) + EPS)
    return (y * g.astype(F32)).astype(x.dtype)


def pad_front(a):
    return jnp.pad(a, [(0, 0), (PAD, 0)] + [(0, 0)] * (a.ndim - 2))


def causal_dwconv(x, w):
    k = w.shape[0]
    return lax.conv_general_dilated(x, w.astype(F32)[:, None, :], window_strides=(1,), padding=[(k - 1, 0)],
                                    dimension_numbers=('NWC', 'WIO', 'NWC'), feature_group_count=x.shape[-1])


def s5_mixer(u, lam_re, lam_im, log_dt, b_re, b_im, c_re, c_im, d, glu_w, glu_b):
    bsz, L, _ = u.shape
    uf = u.astype(F32).reshape(bsz, L, SSM_G, SSM_H)
    lam = lax.complex(lam_re.astype(F32), lam_im.astype(F32))
    dt = jnp.exp(log_dt.astype(F32))[:, None]
    a_bar = jnp.exp(lam * dt)
    b = lax.complex(b_re.astype(F32), b_im.astype(F32))
    b_bar = ((a_bar - 1.0) / lam)[..., None] * b
    bu = lax.complex(jnp.einsum('blgh,gph->blgp', uf, jnp.real(b_bar)),
                     jnp.einsum('blgh,gph->blgp', uf, jnp.imag(b_bar)))
    a_seq = jnp.broadcast_to(a_bar[None, None], (1, L, SSM_G, SSM_P))

    def combine(e1, e2):
        a1, s1 = e1
        a2, s2 = e2
        return a2 * a1, a2 * s1 + s2

    _, states = lax.associative_scan(combine, (a_seq, bu), axis=1)
    y = (jnp.einsum('blgp,ghp->blgh', jnp.real(states), c_re.astype(F32))
         - jnp.einsum('blgp,ghp->blgh', jnp.imag(states), c_im.astype(F32))
         + d.astype(F32) * uf)
    y = jax.nn.gelu(y.reshape(bsz, L, SSM_W))
    z = y @ glu_w.astype(F32) + glu_b.astype(F32)
    val, gate = jnp.split(z, 2, axis=-1)
    return (val * jax.nn.sigmoid(gate)).astype(u.dtype)


def mlstm_mixer(q, k, v, o_pre, i_pre, f_pre, conv_w, norm_g):
    bsz, L, _ = q.shape
    qk = jax.nn.silu(causal_dwconv(jnp.concatenate([q, k], axis=-1).astype(F32), conv_w))
    q, k = jnp.split(qk, 2, axis=-1)
    lp = L + PAD
    nc = lp // BLOCK
    valid_c = (jnp.arange(lp) >= PAD).reshape(nc, BLOCK)

    def heads(a):
        a = pad_front(a.astype(F32)).reshape(bsz, nc, BLOCK, ML_HEADS, ML_DH)
        return a.transpose(0, 3, 1, 2, 4)

    def gates(a):
        return pad_front(a).reshape(bsz, nc, BLOCK, ML_HEADS).transpose(0, 3, 1, 2)

    qh, kh, vh = heads(q), heads(k) * ML_DH ** -0.5, heads(v)
    log_i = jnp.where(valid_c, gates(i_pre), NEG)
    log_f = jnp.where(valid_c, gates(jax.nn.log_sigmoid(f_pre)), 0.0)

    b = jnp.cumsum(log_f, axis=-1)
    g = b[..., -1]
    a_state = g[..., None] - b + log_i
    m_loc = jnp.max(a_state, axis=-1)
    w_state = jnp.exp(a_state - m_loc[..., None])
    c_loc = jnp.einsum('bhcsv,bhcsk->bhcvk', vh * w_state[..., None], kh)
    n_loc = jnp.einsum('bhcs,bhcsk->bhck', w_state, kh)

    def step(carry, xs):
        c_prev, n_prev, m_prev = carry
        g_c, m_loc_c, c_loc_c, n_loc_c = xs
        m_new = jnp.maximum(g_c + m_prev, m_loc_c)
        decay = jnp.exp(g_c + m_prev - m_new)
        scale = jnp.exp(m_loc_c - m_new)
        c_new = decay[..., None, None] * c_prev + scale[..., None, None] * c_loc_c
        n_new = decay[..., None] * n_prev + scale[..., None] * n_loc_c
        return (c_new, n_new, m_new), (c_prev, n_prev, m_prev)

    init = (jnp.zeros((bsz, ML_HEADS, ML_DH, ML_DH), F32), jnp.zeros((bsz, ML_HEADS, ML_DH), F32),
            jnp.zeros((bsz, ML_HEADS), F32))
    xs = (jnp.moveaxis(g, 2, 0), jnp.moveaxis(m_loc, 2, 0), jnp.moveaxis(c_loc, 2, 0), jnp.moveaxis(n_loc, 2, 0))
    _, (c_st, n_st, m_st) = lax.scan(step, init, xs)
    c_st, n_st, m_st = jnp.moveaxis(c_st, 0, 2), jnp.moveaxis(n_st, 0, 2), jnp.moveaxis(m_st, 0, 2)

    t_idx = jnp.arange(BLOCK)
    causal = t_idx[:, None] >= t_idx[None, :]
    d_log = jnp.where(causal, b[..., :, None] - b[..., None, :] + log_i[..., None, :], NEG)
    m_inter = b + m_st[..., None]
    m_comb = jnp.maximum(m_inter, jnp.max(d_log, axis=-1))
    scores = jnp.einsum('bhctd,bhcsd->bhcts', qh, kh) * jnp.exp(d_log - m_comb[..., None])
    inter_w = jnp.exp(m_inter - m_comb)
    num = (jnp.einsum('bhcts,bhcsv->bhctv', scores, vh)
           + inter_w[..., None] * jnp.einsum('bhctk,bhcvk->bhctv', qh, c_st))
    den = jnp.sum(scores, axis=-1) + inter_w * jnp.einsum('bhctk,bhck->bhct', qh, n_st)
    h = num / jnp.maximum(jnp.abs(den), jnp.exp(-m_comb))[..., None]
    h = h.transpose(0, 2, 3, 1, 4).reshape(bsz, lp, ML_HEADS, ML_DH)[:, PAD:]
    h = h * lax.rsqrt(jnp.mean(h * h, axis=-1, keepdims=True) + EPS) * norm_g.astype(F32).reshape(ML_HEADS, ML_DH)
    h = h.reshape(bsz, L, GROUP_W) * jax.nn.sigmoid(o_pre.astype(F32))
    return h.astype(o_pre.dtype)


def fox_mixer(q, k, v, f_pre):
    bsz, L, _ = q.shape
    lp = L + PAD
    nb = lp // BLOCK

    def heads(a):
        return pad_front(a.astype(F32)).reshape(bsz, lp, FX_HEADS, FX_DH).transpose(0, 2, 1, 3)

    qh, kh, vh = heads(q) * FX_DH ** -0.5, heads(k), heads(v)
    pos = jnp.arange(lp)
    log_f = jnp.where(pos >= PAD, pad_front(jax.nn.log_sigmoid(f_pre)).transpose(0, 2, 1), 0.0)
    cum = jnp.cumsum(log_f, axis=-1)
    q_blocks = qh.reshape(bsz, FX_HEADS, nb, BLOCK, FX_DH).transpose(2, 0, 1, 3, 4)
    cum_blocks = cum.reshape(bsz, FX_HEADS, nb, BLOCK).transpose(2, 0, 1, 3)
    starts = jnp.arange(nb) * BLOCK

    def attend(args):
        q_b, cum_b, start = args
        q_pos = start + jnp.arange(BLOCK)
        mask = (pos[None, :] <= q_pos[:, None]) & (pos[None, :] >= PAD)
        logits = jnp.einsum('bhtd,bhsd->bhts', q_b, kh) + cum_b[..., None] - cum[..., None, :]
        p = jax.nn.softmax(jnp.where(mask, logits, NEG), axis=-1)
        return jnp.einsum('bhts,bhsd->bhtd', p, vh)

    out = lax.map(attend, (q_blocks, cum_blocks, starts))
    out = out.transpose(1, 0, 3, 2, 4).reshape(bsz, lp, GROUP_W)[:, PAD:]
    return out.astype(q.dtype)


def pool_mixer(u, pool_w, pool_scale):
    bsz, L, _ = u.shape
    uf = u.astype(F32)
    t = jnp.arange(1, L + 1).astype(F32)
    outs = []
    for grp, w in zip(jnp.split(uf, len(POOL_WINDOWS), axis=-1), POOL_WINDOWS):
        cs = jnp.pad(jnp.cumsum(grp, axis=1), ((0, 0), (w, 0), (0, 0)))
        win_mean = (cs[:, w:] - cs[:, :L]) / jnp.minimum(t, float(w))[None, :, None]
        outs.append(win_mean - grp)
    pooled = jnp.stack(outs, axis=2)
    mixed = jnp.einsum('blgc,gcd->blgd', pooled, pool_w.astype(F32)).reshape(bsz, L, GROUP_W)
    return (mixed * pool_scale.astype(F32)).astype(u.dtype)


def setup_inputs(seed: int = 0) -> dict:
    key = jax.random.key(seed)
    ks = iter(jax.random.split(key, 32))

    def nrm(shape, scale):
        return scale * jax.random.normal(next(ks), shape, F32)

    def gain(shape):
        return 1.0 + nrm(shape, 0.02)

    gate_lin = jnp.linspace(3.0, 6.0, ML_HEADS, dtype=F32)[None]
    log_dt = jax.random.uniform(next(ks), (DEPTH, SSM_G), F32, math.log(1e-3), math.log(1e-1))
    return {
        "x": nrm((BATCH, SEQ, D_MODEL), 1.0),
        "meta_tokens": nrm((N_META, D_MODEL), 1.0),
        "g_pre_mix": gain((DEPTH, D_MODEL)),
        "g_post_mix": gain((DEPTH, D_MODEL)),
        "g_pre_ffn": gain((DEPTH, D_MODEL)),
        "g_post_ffn": gain((DEPTH, D_MODEL)),
        "w_in": nrm((DEPTH, D_MODEL, IN_COLS), D_MODEL ** -0.5),
        "ml_gate_bias": jnp.concatenate([nrm((DEPTH, ML_HEADS), 0.1), gate_lin + nrm((DEPTH, ML_HEADS), 0.1)], axis=-1),
        "fx_gate_bias": jnp.linspace(3.0, 6.0, FX_HEADS, dtype=F32)[None] + nrm((DEPTH, FX_HEADS), 0.1),
        "ssm_lam_re": -0.5 + nrm((DEPTH, SSM_G, SSM_P), 0.01),
        "ssm_lam_im": math.pi * jnp.arange(SSM_P, dtype=F32)[None, None] + nrm((DEPTH, SSM_G, SSM_P), 0.01),
        "ssm_log_dt": log_dt,
        "ssm_b_re": nrm((DEPTH, SSM_G, SSM_P, SSM_H), (2 * SSM_H) ** -0.5),
        "ssm_b_im": nrm((DEPTH, SSM_G, SSM_P, SSM_H), (2 * SSM_H) ** -0.5),
        "ssm_c_re": nrm((DEPTH, SSM_G, SSM_H, SSM_P), (2 * SSM_P) ** -0.5),
        "ssm_c_im": nrm((DEPTH, SSM_G, SSM_H, SSM_P), (2 * SSM_P) ** -0.5),
        "ssm_d": nrm((DEPTH, SSM_G, SSM_H), 1.0),
        "ssm_glu_w": nrm((DEPTH, SSM_W, 2 * SSM_W), SSM_W ** -0.5),
        "ssm_glu_b": nrm((DEPTH, 2 * SSM_W), 0.01),
        "ml_conv_w": nrm((DEPTH, ML_CONV, 2 * GROUP_W), ML_CONV ** -0.5),
        "ml_norm_g": gain((DEPTH, GROUP_W)),
        "pool_w": nrm((DEPTH, len(POOL_WINDOWS), POOL_GW, POOL_GW), POOL_GW ** -0.5),
        "pool_scale": gain((DEPTH, GROUP_W)),
        "w_out": nrm((DEPTH, MIX_W, D_MODEL), MIX_W ** -0.5),
        "mlp_w1": nrm((DEPTH, D_MODEL, D_FF), D_MODEL ** -0.5),
        "mlp_w2": nrm((DEPTH, D_FF, D_MODEL), D_FF ** -0.5),
    }


def reference(x, meta_tokens, g_pre_mix, g_post_mix, g_pre_ffn, g_post_ffn, w_in, ml_gate_bias, fx_gate_bias,
              ssm_lam_re, ssm_lam_im, ssm_log_dt, ssm_b_re, ssm_b_im, ssm_c_re, ssm_c_im, ssm_d, ssm_glu_w,
              ssm_glu_b, ml_conv_w, ml_norm_g, pool_w, pool_scale, w_out, mlp_w1, mlp_w2):
    bsz = x.shape[0]
    meta = jnp.broadcast_to(meta_tokens[None], (bsz, N_META, D_MODEL)).astype(x.dtype)
    h = jnp.concatenate([meta, x], axis=1)
    split_idx = np.cumsum(IN_SPLIT)[:-1].tolist()
    for l in range(DEPTH):
        xn = rmsnorm(h, g_pre_mix[l])
        proj = xn @ w_in[l]
        (s_u, m_q, m_k, m_v, m_o, m_i, m_f, f_q, f_k, f_v, f_f, p_u) = jnp.split(proj, split_idx, axis=-1)
        gb = ml_gate_bias[l].astype(F32)
        y_ssm = s5_mixer(s_u, ssm_lam_re[l], ssm_lam_im[l], ssm_log_dt[l], ssm_b_re[l], ssm_b_im[l],
                         ssm_c_re[l], ssm_c_im[l], ssm_d[l], ssm_glu_w[l], ssm_glu_b[l])
        y_ml = mlstm_mixer(m_q, m_k, m_v, m_o, m_i.astype(F32) + gb[:ML_HEADS], m_f.astype(F32) + gb[ML_HEADS:],
                           ml_conv_w[l], ml_norm_g[l])
        y_fx = fox_mixer(f_q, f_k, f_v, f_f.astype(F32) + fx_gate_bias[l].astype(F32))
        y_pool = pool_mixer(p_u, pool_w[l], pool_scale[l])
        mix = jnp.concatenate([y_ssm, y_ml, y_fx, y_pool], axis=-1) @ w_out[l]
        h = h + rmsnorm(mix, g_post_mix[l])
        hn = rmsnorm(h, g_pre_ffn[l])
        ff = jnp.square(jax.nn.relu(hn @ mlp_w1[l])) @ mlp_w2[l]
        h = h + rmsnorm(ff, g_post_ffn[l])
    return h[:, N_META:]
```

```python
import numpy as np
import concourse.bass as bass
import concourse.mybir as mybir
from concourse.bass_utils import run_bass_kernel_spmd

F32 = mybir.dt.float32
BF16 = mybir.dt.bfloat16
AF = mybir.ActivationFunctionType
ALU = mybir.AluOpType
AX = mybir.AxisListType


class Sched:
    def __init__(self, nc):
        self.nc = nc
        self.engs = {'pe': nc.tensor, 'act': nc.scalar, 'dve': nc.vector, 'pool': nc.gpsimd, 'sp': nc.sync}
        self.ops = []
        self.last_w = {}
        self.readers = {}
        self.dma_cnt = {}
        self.esem = {e: nc.alloc_semaphore("s_" + e) for e in self.engs}
        self.dsem = {}
        self.ebase = {e: 0 for e in self.engs}
        self.dma_issued = {}

    def _deps(self, reads, writes):
        deps = set()
        for k in reads:
            if k in self.last_w:
                deps.add(self.last_w[k])
        for k in writes:
            if k in self.last_w:
                deps.add(self.last_w[k])
            deps.update(self.readers.get(k, ()))
        return deps

    def _record(self, oid, reads, writes):
        for k in writes:
            self.last_w[k] = oid
            self.readers[k] = []
        for k in reads:
            self.readers.setdefault(k, []).append(oid)

    @staticmethod
    def _bank(reads, writes):
        bk = lambda k: 'psb' if k.startswith('psb') else k
        w = [bk(k) for k in writes] + [bk(k) for k in reads if k.startswith('ps')]
        r = [k for k in reads if not k.startswith('ps')]
        return r, w

    def op(self, eng, fn, reads=(), writes=()):
        reads, writes = self._bank(list(reads), list(writes))
        deps = self._deps(reads, writes)
        oid = len(self.ops)
        self.ops.append(dict(eng=eng, fn=fn, deps=deps, dma=None))
        self._record(oid, reads, writes)
        return oid

    def dma(self, eng, out, in_, key, reads=(), writes=(), **kw):
        deps = self._deps(reads, writes)
        oid = len(self.ops)
        n = self.dma_cnt.get(key, 0) + 1
        self.dma_cnt[key] = n
        self.ops.append(dict(eng=eng, fn=lambda e: e.dma_start(out=out, in_=in_, **kw), deps=deps, dma=(key, n)))
        self._record(oid, reads, writes)
        return oid

    def flush(self):
        nc = self.nc
        ops = self.ops
        if not ops:
            return
        eng_list = {e: [] for e in self.engs}
        for oid, o in enumerate(ops):
            o['pos'] = len(eng_list[o['eng']])
            eng_list[o['eng']].append(oid)
        seen = {e: {} for e in self.engs}
        need_inc = set()
        for oid, o in enumerate(ops):
            e = o['eng']
            want = {}
            for d in o['deps']:
                do = ops[d]
                if do['dma'] is not None:
                    key, n = do['dma']
                    want[('d', key)] = max(want.get(('d', key), 0), n)
                else:
                    if do['eng'] == 'pe' and e == 'pe' and o['dma'] is None:
                        continue
                    want[('e', do['eng'])] = max(want.get(('e', do['eng']), -1), do['pos'])
            waits = []
            for k, v in want.items():
                if seen[e].get(k, -1) >= v:
                    continue
                seen[e][k] = v
                waits.append((k, v))
                if k[0] == 'e':
                    need_inc.add(eng_list[k[1]][v])
            o['waits'] = waits
        issued = self.dma_issued
        for oid, o in enumerate(ops):
            nw = []
            for k, v in o['waits']:
                if k[0] == 'd':
                    v = max(v, issued.get(k[1], 0))
                nw.append((k, v))
            o['waits'] = nw
            if o['dma'] is not None:
                issued[o['dma'][0]] = o['dma'][1]
        for k in self.dma_cnt:
            if k not in self.dsem:
                self.dsem[k] = nc.alloc_semaphore("d_%d" % len(self.dsem))
        inc_prefix = {}
        for e, lst in eng_list.items():
            c = self.ebase[e]
            for oid in lst:
                if oid in need_inc:
                    c += 1
                    inc_prefix[oid] = c
            self.ebase[e] = c
        finals = [(k, n) for k, n in self.dma_cnt.items()]

        def emit(e, eng):
            for oid in eng_list[e]:
                o = ops[oid]
                for k, v in o['waits']:
                    if k[0] == 'd':
                        eng.wait_ge(self.dsem[k[1]], 16 * v)
                    else:
                        eng.wait_ge(self.esem[k[1]], inc_prefix[eng_list[k[1]][v]])
                ins = o['fn'](eng)
                if o['dma'] is not None:
                    ins.then_inc(self.dsem[o['dma'][0]], 16)
                elif oid in need_inc:
                    ins.then_inc(self.esem[e], 1)
            if e == 'sp':
                for k, n in finals:
                    eng.wait_ge(self.dsem[k], 16 * n)

        with nc.Block() as block:
            @block.sync
            def _(eng):
                emit('sp', eng)

            @block.tensor
            def _(eng):
                emit('pe', eng)

            @block.scalar
            def _(eng):
                emit('act', eng)

            @block.vector
            def _(eng):
                emit('dve', eng)

            @block.gpsimd
            def _(eng):
                emit('pool', eng)
        nc.all_engine_barrier()
        self.ops = []
        self.last_w = {}
        self.readers = {}


D = 2048
KT = 16
INC = 4620
DFF = 8192
EPS = 1e-6
SEGS = [("s_u", 0), ("m_q", 512), ("m_k", 1024), ("m_v", 1536), ("m_o", 2048),
        ("f_q", 2568), ("f_k", 3080), ("f_v", 3592), ("p_u", 4108)]
PARAMS = [("g_pre_mix", [D]), ("g_post_mix", [D]), ("g_pre_ffn", [D]), ("g_post_ffn", [D]),
          ("w_in", [D, INC]), ("ml_gate_bias", [8]), ("fx_gate_bias", [4]),
          ("ssm_lam_re", [32, 64]), ("ssm_lam_im", [32, 64]), ("ssm_log_dt", [32]),
          ("ssm_b_re", [32, 64, 16]), ("ssm_b_im", [32, 64, 16]), ("ssm_c_re", [32, 16, 64]),
          ("ssm_c_im", [32, 16, 64]), ("ssm_d", [32, 16]), ("ssm_glu_w", [512, 1024]), ("ssm_glu_b", [1024]),
          ("ml_conv_w", [4, 1024]), ("ml_norm_g", [512]), ("pool_w", [4, 128, 128]), ("pool_scale", [512]),
          ("w_out", [D, D]), ("mlp_w1", [D, DFF]), ("mlp_w2", [DFF, D])]
TWO_PI = float(2.0 * np.pi)


def host_consts():
    tri = (np.arange(128)[:, None] <= np.arange(128)[None, :]).astype(np.float32)
    pad = np.zeros((128, 4), np.float32)
    pad[112:, 0] = 1.0
    pad[:112, 1] = -1e30
    pad[:, 2] = ((np.arange(128) // 16) % 2 == 0)
    pad[:, 3] = ((np.arange(128) // 16) % 2 == 1)
    corr = np.zeros((128, 4, 16), np.float32)
    for g, w in enumerate((2, 4, 8, 16)):
        corr[:, g, :] = w / np.minimum(np.arange(1, 17), w)
    iota = np.broadcast_to(np.arange(128, dtype=np.float32)[None, :], (128, 128)).copy()
    ident = np.eye(128, dtype=np.float32)
    return {"c_tri": tri, "c_pad": pad, "c_corr": corr, "c_iota": iota, "c_ident": ident}


def build_program(SEQ, DEPTH, debug=False, stop_after=None):
    nc = bass.Bass("TRN2", target_bir_lowering=False)
    LP = SEQ + 128
    NCH = LP // 128
    blocks = [(t0, min(512, LP - t0)) for t0 in range(0, LP, 512)]

    def din(name, shape):
        return nc.dram_tensor(name, list(shape), F32, kind="ExternalInput").ap()

    x = din("x", [SEQ, D])
    meta = din("meta_tokens", [16, D])
    P = {n: din(n, [DEPTH] + sh) for n, sh in PARAMS}
    c_tri = din("c_tri", [128, 128])
    c_pad = din("c_pad", [128, 4])
    c_corr = din("c_corr", [128, 4, 16])
    c_iota = din("c_iota", [128, 128])
    c_ident = din("c_ident", [128, 128])
    out = nc.dram_tensor("out", [SEQ, D], F32, kind="ExternalOutput").ap()
    dbg_kind = "ExternalOutput" if debug else "Internal"

    def scr(name, shape, dt, dbg=False):
        return nc.dram_tensor(name, list(shape), dt, kind=(dbg_kind if dbg else "Internal")).ap()

    hT = scr("hT", [KT, 128, LP], F32, True)
    uT = scr("uT", [4, 128, LP], BF16, True)
    mqkT = scr("mqkT", [8, 128, LP], BF16, True)
    mv = scr("mv", [LP, 512], BF16, True)
    mo = scr("mo", [LP, 512], BF16, True)
    gates = scr("gates", [LP, 12], F32, True)
    fqT = scr("fqT", [4, 128, LP], BF16, True)
    fkT = scr("fkT", [4, 128, LP], BF16, True)
    fv = scr("fv", [LP, 512], BF16, True)
    puT = scr("puT", [4, 128, LP], BF16, True)
    yT = scr("yT", [KT, 128, LP], BF16, True)
    s5scr = scr("s5scr", [8, 2048], F32)
    winb = scr("winb", [DEPTH, 9, 128, KT, 512], BF16)
    wgb = scr("wgb", [DEPTH, 128, KT, 12], BF16)
    woutb = scr("woutb", [DEPTH, 4, 128, KT, 512], BF16)
    w1b = scr("w1b", [DEPTH, 16, 128, KT, 512], BF16)
    w2b = scr("w2b", [DEPTH, 16, 128, KT, 512], BF16)
    glub = scr("glub", [DEPTH, 2, 128, 4, 512], BF16)
    poolwb = scr("poolwb", [DEPTH, 128, 4, 128], BF16)

    S = Sched(nc)
    ps = [nc.alloc_psum_tensor("ps%d" % i, [128, 512], F32).ap() for i in range(7)]
    psb = nc.alloc_psum_tensor("psb", [128, 1024], BF16).ap()
    PSK = ["ps%d" % i for i in range(7)]

    from contextlib import ExitStack

    uid = [0]

    def sb(es, name, shape, dt):
        uid[0] += 1
        return es.enter_context(nc.sbuf_tensor("%s_%d" % (name, uid[0]), list(shape), dt)).ap()

    pes = ExitStack()
    ident_f = sb(pes, "ident_f", [128, 128], F32)
    ident_b = sb(pes, "ident_b", [128, 128], BF16)
    ones_b = sb(pes, "ones_b", [128, 128], BF16)
    ones_f = sb(pes, "ones_f", [128, 128], F32)
    tri_f = sb(pes, "tri_f", [128, 128], F32)
    tri_b = sb(pes, "tri_b", [128, 128], BF16)
    padc = sb(pes, "padc", [128, 4], F32)
    corr = sb(pes, "corr", [128, 4, 16], F32)
    iota = sb(pes, "iota", [128, 128], F32)
    vecT = sb(pes, "vecT", [128, DEPTH, 112], F32)
    S.dma('sp', ident_f, c_ident, key='c', writes=['ident_f'])
    S.dma('sp', tri_f, c_tri, key='c', writes=['tri_f'])
    S.dma('sp', padc, c_pad, key='c', writes=['padc'])
    S.dma('sp', corr, c_corr, key='c', writes=['corr'])
    S.dma('sp', iota, c_iota, key='c', writes=['iota'])
    S.op('dve', lambda e: e.tensor_copy(ident_b, ident_f), reads=['ident_f'], writes=['ident_b'])
    S.op('dve', lambda e: e.tensor_copy(tri_b, tri_f), reads=['tri_f'], writes=['tri_b'])
    S.op('dve', lambda e: e.memset(ones_b, 1.0), writes=['ones_b'])
    S.op('dve', lambda e: e.memset(ones_f, 1.0), writes=['ones_f'])
    with ExitStack() as es:
        vs = sb(es, "vs", [112, 128], F32)
        for l in range(DEPTH):
            r = 0
            for nm, nrow in (("g_pre_mix", 16), ("g_post_mix", 16), ("g_pre_ffn", 16), ("g_post_ffn", 16),
                             ("pool_scale", 4), ("ssm_d", 4), ("ssm_glu_b", 8)):
                src = P[nm][l]
                if nm == "ssm_d":
                    src = src.rearrange("g h -> (g h)")
                S.dma('sp', vs[r:r + nrow, :], src.rearrange("(r c) -> r c", c=128), key='c', writes=['vs'])
                r += nrow
            S.dma('sp', vs[80:112, :], P["ml_conv_w"][l].rearrange("j (t c) -> (j t) c", c=128), key='c', writes=['vs'])
            S.op('pe', lambda e: e.transpose(ps[0][:, 0:112], vs, ident_f[0:112, 0:112]), reads=['vs', 'ident_f'], writes=['ps0'])
            S.op('dve', lambda e, l=l: e.tensor_copy(vecT[:, l, :], ps[0][:, 0:112]), reads=['ps0'], writes=['vecT'])
        def cast_panel(dst, src_rows, c0, nk, ncol=512):
            sv = src_rows.rearrange("(kt p) c -> p kt c", p=128)
            for k0 in range(0, nk, 4):
                k1 = min(nk, k0 + 4)
                S.dma('pool', dst[:, k0:k1, :], sv[:, k0:k1, c0:c0 + ncol], key='cast')
        for l in range(DEPTH):
            for i, (_, c0) in enumerate(SEGS):
                cast_panel(winb[l, i], P["w_in"][l], c0, KT)
            sv = P["w_in"][l].rearrange("(kt p) c -> p kt c", p=128)
            S.dma('pool', wgb[l][:, :, 0:8], sv[:, :, 2560:2568], key='cast')
            S.dma('pool', wgb[l][:, :, 8:12], sv[:, :, 4104:4108], key='cast')
            for i in range(4):
                cast_panel(woutb[l, i], P["w_out"][l], i * 512, KT)
            for i in range(16):
                cast_panel(w1b[l, i], P["mlp_w1"][l], i * 512, KT)
            for je in range(4):
                for kq in range(4):
                    cast_panel(w2b[l, je * 4 + kq], P["mlp_w2"][l][kq * 2048:(kq + 1) * 2048, :], je * 512, KT)
            for i in range(2):
                cast_panel(glub[l, i], P["ssm_glu_w"][l], i * 512, 4)
            S.dma('pool', poolwb[l], P["pool_w"][l].rearrange("g c d -> c g d"), key='cast')
        xin = [sb(es, "xin%d" % i, [128, D], F32) for i in range(2)]
        hst = [sb(es, "hst%d" % i, [128, KT, 128], F32) for i in range(2)]
        for c in range(NCH):
            xi, hs = xin[c % 2], hst[c % 2]
            kx, kh = "xin%d" % (c % 2), "hst%d" % (c % 2)
            if c == 0:
                S.op('pool', lambda e, xi=xi: e.memset(xi, 0.0), writes=[kx])
                S.dma('sp', xi[112:128, :], meta, key='xin', reads=[kx], writes=[kx])
            else:
                S.dma('sp', xi, x[(c - 1) * 128:c * 128, :], key='xin', writes=[kx])
            for q in range(4):
                pk = ps[q % 2]
                for j in range(4):
                    kt = q * 4 + j
                    S.op('pe', lambda e, pk=pk, j=j, kt=kt, xi=xi: e.transpose(pk[:, j * 128:(j + 1) * 128], xi[:, kt * 128:(kt + 1) * 128], ident_f),
                         reads=[kx, 'ident_f'], writes=[PSK[q % 2]])
                S.op('act' if q % 2 else 'dve',
                     (lambda e, pk=pk, q=q, hs=hs: e.activation(hs[:, q * 4:(q + 1) * 4, :], pk.rearrange("p (j t) -> p j t", j=4), AF.Identity)) if q % 2 else
                     (lambda e, pk=pk, q=q, hs=hs: e.tensor_copy(hs[:, q * 4:(q + 1) * 4, :], pk.rearrange("p (j t) -> p j t", j=4))),
                     reads=[PSK[q % 2]], writes=[kh])
            S.dma('sp', hT[:, :, c * 128:(c + 1) * 128].rearrange("k p t -> p k t"), hs, key='hst', reads=[kh], writes=['hT'])
        S.flush()
    if stop_after == "p0":
        return nc
    def evac(i, out_ap, in_ap, reads, writes):
        if i % 2:
            S.op('act', lambda e: e.activation(out_ap, in_ap, AF.Identity), reads=reads, writes=writes)
        else:
            S.op('dve', lambda e: e.tensor_copy(out_ap, in_ap), reads=reads, writes=writes)

    def sumsq_rstd(src, ksrc, n, sq, rstd, dim):
        for kt in range(KT):
            sl = sq[kt % 2]
            S.op('act', lambda e, sl=sl, kt=kt: e.activation(sl[:, :n], src[:, kt, :n], AF.Square), reads=[ksrc], writes=['sq%d' % (kt % 2)])
            S.op('pe', lambda e, sl=sl, kt=kt: e.matmul(ps[6][:, :n], ones_b, sl[:, :n], start=(kt == 0), stop=(kt == KT - 1)),
                 reads=['sq%d' % (kt % 2), 'ones_b'], writes=['ps6'])
        S.op('act', lambda e: e.activation(rstd[:, :n], ps[6][:, :n], AF.Sqrt, bias=EPS, scale=1.0 / dim), reads=['ps6'], writes=['rstd'])
        S.op('dve', lambda e: e.reciprocal(rstd[:, :n], rstd[:, :n]), reads=['rstd'], writes=['rstd'])

    def phase_A(l):
        with ExitStack() as es:
            hblk = sb(es, "hblk", [128, KT, 512], F32)
            xn = sb(es, "xn", [128, KT, 512], BF16)
            sq = [sb(es, "sq%d" % i, [128, 512], BF16) for i in range(2)]
            rstd = sb(es, "rstd", [128, 512], F32)
            wp = [sb(es, "wp%d" % i, [128, KT, 512], BF16) for i in range(2)]
            wg = sb(es, "wg", [128, KT, 12], BF16)
            gbt = sb(es, "gbt", [128, 12], F32)
            ost = [sb(es, "ost%d" % i, [128, 4, 512], BF16) for i in range(2)]
            tst = [sb(es, "tst%d" % i, [128, 512], BF16) for i in range(2)]
            gst = [sb(es, "gst%d" % i, [128, 12], F32) for i in range(2)]
            S.dma('sp', wg, wgb[l], key='wg', writes=['wg'])
            S.dma('sp', gbt[:, 0:8], P["ml_gate_bias"][l].unsqueeze(0).partition_broadcast(128) if False else P["ml_gate_bias"][l:l + 1, :].partition_broadcast(128), key='wg', writes=['gbt'])
            S.dma('sp', gbt[:, 8:12], P["fx_gate_bias"][l:l + 1, :].partition_broadcast(128), key='wg', writes=['gbt'])
            fm_dest = {"s_u": (uT, 0), "m_q": (mqkT, 0), "m_k": (mqkT, 4), "f_q": (fqT, 0), "f_k": (fkT, 0), "p_u": (puT, 0)}
            tm_dest = {"m_v": mv, "m_o": mo, "f_v": fv}
            wcnt = 0
            ocnt = 0
            tcnt = 0
            gcnt = 0
            def blk(t0, n):
                nonlocal wcnt, ocnt, tcnt, gcnt
                for kt in range(KT):
                    S.dma('sp', hblk[:, kt, :n], hT[kt, :, t0:t0 + n], key='hblk', writes=['hblk'])
                sumsq_rstd(hblk, 'hblk', n, sq, rstd, D)
                for kt in range(KT):
                    S.op('dve', lambda e, kt=kt: e.scalar_tensor_tensor(xn[:, kt, :n], hblk[:, kt, :n], vecT[:, l, kt:kt + 1], rstd[:, :n], ALU.mult, ALU.mult),
                         reads=['hblk', 'rstd', 'vecT'], writes=['xn'])
                for i, (nm, c0) in enumerate(SEGS):
                    w = wp[wcnt % 2]
                    kw = 'wp%d' % (wcnt % 2)
                    wcnt += 1
                    S.dma('sp', w, winb[l, i], key=kw, writes=[kw])
                    if nm in fm_dest:
                        dst, toff = fm_dest[nm]
                        o = ost[ocnt % 2]
                        ko = 'ost%d' % (ocnt % 2)
                        ocnt += 1
                        for m in range(4):
                            for kt in range(KT):
                                S.op('pe', lambda e, w=w, m=m, kt=kt: e.matmul(ps[m][:, :n], w[:, kt, m * 128:(m + 1) * 128], xn[:, kt, :n], start=(kt == 0), stop=(kt == KT - 1)),
                                     reads=[kw, 'xn'], writes=[PSK[m]])
                            evac(m, o[:, m, :n], ps[m][:, :n], [PSK[m]], [ko])
                        for m in range(4):
                            S.dma('sp', dst[toff + m, :, t0:t0 + n], o[:, m, :n], key=ko, reads=[ko])
                    else:
                        dst = tm_dest[nm]
                        for i2 in range(n // 128):
                            pk = 4 + (i2 % 2)
                            ts_ = tst[tcnt % 2]
                            kts = 'tst%d' % (tcnt % 2)
                            tcnt += 1
                            for kt in range(KT):
                                S.op('pe', lambda e, w=w, kt=kt, i2=i2, pk=pk: e.matmul(ps[pk], xn[:, kt, i2 * 128:(i2 + 1) * 128], w[:, kt, :], start=(kt == 0), stop=(kt == KT - 1)),
                                     reads=[kw, 'xn'], writes=[PSK[pk]])
                            evac(i2, ts_, ps[pk], [PSK[pk]], [kts])
                            S.dma('sp', dst[t0 + i2 * 128:t0 + (i2 + 1) * 128, :], ts_, key=kts, reads=[kts])
                for i2 in range(n // 128):
                    g_ = gst[gcnt % 2]
                    kg = 'gst%d' % (gcnt % 2)
                    gcnt += 1
                    for kt in range(KT):
                        S.op('pe', lambda e, kt=kt, i2=i2: e.matmul(ps[6][:, 0:12], xn[:, kt, i2 * 128:(i2 + 1) * 128], wg[:, kt, :], start=(kt == 0), stop=(kt == KT - 1)),
                             reads=['wg', 'xn'], writes=['ps6'])
                    S.op('dve', lambda e, g_=g_: e.tensor_tensor(g_, ps[6][:, 0:12], gbt, ALU.add), reads=['ps6', 'gbt'], writes=[kg])
                    S.dma('sp', gates[t0 + i2 * 128:t0 + (i2 + 1) * 128, :], g_, key=kg, reads=[kg])

            for (t0_, n_) in blocks:
                blk(t0_, n_)
            S.flush()

    def phase_C(l):
        with ExitStack() as es:
            hblk = sb(es, "hblk", [128, KT, 512], F32)
            mix = sb(es, "mix", [128, KT, 512], F32)
            hn = sb(es, "hn", [128, KT, 512], BF16)
            big = sb(es, "big", [128, 64, 512], BF16)
            sq = [sb(es, "sq%d" % i, [128, 512], BF16) for i in range(2)]
            rstd = sb(es, "rstd", [128, 512], F32)
            rl = [sb(es, "rl%d" % i, [128, 512], F32) for i in range(2)]
            wp = [sb(es, "wp%d" % i, [128, KT, 512], BF16) for i in range(2)]
            wcnt = 0
            rcnt = 0

            def post_norm_add(n, gcol):
                sumsq_rstd(mix, 'mix', n, sq, rstd, D)
                for kt in range(KT):
                    S.op('dve', lambda e, kt=kt: e.scalar_tensor_tensor(mix[:, kt, :n], mix[:, kt, :n], vecT[:, l, gcol + kt:gcol + kt + 1], rstd[:, :n], ALU.mult, ALU.mult),
                         reads=['mix', 'rstd', 'vecT'], writes=['mix'])
                    S.op('pool', lambda e, kt=kt: e.tensor_tensor(hblk[:, kt, :n], hblk[:, kt, :n], mix[:, kt, :n], ALU.add), reads=['mix', 'hblk'], writes=['hblk'])

            def blk(t0, n):
                nonlocal wcnt, rcnt
                for kt in range(KT):
                    S.dma('sp', hblk[:, kt, :n], hT[kt, :, t0:t0 + n], key='hblk', writes=['hblk'])
                for kt in range(KT):
                    S.dma('sp', big[:, kt, :n], yT[kt, :, t0:t0 + n], key='ybl', writes=['big'])
                for i in range(4):
                    w = wp[wcnt % 2]
                    kw = 'wp%d' % (wcnt % 2)
                    wcnt += 1
                    S.dma('sp', w, woutb[l, i], key=kw, writes=[kw])
                    for m in range(4):
                        for kt in range(KT):
                            S.op('pe', lambda e, w=w, m=m, kt=kt: e.matmul(ps[m][:, :n], w[:, kt, m * 128:(m + 1) * 128], big[:, kt, :n], start=(kt == 0), stop=(kt == KT - 1)),
                                 reads=[kw, 'big'], writes=[PSK[m]])
                        evac(m, mix[:, i * 4 + m, :n], ps[m][:, :n], [PSK[m]], ['mix'])
                post_norm_add(n, 16)
                if t0 == 0:
                    S.op('pool', lambda e: e.memset(hblk[:, :, 0:112], 0.0), writes=['hblk'])
                sumsq_rstd(hblk, 'hblk', n, sq, rstd, D)
                for kt in range(KT):
                    S.op('dve', lambda e, kt=kt: e.scalar_tensor_tensor(hn[:, kt, :n], hblk[:, kt, :n], vecT[:, l, 32 + kt:33 + kt], rstd[:, :n], ALU.mult, ALU.mult),
                         reads=['hblk', 'rstd', 'vecT'], writes=['hn'])
                for jp in range(16):
                    w = wp[wcnt % 2]
                    kw = 'wp%d' % (wcnt % 2)
                    wcnt += 1
                    S.dma('sp', w, w1b[l, jp], key=kw, writes=[kw])
                    for m in range(4):
                        for kt in range(KT):
                            S.op('pe', lambda e, w=w, m=m, kt=kt: e.matmul(ps[m][:, :n], w[:, kt, m * 128:(m + 1) * 128], hn[:, kt, :n], start=(kt == 0), stop=(kt == KT - 1)),
                                 reads=[kw, 'hn'], writes=[PSK[m]])
                        r_ = rl[rcnt % 2]
                        kr = 'rl%d' % (rcnt % 2)
                        rcnt += 1
                        S.op('act', lambda e, r_=r_, m=m: e.activation(r_[:, :n], ps[m][:, :n], AF.Relu), reads=[PSK[m]], writes=[kr])
                        S.op('dve', lambda e, r_=r_, jp=jp, m=m: e.tensor_tensor(big[:, jp * 4 + m, :n], r_[:, :n], r_[:, :n], ALU.mult), reads=[kr], writes=['big'])
                for je in range(4):
                    for kq in range(4):
                        w = wp[wcnt % 2]
                        kw = 'wp%d' % (wcnt % 2)
                        wcnt += 1
                        S.dma('sp', w, w2b[l, je * 4 + kq], key=kw, writes=[kw])
                        for m in range(4):
                            for kt in range(KT):
                                S.op('pe', lambda e, w=w, m=m, kt=kt, kq=kq: e.matmul(ps[m][:, :n], w[:, kt, m * 128:(m + 1) * 128], big[:, kq * 16 + kt, :n],
                                                                                    start=(kq == 0 and kt == 0), stop=(kq == 3 and kt == KT - 1)),
                                     reads=[kw, 'big'], writes=[PSK[m]])
                    for m in range(4):
                        evac(m, mix[:, je * 4 + m, :n], ps[m][:, :n], [PSK[m]], ['mix'])
                post_norm_add(n, 48)
                for kt in range(KT):
                    S.dma('sp', hT[kt, :, t0:t0 + n], hblk[:, kt, :n], key='hst', reads=['hblk'], writes=['hT'])

            for (t0_, n_) in blocks:
                blk(t0_, n_)
            S.flush()

    def phase_Z():
        with ExitStack() as es:
            hs = [sb(es, "hs%d" % i, [128, KT, 128], F32) for i in range(2)]
            xo = [sb(es, "xo%d" % i, [128, D], F32) for i in range(2)]
            for c in range(1, NCH):
                h_, x_ = hs[c % 2], xo[c % 2]
                kh, kx = 'hs%d' % (c % 2), 'xo%d' % (c % 2)
                S.dma('sp', h_, hT[:, :, c * 128:(c + 1) * 128].rearrange("k p t -> p k t"), key=kh, writes=[kh])
                for q in range(4):
                    for j in range(4):
                        kt = q * 4 + j
                        S.op('pe', lambda e, q=q, j=j, kt=kt, h_=h_: e.transpose(ps[q][:, j * 128:(j + 1) * 128], h_[:, kt, :], ident_f), reads=[kh, 'ident_f'], writes=[PSK[q]])
                    evac(q, x_[:, q * 512:(q + 1) * 512], ps[q], [PSK[q]], [kx])
                S.dma('sp', out[(c - 1) * 128:c * 128, :], x_, key=kx, reads=[kx])
            S.flush()
    def phase_pool(l):
        with ExitStack() as es:
            ub = [sb(es, "ub%d" % i, [128, 16 + LP], F32) for i in range(3)]
            uin = sb(es, "uin", [128, LP], BF16)
            pooled = sb(es, "pooled", [128, LP], BF16)
            t16 = sb(es, "t16", [128, 16], F32)
            pw = sb(es, "pw", [128, 4, 128], BF16)
            ost = [sb(es, "post%d" % i, [128, 512], BF16) for i in range(2)]
            S.dma('sp', pw, poolwb[l], key='pw', writes=['pw'])
            for i in range(3):
                S.op('pool', lambda e, i=i: e.memset(ub[i][:, 0:16], 0.0), writes=['ub%d' % i])
            oc = 0
            for g, w in enumerate((2, 4, 8, 16)):
                S.dma('sp', uin, puT[g], key='uin', writes=['uin'])
                S.op('act', lambda e: e.activation(ub[0][:, 16:], uin, AF.Identity), reads=['uin'], writes=['ub0'])
                a, ka = ub[0], 'ub0'
                for s_ in range(g + 1):
                    m = 2 ** s_
                    d_, kd = ub[1 + s_ % 2], 'ub%d' % (1 + s_ % 2)
                    S.op('dve', lambda e, a=a, d_=d_, m=m: e.tensor_tensor(d_[:, 16:], a[:, 16:], a[:, 16 - m:16 - m + LP], ALU.add), reads=[ka], writes=[kd])
                    a, ka = d_, kd
                S.op('dve', lambda e, a=a, w=w: e.scalar_tensor_tensor(pooled, a[:, 16:], 1.0 / w, ub[0][:, 16:], ALU.mult, ALU.subtract), reads=[ka, 'ub0'], writes=['pooled'])
                S.op('dve', lambda e, a=a, g=g: e.tensor_tensor(t16, a[:, 16 + 112:16 + 128], corr[:, g, :], ALU.mult), reads=[ka, 'corr'], writes=['t16'])
                S.op('dve', lambda e, w=w: e.scalar_tensor_tensor(pooled[:, 112:128], t16, 1.0 / w, ub[0][:, 16 + 112:16 + 128], ALU.mult, ALU.subtract), reads=['t16', 'ub0'], writes=['pooled'])
                for bi, (t0, n) in enumerate(blocks):
                    pk = bi % 2
                    o, ko = ost[oc % 2], 'post%d' % (oc % 2)
                    oc += 1
                    S.op('pe', lambda e, g=g, t0=t0, n=n, pk=pk: e.matmul(ps[pk][:, :n], pw[:, g, :], pooled[:, t0:t0 + n], start=True, stop=True), reads=['pw', 'pooled'], writes=[PSK[pk]])
                    S.op('act', lambda e, o=o, n=n, pk=pk, g=g: e.activation(o[:, :n], ps[pk][:, :n], AF.Identity, scale=vecT[:, l, 64 + g:65 + g]), reads=[PSK[pk], 'vecT'], writes=[ko])
                    S.dma('sp', yT[12 + g, :, t0:t0 + n], o[:, :n], key=ko, reads=[ko])
            S.flush()

    def phase_fox(l):
        SC = float(128 ** -0.5)
        with ExitStack() as es:
            gts = sb(es, "gts", [128, NCH, 12], F32)
            e4 = sb(es, "e4", [128, 4, NCH], F32)
            lfh = sb(es, "lfh", [128, 4, NCH], F32)
            Fin = sb(es, "Fin", [128, 4, NCH], F32)
            tot = sb(es, "tot", [128, 4, NCH], F32)
            Fend = sb(es, "Fend", [128, 4, NCH], F32)
            Fg = sb(es, "Fg", [128, 4, NCH], F32)
            bias = sb(es, "bias", [128, NCH], F32)
            qTt = sb(es, "qTt", [128, LP], BF16)
            kTt = sb(es, "kTt", [128, LP], BF16)
            vtok = sb(es, "vtok", [128, NCH, 128], BF16)
            pT = [sb(es, "pT%d" % i, [128, 512], BF16) for i in range(2)]
            rden = sb(es, "rden", [128, 512], F32)
            ost = [sb(es, "fost%d" % i, [128, 512], BF16) for i in range(2)]
            for c0_ in range(0, NCH, 8):
                c1_ = min(NCH, c0_ + 8)
                S.dma('sp', gts[:, c0_:c1_, :], gates[c0_ * 128:c1_ * 128, :].rearrange("(c p) g -> p c g", p=128), key='gts', writes=['gts'])
            for h in range(4):
                S.op('act', lambda e, h=h: e.activation(e4[:, h, :], gts[:, :, 8 + h], AF.Exp, scale=-1.0), reads=['gts'], writes=['e4'])
            S.op('act', lambda e: e.activation(lfh, e4, AF.Ln, bias=1.0), reads=['e4'], writes=['lfh'])
            S.op('dve', lambda e: e.tensor_scalar(lfh, lfh, -1.0, None, ALU.mult), reads=['lfh'], writes=['lfh'])
            S.op('dve', lambda e: e.tensor_scalar(lfh[:, :, 0], lfh[:, :, 0], padc[:, 0:1], None, ALU.mult), reads=['lfh', 'padc'], writes=['lfh'])
            lff = lfh.rearrange("p h c -> p (h c)")
            S.op('pe', lambda e: e.matmul(ps[0][:, 0:4 * NCH], tri_f, lff, start=True, stop=True), reads=['tri_f', 'lfh'], writes=['ps0'])
            S.op('pe', lambda e: e.matmul(ps[1][:, 0:4 * NCH], ones_f, lff, start=True, stop=True), reads=['ones_f', 'lfh'], writes=['ps1'])
            S.op('dve', lambda e: e.tensor_copy(Fin.rearrange("p h c -> p (h c)"), ps[0][:, 0:4 * NCH]), reads=['ps0'], writes=['Fin'])
            S.op('dve', lambda e: e.tensor_copy(tot.rearrange("p h c -> p (h c)"), ps[1][:, 0:4 * NCH]), reads=['ps1'], writes=['tot'])
            for h in range(4):
                S.op('dve', lambda e, h=h: e.tensor_tensor_scan(Fend[:, h, :], ones_f[:, 0:NCH], tot[:, h, :], 0.0, ALU.mult, ALU.add), reads=['tot', 'ones_f'], writes=['Fend'])
            S.op('dve', lambda e: e.tensor_tensor(Fg, Fend, tot, ALU.subtract), reads=['Fend', 'tot'], writes=['Fg'])
            S.op('dve', lambda e: e.tensor_tensor(Fg, Fg, Fin, ALU.add), reads=['Fg', 'Fin'], writes=['Fg'])
            oc = 0
            pc = 0
            for h in range(4):
                S.dma('sp', qTt, fqT[h], key='fq', writes=['qTt'])
                S.dma('sp', kTt, fkT[h], key='fk', writes=['kTt'])
                for c0_ in range(0, NCH, 8):
                    c1_ = min(NCH, c0_ + 8)
                    S.dma('sp', vtok[:, c0_:c1_, :], fv[c0_ * 128:c1_ * 128, :].rearrange("(c p) (h d) -> p c h d", p=128, h=4)[:, :, h, :], key='fvv', writes=['vtok'])
                for qb, (t0, n) in enumerate(blocks):
                    c0 = t0 // 128
                    clast = c0 + n // 128 - 1
                    pO, pD = 2 + (qb % 2), 4 + (qb % 2)
                    S.op('dve', lambda e, h=h, clast=clast: e.tensor_scalar(bias[:, 0:clast + 1], Fg[:, h, 0:clast + 1], Fend[:, h, clast:clast + 1], -1.0, ALU.subtract, ALU.mult),
                         reads=['Fg', 'Fend'], writes=['bias'])
                    S.op('dve', lambda e: e.tensor_tensor(bias[:, 0:1], bias[:, 0:1], padc[:, 1:2], ALU.add), reads=['bias', 'padc'], writes=['bias'])
                    for kc in range(clast + 1):
                        lo = max(0, kc - c0) * 128
                        pS = pc % 2
                        p_, kp = pT[pc % 2], 'pT%d' % (pc % 2)
                        pc += 1
                        S.op('pe', lambda e, kc=kc, lo=lo, pS=pS, t0=t0, n=n: e.matmul(ps[pS][:, lo:n], kTt[:, kc * 128:(kc + 1) * 128], qTt[:, t0 + lo:t0 + n], start=True, stop=True),
                             reads=['kTt', 'qTt'], writes=[PSK[pS]])
                        S.op('act', lambda e, kc=kc, lo=lo, pS=pS, p_=p_, n=n: e.activation(p_[:, lo:n], ps[pS][:, lo:n], AF.Exp, bias=bias[:, kc:kc + 1], scale=SC),
                             reads=[PSK[pS], 'bias'], writes=[kp])
                        if kc >= c0:
                            S.op('pool', lambda e, p_=p_, lo=lo: e.tensor_tensor(p_[:, lo:lo + 128], p_[:, lo:lo + 128], tri_b, ALU.mult), reads=[kp, 'tri_b'], writes=[kp])
                        S.op('pe', lambda e, kc=kc, lo=lo, p_=p_, n=n, pO=pO, clast=clast: e.matmul(ps[pO][:, lo:n], vtok[:, kc, :], p_[:, lo:n], start=(kc == 0), stop=(kc == clast)),
                             reads=['vtok', kp], writes=[PSK[pO]])
                        S.op('pe', lambda e, kc=kc, lo=lo, p_=p_, n=n, pD=pD, clast=clast: e.matmul(ps[pD][:, lo:n], ones_b, p_[:, lo:n], start=(kc == 0), stop=(kc == clast)),
                             reads=['ones_b', kp], writes=[PSK[pD]])
                    o, ko = ost[oc % 2], 'fost%d' % (oc % 2)
                    oc += 1
                    S.op('dve', lambda e, n=n, pD=pD: e.reciprocal(rden[:, :n], ps[pD][:, :n]), reads=[PSK[pD]], writes=['rden'])
                    S.op('dve', lambda e, n=n, pO=pO, o=o: e.tensor_tensor(o[:, :n], ps[pO][:, :n], rden[:, :n], ALU.mult), reads=[PSK[pO], 'rden'], writes=[ko])
                    S.dma('sp', yT[8 + h, :, t0:t0 + n], o[:, :n], key=ko, reads=[ko])
            S.flush()
    def phase_mlstm(l):
        SC = float(128 ** -0.5)
        with ExitStack() as es:
            qc = sb(es, "qc", [128, 4, LP], BF16)
            kc_ = sb(es, "kc_", [128, 4, LP], BF16)
            cin = sb(es, "cin", [128, 3 + LP], BF16)
            acc = sb(es, "acc", [128, LP], F32)
            mlg = sb(es, "mlg", [128, 512], F32)
            vaug = [sb(es, "vaug%d" % i, [128, 4, 129], BF16) for i in range(2)]
            ot = [sb(es, "ot%d" % i, [128, 512], BF16) for i in range(2)]
            gt = [sb(es, "gt%d" % i, [128, 12], F32) for i in range(2)]
            sg = sb(es, "sg", [128, 512], F32)
            sm = sb(es, "sm", [128, 40], F32)
            Cf = sb(es, "Cf", [128, 4, 129], F32)
            Cb = sb(es, "Cb", [128, 4, 129], BF16)
            pt = [sb(es, "pt%d" % i, [128, 128], BF16) for i in range(2)]
            ktk = [sb(es, "ktk%d" % i, [128, 128], BF16) for i in range(2)]
            vw = [sb(es, "vw%d" % i, [128, 129], BF16) for i in range(2)]
            hh = [sb(es, "hh%d" % i, [128, 128], F32) for i in range(2)]
            junk = sb(es, "junk", [128, 128], F32)
            hs_ = [sb(es, "hsm%d" % i, [128, 8], F32) for i in range(2)]
            yo = [sb(es, "yo%d" % i, [128, 128], BF16) for i in range(2)]
            yst = [sb(es, "yst%d" % i, [128, 4, 128], BF16) for i in range(2)]
            S.dma('sp', mlg, P["ml_norm_g"][l:l + 1, :].partition_broadcast(128), key='mlg', writes=['mlg'])
            S.op('pool', lambda e: e.memset(cin[:, 0:3], 0.0), writes=['cin'])
            S.op('pool', lambda e: e.memset(Cf, 0.0), writes=['Cf'])
            S.op('pool', lambda e: e.memset(Cb, 0.0), writes=['Cb'])
            for i in range(2):
                S.op('pool', lambda e, i=i: e.memset(vaug[i], 1.0), writes=['vaug%d' % i])
            for j in range(8):
                S.dma('sp', cin[:, 3:], mqkT[j], key='cin', reads=['cin'], writes=['cin'])
                wc = lambda tap, j=j: vecT[:, l, 80 + tap * 8 + j:81 + tap * 8 + j]
                S.op('dve', lambda e, wc=wc: e.tensor_scalar(acc, cin[:, 3:3 + LP], wc(3), None, ALU.mult), reads=['cin', 'vecT'], writes=['acc'])
                for tap in range(3):
                    S.op('dve', lambda e, wc=wc, tap=tap: e.scalar_tensor_tensor(acc, cin[:, tap:tap + LP], wc(tap), acc, ALU.mult, ALU.add), reads=['cin', 'acc', 'vecT'], writes=['acc'])
                if j < 4:
                    S.op('act', lambda e: e.activation(acc, acc, AF.Silu), reads=['acc'], writes=['acc'])
                    S.op('dve', lambda e, j=j: e.tensor_scalar(qc[:, j, :], acc, SC, None, ALU.mult), reads=['acc'], writes=['qc'])
                else:
                    S.op('act', lambda e, j=j: e.activation(kc_[:, j - 4, :], acc, AF.Silu), reads=['acc'], writes=['kc_'])
            for c in range(NCH):
                cs = slice(c * 128, (c + 1) * 128)
                va, kva = vaug[c % 2], 'vaug%d' % (c % 2)
                o_, ko = ot[c % 2], 'ot%d' % (c % 2)
                g_, kg = gt[c % 2], 'gt%d' % (c % 2)
                ys, kys = yst[c % 2], 'yst%d' % (c % 2)
                S.dma('sp', va[:, :, 0:128], mv[cs, :].rearrange("p (h d) -> p h d", h=4), key=kva, writes=[kva])
                S.dma('sp', o_, mo[cs, :], key=ko, writes=[ko])
                S.dma('sp', g_, gates[cs, :], key=kg, writes=[kg])
                S.op('act', lambda e, g_=g_: e.activation(sm[:, 0:4], g_[:, 4:8], AF.Exp, scale=-1.0), reads=[kg], writes=['sm_lf'])
                S.op('act', lambda e: e.activation(sm[:, 0:4], sm[:, 0:4], AF.Ln, bias=1.0), reads=['sm_lf'], writes=['sm_lf'])
                if c == 0:
                    S.op('dve', lambda e: e.tensor_scalar(sm[:, 0:4], sm[:, 0:4], -1.0, padc[:, 0:1], ALU.mult, ALU.mult), reads=['sm_lf', 'padc'], writes=['sm_lf'])
                else:
                    S.op('dve', lambda e: e.tensor_scalar(sm[:, 0:4], sm[:, 0:4], -1.0, None, ALU.mult), reads=['sm_lf'], writes=['sm_lf'])
                S.op('pe', lambda e: e.matmul(ps[6][:, 0:4], tri_f, sm[:, 0:4], start=True, stop=True), reads=['tri_f', 'sm_lf'], writes=['ps6'])
                S.op('pe', lambda e: e.matmul(ps[6][:, 4:8], ones_f, sm[:, 0:4], start=True, stop=True), reads=['ones_f', 'sm_lf'], writes=['ps6'])
                S.op('dve', lambda e, g_=g_: e.tensor_tensor(sm[:, 4:8], g_[:, 0:4], ps[6][:, 0:4], ALU.subtract), reads=[kg, 'ps6'], writes=['sm_w'])
                S.op('act', lambda e: e.activation(sm[:, 4:8], sm[:, 4:8], AF.Exp), reads=['sm_w'], writes=['sm_w'])
                if c == 0:
                    S.op('dve', lambda e: e.tensor_scalar(sm[:, 4:8], sm[:, 4:8], padc[:, 0:1], None, ALU.mult), reads=['sm_w', 'padc'], writes=['sm_w'])
                S.op('act', lambda e: e.activation(sm[:, 8:16], ps[6][:, 0:8], AF.Exp), reads=['ps6'], writes=['sm_e'])
                S.op('act', lambda e, o_=o_: e.activation(sg, o_, AF.Sigmoid), reads=[ko], writes=['sg'])
                for h in range(4):
                    i2 = h % 2
                    kch = kc_[:, h, cs]
                    qch = qc[:, h, cs]
                    p_, kp = pt[i2], 'pt%d' % i2
                    S.op('pe', lambda e, kch=kch, qch=qch, i2=i2: e.matmul(ps[i2][:, 0:128], kch, qch, start=True, stop=True), reads=['kc_', 'qc'], writes=[PSK[i2]])
                    S.op('dve', lambda e, h=h, i2=i2, p_=p_: e.scalar_tensor_tensor(p_, ps[i2][:, 0:128], sm[:, 4 + h:5 + h], tri_f, ALU.mult, ALU.mult),
                         reads=[PSK[i2], 'sm_w', 'tri_f'], writes=[kp])
                    S.op('pe', lambda e, kch=kch, i2=i2: e.transpose(psb[:, i2 * 128:(i2 + 1) * 128], kch, ident_b), reads=['kc_', 'ident_b'], writes=['psb%d' % i2])
                    S.op('act', lambda e, i2=i2: e.activation(ktk[i2], psb[:, i2 * 128:(i2 + 1) * 128], AF.Identity), reads=['psb%d' % i2], writes=['ktk%d' % i2])
                    S.op('pe', lambda e, p_=p_, va=va, h=h, i2=i2: e.matmul(ps[2 + i2][:, 0:129], p_, va[:, h, :], start=True, stop=False), reads=[kp, kva], writes=[PSK[2 + i2]])
                    S.op('pe', lambda e, qch=qch, h=h, i2=i2: e.matmul(ps[2 + i2][:, 0:129], qch, Cb[:, h, :], start=False, stop=True), reads=['qc', 'Cb'], writes=[PSK[2 + i2]])
                    hs = hs_[i2]
                    khs = 'hsm%d' % i2
                    nd = ps[2 + i2]
                    S.op('dve', lambda e, nd=nd, hs=hs, h=h: e.tensor_scalar(hs[:, 0:1], nd[:, 128:129], sm[:, 8 + h:9 + h], None, ALU.mult), reads=[PSK[2 + i2], 'sm_e'], writes=[khs])
                    S.op('dve', lambda e, hs=hs: e.scalar_tensor_tensor(hs[:, 1:2], hs[:, 0:1], -1.0, hs[:, 0:1], ALU.mult, ALU.max), reads=[khs], writes=[khs + 'b'])
                    S.op('dve', lambda e, hs=hs: e.tensor_scalar(hs[:, 2:3], hs[:, 1:2], 1.0, None, ALU.max), reads=[khs + 'b'], writes=[khs + 'c'])
                    S.op('dve', lambda e, hs=hs: e.reciprocal(hs[:, 3:4], hs[:, 2:3]), reads=[khs + 'c'], writes=[khs + 'd'])
                    S.op('dve', lambda e, hs=hs, h=h: e.tensor_tensor(hs[:, 4:5], hs[:, 3:4], sm[:, 8 + h:9 + h], ALU.mult), reads=[khs + 'd', 'sm_e'], writes=[khs + 'e'])
                    hh_, khh = hh[i2], 'hh%d' % i2
                    S.op('dve', lambda e, nd=nd, hs=hs, hh_=hh_: e.tensor_scalar(hh_, nd[:, 0:128], hs[:, 4:5], None, ALU.mult), reads=[PSK[2 + i2], khs + 'e'], writes=[khh])
                    S.op('act', lambda e, hh_=hh_: e.activation(junk, hh_, AF.Square), reads=[khh], writes=['junk'])
                    S.op('dve', lambda e, hs=hs: e.reduce_sum(hs[:, 5:6], junk, AX.X), reads=['junk'], writes=[khs + 'f'])
                    S.op('act', lambda e, hs=hs: e.activation(hs[:, 6:7], hs[:, 5:6], AF.Sqrt, bias=EPS, scale=1.0 / 128), reads=[khs + 'f'], writes=[khs + 'g'])
                    S.op('dve', lambda e, hs=hs: e.reciprocal(hs[:, 7:8], hs[:, 6:7]), reads=[khs + 'g'], writes=[khs + 'h'])
                    S.op('dve', lambda e, hs=hs, hh_=hh_, h=h: e.scalar_tensor_tensor(hh_, hh_, hs[:, 7:8], mlg[:, h * 128:(h + 1) * 128], ALU.mult, ALU.mult), reads=[khh, khs + 'h', 'mlg'], writes=[khh])
                    y_, ky = yo[i2], 'yo%d' % i2
                    S.op('pool', lambda e, hh_=hh_, y_=y_, h=h: e.tensor_tensor(y_, hh_, sg[:, h * 128:(h + 1) * 128], ALU.mult), reads=[khh, 'sg'], writes=[ky])
                    S.op('pe', lambda e, y_=y_, i2=i2: e.transpose(psb[:, 256 + i2 * 128:256 + (i2 + 1) * 128], y_, ident_b), reads=[ky, 'ident_b'], writes=['psby%d' % i2])
                    S.op('act', lambda e, ys=ys, h=h, i2=i2: e.activation(ys[:, h, :], psb[:, 256 + i2 * 128:256 + (i2 + 1) * 128], AF.Identity), reads=['psby%d' % i2], writes=[kys])
                    v_, kv_ = vw[i2], 'vw%d' % i2
                    S.op('pool', lambda e, v_=v_, va=va, h=h: e.tensor_scalar(v_, va[:, h, :], sm[:, 4 + h:5 + h], None, ALU.mult), reads=[kva, 'sm_w'], writes=[kv_])
                    S.op('pe', lambda e, v_=v_, i2=i2: e.matmul(ps[4 + i2][:, 0:129], ktk[i2], v_, start=True, stop=True), reads=['ktk%d' % i2, kv_], writes=[PSK[4 + i2]])
                    S.op('dve', lambda e, h=h: e.tensor_scalar(Cf[:, h, :], Cf[:, h, :], sm[:, 12 + h:13 + h], None, ALU.mult), reads=['Cf', 'sm_e'], writes=['Cf'])
                    S.op('dve', lambda e, h=h, i2=i2: e.scalar_tensor_tensor(Cf[:, h, :], ps[4 + i2][:, 0:129], sm[:, 12 + h:13 + h], Cf[:, h, :], ALU.mult, ALU.add),
                         reads=[PSK[4 + i2], 'sm_e', 'Cf'], writes=['Cf'])
                    S.op('act', lambda e, h=h: e.activation(Cb[:, h, :], Cf[:, h, :], AF.Identity), reads=['Cf'], writes=['Cb'])
                S.dma('sp', yT[4:8, :, cs].rearrange("k p t -> p k t"), ys, key=kys, reads=[kys])
            S.flush()
    def phase_s5(l):
        with ExitStack() as es:
            lamr = sb(es, "lamr", [16, 128], F32)
            lami = sb(es, "lami", [16, 128], F32)
            ldt = sb(es, "ldt", [16, 2], F32)
            pm = sb(es, "pm", [16, 12, 128], F32)
            rr = sb(es, "rr", [128, 4, 2048], F32)
            ri = sb(es, "ri", [128, 2048], mybir.dt.int32)
            sm4 = sb(es, "sm4", [128, 4, 16], F32)
            cosT = sb(es, "cosT", [128, 16, 128], F32)
            sinT = sb(es, "sinT", [128, 16, 128], F32)
            Rtab = sb(es, "Rtab", [128, 16, 128], F32)
            cs128 = sb(es, "cs128", [128, 2, 16], F32)
            bnat = sb(es, "bnat", [128, 2, 16, 16], F32)
            bb = sb(es, "bb", [128, 4, 16, 16], F32)
            Xb = sb(es, "Xb", [128, 16, 2, 128], BF16)
            BT = sb(es, "BT", [128, 16, 2, 128], BF16)
            cnat = sb(es, "cnat", [128, 2, 4, 64], F32)
            Xc = sb(es, "Xc", [128, 4, 3, 128], BF16)
            CTm = sb(es, "CTm", [128, 4, 3, 128], BF16)
            CT = sb(es, "CT", [128, 4, 4, 3, 128], BF16)
            gw = sb(es, "gw", [128, 2, 4, 512], BF16)

            def sincos(ang, n, dsin, dcos, pp):
                for k_, (dst, off) in enumerate(((dsin, 0.0), (dcos, float(np.pi / 2)))):
                    a0, a1 = rr[:pp, 0, :n], rr[:pp, 1, :n]
                    S.op('dve', lambda e, a0=a0, off=off: e.tensor_scalar(a0, ang, off, None, ALU.add), reads=['ang'], writes=['rr0'])
                    S.op('dve', lambda e, a0=a0, a1=a1: e.tensor_scalar(a1, a0, 1.0 / TWO_PI, None, ALU.mult), reads=['rr0'], writes=['rr1'])
                    S.op('dve', lambda e, a1=a1: e.tensor_copy(ri[:pp, :n], a1), reads=['rr1'], writes=['ri'])
                    S.op('dve', lambda e, a1=a1: e.tensor_copy(a1, ri[:pp, :n]), reads=['ri'], writes=['rr1'])
                    S.op('dve', lambda e, a0=a0, a1=a1: e.scalar_tensor_tensor(a0, a1, -TWO_PI, a0, ALU.mult, ALU.add), reads=['rr0', 'rr1'], writes=['rr0'])
                    S.op('dve', lambda e, a0=a0, a1=a1: e.tensor_scalar(a1, a0, float(np.pi), -TWO_PI, ALU.is_gt, ALU.mult), reads=['rr0'], writes=['rr1'])
                    S.op('dve', lambda e, a0=a0, a1=a1: e.tensor_tensor(a0, a0, a1, ALU.add), reads=['rr0', 'rr1'], writes=['rr0'])
                    S.op('dve', lambda e, a0=a0, a1=a1: e.tensor_scalar(a1, a0, float(-np.pi), TWO_PI, ALU.is_lt, ALU.mult), reads=['rr0'], writes=['rr1'])
                    S.op('dve', lambda e, a0=a0, a1=a1: e.tensor_tensor(a0, a0, a1, ALU.add), reads=['rr0', 'rr1'], writes=['rr0'])
                    S.op('act', lambda e, a0=a0, dst=dst: e.activation(dst, a0, AF.Sin), reads=['rr0'], writes=['sc%d' % k_])

            S.dma('sp', lamr, P["ssm_lam_re"][l].rearrange("(j g) p -> j (g p)", g=2), key='s5c', writes=['lam'])
            S.dma('sp', lami, P["ssm_lam_im"][l].rearrange("(j g) p -> j (g p)", g=2), key='s5c', writes=['lam'])
            S.dma('sp', ldt, P["ssm_log_dt"][l].rearrange("(j g) -> j g", g=2), key='s5c', writes=['lam'])
            S.dma('sp', bnat[:, 0], P["ssm_b_re"][l].rearrange("(j g) p h -> (g p) j h", g=2), key='s5c', writes=['bnat'])
            S.dma('sp', bnat[:, 1], P["ssm_b_im"][l].rearrange("(j g) p h -> (g p) j h", g=2), key='s5c', writes=['bnat'])
            S.dma('sp', cnat[:, 0], P["ssm_c_re"][l].rearrange("(q g) h p -> (g h) q p", g=8), key='s5c', writes=['cnat'])
            S.dma('sp', cnat[:, 1], P["ssm_c_im"][l].rearrange("(q g) h p -> (g h) q p", g=8), key='s5c', writes=['cnat'])
            for i in range(2):
                S.dma('sp', gw[:, i], glub[l, i], key='s5c', writes=['gw'])
            S.op('act', lambda e: e.activation(ldt, ldt, AF.Exp), reads=['lam'], writes=['dt'])
            for g2 in range(2):
                sl = slice(g2 * 64, (g2 + 1) * 64)
                S.op('dve', lambda e, sl=sl, g2=g2: e.tensor_scalar(pm[:, 0, sl], lami[:, sl], ldt[:, g2:g2 + 1], None, ALU.mult), reads=['lam', 'dt'], writes=['ang'])
                S.op('dve', lambda e, sl=sl, g2=g2: e.tensor_scalar(pm[:, 1, sl], lamr[:, sl], ldt[:, g2:g2 + 1], None, ALU.mult), reads=['lam', 'dt'], writes=['pm1'])
            S.op('act', lambda e: e.activation(pm[:, 2, :], pm[:, 1, :], AF.Exp), reads=['pm1'], writes=['pm2'])
            ang = pm[:, 0, :]
            sincos(ang, 128, pm[:, 3, :], pm[:, 4, :], 16)
            TT = lambda o, a, b, op, r_, w_: S.op('dve', lambda e: e.tensor_tensor(o, a, b, op), reads=r_, writes=w_)
            pmk = ['pm%d' % i for i in range(12)]
            TT(pm[:, 5, :], pm[:, 2, :], pm[:, 4, :], ALU.mult, ['pm2', 'sc1'], ['pm5'])
            S.op('dve', lambda e: e.tensor_scalar(pm[:, 5, :], pm[:, 5, :], -1.0, None, ALU.add), reads=['pm5'], writes=['pm5'])
            TT(pm[:, 6, :], pm[:, 2, :], pm[:, 3, :], ALU.mult, ['pm2', 'sc0'], ['pm6'])
            TT(pm[:, 7, :], pm[:, 5, :], lamr, ALU.mult, ['pm5', 'lam'], ['pm7'])
            TT(pm[:, 9, :], pm[:, 6, :], lami, ALU.mult, ['pm6', 'lam'], ['pm9'])
            TT(pm[:, 7, :], pm[:, 7, :], pm[:, 9, :], ALU.add, ['pm7', 'pm9'], ['pm7'])
            TT(pm[:, 8, :], pm[:, 6, :], lamr, ALU.mult, ['pm6', 'lam'], ['pm8'])
            TT(pm[:, 9, :], pm[:, 5, :], lami, ALU.mult, ['pm5', 'lam'], ['pm9'])
            TT(pm[:, 8, :], pm[:, 8, :], pm[:, 9, :], ALU.subtract, ['pm8', 'pm9'], ['pm8'])
            TT(pm[:, 9, :], lamr, lamr, ALU.mult, ['lam'], ['pm9'])
            TT(pm[:, 10, :], lami, lami, ALU.mult, ['lam'], ['pm10'])
            TT(pm[:, 9, :], pm[:, 9, :], pm[:, 10, :], ALU.add, ['pm9', 'pm10'], ['pm9'])
            S.op('dve', lambda e: e.reciprocal(pm[:, 9, :], pm[:, 9, :]), reads=['pm9'], writes=['pm9'])
            TT(pm[:, 10, :], pm[:, 7, :], pm[:, 9, :], ALU.mult, ['pm7', 'pm9'], ['pm10'])
            TT(pm[:, 11, :], pm[:, 8, :], pm[:, 9, :], ALU.mult, ['pm8', 'pm9'], ['pm11'])
            for k_, row in enumerate((0, 2, 10, 11)):
                S.op('pe', lambda e, k_=k_, row=row: e.transpose(ps[0][:, k_ * 16:(k_ + 1) * 16], pm[:, row, :], ident_f[0:16, 0:16]),
                     reads=['ang', 'pm2', 'pm10', 'pm11', 'ident_f'], writes=['ps0'])
            S.op('dve', lambda e: e.tensor_copy(sm4.rearrange("p a b -> p (a b)"), ps[0][:, 0:64]), reads=['ps0'], writes=['sm4'])
            thT, rT, krT, kiT = sm4[:, 0, :], sm4[:, 1, :], sm4[:, 2, :], sm4[:, 3, :]
            angT = rr[:, 2, :]
            S.op('dve', lambda e: e.tensor_tensor(angT.rearrange("p (j t) -> p j t", j=16), iota.unsqueeze(1).to_broadcast([128, 16, 128]),
                                                  thT.unsqueeze(2).to_broadcast([128, 16, 128]), ALU.mult), reads=['iota', 'sm4'], writes=['ang'])
            ang = angT
            sincos(ang, 2048, sinT.rearrange("p j t -> p (j t)"), cosT.rearrange("p j t -> p (j t)"), 128)
            S.op('dve', lambda e: e.tensor_copy(sinT, sinT), reads=['sc0'], writes=['sinT'])
            S.op('dve', lambda e: e.tensor_copy(cosT, cosT), reads=['sc1'], writes=['cosT'])
            ang128 = rr[:, 3, 0:16]
            S.op('dve', lambda e: e.tensor_scalar(ang128, thT, 128.0, None, ALU.mult), reads=['sm4'], writes=['ang'])
            ang = ang128
            sincos(ang, 16, cs128[:, 1, :], cs128[:, 0, :], 128)
            S.op('dve', lambda e: e.tensor_copy(cs128, cs128), reads=['sc0', 'sc1'], writes=['cs128'])
            S.op('dve', lambda e: e.tensor_copy(Rtab, rT.unsqueeze(2).to_broadcast([128, 16, 128])), reads=['sm4'], writes=['Rtab'])
            S.op('dve', lambda e: e.memset(Rtab[:, :, 0:1], 0.0), reads=['Rtab'], writes=['Rtab'])
            krb = krT.unsqueeze(2).to_broadcast([128, 16, 16])
            kib = kiT.unsqueeze(2).to_broadcast([128, 16, 16])
            TT(bb[:, 0], bnat[:, 0], krb, ALU.mult, ['bnat', 'sm4'], ['bb0'])
            TT(bb[:, 1], bnat[:, 1], kib, ALU.mult, ['bnat', 'sm4'], ['bb1'])
            TT(bb[:, 0], bb[:, 0], bb[:, 1], ALU.subtract, ['bb0', 'bb1'], ['bb0'])
            TT(bb[:, 2], bnat[:, 1], krb, ALU.mult, ['bnat', 'sm4'], ['bb2'])
            TT(bb[:, 3], bnat[:, 0], kib, ALU.mult, ['bnat', 'sm4'], ['bb3'])
            TT(bb[:, 2], bb[:, 2], bb[:, 3], ALU.add, ['bb2', 'bb3'], ['bb2'])
            S.op('pool', lambda e: e.memset(Xb, 0.0), writes=['Xb'])
            for j in range(16):
                for g2 in range(2):
                    col0 = 16 * ((2 * j + g2) % 8)
                    rs_ = slice(g2 * 64, (g2 + 1) * 64)
                    for ri_, src in enumerate((0, 2)):
                        S.op('dve' if ri_ else 'pool', lambda e, j=j, rs_=rs_, col0=col0, ri_=ri_, src=src: e.tensor_copy(Xb[rs_, j, ri_, col0:col0 + 16], bb[rs_, src, j, :]),
                             reads=['bb0', 'bb2'], writes=['Xb'])
            for j0 in range(0, 16, 4):
                for jj in range(4):
                    for ri_ in range(2):
                        k_ = jj * 2 + ri_
                        S.op('pe', lambda e, j0=j0, jj=jj, ri_=ri_, k_=k_: e.transpose(psb[:, k_ * 128:(k_ + 1) * 128], Xb[:, j0 + jj, ri_, :], ident_b), reads=['Xb', 'ident_b'], writes=['psb'])
                S.op('dve', lambda e, j0=j0: e.tensor_copy(BT[:, j0:j0 + 4].rearrange("p j r c -> p (j r c)"), psb), reads=['psb'], writes=['BT'])
            for v_, (src, sgn) in enumerate(((0, 1.0), (0, -1.0), (1, -1.0))):
                for g2 in range(2):
                    S.op('dve', lambda e, v_=v_, src=src, sgn=sgn, g2=g2: e.tensor_scalar(Xc[:, :, v_, g2 * 64:(g2 + 1) * 64], cnat[:, src], padc[:, 2 + g2:3 + g2], sgn, ALU.mult, ALU.mult),
                         reads=['cnat', 'padc'], writes=['Xc'])
            for q in range(4):
                for v_ in range(3):
                    k_ = q * 3 + v_
                    S.op('pe', lambda e, q=q, v_=v_, k_=k_: e.transpose(psb[:, (k_ % 8) * 128:(k_ % 8 + 1) * 128], Xc[:, q, v_, :], ident_b), reads=['Xc', 'ident_b'], writes=['psb'])
                    S.op('dve', lambda e, q=q, v_=v_, k_=k_: e.tensor_copy(CTm[:, q, v_, :], psb[:, (k_ % 8) * 128:(k_ % 8 + 1) * 128]), reads=['psb'], writes=['CTm'])
            S.op('pool', lambda e: e.memset(CT, 0.0), writes=['CT'])
            for jj in range(4):
                for q in range(4):
                    S.op('dve', lambda e, jj=jj, q=q: e.tensor_copy(CT[:, q, jj, :, 32 * jj:32 * jj + 32], CTm[:, q, :, 32 * jj:32 * jj + 32]), reads=['CTm', 'CT'], writes=['CT'])
            S.flush()
            uin = [sb(es, "s5u%d" % i, [128, 4, 128], BF16) for i in range(2)]
            tmp = [sb(es, "s5t%d" % i, [128, 512], F32) for i in range(4)]
            vre = sb(es, "vre", [128, 512], F32)
            vim = sb(es, "vim", [128, 512], F32)
            wre = sb(es, "wre", [128, 512], F32)
            wim = sb(es, "wim", [128, 512], F32)
            Ap = [sb(es, "Ap%d" % i, [128, 512], BF16) for i in range(4)]
            car = sb(es, "car", [128, 6, 16], F32)
            yg = sb(es, "yg", [128, 128], F32)
            t2_ = sb(es, "t2_", [128, 128], F32)
            sgm = sb(es, "sgm", [128, 128], F32)
            gl = [sb(es, "gl%d" % i, [128, 4, 128], BF16) for i in range(2)]
            sgt = sb(es, "sgt", [128, 4, 128], F32)
            yst = [sb(es, "s5y%d" % i, [128, 4, 128], BF16) for i in range(2)]
            S.op('pool', lambda e: e.memset(car, 0.0), writes=['car'])
            for c in range(NCH):
                cs = slice(c * 128, (c + 1) * 128)
                u_, ku = uin[c % 2], 's5u%d' % (c % 2)
                gl_, kgl = gl[c % 2], 'gl%d' % (c % 2)
                ys, kys = yst[c % 2], 's5y%d' % (c % 2)
                S.dma('sp', u_, uT[:, :, cs].rearrange("k p t -> p k t"), key=ku, writes=[ku])
                for qd in range(4):
                    pR, pI, pY = (qd % 2) * 2, (qd % 2) * 2 + 1, 4 + (qd % 2)
                    j4 = slice(4 * qd, 4 * qd + 4)
                    cosq = cosT[:, j4, :].rearrange("p j t -> p (j t)")
                    sinq = sinT[:, j4, :].rearrange("p j t -> p (j t)")
                    for jj in range(4):
                        j = 4 * qd + jj
                        S.op('pe', lambda e, j=j, jj=jj, qd=qd, u_=u_, pR=pR: e.matmul(ps[pR][:, jj * 128:(jj + 1) * 128], BT[:, j, 0, :], u_[:, qd, :], start=True, stop=True), reads=['BT', ku], writes=[PSK[pR]])
                        S.op('pe', lambda e, j=j, jj=jj, qd=qd, u_=u_, pI=pI: e.matmul(ps[pI][:, jj * 128:(jj + 1) * 128], BT[:, j, 1, :], u_[:, qd, :], start=True, stop=True), reads=['BT', ku], writes=[PSK[pI]])
                    S.op('dve', lambda e, pR=pR, cosq=cosq: e.tensor_tensor(tmp[0], ps[pR], cosq, ALU.mult), reads=[PSK[pR], 'cosT'], writes=['s5t0'])
                    S.op('dve', lambda e, pI=pI, sinq=sinq: e.tensor_tensor(tmp[1], ps[pI], sinq, ALU.mult), reads=[PSK[pI], 'sinT'], writes=['s5t1'])
                    S.op('dve', lambda e, pI=pI, cosq=cosq: e.tensor_tensor(tmp[2], ps[pI], cosq, ALU.mult), reads=[PSK[pI], 'cosT'], writes=['s5t2'])
                    S.op('dve', lambda e, pR=pR, sinq=sinq: e.tensor_tensor(tmp[3], ps[pR], sinq, ALU.mult), reads=[PSK[pR], 'sinT'], writes=['s5t3'])
                    S.op('pool', lambda e: e.tensor_tensor(vre, tmp[0], tmp[1], ALU.add), reads=['s5t0', 's5t1'], writes=['vre'])
                    S.op('pool', lambda e: e.tensor_tensor(vim, tmp[2], tmp[3], ALU.subtract), reads=['s5t2', 's5t3'], writes=['vim'])
                    v3r = vre.rearrange("p (j t) -> p j t", j=4)
                    v3i = vim.rearrange("p (j t) -> p j t", j=4)
                    S.op('dve', lambda e, v3r=v3r, j4=j4: e.tensor_tensor(v3r[:, :, 0], v3r[:, :, 0], car[:, 0, j4], ALU.add), reads=['vre', 'car'], writes=['vre'])
                    S.op('dve', lambda e, v3i=v3i, j4=j4: e.tensor_tensor(v3i[:, :, 0], v3i[:, :, 0], car[:, 1, j4], ALU.add), reads=['vim', 'car'], writes=['vim'])
                    Rq = Rtab[:, j4, :].rearrange("p j t -> p (j t)")
                    S.op('dve', lambda e, Rq=Rq: e.tensor_tensor_scan(wre, Rq, vre, 0.0, ALU.mult, ALU.add), reads=['Rtab', 'vre'], writes=['wre'])
                    S.op('dve', lambda e, Rq=Rq: e.tensor_tensor_scan(wim, Rq, vim, 0.0, ALU.mult, ALU.add), reads=['Rtab', 'vim'], writes=['wim'])
                    w3r = wre.rearrange("p (j t) -> p j t", j=4)
                    w3i = wim.rearrange("p (j t) -> p j t", j=4)
                    c128, s128 = cs128[:, 0, j4], cs128[:, 1, j4]
                    S.op('dve', lambda e, w3r=w3r, c128=c128: e.tensor_tensor(car[:, 2, 0:4], w3r[:, :, 127], c128, ALU.mult), reads=['wre', 'cs128'], writes=['car2'])
                    S.op('dve', lambda e, w3i=w3i, s128=s128: e.tensor_tensor(car[:, 3, 0:4], w3i[:, :, 127], s128, ALU.mult), reads=['wim', 'cs128'], writes=['car3'])
                    S.op('dve', lambda e, w3r=w3r, s128=s128: e.tensor_tensor(car[:, 4, 0:4], w3r[:, :, 127], s128, ALU.mult), reads=['wre', 'cs128'], writes=['car4'])
                    S.op('dve', lambda e, w3i=w3i, c128=c128: e.tensor_tensor(car[:, 5, 0:4], w3i[:, :, 127], c128, ALU.mult), reads=['wim', 'cs128'], writes=['car5'])
                    S.op('dve', lambda e: e.tensor_tensor(car[:, 2, 0:4], car[:, 2, 0:4], car[:, 3, 0:4], ALU.subtract), reads=['car2', 'car3'], writes=['car2'])
                    S.op('dve', lambda e: e.tensor_tensor(car[:, 4, 0:4], car[:, 4, 0:4], car[:, 5, 0:4], ALU.add), reads=['car4', 'car5'], writes=['car4'])
                    S.op('dve', lambda e, j4=j4: e.tensor_tensor(car[:, 0, j4], car[:, 2, 0:4], rT[:, j4], ALU.mult), reads=['car2', 'sm4', 'car'], writes=['car'])
                    S.op('dve', lambda e, j4=j4: e.tensor_tensor(car[:, 1, j4], car[:, 4, 0:4], rT[:, j4], ALU.mult), reads=['car4', 'sm4', 'car'], writes=['car'])
                    S.op('pool', lambda e, cosq=cosq: e.tensor_tensor(Ap[0], wre, cosq, ALU.mult), reads=['wre', 'cosT'], writes=['Ap0'])
                    S.op('pool', lambda e, sinq=sinq: e.tensor_tensor(Ap[1], wim, sinq, ALU.mult), reads=['wim', 'sinT'], writes=['Ap1'])
                    S.op('dve', lambda e, sinq=sinq: e.tensor_tensor(Ap[2], wre, sinq, ALU.mult), reads=['wre', 'sinT'], writes=['Ap2'])
                    S.op('dve', lambda e, cosq=cosq: e.tensor_tensor(Ap[3], wim, cosq, ALU.mult), reads=['wim', 'cosT'], writes=['Ap3'])
                    n_mm = 0
                    for jj in range(4):
                        for a_, v_ in ((0, 0), (1, 1), (2, 2), (3, 2)):
                            S.op('pe', lambda e, jj=jj, a_=a_, v_=v_, qd=qd, pY=pY, n_mm=n_mm: e.matmul(ps[pY][:, 0:128], CT[:, qd, jj, v_, :], Ap[a_][:, jj * 128:(jj + 1) * 128],
                                                                                                  start=(n_mm == 0), stop=(n_mm == 15)), reads=['CT', 'Ap%d' % a_], writes=[PSK[pY]])
                            n_mm += 1
                    S.op('dve', lambda e, u_=u_, qd=qd, pY=pY: e.scalar_tensor_tensor(yg, u_[:, qd, :], vecT[:, l, 68 + qd:69 + qd], ps[pY][:, 0:128], ALU.mult, ALU.add), reads=[ku, 'vecT', PSK[pY]], writes=['yg'])
                    S.op('dve', lambda e: e.tensor_tensor(t2_, yg, yg, ALU.mult), reads=['yg'], writes=['t2_'])
                    S.op('dve', lambda e: e.tensor_scalar(t2_, t2_, 0.044715, 1.0, ALU.mult, ALU.add), reads=['t2_'], writes=['t2_'])
                    S.op('dve', lambda e: e.tensor_tensor(t2_, t2_, yg, ALU.mult), reads=['t2_', 'yg'], writes=['t2_'])
                    S.op('act', lambda e: e.activation(sgm, t2_, AF.Sigmoid, scale=1.5957691216057308), reads=['t2_'], writes=['sgm'])
                    S.op('dve', lambda e, gl_=gl_, qd=qd: e.tensor_tensor(gl_[:, qd, :], yg, sgm, ALU.mult), reads=['yg', 'sgm'], writes=[kgl])
                for ot_ in range(8):
                    pz = 2 + ot_ // 4 if False else (0 if ot_ < 4 else 1)
                    for kt in range(4):
                        S.op('pe', lambda e, ot_=ot_, kt=kt, gl_=gl_, pz=pz: e.matmul(ps[pz][:, (ot_ % 4) * 128:(ot_ % 4 + 1) * 128], gw[:, ot_ // 4, kt, (ot_ % 4) * 128:(ot_ % 4 + 1) * 128], gl_[:, kt, :],
                                                                                       start=(kt == 0), stop=(kt == 3)), reads=['gw', kgl], writes=[PSK[pz]])
                for o4 in range(4):
                    S.op('act', lambda e, o4=o4: e.activation(sgt[:, o4, :], ps[1][:, o4 * 128:(o4 + 1) * 128], AF.Sigmoid, bias=vecT[:, l, 76 + o4:77 + o4]), reads=['ps1', 'vecT'], writes=['sgt'])
                    S.op('dve', lambda e, o4=o4, ys=ys: e.scalar_tensor_tensor(ys[:, o4, :], ps[0][:, o4 * 128:(o4 + 1) * 128], vecT[:, l, 72 + o4:73 + o4], sgt[:, o4, :], ALU.add, ALU.mult),
                         reads=['ps0', 'vecT', 'sgt'], writes=[kys])
                S.dma('sp', yT[0:4, :, cs].rearrange("k p t -> p k t"), ys, key=kys, reads=[kys])
            S.flush()

    for l in range(DEPTH):
        phase_A(l)
        if stop_after == "A":
            return nc
        phase_s5(l)
        phase_mlstm(l)
        phase_fox(l)
        phase_pool(l)
        if stop_after == "mix":
            return nc
        phase_C(l)
        if stop_after == "C":
            return nc
    phase_Z()
    pes.close()
    return nc


def _in_maps(inputs, SEQ, DEPTH, n_cores):
    consts = host_consts()
    maps = []
    B = inputs["x"].shape[0]
    for c in range(n_cores):
        m = {"x": np.ascontiguousarray(inputs["x"][c % B], dtype=np.float32), "meta_tokens": np.asarray(inputs["meta_tokens"], np.float32)}
        for n, _ in PARAMS:
            m[n] = np.ascontiguousarray(inputs[n][:DEPTH], dtype=np.float32)
        m.update(consts)
        maps.append(m)
    return maps


def kernel(**inputs):
    SEQ, DEPTH = 4096, 2
    B = inputs["x"].shape[0]
    nc = build_program(SEQ, DEPTH)
    res = run_bass_kernel_spmd(nc, _in_maps(inputs, SEQ, DEPTH, 8), core_ids=list(range(8)))
    return np.stack([np.asarray(res.results[b]["out"], dtype=np.float32) for b in range(B)], axis=0)
```

```python
import numpy as np
import concourse.bass as bass
import concourse.mybir as mybir
from concourse.bass_utils import run_bass_kernel_spmd

F32 = mybir.dt.float32
BF16 = mybir.dt.bfloat16
AF = mybir.ActivationFunctionType
ALU = mybir.AluOpType
AX = mybir.AxisListType


class Sched:
    def __init__(self, nc):
        self.nc = nc
        self.engs = {'pe': nc.tensor, 'act': nc.scalar, 'dve': nc.vector, 'pool': nc.gpsimd, 'sp': nc.sync}
        self.ops = []
        self.last_w = {}
        self.readers = {}
        self.dma_cnt = {}
        self.esem = {e: nc.alloc_semaphore("s_" + e) for e in self.engs}
        self.dsem = {}
        self.ebase = {e: 0 for e in self.engs}
        self.dma_issued = {}

    def _deps(self, reads, writes):
        deps = set()
        for k in reads:
            if k in self.last_w:
                deps.add(self.last_w[k])
        for k in writes:
            if k in self.last_w:
                deps.add(self.last_w[k])
            deps.update(self.readers.get(k, ()))
        return deps

    def _record(self, oid, reads, writes):
        for k in writes:
            self.last_w[k] = oid
            self.readers[k] = []
        for k in reads:
            self.readers.setdefault(k, []).append(oid)

    @staticmethod
    def _bank(reads, writes):
        bk = lambda k: 'psb' if k.startswith('psb') else k
        w = [bk(k) for k in writes] + [bk(k) for k in reads if k.startswith('ps')]
        r = [k for k in reads if not k.startswith('ps')]
        return r, w

    def op(self, eng, fn, reads=(), writes=()):
        reads, writes = self._bank(list(reads), list(writes))
        deps = self._deps(reads, writes)
        oid = len(self.ops)
        self.ops.append(dict(eng=eng, fn=fn, deps=deps, dma=None))
        self._record(oid, reads, writes)
        return oid

    def dma(self, eng, out, in_, key, reads=(), writes=(), **kw):
        deps = self._deps(reads, writes)
        oid = len(self.ops)
        n = self.dma_cnt.get(key, 0) + 1
        self.dma_cnt[key] = n
        self.ops.append(dict(eng=eng, fn=lambda e: e.dma_start(out=out, in_=in_, **kw), deps=deps, dma=(key, n)))
        self._record(oid, reads, writes)
        return oid

    def flush(self):
        nc = self.nc
        ops = self.ops
        if not ops:
            return
        eng_list = {e: [] for e in self.engs}
        for oid, o in enumerate(ops):
            o['pos'] = len(eng_list[o['eng']])
            eng_list[o['eng']].append(oid)
        seen = {e: {} for e in self.engs}
        need_inc = set()
        for oid, o in enumerate(ops):
            e = o['eng']
            want = {}
            for d in o['deps']:
                do = ops[d]
                if do['dma'] is not None:
                    key, n = do['dma']
                    want[('d', key)] = max(want.get(('d', key), 0), n)
                else:
                    if do['eng'] == 'pe' and e == 'pe' and o['dma'] is None:
                        continue
                    want[('e', do['eng'])] = max(want.get(('e', do['eng']), -1), do['pos'])
            waits = []
            for k, v in want.items():
                if seen[e].get(k, -1) >= v:
                    continue
                seen[e][k] = v
                waits.append((k, v))
                if k[0] == 'e':
                    need_inc.add(eng_list[k[1]][v])
            o['waits'] = waits
        issued = self.dma_issued
        for oid, o in enumerate(ops):
            nw = []
            for k, v in o['waits']:
                if k[0] == 'd':
                    v = max(v, issued.get(k[1], 0))
                nw.append((k, v))
            o['waits'] = nw
            if o['dma'] is not None:
                issued[o['dma'][0]] = o['dma'][1]
        for k in self.dma_cnt:
            if k not in self.dsem:
                self.dsem[k] = nc.alloc_semaphore("d_%d" % len(self.dsem))
        inc_prefix = {}
        for e, lst in eng_list.items():
            c = self.ebase[e]
            for oid in lst:
                if oid in need_inc:
                    c += 1
                    inc_prefix[oid] = c
            self.ebase[e] = c
        finals = [(k, n) for k, n in self.dma_cnt.items()]

        def emit(e, eng):
            for oid in eng_list[e]:
                o = ops[oid]
                for k, v in o['waits']:
                    if k[0] == 'd':
                        eng.wait_ge(self.dsem[k[1]], 16 * v)
                    else:
                        eng.wait_ge(self.esem[k[1]], inc_prefix[eng_list[k[1]][v]])
                ins = o['fn'](eng)
                if o['dma'] is not None:
                    ins.then_inc(self.dsem[o['dma'][0]], 16)
                elif oid in need_inc:
                    ins.then_inc(self.esem[e], 1)
            if e == 'sp':
                for k, n in finals:
                    eng.wait_ge(self.dsem[k], 16 * n)

        with nc.Block() as block:
            @block.sync
            def _(eng):
                emit('sp', eng)

            @block.tensor
            def _(eng):
                emit('pe', eng)

            @block.scalar
            def _(eng):
                emit('act', eng)

            @block.vector
            def _(eng):
                emit('dve', eng)

            @block.gpsimd
            def _(eng):
                emit('pool', eng)
        nc.all_engine_barrier()
        self.ops = []
        self.last_w = {}
        self.readers = {}


D = 2048
KT = 16
INC = 4620
DFF = 8192
EPS = 1e-6
SEGS = [("s_u", 0), ("m_q", 512), ("m_k", 1024), ("m_v", 1536), ("m_o", 2048),
        ("f_q", 2568), ("f_k", 3080), ("f_v", 3592), ("p_u", 4108)]
PARAMS = [("g_pre_mix", [D]), ("g_post_mix", [D]), ("g_pre_ffn", [D]), ("g_post_ffn", [D]),
          ("w_in", [D, INC]), ("ml_gate_bias", [8]), ("fx_gate_bias", [4]),
          ("ssm_lam_re", [32, 64]), ("ssm_lam_im", [32, 64]), ("ssm_log_dt", [32]),
          ("ssm_b_re", [32, 64, 16]), ("ssm_b_im", [32, 64, 16]), ("ssm_c_re", [32, 16, 64]),
          ("ssm_c_im", [32, 16, 64]), ("ssm_d", [32, 16]), ("ssm_glu_w", [512, 1024]), ("ssm_glu_b", [1024]),
          ("ml_conv_w", [4, 1024]), ("ml_norm_g", [512]), ("pool_w", [4, 128, 128]), ("pool_scale", [512]),
          ("w_out", [D, D]), ("mlp_w1", [D, DFF]), ("mlp_w2", [DFF, D])]
TWO_PI = float(2.0 * np.pi)


def host_consts():
    tri = (np.arange(128)[:, None] <= np.arange(128)[None, :]).astype(np.float32)
    pad = np.zeros((128, 4), np.float32)
    pad[112:, 0] = 1.0
    pad[:112, 1] = -1e30
    pad[:, 2] = ((np.arange(128) // 16) % 2 == 0)
    pad[:, 3] = ((np.arange(128) // 16) % 2 == 1)
    corr = np.zeros((128, 4, 16), np.float32)
    for g, w in enumerate((2, 4, 8, 16)):
        corr[:, g, :] = w / np.minimum(np.arange(1, 17), w)
    iota = np.broadcast_to(np.arange(128, dtype=np.float32)[None, :], (128, 128)).copy()
    ident = np.eye(128, dtype=np.float32)
    return {"c_tri": tri, "c_pad": pad, "c_corr": corr, "c_iota": iota, "c_ident": ident}


def build_program(SEQ, DEPTH, debug=False, stop_after=None):
    nc = bass.Bass("TRN2", target_bir_lowering=False)
    LP = SEQ + 128
    NCH = LP // 128
    blocks = [(t0, min(512, LP - t0)) for t0 in range(0, LP, 512)]

    def din(name, shape):
        return nc.dram_tensor(name, list(shape), F32, kind="ExternalInput").ap()

    x = din("x", [SEQ, D])
    meta = din("meta_tokens", [16, D])
    P = {n: din(n, [DEPTH] + sh) for n, sh in PARAMS}
    c_tri = din("c_tri", [128, 128])
    c_pad = din("c_pad", [128, 4])
    c_corr = din("c_corr", [128, 4, 16])
    c_iota = din("c_iota", [128, 128])
    c_ident = din("c_ident", [128, 128])
    out = nc.dram_tensor("out", [SEQ, D], F32, kind="ExternalOutput").ap()
    dbg_kind = "ExternalOutput" if debug else "Internal"

    def scr(name, shape, dt, dbg=False):
        return nc.dram_tensor(name, list(shape), dt, kind=(dbg_kind if dbg else "Internal")).ap()

    hT = scr("hT", [KT, 128, LP], F32, True)
    uT = scr("uT", [4, 128, LP], BF16, True)
    mqkT = scr("mqkT", [8, 128, LP], BF16, True)
    mv = scr("mv", [LP, 512], BF16, True)
    mo = scr("mo", [LP, 512], BF16, True)
    gates = scr("gates", [LP, 12], F32, True)
    fqT = scr("fqT", [4, 128, LP], BF16, True)
    fkT = scr("fkT", [4, 128, LP], BF16, True)
    fv = scr("fv", [LP, 512], BF16, True)
    puT = scr("puT", [4, 128, LP], BF16, True)
    yT = scr("yT", [KT, 128, LP], BF16, True)
    s5scr = scr("s5scr", [8, 2048], F32)
    winb = scr("winb", [DEPTH, 9, 128, KT, 512], BF16)
    wgb = scr("wgb", [DEPTH, 128, KT, 12], BF16)
    woutb = scr("woutb", [DEPTH, 4, 128, KT, 512], BF16)
    w1b = scr("w1b", [DEPTH, 16, 128, KT, 512], BF16)
    w2b = scr("w2b", [DEPTH, 16, 128, KT, 512], BF16)
    glub = scr("glub", [DEPTH, 2, 128, 4, 512], BF16)
    poolwb = scr("poolwb", [DEPTH, 128, 4, 128], BF16)

    S = Sched(nc)
    ps = [nc.alloc_psum_tensor("ps%d" % i, [128, 512], F32).ap() for i in range(7)]
    psb = nc.alloc_psum_tensor("psb", [128, 1024], BF16).ap()
    PSK = ["ps%d" % i for i in range(7)]

    from contextlib import ExitStack

    uid = [0]

    def sb(es, name, shape, dt):
        uid[0] += 1
        return es.enter_context(nc.sbuf_tensor("%s_%d" % (name, uid[0]), list(shape), dt)).ap()

    pes = ExitStack()
    ident_f = sb(pes, "ident_f", [128, 128], F32)
    ident_b = sb(pes, "ident_b", [128, 128], BF16)
    ones_b = sb(pes, "ones_b", [128, 128], BF16)
    ones_f = sb(pes, "ones_f", [128, 128], F32)
    tri_f = sb(pes, "tri_f", [128, 128], F32)
    tri_b = sb(pes, "tri_b", [128, 128], BF16)
    padc = sb(pes, "padc", [128, 4], F32)
    corr = sb(pes, "corr", [128, 4, 16], F32)
    iota = sb(pes, "iota", [128, 128], F32)
    vecT = sb(pes, "vecT", [128, DEPTH, 112], F32)
    S.dma('sp', ident_f, c_ident, key='c', writes=['ident_f'])
    S.dma('sp', tri_f, c_tri, key='c', writes=['tri_f'])
    S.dma('sp', padc, c_pad, key='c', writes=['padc'])
    S.dma('sp', corr, c_corr, key='c', writes=['corr'])
    S.dma('sp', iota, c_iota, key='c', writes=['iota'])
    S.op('dve', lambda e: e.tensor_copy(ident_b, ident_f), reads=['ident_f'], writes=['ident_b'])
    S.op('dve', lambda e: e.tensor_copy(tri_b, tri_f), reads=['tri_f'], writes=['tri_b'])
    S.op('dve', lambda e: e.memset(ones_b, 1.0), writes=['ones_b'])
    S.op('dve', lambda e: e.memset(ones_f, 1.0), writes=['ones_f'])
    with ExitStack() as es:
        vs = sb(es, "vs", [112, 128], F32)
        for l in range(DEPTH):
            r = 0
            for nm, nrow in (("g_pre_mix", 16), ("g_post_mix", 16), ("g_pre_ffn", 16), ("g_post_ffn", 16),
                             ("pool_scale", 4), ("ssm_d", 4), ("ssm_glu_b", 8)):
                src = P[nm][l]
                if nm == "ssm_d":
                    src = src.rearrange("g h -> (g h)")
                S.dma('sp', vs[r:r + nrow, :], src.rearrange("(r c) -> r c", c=128), key='c', writes=['vs'])
                r += nrow
            S.dma('sp', vs[80:112, :], P["ml_conv_w"][l].rearrange("j (t c) -> (j t) c", c=128), key='c', writes=['vs'])
            S.op('pe', lambda e: e.transpose(ps[0][:, 0:112], vs, ident_f[0:112, 0:112]), reads=['vs', 'ident_f'], writes=['ps0'])
            S.op('dve', lambda e, l=l: e.tensor_copy(vecT[:, l, :], ps[0][:, 0:112]), reads=['ps0'], writes=['vecT'])
        def cast_panel(dst, src_rows, c0, nk, ncol=512):
            sv = src_rows.rearrange("(kt p) c -> p kt c", p=128)
            for k0 in range(0, nk, 4):
                k1 = min(nk, k0 + 4)
                S.dma('pool', dst[:, k0:k1, :], sv[:, k0:k1, c0:c0 + ncol], key='cast')
        for l in range(DEPTH):
            for i, (_, c0) in enumerate(SEGS):
                cast_panel(winb[l, i], P["w_in"][l], c0, KT)
            sv = P["w_in"][l].rearrange("(kt p) c -> p kt c", p=128)
            S.dma('pool', wgb[l][:, :, 0:8], sv[:, :, 2560:2568], key='cast')
            S.dma('pool', wgb[l][:, :, 8:12], sv[:, :, 4104:4108], key='cast')
            for i in range(4):
                cast_panel(woutb[l, i], P["w_out"][l], i * 512, KT)
            for i in range(16):
                cast_panel(w1b[l, i], P["mlp_w1"][l], i * 512, KT)
            for je in range(4):
                for kq in range(4):
                    cast_panel(w2b[l, je * 4 + kq], P["mlp_w2"][l][kq * 2048:(kq + 1) * 2048, :], je * 512, KT)
            for i in range(2):
                cast_panel(glub[l, i], P["ssm_glu_w"][l], i * 512, 4)
            S.dma('pool', poolwb[l], P["pool_w"][l].rearrange("g c d -> c g d"), key='cast')
        xin = [sb(es, "xin%d" % i, [128, D], F32) for i in range(2)]
        hst = [sb(es, "hst%d" % i, [128, KT, 128], F32) for i in range(2)]
        for c in range(NCH):
            xi, hs = xin[c % 2], hst[c % 2]
            kx, kh = "xin%d" % (c % 2), "hst%d" % (c % 2)
            if c == 0:
                S.op('pool', lambda e, xi=xi: e.memset(xi, 0.0), writes=[kx])
                S.dma('sp', xi[112:128, :], meta, key='xin', reads=[kx], writes=[kx])
            else:
                S.dma('sp', xi, x[(c - 1) * 128:c * 128, :], key='xin', writes=[kx])
            for q in range(4):
                pk = ps[q % 2]
                for j in range(4):
                    kt = q * 4 + j
                    S.op('pe', lambda e, pk=pk, j=j, kt=kt, xi=xi: e.transpose(pk[:, j * 128:(j + 1) * 128], xi[:, kt * 128:(kt + 1) * 128], ident_f),
                         reads=[kx, 'ident_f'], writes=[PSK[q % 2]])
                S.op('act' if q % 2 else 'dve',
                     (lambda e, pk=pk, q=q, hs=hs: e.activation(hs[:, q * 4:(q + 1) * 4, :], pk.rearrange("p (j t) -> p j t", j=4), AF.Identity)) if q % 2 else
                     (lambda e, pk=pk, q=q, hs=hs: e.tensor_copy(hs[:, q * 4:(q + 1) * 4, :], pk.rearrange("p (j t) -> p j t", j=4))),
                     reads=[PSK[q % 2]], writes=[kh])
            S.dma('sp', hT[:, :, c * 128:(c + 1) * 128].rearrange("k p t -> p k t"), hs, key='hst', reads=[kh], writes=['hT'])
        S.flush()
    if stop_after == "p0":
        return nc
    def evac(i, out_ap, in_ap, reads, writes):
        if i % 2:
            S.op('act', lambda e: e.activation(out_ap, in_ap, AF.Identity), reads=reads, writes=writes)
        else:
            S.op('dve', lambda e: e.tensor_copy(out_ap, in_ap), reads=reads, writes=writes)

    def sumsq_steps(src, ksrc, n, sq, rstd, dim, krstd='rstd', sq_eng='act', ksq='sq'):
        kf = ksrc if callable(ksrc) else (lambda kt: ksrc)
        steps = []
        for kt in range(KT):
            def st(kt=kt):
                sl = sq[kt % 2]
                if sq_eng == 'act':
                    S.op('act', lambda e: e.activation(sl[:, :n], src[:, kt, :n], AF.Square), reads=[kf(kt)], writes=['%s%d' % (ksq, kt % 2)])
                else:
                    S.op(sq_eng, lambda e: e.tensor_tensor(sl[:, :n], src[:, kt, :n], src[:, kt, :n], ALU.mult), reads=[kf(kt)], writes=['%s%d' % (ksq, kt % 2)])
                S.op('pe', lambda e: e.matmul(ps[6][:, :n], ones_b, sl[:, :n], start=(kt == 0), stop=(kt == KT - 1)),
                     reads=['%s%d' % (ksq, kt % 2), 'ones_b'], writes=['ps6'])
            steps.append(st)

        def fin():
            S.op('act', lambda e: e.activation(rstd[:, :n], ps[6][:, :n], AF.Sqrt, bias=EPS, scale=1.0 / dim), reads=['ps6'], writes=[krstd])
            S.op('dve', lambda e: e.reciprocal(rstd[:, :n], rstd[:, :n]), reads=[krstd], writes=[krstd])
        steps.append(fin)
        return steps

    def sumsq_rstd(src, ksrc, n, sq, rstd, dim):
        for st in sumsq_steps(src, ksrc, n, sq, rstd, dim):
            st()

    def phase_A(l):
        with ExitStack() as es:
            hblk2 = [sb(es, "hblk%d" % i, [128, KT, 512], F32) for i in range(2)]
            xn2 = [sb(es, "xn%d" % i, [128, KT, 512], BF16) for i in range(2)]
            sq = [sb(es, "sq%d" % i, [128, 512], BF16) for i in range(2)]
            rstd2 = [sb(es, "rstd%d" % i, [128, 512], F32) for i in range(2)]
            wp = [sb(es, "wp%d" % i, [128, KT, 512], BF16) for i in range(2)]
            wg = sb(es, "wg", [128, KT, 12], BF16)
            gbt = sb(es, "gbt", [128, 12], F32)
            ost = [sb(es, "ost%d" % i, [128, 4, 512], BF16) for i in range(2)]
            tst = [sb(es, "tst%d" % i, [128, 512], BF16) for i in range(2)]
            gst = [sb(es, "gst%d" % i, [128, 12], F32) for i in range(2)]
            S.dma('sp', wg, wgb[l], key='wg', writes=['wg'])
            S.dma('sp', gbt[:, 0:8], P["ml_gate_bias"][l:l + 1, :].partition_broadcast(128), key='wg', writes=['gbt'])
            S.dma('sp', gbt[:, 8:12], P["fx_gate_bias"][l:l + 1, :].partition_broadcast(128), key='wg', writes=['gbt'])
            fm_dest = {"s_u": (uT, 0), "m_q": (mqkT, 0), "m_k": (mqkT, 4), "f_q": (fqT, 0), "f_k": (fkT, 0), "p_u": (puT, 0)}
            tm_dest = {"m_v": mv, "m_o": mo, "f_v": fv}
            wcnt = 0
            ocnt = 0
            tcnt = 0
            gcnt = 0

            def norm_steps(bi, t0, n):
                s_ = bi % 2
                hb, xb_, rs_ = hblk2[s_], xn2[s_], rstd2[s_]
                kh, kx, kr = 'hblk%d' % s_, 'xn%d' % s_, 'rstd%d' % s_

                def ld():
                    for kt in range(KT):
                        S.dma('sp', hb[:, kt, :n], hT[kt, :, t0:t0 + n], key=kh, writes=[kh])
                steps = [ld] + sumsq_steps(hb, kh, n, sq, rs_, D, krstd=kr, sq_eng='dve')
                for kt in range(KT):
                    def st(kt=kt):
                        S.op('dve', lambda e: e.scalar_tensor_tensor(xb_[:, kt, :n], hb[:, kt, :n], vecT[:, l, kt:kt + 1], rs_[:, :n], ALU.mult, ALU.mult),
                             reads=[kh, kr, 'vecT'], writes=[kx])
                    steps.append(st)
                return steps

            def blk(bi, t0, n, bg):
                nonlocal wcnt, ocnt, tcnt, gcnt
                xn = xn2[bi % 2]
                kxn = 'xn%d' % (bi % 2)
                if bg:
                    bg[0]()
                    bg = bg[1:]
                per = (len(bg) + len(SEGS) - 1) // len(SEGS) if bg else 0
                for i, (nm, c0) in enumerate(SEGS):
                    w = wp[wcnt % 2]
                    kw = 'wp%d' % (wcnt % 2)
                    wcnt += 1
                    S.dma('sp', w, winb[l, i], key=kw, writes=[kw])
                    if nm in fm_dest:
                        dst, toff = fm_dest[nm]
                        o = ost[ocnt % 2]
                        ko = 'ost%d' % (ocnt % 2)
                        ocnt += 1
                        for m in range(4):
                            for kt in range(KT):
                                S.op('pe', lambda e, w=w, m=m, kt=kt: e.matmul(ps[m][:, :n], w[:, kt, m * 128:(m + 1) * 128], xn[:, kt, :n], start=(kt == 0), stop=(kt == KT - 1)),
                                     reads=[kw, kxn], writes=[PSK[m]])
                            evac(1, o[:, m, :n], ps[m][:, :n], [PSK[m]], [ko])
                        for m in range(4):
                            S.dma('sp', dst[toff + m, :, t0:t0 + n], o[:, m, :n], key=ko, reads=[ko])
                    else:
                        dst = tm_dest[nm]
                        for i2 in range(n // 128):
                            pk = 4 + (i2 % 2)
                            ts_ = tst[tcnt % 2]
                            kts = 'tst%d' % (tcnt % 2)
                            tcnt += 1
                            for kt in range(KT):
                                S.op('pe', lambda e, w=w, kt=kt, i2=i2, pk=pk: e.matmul(ps[pk], xn[:, kt, i2 * 128:(i2 + 1) * 128], w[:, kt, :], start=(kt == 0), stop=(kt == KT - 1)),
                                     reads=[kw, kxn], writes=[PSK[pk]])
                            evac(1, ts_, ps[pk], [PSK[pk]], [kts])
                            S.dma('sp', dst[t0 + i2 * 128:t0 + (i2 + 1) * 128, :], ts_, key=kts, reads=[kts])
                    for st_ in bg[i * per:(i + 1) * per]:
                        st_()
                for st_ in bg[len(SEGS) * per:]:
                    st_()
                for i2 in range(n // 128):
                    g_ = gst[gcnt % 2]
                    kg = 'gst%d' % (gcnt % 2)
                    gcnt += 1
                    for kt in range(KT):
                        S.op('pe', lambda e, kt=kt, i2=i2: e.matmul(ps[6][:, 0:12], xn[:, kt, i2 * 128:(i2 + 1) * 128], wg[:, kt, :], start=(kt == 0), stop=(kt == KT - 1)),
                             reads=['wg', kxn], writes=['ps6'])
                    S.op('dve', lambda e, g_=g_: e.tensor_tensor(g_, ps[6][:, 0:12], gbt, ALU.add), reads=['ps6', 'gbt'], writes=[kg])
                    S.dma('sp', gates[t0 + i2 * 128:t0 + (i2 + 1) * 128, :], g_, key=kg, reads=[kg])

            pend = norm_steps(0, *blocks[0])
            for st_ in pend:
                st_()
            for bi, (t0_, n_) in enumerate(blocks):
                bg = norm_steps(bi + 1, *blocks[bi + 1]) if bi + 1 < len(blocks) else []
                blk(bi, t0_, n_, bg)
            S.flush()

    def phase_C(l):
        with ExitStack() as es:
            hblk = sb(es, "hblk", [128, KT, 512], F32)
            mix = sb(es, "mix", [128, KT, 512], F32)
            hn = sb(es, "hn", [128, KT, 512], BF16)
            big = sb(es, "big", [128, 64, 512], BF16)
            sq = [sb(es, "sq%d" % i, [128, 512], BF16) for i in range(2)]
            rstd = sb(es, "rstd", [128, 512], F32)
            rl = [sb(es, "rl%d" % i, [128, 512], F32) for i in range(2)]
            wp = [sb(es, "wp%d" % i, [128, KT, 512], BF16) for i in range(2)]
            wcnt = 0
            rcnt = 0

            def post_norm_add(n, gcol):
                sumsq_rstd(mix, lambda kt: 'mx%d' % kt, n, sq, rstd, D)
                for kt in range(KT):
                    S.op('dve', lambda e, kt=kt: e.scalar_tensor_tensor(mix[:, kt, :n], mix[:, kt, :n], vecT[:, l, gcol + kt:gcol + kt + 1], rstd[:, :n], ALU.mult, ALU.mult),
                         reads=['mx%d' % kt, 'rstd', 'vecT'], writes=['mx%d' % kt])
                    S.op('pool' if kt % 3 else 'dve', lambda e, kt=kt: e.tensor_tensor(hblk[:, kt, :n], hblk[:, kt, :n], mix[:, kt, :n], ALU.add), reads=['mx%d' % kt, 'hb%d' % kt], writes=['hb%d' % kt])

            def blk(t0, n):
                nonlocal wcnt, rcnt
                for kt in range(KT):
                    S.dma('sp', hblk[:, kt, :n], hT[kt, :, t0:t0 + n], key='hblk', writes=['hb%d' % kt])
                for kt in range(KT):
                    S.dma('sp', big[:, kt, :n], yT[kt, :, t0:t0 + n], key='ybl', writes=['big'])
                for i in range(4):
                    w = wp[wcnt % 2]
                    kw = 'wp%d' % (wcnt % 2)
                    wcnt += 1
                    S.dma('sp', w, woutb[l, i], key=kw, writes=[kw])
                    for m in range(4):
                        for kt in range(KT):
                            S.op('pe', lambda e, w=w, m=m, kt=kt: e.matmul(ps[m][:, :n], w[:, kt, m * 128:(m + 1) * 128], big[:, kt, :n], start=(kt == 0), stop=(kt == KT - 1)),
                                 reads=[kw, 'big'], writes=[PSK[m]])
                        evac(m, mix[:, i * 4 + m, :n], ps[m][:, :n], [PSK[m]], ['mx%d' % (i * 4 + m)])
                post_norm_add(n, 16)
                if t0 == 0:
                    S.op('pool', lambda e: e.memset(hblk[:, :, 0:112], 0.0), writes=['hb%d' % k_ for k_ in range(KT)])
                sumsq_rstd(hblk, lambda kt: 'hb%d' % kt, n, sq, rstd, D)
                for kt in range(KT):
                    S.op('dve', lambda e, kt=kt: e.scalar_tensor_tensor(hn[:, kt, :n], hblk[:, kt, :n], vecT[:, l, 32 + kt:33 + kt], rstd[:, :n], ALU.mult, ALU.mult),
                         reads=['hb%d' % kt, 'rstd', 'vecT'], writes=['hn'])
                for jp in range(16):
                    w = wp[wcnt % 2]
                    kw = 'wp%d' % (wcnt % 2)
                    wcnt += 1
                    S.dma('sp', w, w1b[l, jp], key=kw, writes=[kw])
                    for m in range(4):
                        for kt in range(KT):
                            S.op('pe', lambda e, w=w, m=m, kt=kt: e.matmul(ps[m][:, :n], w[:, kt, m * 128:(m + 1) * 128], hn[:, kt, :n], start=(kt == 0), stop=(kt == KT - 1)),
                                 reads=[kw, 'hn'], writes=[PSK[m]])
                        r_ = rl[rcnt % 2]
                        kr = 'rl%d' % (rcnt % 2)
                        rcnt += 1
                        S.op('act', lambda e, r_=r_, m=m: e.activation(r_[:, :n], ps[m][:, :n], AF.Relu), reads=[PSK[m]], writes=[kr])
                        S.op('dve', lambda e, r_=r_, jp=jp, m=m: e.tensor_tensor(big[:, jp * 4 + m, :n], r_[:, :n], r_[:, :n], ALU.mult), reads=[kr], writes=['big'])
                for je in range(4):
                    for kq in range(4):
                        w = wp[wcnt % 2]
                        kw = 'wp%d' % (wcnt % 2)
                        wcnt += 1
                        S.dma('sp', w, w2b[l, je * 4 + kq], key=kw, writes=[kw])
                        for m in range(4):
                            for kt in range(KT):
                                S.op('pe', lambda e, w=w, m=m, kt=kt, kq=kq: e.matmul(ps[m][:, :n], w[:, kt, m * 128:(m + 1) * 128], big[:, kq * 16 + kt, :n],
                                                                                    start=(kq == 0 and kt == 0), stop=(kq == 3 and kt == KT - 1)),
                                     reads=[kw, 'big'], writes=[PSK[m]])
                    for m in range(4):
                        evac(m, mix[:, je * 4 + m, :n], ps[m][:, :n], [PSK[m]], ['mx%d' % (je * 4 + m)])
                post_norm_add(n, 48)
                for kt in range(KT):
                    S.dma('sp', hT[kt, :, t0:t0 + n], hblk[:, kt, :n], key='hst', reads=['hb%d' % kt], writes=['hT'])

            for (t0_, n_) in blocks:
                blk(t0_, n_)
            S.flush()

    def phase_Z():
        with ExitStack() as es:
            hs = [sb(es, "hs%d" % i, [128, KT, 128], F32) for i in range(2)]
            xo = [sb(es, "xo%d" % i, [128, D], F32) for i in range(2)]
            for c in range(1, NCH):
                h_, x_ = hs[c % 2], xo[c % 2]
                kh, kx = 'hs%d' % (c % 2), 'xo%d' % (c % 2)
                S.dma('sp', h_, hT[:, :, c * 128:(c + 1) * 128].rearrange("k p t -> p k t"), key=kh, writes=[kh])
                for q in range(4):
                    for j in range(4):
                        kt = q * 4 + j
                        S.op('pe', lambda e, q=q, j=j, kt=kt, h_=h_: e.transpose(ps[q][:, j * 128:(j + 1) * 128], h_[:, kt, :], ident_f), reads=[kh, 'ident_f'], writes=[PSK[q]])
                    evac(q, x_[:, q * 512:(q + 1) * 512], ps[q], [PSK[q]], [kx])
                S.dma('sp', out[(c - 1) * 128:c * 128, :], x_, key=kx, reads=[kx])
            S.flush()
    def phase_pool(l):
        with ExitStack() as es:
            ub = [sb(es, "ub%d" % i, [128, 16 + LP], F32) for i in range(3)]
            uin = sb(es, "uin", [128, LP], BF16)
            pooled = sb(es, "pooled", [128, LP], BF16)
            t16 = sb(es, "t16", [128, 16], F32)
            pw = sb(es, "pw", [128, 4, 128], BF16)
            ost = [sb(es, "post%d" % i, [128, 512], BF16) for i in range(2)]
            S.dma('sp', pw, poolwb[l], key='pw', writes=['pw'])
            for i in range(3):
                S.op('pool', lambda e, i=i: e.memset(ub[i][:, 0:16], 0.0), writes=['ub%d' % i])
            oc = 0
            for g, w in enumerate((2, 4, 8, 16)):
                S.dma('sp', uin, puT[g], key='uin', writes=['uin'])
                S.op('act', lambda e: e.activation(ub[0][:, 16:], uin, AF.Identity), reads=['uin'], writes=['ub0'])
                a, ka = ub[0], 'ub0'
                for s_ in range(g + 1):
                    m = 2 ** s_
                    d_, kd = ub[1 + s_ % 2], 'ub%d' % (1 + s_ % 2)
                    S.op('dve', lambda e, a=a, d_=d_, m=m: e.tensor_tensor(d_[:, 16:], a[:, 16:], a[:, 16 - m:16 - m + LP], ALU.add), reads=[ka], writes=[kd])
                    a, ka = d_, kd
                S.op('dve', lambda e, a=a, w=w: e.scalar_tensor_tensor(pooled, a[:, 16:], 1.0 / w, ub[0][:, 16:], ALU.mult, ALU.subtract), reads=[ka, 'ub0'], writes=['pooled'])
                S.op('dve', lambda e, a=a, g=g: e.tensor_tensor(t16, a[:, 16 + 112:16 + 128], corr[:, g, :], ALU.mult), reads=[ka, 'corr'], writes=['t16'])
                S.op('dve', lambda e, w=w: e.scalar_tensor_tensor(pooled[:, 112:128], t16, 1.0 / w, ub[0][:, 16 + 112:16 + 128], ALU.mult, ALU.subtract), reads=['t16', 'ub0'], writes=['pooled'])
                for bi, (t0, n) in enumerate(blocks):
                    pk = bi % 2
                    o, ko = ost[oc % 2], 'post%d' % (oc % 2)
                    oc += 1
                    S.op('pe', lambda e, g=g, t0=t0, n=n, pk=pk: e.matmul(ps[pk][:, :n], pw[:, g, :], pooled[:, t0:t0 + n], start=True, stop=True), reads=['pw', 'pooled'], writes=[PSK[pk]])
                    S.op('act', lambda e, o=o, n=n, pk=pk, g=g: e.activation(o[:, :n], ps[pk][:, :n], AF.Identity, scale=vecT[:, l, 64 + g:65 + g]), reads=[PSK[pk], 'vecT'], writes=[ko])
                    S.dma('sp', yT[12 + g, :, t0:t0 + n], o[:, :n], key=ko, reads=[ko])
            S.flush()

    def phase_fox(l):
        SC = float(128 ** -0.5)
        with ExitStack() as es:
            gts = sb(es, "gts", [128, NCH, 12], F32)
            e4 = sb(es, "e4", [128, 4, NCH], F32)
            lfh = sb(es, "lfh", [128, 4, NCH], F32)
            Fin = sb(es, "Fin", [128, 4, NCH], F32)
            tot = sb(es, "tot", [128, 4, NCH], F32)
            Fend = sb(es, "Fend", [128, 4, NCH], F32)
            Fg = sb(es, "Fg", [128, 4, NCH], F32)
            bias = sb(es, "bias", [128, NCH], F32)
            qTt = sb(es, "qTt", [128, LP], BF16)
            kTt = sb(es, "kTt", [128, LP], BF16)
            vtok = sb(es, "vtok", [128, NCH, 128], BF16)
            pT = [sb(es, "pT%d" % i, [128, 512], BF16) for i in range(2)]
            rden = sb(es, "rden", [128, 512], F32)
            ost = [sb(es, "fost%d" % i, [128, 512], BF16) for i in range(2)]
            for c0_ in range(0, NCH, 8):
                c1_ = min(NCH, c0_ + 8)
                S.dma('sp', gts[:, c0_:c1_, :], gates[c0_ * 128:c1_ * 128, :].rearrange("(c p) g -> p c g", p=128), key='gts', writes=['gts'])
            for h in range(4):
                S.op('act', lambda e, h=h: e.activation(e4[:, h, :], gts[:, :, 8 + h], AF.Exp, scale=-1.0), reads=['gts'], writes=['e4'])
            S.op('act', lambda e: e.activation(lfh, e4, AF.Ln, bias=1.0), reads=['e4'], writes=['lfh'])
            S.op('dve', lambda e: e.tensor_scalar(lfh, lfh, -1.0, None, ALU.mult), reads=['lfh'], writes=['lfh'])
            S.op('dve', lambda e: e.tensor_scalar(lfh[:, :, 0], lfh[:, :, 0], padc[:, 0:1], None, ALU.mult), reads=['lfh', 'padc'], writes=['lfh'])
            lff = lfh.rearrange("p h c -> p (h c)")
            S.op('pe', lambda e: e.matmul(ps[0][:, 0:4 * NCH], tri_f, lff, start=True, stop=True), reads=['tri_f', 'lfh'], writes=['ps0'])
            S.op('pe', lambda e: e.matmul(ps[1][:, 0:4 * NCH], ones_f, lff, start=True, stop=True), reads=['ones_f', 'lfh'], writes=['ps1'])
            S.op('dve', lambda e: e.tensor_copy(Fin.rearrange("p h c -> p (h c)"), ps[0][:, 0:4 * NCH]), reads=['ps0'], writes=['Fin'])
            S.op('dve', lambda e: e.tensor_copy(tot.rearrange("p h c -> p (h c)"), ps[1][:, 0:4 * NCH]), reads=['ps1'], writes=['tot'])
            for h in range(4):
                S.op('dve', lambda e, h=h: e.tensor_tensor_scan(Fend[:, h, :], ones_f[:, 0:NCH], tot[:, h, :], 0.0, ALU.mult, ALU.add), reads=['tot', 'ones_f'], writes=['Fend'])
            S.op('dve', lambda e: e.tensor_tensor(Fg, Fend, tot, ALU.subtract), reads=['Fend', 'tot'], writes=['Fg'])
            S.op('dve', lambda e: e.tensor_tensor(Fg, Fg, Fin, ALU.add), reads=['Fg', 'Fin'], writes=['Fg'])
            oc = 0
            pc = 0
            for h in range(4):
                S.dma('sp', qTt, fqT[h], key='fq', writes=['qTt'])
                S.dma('sp', kTt, fkT[h], key='fk', writes=['kTt'])
                for c0_ in range(0, NCH, 8):
                    c1_ = min(NCH, c0_ + 8)
                    S.dma('sp', vtok[:, c0_:c1_, :], fv[c0_ * 128:c1_ * 128, :].rearrange("(c p) (h d) -> p c h d", p=128, h=4)[:, :, h, :], key='fvv', writes=['vtok'])
                for qb, (t0, n) in enumerate(blocks):
                    c0 = t0 // 128
                    clast = c0 + n // 128 - 1
                    pO, pD = 2 + (qb % 2), 4 + (qb % 2)
                    S.op('dve', lambda e, h=h, clast=clast: e.tensor_scalar(bias[:, 0:clast + 1], Fg[:, h, 0:clast + 1], Fend[:, h, clast:clast + 1], -1.0, ALU.subtract, ALU.mult),
                         reads=['Fg', 'Fend'], writes=['bias'])
                    S.op('dve', lambda e: e.tensor_tensor(bias[:, 0:1], bias[:, 0:1], padc[:, 1:2], ALU.add), reads=['bias', 'padc'], writes=['bias'])
                    for kc in range(clast + 1):
                        lo = max(0, kc - c0) * 128
                        pS = pc % 2
                        p_, kp = pT[pc % 2], 'pT%d' % (pc % 2)
                        pc += 1
                        S.op('pe', lambda e, kc=kc, lo=lo, pS=pS, t0=t0, n=n: e.matmul(ps[pS][:, lo:n], kTt[:, kc * 128:(kc + 1) * 128], qTt[:, t0 + lo:t0 + n], start=True, stop=True),
                             reads=['kTt', 'qTt'], writes=[PSK[pS]])
                        S.op('act', lambda e, kc=kc, lo=lo, pS=pS, p_=p_, n=n: e.activation(p_[:, lo:n], ps[pS][:, lo:n], AF.Exp, bias=bias[:, kc:kc + 1], scale=SC),
                             reads=[PSK[pS], 'bias'], writes=[kp])
                        if kc >= c0:
                            S.op('pool', lambda e, p_=p_, lo=lo: e.tensor_tensor(p_[:, lo:lo + 128], p_[:, lo:lo + 128], tri_b, ALU.mult), reads=[kp, 'tri_b'], writes=[kp])
                        S.op('pe', lambda e, kc=kc, lo=lo, p_=p_, n=n, pO=pO, clast=clast: e.matmul(ps[pO][:, lo:n], vtok[:, kc, :], p_[:, lo:n], start=(kc == 0), stop=(kc == clast)),
                             reads=['vtok', kp], writes=[PSK[pO]])
                        S.op('pe', lambda e, kc=kc, lo=lo, p_=p_, n=n, pD=pD, clast=clast: e.matmul(ps[pD][:, lo:n], ones_b, p_[:, lo:n], start=(kc == 0), stop=(kc == clast)),
                             reads=['ones_b', kp], writes=[PSK[pD]])
                    o, ko = ost[oc % 2], 'fost%d' % (oc % 2)
                    oc += 1
                    S.op('dve', lambda e, n=n, pD=pD: e.reciprocal(rden[:, :n], ps[pD][:, :n]), reads=[PSK[pD]], writes=['rden'])
                    S.op('dve', lambda e, n=n, pO=pO, o=o: e.tensor_tensor(o[:, :n], ps[pO][:, :n], rden[:, :n], ALU.mult), reads=[PSK[pO], 'rden'], writes=[ko])
                    S.dma('sp', yT[8 + h, :, t0:t0 + n], o[:, :n], key=ko, reads=[ko])
            S.flush()
    def phase_mlstm(l):
        SC = float(128 ** -0.5)
        with ExitStack() as es:
            qc = sb(es, "qc", [128, 4, LP], BF16)
            kc_ = sb(es, "kc_", [128, 4, LP], BF16)
            cin = sb(es, "cin", [128, 3 + LP], BF16)
            acc = sb(es, "acc", [128, LP], F32)
            mlg = sb(es, "mlg", [128, 512], F32)
            vaug = [sb(es, "vaug%d" % i, [128, 4, 129], BF16) for i in range(2)]
            ot = [sb(es, "ot%d" % i, [128, 512], BF16) for i in range(2)]
            gt = [sb(es, "gt%d" % i, [128, 12], F32) for i in range(2)]
            sg = sb(es, "sg", [128, 512], F32)
            sm = sb(es, "sm", [128, 40], F32)
            Cf = sb(es, "Cf", [128, 4, 129], F32)
            Cb = sb(es, "Cb", [128, 4, 129], BF16)
            pt = [sb(es, "pt%d" % i, [128, 128], BF16) for i in range(2)]
            ktk = [sb(es, "ktk%d" % i, [128, 128], BF16) for i in range(2)]
            vw = [sb(es, "vw%d" % i, [128, 129], BF16) for i in range(2)]
            hh = [sb(es, "hh%d" % i, [128, 128], F32) for i in range(2)]
            junk = sb(es, "junk", [128, 128], F32)
            hs_ = [sb(es, "hsm%d" % i, [128, 8], F32) for i in range(2)]
            yo = [sb(es, "yo%d" % i, [128, 128], BF16) for i in range(2)]
            yst = [sb(es, "yst%d" % i, [128, 4, 128], BF16) for i in range(2)]
            S.dma('sp', mlg, P["ml_norm_g"][l:l + 1, :].partition_broadcast(128), key='mlg', writes=['mlg'])
            S.op('pool', lambda e: e.memset(cin[:, 0:3], 0.0), writes=['cin'])
            S.op('pool', lambda e: e.memset(Cf, 0.0), writes=['Cf'])
            S.op('pool', lambda e: e.memset(Cb, 0.0), writes=['Cb'])
            for i in range(2):
                S.op('pool', lambda e, i=i: e.memset(vaug[i], 1.0), writes=['vaug%d' % i])
            for j in range(8):
                S.dma('sp', cin[:, 3:], mqkT[j], key='cin', reads=['cin'], writes=['cin'])
                wc = lambda tap, j=j: vecT[:, l, 80 + tap * 8 + j:81 + tap * 8 + j]
                S.op('dve', lambda e, wc=wc: e.tensor_scalar(acc, cin[:, 3:3 + LP], wc(3), None, ALU.mult), reads=['cin', 'vecT'], writes=['acc'])
                for tap in range(3):
                    S.op('dve', lambda e, wc=wc, tap=tap: e.scalar_tensor_tensor(acc, cin[:, tap:tap + LP], wc(tap), acc, ALU.mult, ALU.add), reads=['cin', 'acc', 'vecT'], writes=['acc'])
                if j < 4:
                    S.op('act', lambda e: e.activation(acc, acc, AF.Silu), reads=['acc'], writes=['acc'])
                    S.op('dve', lambda e, j=j: e.tensor_scalar(qc[:, j, :], acc, SC, None, ALU.mult), reads=['acc'], writes=['qc'])
                else:
                    S.op('act', lambda e, j=j: e.activation(kc_[:, j - 4, :], acc, AF.Silu), reads=['acc'], writes=['kc_'])
            for c in range(NCH):
                cs = slice(c * 128, (c + 1) * 128)
                va, kva = vaug[c % 2], 'vaug%d' % (c % 2)
                o_, ko = ot[c % 2], 'ot%d' % (c % 2)
                g_, kg = gt[c % 2], 'gt%d' % (c % 2)
                ys, kys = yst[c % 2], 'yst%d' % (c % 2)
                S.dma('sp', va[:, :, 0:128], mv[cs, :].rearrange("p (h d) -> p h d", h=4), key=kva, writes=[kva])
                S.dma('sp', o_, mo[cs, :], key=ko, writes=[ko])
                S.dma('sp', g_, gates[cs, :], key=kg, writes=[kg])
                S.op('act', lambda e, g_=g_: e.activation(sm[:, 0:4], g_[:, 4:8], AF.Exp, scale=-1.0), reads=[kg], writes=['sm_lf'])
                S.op('act', lambda e: e.activation(sm[:, 0:4], sm[:, 0:4], AF.Ln, bias=1.0), reads=['sm_lf'], writes=['sm_lf'])
                if c == 0:
                    S.op('dve', lambda e: e.tensor_scalar(sm[:, 0:4], sm[:, 0:4], -1.0, padc[:, 0:1], ALU.mult, ALU.mult), reads=['sm_lf', 'padc'], writes=['sm_lf'])
                else:
                    S.op('dve', lambda e: e.tensor_scalar(sm[:, 0:4], sm[:, 0:4], -1.0, None, ALU.mult), reads=['sm_lf'], writes=['sm_lf'])
                S.op('pe', lambda e: e.matmul(ps[6][:, 0:4], tri_f, sm[:, 0:4], start=True, stop=True), reads=['tri_f', 'sm_lf'], writes=['ps6'])
                S.op('pe', lambda e: e.matmul(ps[6][:, 4:8], ones_f, sm[:, 0:4], start=True, stop=True), reads=['ones_f', 'sm_lf'], writes=['ps6'])
                S.op('dve', lambda e, g_=g_: e.tensor_tensor(sm[:, 4:8], g_[:, 0:4], ps[6][:, 0:4], ALU.subtract), reads=[kg, 'ps6'], writes=['sm_w'])
                S.op('act', lambda e: e.activation(sm[:, 4:8], sm[:, 4:8], AF.Exp), reads=['sm_w'], writes=['sm_w'])
                if c == 0:
                    S.op('dve', lambda e: e.tensor_scalar(sm[:, 4:8], sm[:, 4:8], padc[:, 0:1], None, ALU.mult), reads=['sm_w', 'padc'], writes=['sm_w'])
                S.op('act', lambda e: e.activation(sm[:, 8:16], ps[6][:, 0:8], AF.Exp), reads=['ps6'], writes=['sm_e'])
                S.op('act', lambda e, o_=o_: e.activation(sg, o_, AF.Sigmoid), reads=[ko], writes=['sg'])
                for h in range(4):
                    i2 = h % 2
                    kch = kc_[:, h, cs]
                    qch = qc[:, h, cs]
                    p_, kp = pt[i2], 'pt%d' % i2
                    S.op('pe', lambda e, kch=kch, qch=qch, i2=i2: e.matmul(ps[i2][:, 0:128], kch, qch, start=True, stop=True), reads=['kc_', 'qc'], writes=[PSK[i2]])
                    S.op('dve', lambda e, h=h, i2=i2, p_=p_: e.scalar_tensor_tensor(p_, ps[i2][:, 0:128], sm[:, 4 + h:5 + h], tri_f, ALU.mult, ALU.mult),
                         reads=[PSK[i2], 'sm_w', 'tri_f'], writes=[kp])
                    S.op('pe', lambda e, kch=kch, i2=i2: e.transpose(psb[:, i2 * 128:(i2 + 1) * 128], kch, ident_b), reads=['kc_', 'ident_b'], writes=['psb%d' % i2])
                    S.op('act', lambda e, i2=i2: e.activation(ktk[i2], psb[:, i2 * 128:(i2 + 1) * 128], AF.Identity), reads=['psb%d' % i2], writes=['ktk%d' % i2])
                    S.op('pe', lambda e, p_=p_, va=va, h=h, i2=i2: e.matmul(ps[2 + i2][:, 0:129], p_, va[:, h, :], start=True, stop=False), reads=[kp, kva], writes=[PSK[2 + i2]])
                    S.op('pe', lambda e, qch=qch, h=h, i2=i2: e.matmul(ps[2 + i2][:, 0:129], qch, Cb[:, h, :], start=False, stop=True), reads=['qc', 'Cb'], writes=[PSK[2 + i2]])
                    hs = hs_[i2]
                    khs = 'hsm%d' % i2
                    nd = ps[2 + i2]
                    S.op('dve', lambda e, nd=nd, hs=hs, h=h: e.tensor_scalar(hs[:, 0:1], nd[:, 128:129], sm[:, 8 + h:9 + h], None, ALU.mult), reads=[PSK[2 + i2], 'sm_e'], writes=[khs])
                    S.op('dve', lambda e, hs=hs: e.scalar_tensor_tensor(hs[:, 1:2], hs[:, 0:1], -1.0, hs[:, 0:1], ALU.mult, ALU.max), reads=[khs], writes=[khs + 'b'])
                    S.op('dve', lambda e, hs=hs: e.tensor_scalar(hs[:, 2:3], hs[:, 1:2], 1.0, None, ALU.max), reads=[khs + 'b'], writes=[khs + 'c'])
                    S.op('dve', lambda e, hs=hs: e.reciprocal(hs[:, 3:4], hs[:, 2:3]), reads=[khs + 'c'], writes=[khs + 'd'])
                    S.op('dve', lambda e, hs=hs, h=h: e.tensor_tensor(hs[:, 4:5], hs[:, 3:4], sm[:, 8 + h:9 + h], ALU.mult), reads=[khs + 'd', 'sm_e'], writes=[khs + 'e'])
                    hh_, khh = hh[i2], 'hh%d' % i2
                    S.op('dve', lambda e, nd=nd, hs=hs, hh_=hh_: e.tensor_scalar(hh_, nd[:, 0:128], hs[:, 4:5], None, ALU.mult), reads=[PSK[2 + i2], khs + 'e'], writes=[khh])
                    S.op('act', lambda e, hh_=hh_: e.activation(junk, hh_, AF.Square), reads=[khh], writes=['junk'])
                    S.op('dve', lambda e, hs=hs: e.reduce_sum(hs[:, 5:6], junk, AX.X), reads=['junk'], writes=[khs + 'f'])
                    S.op('act', lambda e, hs=hs: e.activation(hs[:, 6:7], hs[:, 5:6], AF.Sqrt, bias=EPS, scale=1.0 / 128), reads=[khs + 'f'], writes=[khs + 'g'])
                    S.op('dve', lambda e, hs=hs: e.reciprocal(hs[:, 7:8], hs[:, 6:7]), reads=[khs + 'g'], writes=[khs + 'h'])
                    S.op('dve', lambda e, hs=hs, hh_=hh_, h=h: e.scalar_tensor_tensor(hh_, hh_, hs[:, 7:8], mlg[:, h * 128:(h + 1) * 128], ALU.mult, ALU.mult), reads=[khh, khs + 'h', 'mlg'], writes=[khh])
                    y_, ky = yo[i2], 'yo%d' % i2
                    S.op('pool', lambda e, hh_=hh_, y_=y_, h=h: e.tensor_tensor(y_, hh_, sg[:, h * 128:(h + 1) * 128], ALU.mult), reads=[khh, 'sg'], writes=[ky])
                    S.op('pe', lambda e, y_=y_, i2=i2: e.transpose(psb[:, 256 + i2 * 128:256 + (i2 + 1) * 128], y_, ident_b), reads=[ky, 'ident_b'], writes=['psby%d' % i2])
                    S.op('act', lambda e, ys=ys, h=h, i2=i2: e.activation(ys[:, h, :], psb[:, 256 + i2 * 128:256 + (i2 + 1) * 128], AF.Identity), reads=['psby%d' % i2], writes=[kys])
                    v_, kv_ = vw[i2], 'vw%d' % i2
                    S.op('pool', lambda e, v_=v_, va=va, h=h: e.tensor_scalar(v_, va[:, h, :], sm[:, 4 + h:5 + h], None, ALU.mult), reads=[kva, 'sm_w'], writes=[kv_])
                    S.op('pe', lambda e, v_=v_, i2=i2: e.matmul(ps[4 + i2][:, 0:129], ktk[i2], v_, start=True, stop=True), reads=['ktk%d' % i2, kv_], writes=[PSK[4 + i2]])
                    S.op('dve', lambda e, h=h: e.tensor_scalar(Cf[:, h, :], Cf[:, h, :], sm[:, 12 + h:13 + h], None, ALU.mult), reads=['Cf', 'sm_e'], writes=['Cf'])
                    S.op('dve', lambda e, h=h, i2=i2: e.scalar_tensor_tensor(Cf[:, h, :], ps[4 + i2][:, 0:129], sm[:, 12 + h:13 + h], Cf[:, h, :], ALU.mult, ALU.add),
                         reads=[PSK[4 + i2], 'sm_e', 'Cf'], writes=['Cf'])
                    S.op('act', lambda e, h=h: e.activation(Cb[:, h, :], Cf[:, h, :], AF.Identity), reads=['Cf'], writes=['Cb'])
                S.dma('sp', yT[4:8, :, cs].rearrange("k p t -> p k t"), ys, key=kys, reads=[kys])
            S.flush()
    def phase_s5(l):
        with ExitStack() as es:
            lamr = sb(es, "lamr", [16, 128], F32)
            lami = sb(es, "lami", [16, 128], F32)
            ldt = sb(es, "ldt", [16, 2], F32)
            pm = sb(es, "pm", [16, 12, 128], F32)
            rr = sb(es, "rr", [128, 4, 2048], F32)
            ri = sb(es, "ri", [128, 2048], mybir.dt.int32)
            sm4 = sb(es, "sm4", [128, 4, 16], F32)
            cosT = sb(es, "cosT", [128, 16, 128], F32)
            sinT = sb(es, "sinT", [128, 16, 128], F32)
            Rtab = sb(es, "Rtab", [128, 16, 128], F32)
            cs128 = sb(es, "cs128", [128, 2, 16], F32)
            bnat = sb(es, "bnat", [128, 2, 16, 16], F32)
            bb = sb(es, "bb", [128, 4, 16, 16], F32)
            Xb = sb(es, "Xb", [128, 16, 2, 128], BF16)
            BT = sb(es, "BT", [128, 16, 2, 128], BF16)
            cnat = sb(es, "cnat", [128, 2, 4, 64], F32)
            Xc = sb(es, "Xc", [128, 4, 3, 128], BF16)
            CTm = sb(es, "CTm", [128, 4, 3, 128], BF16)
            CT = sb(es, "CT", [128, 4, 4, 3, 128], BF16)
            gw = sb(es, "gw", [128, 2, 4, 512], BF16)

            def sincos(ang, n, dsin, dcos, pp):
                for k_, (dst, off) in enumerate(((dsin, 0.0), (dcos, float(np.pi / 2)))):
                    a0, a1 = rr[:pp, 0, :n], rr[:pp, 1, :n]
                    S.op('dve', lambda e, a0=a0, off=off: e.tensor_scalar(a0, ang, off, None, ALU.add), reads=['ang'], writes=['rr0'])
                    S.op('dve', lambda e, a0=a0, a1=a1: e.tensor_scalar(a1, a0, 1.0 / TWO_PI, None, ALU.mult), reads=['rr0'], writes=['rr1'])
                    S.op('dve', lambda e, a1=a1: e.tensor_copy(ri[:pp, :n], a1), reads=['rr1'], writes=['ri'])
                    S.op('dve', lambda e, a1=a1: e.tensor_copy(a1, ri[:pp, :n]), reads=['ri'], writes=['rr1'])
                    S.op('dve', lambda e, a0=a0, a1=a1: e.scalar_tensor_tensor(a0, a1, -TWO_PI, a0, ALU.mult, ALU.add), reads=['rr0', 'rr1'], writes=['rr0'])
                    S.op('dve', lambda e, a0=a0, a1=a1: e.tensor_scalar(a1, a0, float(np.pi), -TWO_PI, ALU.is_gt, ALU.mult), reads=['rr0'], writes=['rr1'])
                    S.op('dve', lambda e, a0=a0, a1=a1: e.tensor_tensor(a0, a0, a1, ALU.add), reads=['rr0', 'rr1'], writes=['rr0'])
                    S.op('dve', lambda e, a0=a0, a1=a1: e.tensor_scalar(a1, a0, float(-np.pi), TWO_PI, ALU.is_lt, ALU.mult), reads=['rr0'], writes=['rr1'])
                    S.op('dve', lambda e, a0=a0, a1=a1: e.tensor_tensor(a0, a0, a1, ALU.add), reads=['rr0', 'rr1'], writes=['rr0'])
                    S.op('act', lambda e, a0=a0, dst=dst: e.activation(dst, a0, AF.Sin), reads=['rr0'], writes=['sc%d' % k_])

            S.dma('sp', lamr, P["ssm_lam_re"][l].rearrange("(j g) p -> j (g p)", g=2), key='s5c', writes=['lam'])
            S.dma('sp', lami, P["ssm_lam_im"][l].rearrange("(j g) p -> j (g p)", g=2), key='s5c', writes=['lam'])
            S.dma('sp', ldt, P["ssm_log_dt"][l].rearrange("(j g) -> j g", g=2), key='s5c', writes=['lam'])
            S.dma('sp', bnat[:, 0], P["ssm_b_re"][l].rearrange("(j g) p h -> (g p) j h", g=2), key='s5c', writes=['bnat'])
            S.dma('sp', bnat[:, 1], P["ssm_b_im"][l].rearrange("(j g) p h -> (g p) j h", g=2), key='s5c', writes=['bnat'])
            S.dma('sp', cnat[:, 0], P["ssm_c_re"][l].rearrange("(q g) h p -> (g h) q p", g=8), key='s5c', writes=['cnat'])
            S.dma('sp', cnat[:, 1], P["ssm_c_im"][l].rearrange("(q g) h p -> (g h) q p", g=8), key='s5c', writes=['cnat'])
            for i in range(2):
                S.dma('sp', gw[:, i], glub[l, i], key='s5c', writes=['gw'])
            S.op('act', lambda e: e.activation(ldt, ldt, AF.Exp), reads=['lam'], writes=['dt'])
            for g2 in range(2):
                sl = slice(g2 * 64, (g2 + 1) * 64)
                S.op('dve', lambda e, sl=sl, g2=g2: e.tensor_scalar(pm[:, 0, sl], lami[:, sl], ldt[:, g2:g2 + 1], None, ALU.mult), reads=['lam', 'dt'], writes=['ang'])
                S.op('dve', lambda e, sl=sl, g2=g2: e.tensor_scalar(pm[:, 1, sl], lamr[:, sl], ldt[:, g2:g2 + 1], None, ALU.mult), reads=['lam', 'dt'], writes=['pm1'])
            S.op('act', lambda e: e.activation(pm[:, 2, :], pm[:, 1, :], AF.Exp), reads=['pm1'], writes=['pm2'])
            ang = pm[:, 0, :]
            sincos(ang, 128, pm[:, 3, :], pm[:, 4, :], 16)
            TT = lambda o, a, b, op, r_, w_: S.op('dve', lambda e: e.tensor_tensor(o, a, b, op), reads=r_, writes=w_)
            pmk = ['pm%d' % i for i in range(12)]
            TT(pm[:, 5, :], pm[:, 2, :], pm[:, 4, :], ALU.mult, ['pm2', 'sc1'], ['pm5'])
            S.op('dve', lambda e: e.tensor_scalar(pm[:, 5, :], pm[:, 5, :], -1.0, None, ALU.add), reads=['pm5'], writes=['pm5'])
            TT(pm[:, 6, :], pm[:, 2, :], pm[:, 3, :], ALU.mult, ['pm2', 'sc0'], ['pm6'])
            TT(pm[:, 7, :], pm[:, 5, :], lamr, ALU.mult, ['pm5', 'lam'], ['pm7'])
            TT(pm[:, 9, :], pm[:, 6, :], lami, ALU.mult, ['pm6', 'lam'], ['pm9'])
            TT(pm[:, 7, :], pm[:, 7, :], pm[:, 9, :], ALU.add, ['pm7', 'pm9'], ['pm7'])
            TT(pm[:, 8, :], pm[:, 6, :], lamr, ALU.mult, ['pm6', 'lam'], ['pm8'])
            TT(pm[:, 9, :], pm[:, 5, :], lami, ALU.mult, ['pm5', 'lam'], ['pm9'])
            TT(pm[:, 8, :], pm[:, 8, :], pm[:, 9, :], ALU.subtract, ['pm8', 'pm9'], ['pm8'])
            TT(pm[:, 9, :], lamr, lamr, ALU.mult, ['lam'], ['pm9'])
            TT(pm[:, 10, :], lami, lami, ALU.mult, ['lam'], ['pm10'])
            TT(pm[:, 9, :], pm[:, 9, :], pm[:, 10, :], ALU.add, ['pm9', 'pm10'], ['pm9'])
            S.op('dve', lambda e: e.reciprocal(pm[:, 9, :], pm[:, 9, :]), reads=['pm9'], writes=['pm9'])
            TT(pm[:, 10, :], pm[:, 7, :], pm[:, 9, :], ALU.mult, ['pm7', 'pm9'], ['pm10'])
            TT(pm[:, 11, :], pm[:, 8, :], pm[:, 9, :], ALU.mult, ['pm8', 'pm9'], ['pm11'])
            for k_, row in enumerate((0, 2, 10, 11)):
                S.op('pe', lambda e, k_=k_, row=row: e.transpose(ps[0][:, k_ * 16:(k_ + 1) * 16], pm[:, row, :], ident_f[0:16, 0:16]),
                     reads=['ang', 'pm2', 'pm10', 'pm11', 'ident_f'], writes=['ps0'])
            S.op('dve', lambda e: e.tensor_copy(sm4.rearrange("p a b -> p (a b)"), ps[0][:, 0:64]), reads=['ps0'], writes=['sm4'])
            thT, rT, krT, kiT = sm4[:, 0, :], sm4[:, 1, :], sm4[:, 2, :], sm4[:, 3, :]
            angT = rr[:, 2, :]
            S.op('dve', lambda e: e.tensor_tensor(angT.rearrange("p (j t) -> p j t", j=16), iota.unsqueeze(1).to_broadcast([128, 16, 128]),
                                                  thT.unsqueeze(2).to_broadcast([128, 16, 128]), ALU.mult), reads=['iota', 'sm4'], writes=['ang'])
            ang = angT
            sincos(ang, 2048, sinT.rearrange("p j t -> p (j t)"), cosT.rearrange("p j t -> p (j t)"), 128)
            S.op('dve', lambda e: e.tensor_copy(sinT, sinT), reads=['sc0'], writes=['sinT'])
            S.op('dve', lambda e: e.tensor_copy(cosT, cosT), reads=['sc1'], writes=['cosT'])
            ang128 = rr[:, 3, 0:16]
            S.op('dve', lambda e: e.tensor_scalar(ang128, thT, 128.0, None, ALU.mult), reads=['sm4'], writes=['ang'])
            ang = ang128
            sincos(ang, 16, cs128[:, 1, :], cs128[:, 0, :], 128)
            S.op('dve', lambda e: e.tensor_copy(cs128, cs128), reads=['sc0', 'sc1'], writes=['cs128'])
            S.op('dve', lambda e: e.tensor_copy(Rtab, rT.unsqueeze(2).to_broadcast([128, 16, 128])), reads=['sm4'], writes=['Rtab'])
            S.op('dve', lambda e: e.memset(Rtab[:, :, 0:1], 0.0), reads=['Rtab'], writes=['Rtab'])
            krb = krT.unsqueeze(2).to_broadcast([128, 16, 16])
            kib = kiT.unsqueeze(2).to_broadcast([128, 16, 16])
            TT(bb[:, 0], bnat[:, 0], krb, ALU.mult, ['bnat', 'sm4'], ['bb0'])
            TT(bb[:, 1], bnat[:, 1], kib, ALU.mult, ['bnat', 'sm4'], ['bb1'])
            TT(bb[:, 0], bb[:, 0], bb[:, 1], ALU.subtract, ['bb0', 'bb1'], ['bb0'])
            TT(bb[:, 2], bnat[:, 1], krb, ALU.mult, ['bnat', 'sm4'], ['bb2'])
            TT(bb[:, 3], bnat[:, 0], kib, ALU.mult, ['bnat', 'sm4'], ['bb3'])
            TT(bb[:, 2], bb[:, 2], bb[:, 3], ALU.add, ['bb2', 'bb3'], ['bb2'])
            S.op('pool', lambda e: e.memset(Xb, 0.0), writes=['Xb'])
            for j in range(16):
                for g2 in range(2):
                    col0 = 16 * ((2 * j + g2) % 8)
                    rs_ = slice(g2 * 64, (g2 + 1) * 64)
                    for ri_, src in enumerate((0, 2)):
                        S.op('dve' if ri_ else 'pool', lambda e, j=j, rs_=rs_, col0=col0, ri_=ri_, src=src: e.tensor_copy(Xb[rs_, j, ri_, col0:col0 + 16], bb[rs_, src, j, :]),
                             reads=['bb0', 'bb2'], writes=['Xb'])
            for j0 in range(0, 16, 4):
                for jj in range(4):
                    for ri_ in range(2):
                        k_ = jj * 2 + ri_
                        S.op('pe', lambda e, j0=j0, jj=jj, ri_=ri_, k_=k_: e.transpose(psb[:, k_ * 128:(k_ + 1) * 128], Xb[:, j0 + jj, ri_, :], ident_b), reads=['Xb', 'ident_b'], writes=['psb'])
                S.op('dve', lambda e, j0=j0: e.tensor_copy(BT[:, j0:j0 + 4].rearrange("p j r c -> p (j r c)"), psb), reads=['psb'], writes=['BT'])
            for v_, (src, sgn) in enumerate(((0, 1.0), (0, -1.0), (1, -1.0))):
                for g2 in range(2):
                    S.op('dve', lambda e, v_=v_, src=src, sgn=sgn, g2=g2: e.tensor_scalar(Xc[:, :, v_, g2 * 64:(g2 + 1) * 64], cnat[:, src], padc[:, 2 + g2:3 + g2], sgn, ALU.mult, ALU.mult),
                         reads=['cnat', 'padc'], writes=['Xc'])
            for q in range(4):
                for v_ in range(3):
                    k_ = q * 3 + v_
                    S.op('pe', lambda e, q=q, v_=v_, k_=k_: e.transpose(psb[:, (k_ % 8) * 128:(k_ % 8 + 1) * 128], Xc[:, q, v_, :], ident_b), reads=['Xc', 'ident_b'], writes=['psb'])
                    S.op('dve', lambda e, q=q, v_=v_, k_=k_: e.tensor_copy(CTm[:, q, v_, :], psb[:, (k_ % 8) * 128:(k_ % 8 + 1) * 128]), reads=['psb'], writes=['CTm'])
            S.op('pool', lambda e: e.memset(CT, 0.0), writes=['CT'])
            for jj in range(4):
                for q in range(4):
                    S.op('dve', lambda e, jj=jj, q=q: e.tensor_copy(CT[:, q, jj, :, 32 * jj:32 * jj + 32], CTm[:, q, :, 32 * jj:32 * jj + 32]), reads=['CTm', 'CT'], writes=['CT'])
            S.flush()
            uin = [sb(es, "s5u%d" % i, [128, 4, 128], BF16) for i in range(2)]
            tmp = [sb(es, "s5t%d" % i, [128, 512], F32) for i in range(4)]
            vre = sb(es, "vre", [128, 512], F32)
            vim = sb(es, "vim", [128, 512], F32)
            wre = sb(es, "wre", [128, 512], F32)
            wim = sb(es, "wim", [128, 512], F32)
            Ap = [sb(es, "Ap%d" % i, [128, 512], BF16) for i in range(4)]
            car = sb(es, "car", [128, 6, 16], F32)
            yg = sb(es, "yg", [128, 128], F32)
            t2_ = sb(es, "t2_", [128, 128], F32)
            sgm = sb(es, "sgm", [128, 128], F32)
            gl = [sb(es, "gl%d" % i, [128, 4, 128], BF16) for i in range(2)]
            sgt = sb(es, "sgt", [128, 4, 128], F32)
            yst = [sb(es, "s5y%d" % i, [128, 4, 128], BF16) for i in range(2)]
            S.op('pool', lambda e: e.memset(car, 0.0), writes=['car'])
            for c in range(NCH):
                cs = slice(c * 128, (c + 1) * 128)
                u_, ku = uin[c % 2], 's5u%d' % (c % 2)
                gl_, kgl = gl[c % 2], 'gl%d' % (c % 2)
                ys, kys = yst[c % 2], 's5y%d' % (c % 2)
                S.dma('sp', u_, uT[:, :, cs].rearrange("k p t -> p k t"), key=ku, writes=[ku])
                for qd in range(4):
                    pR, pI, pY = (qd % 2) * 2, (qd % 2) * 2 + 1, 4 + (qd % 2)
                    j4 = slice(4 * qd, 4 * qd + 4)
                    cosq = cosT[:, j4, :].rearrange("p j t -> p (j t)")
                    sinq = sinT[:, j4, :].rearrange("p j t -> p (j t)")
                    for jj in range(4):
                        j = 4 * qd + jj
                        S.op('pe', lambda e, j=j, jj=jj, qd=qd, u_=u_, pR=pR: e.matmul(ps[pR][:, jj * 128:(jj + 1) * 128], BT[:, j, 0, :], u_[:, qd, :], start=True, stop=True), reads=['BT', ku], writes=[PSK[pR]])
                        S.op('pe', lambda e, j=j, jj=jj, qd=qd, u_=u_, pI=pI: e.matmul(ps[pI][:, jj * 128:(jj + 1) * 128], BT[:, j, 1, :], u_[:, qd, :], start=True, stop=True), reads=['BT', ku], writes=[PSK[pI]])
                    S.op('dve', lambda e, pR=pR, cosq=cosq: e.tensor_tensor(tmp[0], ps[pR], cosq, ALU.mult), reads=[PSK[pR], 'cosT'], writes=['s5t0'])
                    S.op('dve', lambda e, pI=pI, sinq=sinq: e.tensor_tensor(tmp[1], ps[pI], sinq, ALU.mult), reads=[PSK[pI], 'sinT'], writes=['s5t1'])
                    S.op('dve', lambda e, pI=pI, cosq=cosq: e.tensor_tensor(tmp[2], ps[pI], cosq, ALU.mult), reads=[PSK[pI], 'cosT'], writes=['s5t2'])
                    S.op('dve', lambda e, pR=pR, sinq=sinq: e.tensor_tensor(tmp[3], ps[pR], sinq, ALU.mult), reads=[PSK[pR], 'sinT'], writes=['s5t3'])
                    S.op('pool', lambda e: e.tensor_tensor(vre, tmp[0], tmp[1], ALU.add), reads=['s5t0', 's5t1'], writes=['vre'])
                    S.op('pool', lambda e: e.tensor_tensor(vim, tmp[2], tmp[3], ALU.subtract), reads=['s5t2', 's5t3'], writes=['vim'])
                    v3r = vre.rearrange("p (j t) -> p j t", j=4)
                    v3i = vim.rearrange("p (j t) -> p j t", j=4)
                    S.op('dve', lambda e, v3r=v3r, j4=j4: e.tensor_tensor(v3r[:, :, 0], v3r[:, :, 0], car[:, 0, j4], ALU.add), reads=['vre', 'car'], writes=['vre'])
                    S.op('dve', lambda e, v3i=v3i, j4=j4: e.tensor_tensor(v3i[:, :, 0], v3i[:, :, 0], car[:, 1, j4], ALU.add), reads=['vim', 'car'], writes=['vim'])
                    Rq = Rtab[:, j4, :].rearrange("p j t -> p (j t)")
                    S.op('dve', lambda e, Rq=Rq: e.tensor_tensor_scan(wre, Rq, vre, 0.0, ALU.mult, ALU.add), reads=['Rtab', 'vre'], writes=['wre'])
                    S.op('dve', lambda e, Rq=Rq: e.tensor_tensor_scan(wim, Rq, vim, 0.0, ALU.mult, ALU.add), reads=['Rtab', 'vim'], writes=['wim'])
                    w3r = wre.rearrange("p (j t) -> p j t", j=4)
                    w3i = wim.rearrange("p (j t) -> p j t", j=4)
                    c128, s128 = cs128[:, 0, j4], cs128[:, 1, j4]
                    S.op('dve', lambda e, w3r=w3r, c128=c128: e.tensor_tensor(car[:, 2, 0:4], w3r[:, :, 127], c128, ALU.mult), reads=['wre', 'cs128'], writes=['car2'])
                    S.op('dve', lambda e, w3i=w3i, s128=s128: e.tensor_tensor(car[:, 3, 0:4], w3i[:, :, 127], s128, ALU.mult), reads=['wim', 'cs128'], writes=['car3'])
                    S.op('dve', lambda e, w3r=w3r, s128=s128: e.tensor_tensor(car[:, 4, 0:4], w3r[:, :, 127], s128, ALU.mult), reads=['wre', 'cs128'], writes=['car4'])
                    S.op('dve', lambda e, w3i=w3i, c128=c128: e.tensor_tensor(car[:, 5, 0:4], w3i[:, :, 127], c128, ALU.mult), reads=['wim', 'cs128'], writes=['car5'])
                    S.op('dve', lambda e: e.tensor_tensor(car[:, 2, 0:4], car[:, 2, 0:4], car[:, 3, 0:4], ALU.subtract), reads=['car2', 'car3'], writes=['car2'])
                    S.op('dve', lambda e: e.tensor_tensor(car[:, 4, 0:4], car[:, 4, 0:4], car[:, 5, 0:4], ALU.add), reads=['car4', 'car5'], writes=['car4'])
                    S.op('dve', lambda e, j4=j4: e.tensor_tensor(car[:, 0, j4], car[:, 2, 0:4], rT[:, j4], ALU.mult), reads=['car2', 'sm4', 'car'], writes=['car'])
                    S.op('dve', lambda e, j4=j4: e.tensor_tensor(car[:, 1, j4], car[:, 4, 0:4], rT[:, j4], ALU.mult), reads=['car4', 'sm4', 'car'], writes=['car'])
                    S.op('pool', lambda e, cosq=cosq: e.tensor_tensor(Ap[0], wre, cosq, ALU.mult), reads=['wre', 'cosT'], writes=['Ap0'])
                    S.op('pool', lambda e, sinq=sinq: e.tensor_tensor(Ap[1], wim, sinq, ALU.mult), reads=['wim', 'sinT'], writes=['Ap1'])
                    S.op('dve', lambda e, sinq=sinq: e.tensor_tensor(Ap[2], wre, sinq, ALU.mult), reads=['wre', 'sinT'], writes=['Ap2'])
                    S.op('dve', lambda e, cosq=cosq: e.tensor_tensor(Ap[3], wim, cosq, ALU.mult), reads=['wim', 'cosT'], writes=['Ap3'])
                    n_mm = 0
                    for jj in range(4):
                        for a_, v_ in ((0, 0), (1, 1), (2, 2), (3, 2)):
                            S.op('pe', lambda e, jj=jj, a_=a_, v_=v_, qd=qd, pY=pY, n_mm=n_mm: e.matmul(ps[pY][:, 0:128], CT[:, qd, jj, v_, :], Ap[a_][:, jj * 128:(jj + 1) * 128],
                                                                                                  start=(n_mm == 0), stop=(n_mm == 15)), reads=['CT', 'Ap%d' % a_], writes=[PSK[pY]])
                            n_mm += 1
                    S.op('dve', lambda e, u_=u_, qd=qd, pY=pY: e.scalar_tensor_tensor(yg, u_[:, qd, :], vecT[:, l, 68 + qd:69 + qd], ps[pY][:, 0:128], ALU.mult, ALU.add), reads=[ku, 'vecT', PSK[pY]], writes=['yg'])
                    S.op('dve', lambda e: e.tensor_tensor(t2_, yg, yg, ALU.mult), reads=['yg'], writes=['t2_'])
                    S.op('dve', lambda e: e.tensor_scalar(t2_, t2_, 0.044715, 1.0, ALU.mult, ALU.add), reads=['t2_'], writes=['t2_'])
                    S.op('dve', lambda e: e.tensor_tensor(t2_, t2_, yg, ALU.mult), reads=['t2_', 'yg'], writes=['t2_'])
                    S.op('act', lambda e: e.activation(sgm, t2_, AF.Sigmoid, scale=1.5957691216057308), reads=['t2_'], writes=['sgm'])
                    S.op('dve', lambda e, gl_=gl_, qd=qd: e.tensor_tensor(gl_[:, qd, :], yg, sgm, ALU.mult), reads=['yg', 'sgm'], writes=[kgl])
                for ot_ in range(8):
                    pz = 2 + ot_ // 4 if False else (0 if ot_ < 4 else 1)
                    for kt in range(4):
                        S.op('pe', lambda e, ot_=ot_, kt=kt, gl_=gl_, pz=pz: e.matmul(ps[pz][:, (ot_ % 4) * 128:(ot_ % 4 + 1) * 128], gw[:, ot_ // 4, kt, (ot_ % 4) * 128:(ot_ % 4 + 1) * 128], gl_[:, kt, :],
                                                                                       start=(kt == 0), stop=(kt == 3)), reads=['gw', kgl], writes=[PSK[pz]])
                for o4 in range(4):
                    S.op('act', lambda e, o4=o4: e.activation(sgt[:, o4, :], ps[1][:, o4 * 128:(o4 + 1) * 128], AF.Sigmoid, bias=vecT[:, l, 76 + o4:77 + o4]), reads=['ps1', 'vecT'], writes=['sgt'])
                    S.op('dve', lambda e, o4=o4, ys=ys: e.scalar_tensor_tensor(ys[:, o4, :], ps[0][:, o4 * 128:(o4 + 1) * 128], vecT[:, l, 72 + o4:73 + o4], sgt[:, o4, :], ALU.add, ALU.mult),
                         reads=['ps0', 'vecT', 'sgt'], writes=[kys])
                S.dma('sp', yT[0:4, :, cs].rearrange("k p t -> p k t"), ys, key=kys, reads=[kys])
            S.flush()

    for l in range(DEPTH):
        phase_A(l)
        if stop_after == "A":
            return nc
        phase_s5(l)
        phase_mlstm(l)
        phase_fox(l)
        phase_pool(l)
        if stop_after == "mix":
            return nc
        phase_C(l)
        if stop_after == "C":
            return nc
    phase_Z()
    pes.close()
    return nc


def _in_maps(inputs, SEQ, DEPTH, n_cores):
    consts = host_consts()
    maps = []
    B = inputs["x"].shape[0]
    for c in range(n_cores):
        m = {"x": np.ascontiguousarray(inputs["x"][c % B], dtype=np.float32), "meta_tokens": np.asarray(inputs["meta_tokens"], np.float32)}
        for n, _ in PARAMS:
            m[n] = np.ascontiguousarray(inputs[n][:DEPTH], dtype=np.float32)
        m.update(consts)
        maps.append(m)
    return maps


def kernel(**inputs):
    SEQ, DEPTH = 4096, 2
    B = inputs["x"].shape[0]
    nc = build_program(SEQ, DEPTH)
    res = run_bass_kernel_spmd(nc, _in_maps(inputs, SEQ, DEPTH, 8), core_ids=list(range(8)))
    return np.stack([np.asarray(res.results[b]["out"], dtype=np.float32) for b in range(B)], axis=0)
```

```python
import numpy as np
import concourse.bass as bass
import concourse.mybir as mybir
from concourse.bass_utils import run_bass_kernel_spmd

F32 = mybir.dt.float32
BF16 = mybir.dt.bfloat16
AF = mybir.ActivationFunctionType
ALU = mybir.AluOpType
AX = mybir.AxisListType


class Sched:
    def __init__(self, nc):
        self.nc = nc
        self.engs = {'pe': nc.tensor, 'act': nc.scalar, 'dve': nc.vector, 'pool': nc.gpsimd, 'sp': nc.sync}
        self.ops = []
        self.last_w = {}
        self.readers = {}
        self.dma_cnt = {}
        self.esem = {e: nc.alloc_semaphore("s_" + e) for e in self.engs}
        self.dsem = {}
        self.ebase = {e: 0 for e in self.engs}
        self.dma_issued = {}

    def _deps(self, reads, writes):
        deps = set()
        for k in reads:
            if k in self.last_w:
                deps.add(self.last_w[k])
        for k in writes:
            if k in self.last_w:
                deps.add(self.last_w[k])
            deps.update(self.readers.get(k, ()))
        return deps

    def _record(self, oid, reads, writes):
        for k in writes:
            self.last_w[k] = oid
            self.readers[k] = []
        for k in reads:
            self.readers.setdefault(k, []).append(oid)

    @staticmethod
    def _bank(reads, writes):
        bk = lambda k: 'psb' if k.startswith('psb') else k
        w = [bk(k) for k in writes] + [bk(k) for k in reads if k.startswith('ps')]
        r = [k for k in reads if not k.startswith('ps')]
        return r, w

    def op(self, eng, fn, reads=(), writes=()):
        reads, writes = self._bank(list(reads), list(writes))
        deps = self._deps(reads, writes)
        oid = len(self.ops)
        self.ops.append(dict(eng=eng, fn=fn, deps=deps, dma=None))
        self._record(oid, reads, writes)
        return oid

    def dma(self, eng, out, in_, key, reads=(), writes=(), **kw):
        deps = self._deps(reads, writes)
        oid = len(self.ops)
        n = self.dma_cnt.get(key, 0) + 1
        self.dma_cnt[key] = n
        self.ops.append(dict(eng=eng, fn=lambda e: e.dma_start(out=out, in_=in_, **kw), deps=deps, dma=(key, n)))
        self._record(oid, reads, writes)
        return oid

    def flush(self):
        nc = self.nc
        ops = self.ops
        if not ops:
            return
        eng_list = {e: [] for e in self.engs}
        for oid, o in enumerate(ops):
            o['pos'] = len(eng_list[o['eng']])
            eng_list[o['eng']].append(oid)
        seen = {e: {} for e in self.engs}
        need_inc = set()
        for oid, o in enumerate(ops):
            e = o['eng']
            want = {}
            for d in o['deps']:
                do = ops[d]
                if do['dma'] is not None:
                    key, n = do['dma']
                    want[('d', key)] = max(want.get(('d', key), 0), n)
                else:
                    if do['eng'] == 'pe' and e == 'pe' and o['dma'] is None:
                        continue
                    want[('e', do['eng'])] = max(want.get(('e', do['eng']), -1), do['pos'])
            waits = []
            for k, v in want.items():
                if seen[e].get(k, -1) >= v:
                    continue
                seen[e][k] = v
                waits.append((k, v))
                if k[0] == 'e':
                    need_inc.add(eng_list[k[1]][v])
            o['waits'] = waits
        issued = self.dma_issued
        for oid, o in enumerate(ops):
            nw = []
            for k, v in o['waits']:
                if k[0] == 'd':
                    v = max(v, issued.get(k[1], 0))
                nw.append((k, v))
            o['waits'] = nw
            if o['dma'] is not None:
                issued[o['dma'][0]] = o['dma'][1]
        for k in self.dma_cnt:
            if k not in self.dsem:
                self.dsem[k] = nc.alloc_semaphore("d_%d" % len(self.dsem))
        inc_prefix = {}
        for e, lst in eng_list.items():
            c = self.ebase[e]
            for oid in lst:
                if oid in need_inc:
                    c += 1
                    inc_prefix[oid] = c
            self.ebase[e] = c
        finals = [(k, n) for k, n in self.dma_cnt.items()]

        def emit(e, eng):
            for oid in eng_list[e]:
                o = ops[oid]
                for k, v in o['waits']:
                    if k[0] == 'd':
                        eng.wait_ge(self.dsem[k[1]], 16 * v)
                    else:
                        eng.wait_ge(self.esem[k[1]], inc_prefix[eng_list[k[1]][v]])
                ins = o['fn'](eng)
                if o['dma'] is not None:
                    ins.then_inc(self.dsem[o['dma'][0]], 16)
                elif oid in need_inc:
                    ins.then_inc(self.esem[e], 1)
            if e == 'sp':
                for k, n in finals:
                    eng.wait_ge(self.dsem[k], 16 * n)

        with nc.Block() as block:
            @block.sync
            def _(eng):
                emit('sp', eng)

            @block.tensor
            def _(eng):
                emit('pe', eng)

            @block.scalar
            def _(eng):
                emit('act', eng)

            @block.vector
            def _(eng):
                emit('dve', eng)

            @block.gpsimd
            def _(eng):
                emit('pool', eng)
        nc.all_engine_barrier()
        self.ops = []
        self.last_w = {}
        self.readers = {}


D = 2048
KT = 16
INC = 4620
DFF = 8192
EPS = 1e-6
SEGS = [("s_u", 0), ("m_q", 512), ("m_k", 1024), ("m_v", 1536), ("m_o", 2048),
        ("f_q", 2568), ("f_k", 3080), ("f_v", 3592), ("p_u", 4108)]
PARAMS = [("g_pre_mix", [D]), ("g_post_mix", [D]), ("g_pre_ffn", [D]), ("g_post_ffn", [D]),
          ("w_in", [D, INC]), ("ml_gate_bias", [8]), ("fx_gate_bias", [4]),
          ("ssm_lam_re", [32, 64]), ("ssm_lam_im", [32, 64]), ("ssm_log_dt", [32]),
          ("ssm_b_re", [32, 64, 16]), ("ssm_b_im", [32, 64, 16]), ("ssm_c_re", [32, 16, 64]),
          ("ssm_c_im", [32, 16, 64]), ("ssm_d", [32, 16]), ("ssm_glu_w", [512, 1024]), ("ssm_glu_b", [1024]),
          ("ml_conv_w", [4, 1024]), ("ml_norm_g", [512]), ("pool_w", [4, 128, 128]), ("pool_scale", [512]),
          ("w_out", [D, D]), ("mlp_w1", [D, DFF]), ("mlp_w2", [DFF, D])]
TWO_PI = float(2.0 * np.pi)


def host_consts():
    tri = (np.arange(128)[:, None] <= np.arange(128)[None, :]).astype(np.float32)
    pad = np.zeros((128, 4), np.float32)
    pad[112:, 0] = 1.0
    pad[:112, 1] = -1e30
    pad[:, 2] = ((np.arange(128) // 16) % 2 == 0)
    pad[:, 3] = ((np.arange(128) // 16) % 2 == 1)
    corr = np.zeros((128, 4, 16), np.float32)
    for g, w in enumerate((2, 4, 8, 16)):
        corr[:, g, :] = w / np.minimum(np.arange(1, 17), w)
    iota = np.broadcast_to(np.arange(128, dtype=np.float32)[None, :], (128, 128)).copy()
    ident = np.eye(128, dtype=np.float32)
    return {"c_tri": tri, "c_pad": pad, "c_corr": corr, "c_iota": iota, "c_ident": ident}


def build_program(SEQ, DEPTH, debug=False, stop_after=None):
    nc = bass.Bass("TRN2", target_bir_lowering=False)
    LP = SEQ + 128
    NCH = LP // 128
    blocks = [(t0, min(512, LP - t0)) for t0 in range(0, LP, 512)]

    def din(name, shape):
        return nc.dram_tensor(name, list(shape), F32, kind="ExternalInput").ap()

    x = din("x", [SEQ, D])
    meta = din("meta_tokens", [16, D])
    P = {n: din(n, [DEPTH] + sh) for n, sh in PARAMS}
    c_tri = din("c_tri", [128, 128])
    c_pad = din("c_pad", [128, 4])
    c_corr = din("c_corr", [128, 4, 16])
    c_iota = din("c_iota", [128, 128])
    c_ident = din("c_ident", [128, 128])
    out = nc.dram_tensor("out", [SEQ, D], F32, kind="ExternalOutput").ap()
    dbg_kind = "ExternalOutput" if debug else "Internal"

    def scr(name, shape, dt, dbg=False):
        return nc.dram_tensor(name, list(shape), dt, kind=(dbg_kind if dbg else "Internal")).ap()

    hT = scr("hT", [KT, 128, LP], F32, True)
    uT = scr("uT", [4, 128, LP], BF16, True)
    mqkT = scr("mqkT", [8, 128, LP], BF16, True)
    mv = scr("mv", [LP, 512], BF16, True)
    mo = scr("mo", [LP, 512], BF16, True)
    gates = scr("gates", [LP, 12], F32, True)
    fqT = scr("fqT", [4, 128, LP], BF16, True)
    fkT = scr("fkT", [4, 128, LP], BF16, True)
    fv = scr("fv", [LP, 512], BF16, True)
    puT = scr("puT", [4, 128, LP], BF16, True)
    yT = scr("yT", [KT, 128, LP], BF16, True)
    s5scr = scr("s5scr", [8, 2048], F32)
    winb = scr("winb", [DEPTH, 9, 128, KT, 512], BF16)
    wgb = scr("wgb", [DEPTH, 128, KT, 12], BF16)
    woutb = scr("woutb", [DEPTH, 4, 128, KT, 512], BF16)
    w1b = scr("w1b", [DEPTH, 16, 128, KT, 512], BF16)
    w2b = scr("w2b", [DEPTH, 16, 128, KT, 512], BF16)
    glub = scr("glub", [DEPTH, 2, 128, 4, 512], BF16)
    poolwb = scr("poolwb", [DEPTH, 128, 4, 128], BF16)

    S = Sched(nc)
    ps = [nc.alloc_psum_tensor("ps%d" % i, [128, 512], F32).ap() for i in range(7)]
    psb = nc.alloc_psum_tensor("psb", [128, 1024], BF16).ap()
    PSK = ["ps%d" % i for i in range(7)]

    from contextlib import ExitStack

    uid = [0]

    def sb(es, name, shape, dt):
        uid[0] += 1
        return es.enter_context(nc.sbuf_tensor("%s_%d" % (name, uid[0]), list(shape), dt)).ap()

    pes = ExitStack()
    ident_f = sb(pes, "ident_f", [128, 128], F32)
    ident_b = sb(pes, "ident_b", [128, 128], BF16)
    ones_b = sb(pes, "ones_b", [128, 128], BF16)
    ones_f = sb(pes, "ones_f", [128, 128], F32)
    tri_f = sb(pes, "tri_f", [128, 128], F32)
    tri_b = sb(pes, "tri_b", [128, 128], BF16)
    padc = sb(pes, "padc", [128, 4], F32)
    corr = sb(pes, "corr", [128, 4, 16], F32)
    iota = sb(pes, "iota", [128, 128], F32)
    vecT = sb(pes, "vecT", [128, DEPTH, 112], F32)
    S.dma('sp', ident_f, c_ident, key='c', writes=['ident_f'])
    S.dma('sp', tri_f, c_tri, key='c', writes=['tri_f'])
    S.dma('sp', padc, c_pad, key='c', writes=['padc'])
    S.dma('sp', corr, c_corr, key='c', writes=['corr'])
    S.dma('sp', iota, c_iota, key='c', writes=['iota'])
    S.op('dve', lambda e: e.tensor_copy(ident_b, ident_f), reads=['ident_f'], writes=['ident_b'])
    S.op('dve', lambda e: e.tensor_copy(tri_b, tri_f), reads=['tri_f'], writes=['tri_b'])
    S.op('dve', lambda e: e.memset(ones_b, 1.0), writes=['ones_b'])
    S.op('dve', lambda e: e.memset(ones_f, 1.0), writes=['ones_f'])
    with ExitStack() as es:
        vs = sb(es, "vs", [112, 128], F32)
        for l in range(DEPTH):
            r = 0
            for nm, nrow in (("g_pre_mix", 16), ("g_post_mix", 16), ("g_pre_ffn", 16), ("g_post_ffn", 16),
                             ("pool_scale", 4), ("ssm_d", 4), ("ssm_glu_b", 8)):
                src = P[nm][l]
                if nm == "ssm_d":
                    src = src.rearrange("g h -> (g h)")
                S.dma('sp', vs[r:r + nrow, :], src.rearrange("(r c) -> r c", c=128), key='c', writes=['vs'])
                r += nrow
            S.dma('sp', vs[80:112, :], P["ml_conv_w"][l].rearrange("j (t c) -> (j t) c", c=128), key='c', writes=['vs'])
            S.op('pe', lambda e: e.transpose(ps[0][:, 0:112], vs, ident_f[0:112, 0:112]), reads=['vs', 'ident_f'], writes=['ps0'])
            S.op('dve', lambda e, l=l: e.tensor_copy(vecT[:, l, :], ps[0][:, 0:112]), reads=['ps0'], writes=['vecT'])
        def cast_panel(dst, src_rows, c0, nk, ncol=512):
            sv = src_rows.rearrange("(kt p) c -> p kt c", p=128)
            for k0 in range(0, nk, 4):
                k1 = min(nk, k0 + 4)
                S.dma('pool', dst[:, k0:k1, :], sv[:, k0:k1, c0:c0 + ncol], key='cast')
        for l in range(DEPTH):
            for i, (_, c0) in enumerate(SEGS):
                cast_panel(winb[l, i], P["w_in"][l], c0, KT)
            sv = P["w_in"][l].rearrange("(kt p) c -> p kt c", p=128)
            S.dma('pool', wgb[l][:, :, 0:8], sv[:, :, 2560:2568], key='cast')
            S.dma('pool', wgb[l][:, :, 8:12], sv[:, :, 4104:4108], key='cast')
            for i in range(4):
                cast_panel(woutb[l, i], P["w_out"][l], i * 512, KT)
            for i in range(16):
                cast_panel(w1b[l, i], P["mlp_w1"][l], i * 512, KT)
            for je in range(4):
                for kq in range(4):
                    cast_panel(w2b[l, je * 4 + kq], P["mlp_w2"][l][kq * 2048:(kq + 1) * 2048, :], je * 512, KT)
            for i in range(2):
                cast_panel(glub[l, i], P["ssm_glu_w"][l], i * 512, 4)
            S.dma('pool', poolwb[l], P["pool_w"][l].rearrange("g c d -> c g d"), key='cast')
        xin = [sb(es, "xin%d" % i, [128, D], F32) for i in range(2)]
        hst = [sb(es, "hst%d" % i, [128, KT, 128], F32) for i in range(2)]
        for c in range(NCH):
            xi, hs = xin[c % 2], hst[c % 2]
            kx, kh = "xin%d" % (c % 2), "hst%d" % (c % 2)
            if c == 0:
                S.op('pool', lambda e, xi=xi: e.memset(xi, 0.0), writes=[kx])
                S.dma('sp', xi[112:128, :], meta, key='xin', reads=[kx], writes=[kx])
            else:
                S.dma('sp', xi, x[(c - 1) * 128:c * 128, :], key='xin', writes=[kx])
            for q in range(4):
                pk = ps[q % 2]
                for j in range(4):
                    kt = q * 4 + j
                    S.op('pe', lambda e, pk=pk, j=j, kt=kt, xi=xi: e.transpose(pk[:, j * 128:(j + 1) * 128], xi[:, kt * 128:(kt + 1) * 128], ident_f),
                         reads=[kx, 'ident_f'], writes=[PSK[q % 2]])
                S.op('act' if q % 2 else 'dve',
                     (lambda e, pk=pk, q=q, hs=hs: e.activation(hs[:, q * 4:(q + 1) * 4, :], pk.rearrange("p (j t) -> p j t", j=4), AF.Identity)) if q % 2 else
                     (lambda e, pk=pk, q=q, hs=hs: e.tensor_copy(hs[:, q * 4:(q + 1) * 4, :], pk.rearrange("p (j t) -> p j t", j=4))),
                     reads=[PSK[q % 2]], writes=[kh])
            S.dma('sp', hT[:, :, c * 128:(c + 1) * 128].rearrange("k p t -> p k t"), hs, key='hst', reads=[kh], writes=['hT'])
        S.flush()
    if stop_after == "p0":
        return nc
    def evac(i, out_ap, in_ap, reads, writes):
        if i % 2:
            S.op('act', lambda e: e.activation(out_ap, in_ap, AF.Identity), reads=reads, writes=writes)
        else:
            S.op('dve', lambda e: e.tensor_copy(out_ap, in_ap), reads=reads, writes=writes)

    def sumsq_steps(src, ksrc, n, sq, rstd, dim, krstd='rstd', sq_eng='act', ksq='sq'):
        kf = ksrc if callable(ksrc) else (lambda kt: ksrc)
        steps = []
        for kt in range(KT):
            def st(kt=kt):
                sl = sq[kt % 2]
                if sq_eng == 'act':
                    S.op('act', lambda e: e.activation(sl[:, :n], src[:, kt, :n], AF.Square), reads=[kf(kt)], writes=['%s%d' % (ksq, kt % 2)])
                else:
                    S.op(sq_eng, lambda e: e.tensor_tensor(sl[:, :n], src[:, kt, :n], src[:, kt, :n], ALU.mult), reads=[kf(kt)], writes=['%s%d' % (ksq, kt % 2)])
                S.op('pe', lambda e: e.matmul(ps[6][:, :n], ones_b, sl[:, :n], start=(kt == 0), stop=(kt == KT - 1)),
                     reads=['%s%d' % (ksq, kt % 2), 'ones_b'], writes=['ps6'])
            steps.append(st)

        def fin():
            S.op('act', lambda e: e.activation(rstd[:, :n], ps[6][:, :n], AF.Sqrt, bias=EPS, scale=1.0 / dim), reads=['ps6'], writes=[krstd])
            S.op('dve', lambda e: e.reciprocal(rstd[:, :n], rstd[:, :n]), reads=[krstd], writes=[krstd])
        steps.append(fin)
        return steps

    def sumsq_rstd(src, ksrc, n, sq, rstd, dim):
        for st in sumsq_steps(src, ksrc, n, sq, rstd, dim):
            st()

    def phase_A(l):
        with ExitStack() as es:
            hblk2 = [sb(es, "hblk%d" % i, [128, KT, 512], F32) for i in range(2)]
            xn2 = [sb(es, "xn%d" % i, [128, KT, 512], BF16) for i in range(2)]
            sq = [sb(es, "sq%d" % i, [128, 512], BF16) for i in range(2)]
            rstd2 = [sb(es, "rstd%d" % i, [128, 512], F32) for i in range(2)]
            wp = [sb(es, "wp%d" % i, [128, KT, 512], BF16) for i in range(2)]
            wg = sb(es, "wg", [128, KT, 12], BF16)
            gbt = sb(es, "gbt", [128, 12], F32)
            ost = [sb(es, "ost%d" % i, [128, 4, 512], BF16) for i in range(2)]
            tst = [sb(es, "tst%d" % i, [128, 512], BF16) for i in range(2)]
            gst = [sb(es, "gst%d" % i, [128, 12], F32) for i in range(2)]
            S.dma('sp', wg, wgb[l], key='wg', writes=['wg'])
            S.dma('sp', gbt[:, 0:8], P["ml_gate_bias"][l:l + 1, :].partition_broadcast(128), key='wg', writes=['gbt'])
            S.dma('sp', gbt[:, 8:12], P["fx_gate_bias"][l:l + 1, :].partition_broadcast(128), key='wg', writes=['gbt'])
            fm_dest = {"s_u": (uT, 0), "m_q": (mqkT, 0), "m_k": (mqkT, 4), "f_q": (fqT, 0), "f_k": (fkT, 0), "p_u": (puT, 0)}
            tm_dest = {"m_v": mv, "m_o": mo, "f_v": fv}
            wcnt = 0
            ocnt = 0
            tcnt = 0
            gcnt = 0

            def norm_steps(bi, t0, n):
                s_ = bi % 2
                hb, xb_, rs_ = hblk2[s_], xn2[s_], rstd2[s_]
                kh, kx, kr = 'hblk%d' % s_, 'xn%d' % s_, 'rstd%d' % s_

                def ld():
                    for kt in range(KT):
                        S.dma('sp', hb[:, kt, :n], hT[kt, :, t0:t0 + n], key=kh, writes=[kh])
                steps = [ld] + sumsq_steps(hb, kh, n, sq, rs_, D, krstd=kr, sq_eng='dve')
                for kt in range(KT):
                    def st(kt=kt):
                        S.op('dve', lambda e: e.scalar_tensor_tensor(xb_[:, kt, :n], hb[:, kt, :n], vecT[:, l, kt:kt + 1], rs_[:, :n], ALU.mult, ALU.mult),
                             reads=[kh, kr, 'vecT'], writes=[kx])
                    steps.append(st)
                return steps

            def blk(bi, t0, n, bg):
                nonlocal wcnt, ocnt, tcnt, gcnt
                xn = xn2[bi % 2]
                kxn = 'xn%d' % (bi % 2)
                if bg:
                    bg[0]()
                    bg = bg[1:]
                per = (len(bg) + len(SEGS) - 1) // len(SEGS) if bg else 0
                for i, (nm, c0) in enumerate(SEGS):
                    w = wp[wcnt % 2]
                    kw = 'wp%d' % (wcnt % 2)
                    wcnt += 1
                    S.dma('sp', w, winb[l, i], key=kw, writes=[kw])
                    if nm in fm_dest:
                        dst, toff = fm_dest[nm]
                        o = ost[ocnt % 2]
                        ko = 'ost%d' % (ocnt % 2)
                        ocnt += 1
                        for m in range(4):
                            for kt in range(KT):
                                S.op('pe', lambda e, w=w, m=m, kt=kt: e.matmul(ps[m][:, :n], w[:, kt, m * 128:(m + 1) * 128], xn[:, kt, :n], start=(kt == 0), stop=(kt == KT - 1)),
                                     reads=[kw, kxn], writes=[PSK[m]])
                            evac(1, o[:, m, :n], ps[m][:, :n], [PSK[m]], [ko])
                        for m in range(4):
                            S.dma('sp', dst[toff + m, :, t0:t0 + n], o[:, m, :n], key=ko, reads=[ko])
                    else:
                        dst = tm_dest[nm]
                        for i2 in range(n // 128):
                            pk = 4 + (i2 % 2)
                            ts_ = tst[tcnt % 2]
                            kts = 'tst%d' % (tcnt % 2)
                            tcnt += 1
                            for kt in range(KT):
                                S.op('pe', lambda e, w=w, kt=kt, i2=i2, pk=pk: e.matmul(ps[pk], xn[:, kt, i2 * 128:(i2 + 1) * 128], w[:, kt, :], start=(kt == 0), stop=(kt == KT - 1)),
                                     reads=[kw, kxn], writes=[PSK[pk]])
                            evac(1, ts_, ps[pk], [PSK[pk]], [kts])
                            S.dma('sp', dst[t0 + i2 * 128:t0 + (i2 + 1) * 128, :], ts_, key=kts, reads=[kts])
                    for st_ in bg[i * per:(i + 1) * per]:
                        st_()
                for st_ in bg[len(SEGS) * per:]:
                    st_()
                for i2 in range(n // 128):
                    g_ = gst[gcnt % 2]
                    kg = 'gst%d' % (gcnt % 2)
                    gcnt += 1
                    for kt in range(KT):
                        S.op('pe', lambda e, kt=kt, i2=i2: e.matmul(ps[6][:, 0:12], xn[:, kt, i2 * 128:(i2 + 1) * 128], wg[:, kt, :], start=(kt == 0), stop=(kt == KT - 1)),
                             reads=['wg', kxn], writes=['ps6'])
                    S.op('dve', lambda e, g_=g_: e.tensor_tensor(g_, ps[6][:, 0:12], gbt, ALU.add), reads=['ps6', 'gbt'], writes=[kg])
                    S.dma('sp', gates[t0 + i2 * 128:t0 + (i2 + 1) * 128, :], g_, key=kg, reads=[kg])

            pend = norm_steps(0, *blocks[0])
            for st_ in pend:
                st_()
            for bi, (t0_, n_) in enumerate(blocks):
                bg = norm_steps(bi + 1, *blocks[bi + 1]) if bi + 1 < len(blocks) else []
                blk(bi, t0_, n_, bg)
            S.flush()

    def phase_C(l):
        with ExitStack() as es:
            hblk = sb(es, "hblk", [128, KT, 512], F32)
            mix = sb(es, "mix", [128, KT, 512], F32)
            hn = sb(es, "hn", [128, KT, 512], BF16)
            big = sb(es, "big", [128, 64, 512], BF16)
            sq = [sb(es, "sq%d" % i, [128, 512], BF16) for i in range(2)]
            rstd = sb(es, "rstd", [128, 512], F32)
            rl = [sb(es, "rl%d" % i, [128, 512], F32) for i in range(2)]
            wp = [sb(es, "wp%d" % i, [128, KT, 512], BF16) for i in range(2)]
            wcnt = 0
            rcnt = 0

            def post_norm_add(n, gcol):
                sumsq_rstd(mix, lambda kt: 'mx%d' % kt, n, sq, rstd, D)
                for kt in range(KT):
                    S.op('dve', lambda e, kt=kt: e.scalar_tensor_tensor(mix[:, kt, :n], mix[:, kt, :n], vecT[:, l, gcol + kt:gcol + kt + 1], rstd[:, :n], ALU.mult, ALU.mult),
                         reads=['mx%d' % kt, 'rstd', 'vecT'], writes=['mx%d' % kt])
                    S.op('pool' if kt % 3 else 'dve', lambda e, kt=kt: e.tensor_tensor(hblk[:, kt, :n], hblk[:, kt, :n], mix[:, kt, :n], ALU.add), reads=['mx%d' % kt, 'hb%d' % kt], writes=['hb%d' % kt])

            def load_y(t0, n):
                for kt in range(KT):
                    S.dma('sp', big[:, kt, :n], yT[kt, :, t0:t0 + n], key='ybl', writes=['big'])

            def blk(bi, t0, n):
                nonlocal wcnt, rcnt
                if bi == 0:
                    load_y(t0, n)
                for kt in range(KT):
                    S.dma('sp', hblk[:, kt, :n], hT[kt, :, t0:t0 + n], key='hblk', writes=['hb%d' % kt])
                for i in range(4):
                    w = wp[wcnt % 2]
                    kw = 'wp%d' % (wcnt % 2)
                    wcnt += 1
                    S.dma('sp', w, woutb[l, i], key=kw, writes=[kw])
                    for m in range(4):
                        for kt in range(KT):
                            S.op('pe', lambda e, w=w, m=m, kt=kt: e.matmul(ps[m][:, :n], w[:, kt, m * 128:(m + 1) * 128], big[:, kt, :n], start=(kt == 0), stop=(kt == KT - 1)),
                                 reads=[kw, 'big'], writes=[PSK[m]])
                        evac(m, mix[:, i * 4 + m, :n], ps[m][:, :n], [PSK[m]], ['mx%d' % (i * 4 + m)])
                post_norm_add(n, 16)
                if t0 == 0:
                    S.op('pool', lambda e: e.memset(hblk[:, :, 0:112], 0.0), writes=['hb%d' % k_ for k_ in range(KT)])
                sumsq_rstd(hblk, lambda kt: 'hb%d' % kt, n, sq, rstd, D)
                for kt in range(KT):
                    S.op('dve', lambda e, kt=kt: e.scalar_tensor_tensor(hn[:, kt, :n], hblk[:, kt, :n], vecT[:, l, 32 + kt:33 + kt], rstd[:, :n], ALU.mult, ALU.mult),
                         reads=['hb%d' % kt, 'rstd', 'vecT'], writes=['hn'])
                for jp in range(16):
                    w = wp[wcnt % 2]
                    kw = 'wp%d' % (wcnt % 2)
                    wcnt += 1
                    S.dma('sp', w, w1b[l, jp], key=kw, writes=[kw])
                    for m in range(4):
                        for kt in range(KT):
                            S.op('pe', lambda e, w=w, m=m, kt=kt: e.matmul(ps[m][:, :n], w[:, kt, m * 128:(m + 1) * 128], hn[:, kt, :n], start=(kt == 0), stop=(kt == KT - 1)),
                                 reads=[kw, 'hn'], writes=[PSK[m]])
                        r_ = rl[rcnt % 2]
                        kr = 'rl%d' % (rcnt % 2)
                        rcnt += 1
                        S.op('act', lambda e, r_=r_, m=m: e.activation(r_[:, :n], ps[m][:, :n], AF.Relu), reads=[PSK[m]], writes=[kr])
                        S.op('dve', lambda e, r_=r_, jp=jp, m=m: e.tensor_tensor(big[:, jp * 4 + m, :n], r_[:, :n], r_[:, :n], ALU.mult), reads=[kr], writes=['big'])
                for je in range(4):
                    for kq in range(4):
                        w = wp[wcnt % 2]
                        kw = 'wp%d' % (wcnt % 2)
                        wcnt += 1
                        S.dma('sp', w, w2b[l, je * 4 + kq], key=kw, writes=[kw])
                        for m in range(4):
                            for kt in range(KT):
                                S.op('pe', lambda e, w=w, m=m, kt=kt, kq=kq: e.matmul(ps[m][:, :n], w[:, kt, m * 128:(m + 1) * 128], big[:, kq * 16 + kt, :n],
                                                                                    start=(kq == 0 and kt == 0), stop=(kq == 3 and kt == KT - 1)),
                                     reads=[kw, 'big'], writes=[PSK[m]])
                    for m in range(4):
                        evac(m, mix[:, je * 4 + m, :n], ps[m][:, :n], [PSK[m]], ['mx%d' % (je * 4 + m)])
                if bi + 1 < len(blocks):
                    load_y(*blocks[bi + 1])
                post_norm_add(n, 48)
                for kt in range(KT):
                    S.dma('sp', hT[kt, :, t0:t0 + n], hblk[:, kt, :n], key='hst', reads=['hb%d' % kt], writes=['hT'])

            for bi_, (t0_, n_) in enumerate(blocks):
                blk(bi_, t0_, n_)
            S.flush()

    def phase_Z():
        with ExitStack() as es:
            hs = [sb(es, "hs%d" % i, [128, KT, 128], F32) for i in range(2)]
            xo = [sb(es, "xo%d" % i, [128, D], F32) for i in range(2)]
            for c in range(1, NCH):
                h_, x_ = hs[c % 2], xo[c % 2]
                kh, kx = 'hs%d' % (c % 2), 'xo%d' % (c % 2)
                S.dma('sp', h_, hT[:, :, c * 128:(c + 1) * 128].rearrange("k p t -> p k t"), key=kh, writes=[kh])
                for q in range(4):
                    for j in range(4):
                        kt = q * 4 + j
                        S.op('pe', lambda e, q=q, j=j, kt=kt, h_=h_: e.transpose(ps[q][:, j * 128:(j + 1) * 128], h_[:, kt, :], ident_f), reads=[kh, 'ident_f'], writes=[PSK[q]])
                    evac(q, x_[:, q * 512:(q + 1) * 512], ps[q], [PSK[q]], [kx])
                S.dma('sp', out[(c - 1) * 128:c * 128, :], x_, key=kx, reads=[kx])
            S.flush()
    def phase_pool(l):
        with ExitStack() as es:
            ub = [sb(es, "ub%d" % i, [128, 16 + LP], F32) for i in range(3)]
            uin = sb(es, "uin", [128, LP], BF16)
            pooled = sb(es, "pooled", [128, LP], BF16)
            t16 = sb(es, "t16", [128, 16], F32)
            pw = sb(es, "pw", [128, 4, 128], BF16)
            ost = [sb(es, "post%d" % i, [128, 512], BF16) for i in range(2)]
            S.dma('sp', pw, poolwb[l], key='pw', writes=['pw'])
            for i in range(3):
                S.op('pool', lambda e, i=i: e.memset(ub[i][:, 0:16], 0.0), writes=['ub%d' % i])
            oc = 0
            for g, w in enumerate((2, 4, 8, 16)):
                S.dma('sp', uin, puT[g], key='uin', writes=['uin'])
                S.op('act', lambda e: e.activation(ub[0][:, 16:], uin, AF.Identity), reads=['uin'], writes=['ub0'])
                a, ka = ub[0], 'ub0'
                for s_ in range(g + 1):
                    m = 2 ** s_
                    d_, kd = ub[1 + s_ % 2], 'ub%d' % (1 + s_ % 2)
                    S.op('dve', lambda e, a=a, d_=d_, m=m: e.tensor_tensor(d_[:, 16:], a[:, 16:], a[:, 16 - m:16 - m + LP], ALU.add), reads=[ka], writes=[kd])
                    a, ka = d_, kd
                S.op('dve', lambda e, a=a, w=w: e.scalar_tensor_tensor(pooled, a[:, 16:], 1.0 / w, ub[0][:, 16:], ALU.mult, ALU.subtract), reads=[ka, 'ub0'], writes=['pooled'])
                S.op('dve', lambda e, a=a, g=g: e.tensor_tensor(t16, a[:, 16 + 112:16 + 128], corr[:, g, :], ALU.mult), reads=[ka, 'corr'], writes=['t16'])
                S.op('dve', lambda e, w=w: e.scalar_tensor_tensor(pooled[:, 112:128], t16, 1.0 / w, ub[0][:, 16 + 112:16 + 128], ALU.mult, ALU.subtract), reads=['t16', 'ub0'], writes=['pooled'])
                for bi, (t0, n) in enumerate(blocks):
                    pk = bi % 2
                    o, ko = ost[oc % 2], 'post%d' % (oc % 2)
                    oc += 1
                    S.op('pe', lambda e, g=g, t0=t0, n=n, pk=pk: e.matmul(ps[pk][:, :n], pw[:, g, :], pooled[:, t0:t0 + n], start=True, stop=True), reads=['pw', 'pooled'], writes=[PSK[pk]])
                    S.op('act', lambda e, o=o, n=n, pk=pk, g=g: e.activation(o[:, :n], ps[pk][:, :n], AF.Identity, scale=vecT[:, l, 64 + g:65 + g]), reads=[PSK[pk], 'vecT'], writes=[ko])
                    S.dma('sp', yT[12 + g, :, t0:t0 + n], o[:, :n], key=ko, reads=[ko])
            S.flush()

    def phase_fox(l):
        SC = float(128 ** -0.5)
        with ExitStack() as es:
            gts = sb(es, "gts", [128, NCH, 12], F32)
            e4 = sb(es, "e4", [128, 4, NCH], F32)
            lfh = sb(es, "lfh", [128, 4, NCH], F32)
            Fin = sb(es, "Fin", [128, 4, NCH], F32)
            tot = sb(es, "tot", [128, 4, NCH], F32)
            Fend = sb(es, "Fend", [128, 4, NCH], F32)
            Fg = sb(es, "Fg", [128, 4, NCH], F32)
            bias = sb(es, "bias", [128, NCH], F32)
            qTt = sb(es, "qTt", [128, LP], BF16)
            kTt = sb(es, "kTt", [128, LP], BF16)
            vtok = sb(es, "vtok", [128, NCH, 128], BF16)
            pT = [sb(es, "pT%d" % i, [128, 512], BF16) for i in range(2)]
            rden = sb(es, "rden", [128, 512], F32)
            ost = [sb(es, "fost%d" % i, [128, 512], BF16) for i in range(2)]
            for c0_ in range(0, NCH, 8):
                c1_ = min(NCH, c0_ + 8)
                S.dma('sp', gts[:, c0_:c1_, :], gates[c0_ * 128:c1_ * 128, :].rearrange("(c p) g -> p c g", p=128), key='gts', writes=['gts'])
            for h in range(4):
                S.op('act', lambda e, h=h: e.activation(e4[:, h, :], gts[:, :, 8 + h], AF.Exp, scale=-1.0), reads=['gts'], writes=['e4'])
            S.op('act', lambda e: e.activation(lfh, e4, AF.Ln, bias=1.0), reads=['e4'], writes=['lfh'])
            S.op('dve', lambda e: e.tensor_scalar(lfh, lfh, -1.0, None, ALU.mult), reads=['lfh'], writes=['lfh'])
            S.op('dve', lambda e: e.tensor_scalar(lfh[:, :, 0], lfh[:, :, 0], padc[:, 0:1], None, ALU.mult), reads=['lfh', 'padc'], writes=['lfh'])
            lff = lfh.rearrange("p h c -> p (h c)")
            S.op('pe', lambda e: e.matmul(ps[0][:, 0:4 * NCH], tri_f, lff, start=True, stop=True), reads=['tri_f', 'lfh'], writes=['ps0'])
            S.op('pe', lambda e: e.matmul(ps[1][:, 0:4 * NCH], ones_f, lff, start=True, stop=True), reads=['ones_f', 'lfh'], writes=['ps1'])
            S.op('dve', lambda e: e.tensor_copy(Fin.rearrange("p h c -> p (h c)"), ps[0][:, 0:4 * NCH]), reads=['ps0'], writes=['Fin'])
            S.op('dve', lambda e: e.tensor_copy(tot.rearrange("p h c -> p (h c)"), ps[1][:, 0:4 * NCH]), reads=['ps1'], writes=['tot'])
            for h in range(4):
                S.op('dve', lambda e, h=h: e.tensor_tensor_scan(Fend[:, h, :], ones_f[:, 0:NCH], tot[:, h, :], 0.0, ALU.mult, ALU.add), reads=['tot', 'ones_f'], writes=['Fend'])
            S.op('dve', lambda e: e.tensor_tensor(Fg, Fend, tot, ALU.subtract), reads=['Fend', 'tot'], writes=['Fg'])
            S.op('dve', lambda e: e.tensor_tensor(Fg, Fg, Fin, ALU.add), reads=['Fg', 'Fin'], writes=['Fg'])
            oc = 0
            pc = 0
            for h in range(4):
                S.dma('sp', qTt, fqT[h], key='fq', writes=['qTt'])
                S.dma('sp', kTt, fkT[h], key='fk', writes=['kTt'])
                for c0_ in range(0, NCH, 8):
                    c1_ = min(NCH, c0_ + 8)
                    S.dma('sp', vtok[:, c0_:c1_, :], fv[c0_ * 128:c1_ * 128, :].rearrange("(c p) (h d) -> p c h d", p=128, h=4)[:, :, h, :], key='fvv', writes=['vtok'])
                for qb, (t0, n) in enumerate(blocks):
                    c0 = t0 // 128
                    clast = c0 + n // 128 - 1
                    pO, pD = 2 + (qb % 2), 4 + (qb % 2)
                    S.op('dve', lambda e, h=h, clast=clast: e.tensor_scalar(bias[:, 0:clast + 1], Fg[:, h, 0:clast + 1], Fend[:, h, clast:clast + 1], -1.0, ALU.subtract, ALU.mult),
                         reads=['Fg', 'Fend'], writes=['bias'])
                    S.op('dve', lambda e: e.tensor_tensor(bias[:, 0:1], bias[:, 0:1], padc[:, 1:2], ALU.add), reads=['bias', 'padc'], writes=['bias'])
                    for kc in range(clast + 1):
                        lo = max(0, kc - c0) * 128
                        pS = pc % 2
                        p_, kp = pT[pc % 2], 'pT%d' % (pc % 2)
                        pc += 1
                        S.op('pe', lambda e, kc=kc, lo=lo, pS=pS, t0=t0, n=n: e.matmul(ps[pS][:, lo:n], kTt[:, kc * 128:(kc + 1) * 128], qTt[:, t0 + lo:t0 + n], start=True, stop=True),
                             reads=['kTt', 'qTt'], writes=[PSK[pS]])
                        S.op('act', lambda e, kc=kc, lo=lo, pS=pS, p_=p_, n=n: e.activation(p_[:, lo:n], ps[pS][:, lo:n], AF.Exp, bias=bias[:, kc:kc + 1], scale=SC),
                             reads=[PSK[pS], 'bias'], writes=[kp])
                        if kc >= c0:
                            S.op('pool', lambda e, p_=p_, lo=lo: e.tensor_tensor(p_[:, lo:lo + 128], p_[:, lo:lo + 128], tri_b, ALU.mult), reads=[kp, 'tri_b'], writes=[kp])
                        S.op('pe', lambda e, kc=kc, lo=lo, p_=p_, n=n, pO=pO, clast=clast: e.matmul(ps[pO][:, lo:n], vtok[:, kc, :], p_[:, lo:n], start=(kc == 0), stop=(kc == clast)),
                             reads=['vtok', kp], writes=[PSK[pO]])
                        S.op('pe', lambda e, kc=kc, lo=lo, p_=p_, n=n, pD=pD, clast=clast: e.matmul(ps[pD][:, lo:n], ones_b, p_[:, lo:n], start=(kc == 0), stop=(kc == clast)),
                             reads=['ones_b', kp], writes=[PSK[pD]])
                    o, ko = ost[oc % 2], 'fost%d' % (oc % 2)
                    oc += 1
                    S.op('dve', lambda e, n=n, pD=pD: e.reciprocal(rden[:, :n], ps[pD][:, :n]), reads=[PSK[pD]], writes=['rden'])
                    S.op('dve', lambda e, n=n, pO=pO, o=o: e.tensor_tensor(o[:, :n], ps[pO][:, :n], rden[:, :n], ALU.mult), reads=[PSK[pO], 'rden'], writes=[ko])
                    S.dma('sp', yT[8 + h, :, t0:t0 + n], o[:, :n], key=ko, reads=[ko])
            S.flush()
    def phase_mlstm(l):
        SC = float(128 ** -0.5)
        with ExitStack() as es:
            qc = sb(es, "qc", [128, 4, LP], BF16)
            kc_ = sb(es, "kc_", [128, 4, LP], BF16)
            cin = sb(es, "cin", [128, 3 + LP], BF16)
            acc = sb(es, "acc", [128, LP], F32)
            mlg = sb(es, "mlg", [128, 512], F32)
            vaug = [sb(es, "vaug%d" % i, [128, 4, 129], BF16) for i in range(2)]
            ot = [sb(es, "ot%d" % i, [128, 512], BF16) for i in range(2)]
            gt = [sb(es, "gt%d" % i, [128, 12], F32) for i in range(2)]
            sg = sb(es, "sg", [128, 512], F32)
            sm = sb(es, "sm", [128, 40], F32)
            Cf = sb(es, "Cf", [128, 4, 129], F32)
            Cb = sb(es, "Cb", [128, 4, 129], BF16)
            pt = [sb(es, "pt%d" % i, [128, 128], BF16) for i in range(2)]
            ktk = [sb(es, "ktk%d" % i, [128, 128], BF16) for i in range(2)]
            vw = [sb(es, "vw%d" % i, [128, 129], BF16) for i in range(2)]
            hh = [sb(es, "hh%d" % i, [128, 128], F32) for i in range(2)]
            junk = sb(es, "junk", [128, 128], F32)
            hs_ = [sb(es, "hsm%d" % i, [128, 8], F32) for i in range(2)]
            yo = [sb(es, "yo%d" % i, [128, 128], BF16) for i in range(2)]
            yst = [sb(es, "yst%d" % i, [128, 4, 128], BF16) for i in range(2)]
            S.dma('sp', mlg, P["ml_norm_g"][l:l + 1, :].partition_broadcast(128), key='mlg', writes=['mlg'])
            S.op('pool', lambda e: e.memset(cin[:, 0:3], 0.0), writes=['cin'])
            S.op('pool', lambda e: e.memset(Cf, 0.0), writes=['Cf'])
            S.op('pool', lambda e: e.memset(Cb, 0.0), writes=['Cb'])
            for i in range(2):
                S.op('pool', lambda e, i=i: e.memset(vaug[i], 1.0), writes=['vaug%d' % i])
            for j in range(8):
                S.dma('sp', cin[:, 3:], mqkT[j], key='cin', reads=['cin'], writes=['cin'])
                wc = lambda tap, j=j: vecT[:, l, 80 + tap * 8 + j:81 + tap * 8 + j]
                S.op('dve', lambda e, wc=wc: e.tensor_scalar(acc, cin[:, 3:3 + LP], wc(3), None, ALU.mult), reads=['cin', 'vecT'], writes=['acc'])
                for tap in range(3):
                    S.op('dve', lambda e, wc=wc, tap=tap: e.scalar_tensor_tensor(acc, cin[:, tap:tap + LP], wc(tap), acc, ALU.mult, ALU.add), reads=['cin', 'acc', 'vecT'], writes=['acc'])
                if j < 4:
                    S.op('act', lambda e: e.activation(acc, acc, AF.Silu), reads=['acc'], writes=['acc'])
                    S.op('dve', lambda e, j=j: e.tensor_scalar(qc[:, j, :], acc, SC, None, ALU.mult), reads=['acc'], writes=['qc'])
                else:
                    S.op('act', lambda e, j=j: e.activation(kc_[:, j - 4, :], acc, AF.Silu), reads=['acc'], writes=['kc_'])
            for c in range(NCH):
                cs = slice(c * 128, (c + 1) * 128)
                va, kva = vaug[c % 2], 'vaug%d' % (c % 2)
                o_, ko = ot[c % 2], 'ot%d' % (c % 2)
                g_, kg = gt[c % 2], 'gt%d' % (c % 2)
                ys, kys = yst[c % 2], 'yst%d' % (c % 2)
                S.dma('sp', va[:, :, 0:128], mv[cs, :].rearrange("p (h d) -> p h d", h=4), key=kva, writes=[kva])
                S.dma('sp', o_, mo[cs, :], key=ko, writes=[ko])
                S.dma('sp', g_, gates[cs, :], key=kg, writes=[kg])
                S.op('act', lambda e, g_=g_: e.activation(sm[:, 0:4], g_[:, 4:8], AF.Exp, scale=-1.0), reads=[kg], writes=['sm_lf'])
                S.op('act', lambda e: e.activation(sm[:, 0:4], sm[:, 0:4], AF.Ln, bias=1.0), reads=['sm_lf'], writes=['sm_lf'])
                if c == 0:
                    S.op('dve', lambda e: e.tensor_scalar(sm[:, 0:4], sm[:, 0:4], -1.0, padc[:, 0:1], ALU.mult, ALU.mult), reads=['sm_lf', 'padc'], writes=['sm_lf'])
                else:
                    S.op('dve', lambda e: e.tensor_scalar(sm[:, 0:4], sm[:, 0:4], -1.0, None, ALU.mult), reads=['sm_lf'], writes=['sm_lf'])
                S.op('pe', lambda e: e.matmul(ps[6][:, 0:4], tri_f, sm[:, 0:4], start=True, stop=True), reads=['tri_f', 'sm_lf'], writes=['ps6'])
                S.op('pe', lambda e: e.matmul(ps[6][:, 4:8], ones_f, sm[:, 0:4], start=True, stop=True), reads=['ones_f', 'sm_lf'], writes=['ps6'])
                S.op('dve', lambda e, g_=g_: e.tensor_tensor(sm[:, 4:8], g_[:, 0:4], ps[6][:, 0:4], ALU.subtract), reads=[kg, 'ps6'], writes=['sm_w'])
                S.op('act', lambda e: e.activation(sm[:, 4:8], sm[:, 4:8], AF.Exp), reads=['sm_w'], writes=['sm_w'])
                if c == 0:
                    S.op('dve', lambda e: e.tensor_scalar(sm[:, 4:8], sm[:, 4:8], padc[:, 0:1], None, ALU.mult), reads=['sm_w', 'padc'], writes=['sm_w'])
                S.op('act', lambda e: e.activation(sm[:, 8:16], ps[6][:, 0:8], AF.Exp), reads=['ps6'], writes=['sm_e'])
                S.op('act', lambda e, o_=o_: e.activation(sg, o_, AF.Sigmoid), reads=[ko], writes=['sg'])
                for h in range(4):
                    i2 = h % 2
                    kch = kc_[:, h, cs]
                    qch = qc[:, h, cs]
                    p_, kp = pt[i2], 'pt%d' % i2
                    S.op('pe', lambda e, kch=kch, qch=qch, i2=i2: e.matmul(ps[i2][:, 0:128], kch, qch, start=True, stop=True), reads=['kc_', 'qc'], writes=[PSK[i2]])
                    S.op('dve', lambda e, h=h, i2=i2, p_=p_: e.scalar_tensor_tensor(p_, ps[i2][:, 0:128], sm[:, 4 + h:5 + h], tri_f, ALU.mult, ALU.mult),
                         reads=[PSK[i2], 'sm_w', 'tri_f'], writes=[kp])
                    S.op('pe', lambda e, kch=kch, i2=i2: e.transpose(psb[:, i2 * 128:(i2 + 1) * 128], kch, ident_b), reads=['kc_', 'ident_b'], writes=['psb%d' % i2])
                    S.op('act', lambda e, i2=i2: e.activation(ktk[i2], psb[:, i2 * 128:(i2 + 1) * 128], AF.Identity), reads=['psb%d' % i2], writes=['ktk%d' % i2])
                    S.op('pe', lambda e, p_=p_, va=va, h=h, i2=i2: e.matmul(ps[2 + i2][:, 0:129], p_, va[:, h, :], start=True, stop=False), reads=[kp, kva], writes=[PSK[2 + i2]])
                    S.op('pe', lambda e, qch=qch, h=h, i2=i2: e.matmul(ps[2 + i2][:, 0:129], qch, Cb[:, h, :], start=False, stop=True), reads=['qc', 'Cb'], writes=[PSK[2 + i2]])
                    hs = hs_[i2]
                    khs = 'hsm%d' % i2
                    nd = ps[2 + i2]
                    S.op('dve', lambda e, nd=nd, hs=hs, h=h: e.tensor_scalar(hs[:, 0:1], nd[:, 128:129], sm[:, 8 + h:9 + h], None, ALU.mult), reads=[PSK[2 + i2], 'sm_e'], writes=[khs])
                    S.op('dve', lambda e, hs=hs: e.scalar_tensor_tensor(hs[:, 1:2], hs[:, 0:1], -1.0, hs[:, 0:1], ALU.mult, ALU.max), reads=[khs], writes=[khs + 'b'])
                    S.op('dve', lambda e, hs=hs: e.tensor_scalar(hs[:, 2:3], hs[:, 1:2], 1.0, None, ALU.max), reads=[khs + 'b'], writes=[khs + 'c'])
                    S.op('dve', lambda e, hs=hs: e.reciprocal(hs[:, 3:4], hs[:, 2:3]), reads=[khs + 'c'], writes=[khs + 'd'])
                    S.op('dve', lambda e, hs=hs, h=h: e.tensor_tensor(hs[:, 4:5], hs[:, 3:4], sm[:, 8 + h:9 + h], ALU.mult), reads=[khs + 'd', 'sm_e'], writes=[khs + 'e'])
                    hh_, khh = hh[i2], 'hh%d' % i2
                    S.op('dve', lambda e, nd=nd, hs=hs, hh_=hh_: e.tensor_scalar(hh_, nd[:, 0:128], hs[:, 4:5], None, ALU.mult), reads=[PSK[2 + i2], khs + 'e'], writes=[khh])
                    S.op('act', lambda e, hh_=hh_: e.activation(junk, hh_, AF.Square), reads=[khh], writes=['junk'])
                    S.op('dve', lambda e, hs=hs: e.reduce_sum(hs[:, 5:6], junk, AX.X), reads=['junk'], writes=[khs + 'f'])
                    S.op('act', lambda e, hs=hs: e.activation(hs[:, 6:7], hs[:, 5:6], AF.Sqrt, bias=EPS, scale=1.0 / 128), reads=[khs + 'f'], writes=[khs + 'g'])
                    S.op('dve', lambda e, hs=hs: e.reciprocal(hs[:, 7:8], hs[:, 6:7]), reads=[khs + 'g'], writes=[khs + 'h'])
                    S.op('dve', lambda e, hs=hs, hh_=hh_, h=h: e.scalar_tensor_tensor(hh_, hh_, hs[:, 7:8], mlg[:, h * 128:(h + 1) * 128], ALU.mult, ALU.mult), reads=[khh, khs + 'h', 'mlg'], writes=[khh])
                    y_, ky = yo[i2], 'yo%d' % i2
                    S.op('pool', lambda e, hh_=hh_, y_=y_, h=h: e.tensor_tensor(y_, hh_, sg[:, h * 128:(h + 1) * 128], ALU.mult), reads=[khh, 'sg'], writes=[ky])
                    S.op('pe', lambda e, y_=y_, i2=i2: e.transpose(psb[:, 256 + i2 * 128:256 + (i2 + 1) * 128], y_, ident_b), reads=[ky, 'ident_b'], writes=['psby%d' % i2])
                    S.op('act', lambda e, ys=ys, h=h, i2=i2: e.activation(ys[:, h, :], psb[:, 256 + i2 * 128:256 + (i2 + 1) * 128], AF.Identity), reads=['psby%d' % i2], writes=[kys])
                    v_, kv_ = vw[i2], 'vw%d' % i2
                    S.op('pool', lambda e, v_=v_, va=va, h=h: e.tensor_scalar(v_, va[:, h, :], sm[:, 4 + h:5 + h], None, ALU.mult), reads=[kva, 'sm_w'], writes=[kv_])
                    S.op('pe', lambda e, v_=v_, i2=i2: e.matmul(ps[4 + i2][:, 0:129], ktk[i2], v_, start=True, stop=True), reads=['ktk%d' % i2, kv_], writes=[PSK[4 + i2]])
                    S.op('dve', lambda e, h=h: e.tensor_scalar(Cf[:, h, :], Cf[:, h, :], sm[:, 12 + h:13 + h], None, ALU.mult), reads=['Cf', 'sm_e'], writes=['Cf'])
                    S.op('dve', lambda e, h=h, i2=i2: e.scalar_tensor_tensor(Cf[:, h, :], ps[4 + i2][:, 0:129], sm[:, 12 + h:13 + h], Cf[:, h, :], ALU.mult, ALU.add),
                         reads=[PSK[4 + i2], 'sm_e', 'Cf'], writes=['Cf'])
                    S.op('act', lambda e, h=h: e.activation(Cb[:, h, :], Cf[:, h, :], AF.Identity), reads=['Cf'], writes=['Cb'])
                S.dma('sp', yT[4:8, :, cs].rearrange("k p t -> p k t"), ys, key=kys, reads=[kys])
            S.flush()
    def phase_s5(l):
        with ExitStack() as es:
            lamr = sb(es, "lamr", [16, 128], F32)
            lami = sb(es, "lami", [16, 128], F32)
            ldt = sb(es, "ldt", [16, 2], F32)
            pm = sb(es, "pm", [16, 12, 128], F32)
            rr = sb(es, "rr", [128, 4, 2048], F32)
            ri = sb(es, "ri", [128, 2048], mybir.dt.int32)
            sm4 = sb(es, "sm4", [128, 4, 16], F32)
            cosT = sb(es, "cosT", [128, 16, 128], F32)
            sinT = sb(es, "sinT", [128, 16, 128], F32)
            Rtab = sb(es, "Rtab", [128, 16, 128], F32)
            cs128 = sb(es, "cs128", [128, 2, 16], F32)
            bnat = sb(es, "bnat", [128, 2, 16, 16], F32)
            bb = sb(es, "bb", [128, 4, 16, 16], F32)
            Xb = sb(es, "Xb", [128, 16, 2, 128], BF16)
            BT = sb(es, "BT", [128, 16, 2, 128], BF16)
            cnat = sb(es, "cnat", [128, 2, 4, 64], F32)
            Xc = sb(es, "Xc", [128, 4, 3, 128], BF16)
            CTm = sb(es, "CTm", [128, 4, 3, 128], BF16)
            CT = sb(es, "CT", [128, 4, 4, 3, 128], BF16)
            gw = sb(es, "gw", [128, 2, 4, 512], BF16)

            def sincos(ang, n, dsin, dcos, pp):
                for k_, (dst, off) in enumerate(((dsin, 0.0), (dcos, float(np.pi / 2)))):
                    a0, a1 = rr[:pp, 0, :n], rr[:pp, 1, :n]
                    S.op('dve', lambda e, a0=a0, off=off: e.tensor_scalar(a0, ang, off, None, ALU.add), reads=['ang'], writes=['rr0'])
                    S.op('dve', lambda e, a0=a0, a1=a1: e.tensor_scalar(a1, a0, 1.0 / TWO_PI, None, ALU.mult), reads=['rr0'], writes=['rr1'])
                    S.op('dve', lambda e, a1=a1: e.tensor_copy(ri[:pp, :n], a1), reads=['rr1'], writes=['ri'])
                    S.op('dve', lambda e, a1=a1: e.tensor_copy(a1, ri[:pp, :n]), reads=['ri'], writes=['rr1'])
                    S.op('dve', lambda e, a0=a0, a1=a1: e.scalar_tensor_tensor(a0, a1, -TWO_PI, a0, ALU.mult, ALU.add), reads=['rr0', 'rr1'], writes=['rr0'])
                    S.op('dve', lambda e, a0=a0, a1=a1: e.tensor_scalar(a1, a0, float(np.pi), -TWO_PI, ALU.is_gt, ALU.mult), reads=['rr0'], writes=['rr1'])
                    S.op('dve', lambda e, a0=a0, a1=a1: e.tensor_tensor(a0, a0, a1, ALU.add), reads=['rr0', 'rr1'], writes=['rr0'])
                    S.op('dve', lambda e, a0=a0, a1=a1: e.tensor_scalar(a1, a0, float(-np.pi), TWO_PI, ALU.is_lt, ALU.mult), reads=['rr0'], writes=['rr1'])
                    S.op('dve', lambda e, a0=a0, a1=a1: e.tensor_tensor(a0, a0, a1, ALU.add), reads=['rr0', 'rr1'], writes=['rr0'])
                    S.op('act', lambda e, a0=a0, dst=dst: e.activation(dst, a0, AF.Sin), reads=['rr0'], writes=['sc%d' % k_])

            S.dma('sp', lamr, P["ssm_lam_re"][l].rearrange("(j g) p -> j (g p)", g=2), key='s5c', writes=['lam'])
            S.dma('sp', lami, P["ssm_lam_im"][l].rearrange("(j g) p -> j (g p)", g=2), key='s5c', writes=['lam'])
            S.dma('sp', ldt, P["ssm_log_dt"][l].rearrange("(j g) -> j g", g=2), key='s5c', writes=['lam'])
            S.dma('sp', bnat[:, 0], P["ssm_b_re"][l].rearrange("(j g) p h -> (g p) j h", g=2), key='s5c', writes=['bnat'])
            S.dma('sp', bnat[:, 1], P["ssm_b_im"][l].rearrange("(j g) p h -> (g p) j h", g=2), key='s5c', writes=['bnat'])
            S.dma('sp', cnat[:, 0], P["ssm_c_re"][l].rearrange("(q g) h p -> (g h) q p", g=8), key='s5c', writes=['cnat'])
            S.dma('sp', cnat[:, 1], P["ssm_c_im"][l].rearrange("(q g) h p -> (g h) q p", g=8), key='s5c', writes=['cnat'])
            for i in range(2):
                S.dma('sp', gw[:, i], glub[l, i], key='s5c', writes=['gw'])
            S.op('act', lambda e: e.activation(ldt, ldt, AF.Exp), reads=['lam'], writes=['dt'])
            for g2 in range(2):
                sl = slice(g2 * 64, (g2 + 1) * 64)
                S.op('dve', lambda e, sl=sl, g2=g2: e.tensor_scalar(pm[:, 0, sl], lami[:, sl], ldt[:, g2:g2 + 1], None, ALU.mult), reads=['lam', 'dt'], writes=['ang'])
                S.op('dve', lambda e, sl=sl, g2=g2: e.tensor_scalar(pm[:, 1, sl], lamr[:, sl], ldt[:, g2:g2 + 1], None, ALU.mult), reads=['lam', 'dt'], writes=['pm1'])
            S.op('act', lambda e: e.activation(pm[:, 2, :], pm[:, 1, :], AF.Exp), reads=['pm1'], writes=['pm2'])
            ang = pm[:, 0, :]
            sincos(ang, 128, pm[:, 3, :], pm[:, 4, :], 16)
            TT = lambda o, a, b, op, r_, w_: S.op('dve', lambda e: e.tensor_tensor(o, a, b, op), reads=r_, writes=w_)
            pmk = ['pm%d' % i for i in range(12)]
            TT(pm[:, 5, :], pm[:, 2, :], pm[:, 4, :], ALU.mult, ['pm2', 'sc1'], ['pm5'])
            S.op('dve', lambda e: e.tensor_scalar(pm[:, 5, :], pm[:, 5, :], -1.0, None, ALU.add), reads=['pm5'], writes=['pm5'])
            TT(pm[:, 6, :], pm[:, 2, :], pm[:, 3, :], ALU.mult, ['pm2', 'sc0'], ['pm6'])
            TT(pm[:, 7, :], pm[:, 5, :], lamr, ALU.mult, ['pm5', 'lam'], ['pm7'])
            TT(pm[:, 9, :], pm[:, 6, :], lami, ALU.mult, ['pm6', 'lam'], ['pm9'])
            TT(pm[:, 7, :], pm[:, 7, :], pm[:, 9, :], ALU.add, ['pm7', 'pm9'], ['pm7'])
            TT(pm[:, 8, :], pm[:, 6, :], lamr, ALU.mult, ['pm6', 'lam'], ['pm8'])
            TT(pm[:, 9, :], pm[:, 5, :], lami, ALU.mult, ['pm5', 'lam'], ['pm9'])
            TT(pm[:, 8, :], pm[:, 8, :], pm[:, 9, :], ALU.subtract, ['pm8', 'pm9'], ['pm8'])
            TT(pm[:, 9, :], lamr, lamr, ALU.mult, ['lam'], ['pm9'])
            TT(pm[:, 10, :], lami, lami, ALU.mult, ['lam'], ['pm10'])
            TT(pm[:, 9, :], pm[:, 9, :], pm[:, 10, :], ALU.add, ['pm9', 'pm10'], ['pm9'])
            S.op('dve', lambda e: e.reciprocal(pm[:, 9, :], pm[:, 9, :]), reads=['pm9'], writes=['pm9'])
            TT(pm[:, 10, :], pm[:, 7, :], pm[:, 9, :], ALU.mult, ['pm7', 'pm9'], ['pm10'])
            TT(pm[:, 11, :], pm[:, 8, :], pm[:, 9, :], ALU.mult, ['pm8', 'pm9'], ['pm11'])
            for k_, row in enumerate((0, 2, 10, 11)):
                S.op('pe', lambda e, k_=k_, row=row: e.transpose(ps[0][:, k_ * 16:(k_ + 1) * 16], pm[:, row, :], ident_f[0:16, 0:16]),
                     reads=['ang', 'pm2', 'pm10', 'pm11', 'ident_f'], writes=['ps0'])
            S.op('dve', lambda e: e.tensor_copy(sm4.rearrange("p a b -> p (a b)"), ps[0][:, 0:64]), reads=['ps0'], writes=['sm4'])
            thT, rT, krT, kiT = sm4[:, 0, :], sm4[:, 1, :], sm4[:, 2, :], sm4[:, 3, :]
            angT = rr[:, 2, :]
            S.op('dve', lambda e: e.tensor_tensor(angT.rearrange("p (j t) -> p j t", j=16), iota.unsqueeze(1).to_broadcast([128, 16, 128]),
                                                  thT.unsqueeze(2).to_broadcast([128, 16, 128]), ALU.mult), reads=['iota', 'sm4'], writes=['ang'])
            ang = angT
            sincos(ang, 2048, sinT.rearrange("p j t -> p (j t)"), cosT.rearrange("p j t -> p (j t)"), 128)
            S.op('dve', lambda e: e.tensor_copy(sinT, sinT), reads=['sc0'], writes=['sinT'])
            S.op('dve', lambda e: e.tensor_copy(cosT, cosT), reads=['sc1'], writes=['cosT'])
            ang128 = rr[:, 3, 0:16]
            S.op('dve', lambda e: e.tensor_scalar(ang128, thT, 128.0, None, ALU.mult), reads=['sm4'], writes=['ang'])
            ang = ang128
            sincos(ang, 16, cs128[:, 1, :], cs128[:, 0, :], 128)
            S.op('dve', lambda e: e.tensor_copy(cs128, cs128), reads=['sc0', 'sc1'], writes=['cs128'])
            S.op('dve', lambda e: e.tensor_copy(Rtab, rT.unsqueeze(2).to_broadcast([128, 16, 128])), reads=['sm4'], writes=['Rtab'])
            S.op('dve', lambda e: e.memset(Rtab[:, :, 0:1], 0.0), reads=['Rtab'], writes=['Rtab'])
            krb = krT.unsqueeze(2).to_broadcast([128, 16, 16])
            kib = kiT.unsqueeze(2).to_broadcast([128, 16, 16])
            TT(bb[:, 0], bnat[:, 0], krb, ALU.mult, ['bnat', 'sm4'], ['bb0'])
            TT(bb[:, 1], bnat[:, 1], kib, ALU.mult, ['bnat', 'sm4'], ['bb1'])
            TT(bb[:, 0], bb[:, 0], bb[:, 1], ALU.subtract, ['bb0', 'bb1'], ['bb0'])
            TT(bb[:, 2], bnat[:, 1], krb, ALU.mult, ['bnat', 'sm4'], ['bb2'])
            TT(bb[:, 3], bnat[:, 0], kib, ALU.mult, ['bnat', 'sm4'], ['bb3'])
            TT(bb[:, 2], bb[:, 2], bb[:, 3], ALU.add, ['bb2', 'bb3'], ['bb2'])
            S.op('pool', lambda e: e.memset(Xb, 0.0), writes=['Xb'])
            for j in range(16):
                for g2 in range(2):
                    col0 = 16 * ((2 * j + g2) % 8)
                    rs_ = slice(g2 * 64, (g2 + 1) * 64)
                    for ri_, src in enumerate((0, 2)):
                        S.op('dve' if ri_ else 'pool', lambda e, j=j, rs_=rs_, col0=col0, ri_=ri_, src=src: e.tensor_copy(Xb[rs_, j, ri_, col0:col0 + 16], bb[rs_, src, j, :]),
                             reads=['bb0', 'bb2'], writes=['Xb'])
            for j0 in range(0, 16, 4):
                for jj in range(4):
                    for ri_ in range(2):
                        k_ = jj * 2 + ri_
                        S.op('pe', lambda e, j0=j0, jj=jj, ri_=ri_, k_=k_: e.transpose(psb[:, k_ * 128:(k_ + 1) * 128], Xb[:, j0 + jj, ri_, :], ident_b), reads=['Xb', 'ident_b'], writes=['psb'])
                S.op('dve', lambda e, j0=j0: e.tensor_copy(BT[:, j0:j0 + 4].rearrange("p j r c -> p (j r c)"), psb), reads=['psb'], writes=['BT'])
            for v_, (src, sgn) in enumerate(((0, 1.0), (0, -1.0), (1, -1.0))):
                for g2 in range(2):
                    S.op('dve', lambda e, v_=v_, src=src, sgn=sgn, g2=g2: e.tensor_scalar(Xc[:, :, v_, g2 * 64:(g2 + 1) * 64], cnat[:, src], padc[:, 2 + g2:3 + g2], sgn, ALU.mult, ALU.mult),
                         reads=['cnat', 'padc'], writes=['Xc'])
            for q in range(4):
                for v_ in range(3):
                    k_ = q * 3 + v_
                    S.op('pe', lambda e, q=q, v_=v_, k_=k_: e.transpose(psb[:, (k_ % 8) * 128:(k_ % 8 + 1) * 128], Xc[:, q, v_, :], ident_b), reads=['Xc', 'ident_b'], writes=['psb'])
                    S.op('dve', lambda e, q=q, v_=v_, k_=k_: e.tensor_copy(CTm[:, q, v_, :], psb[:, (k_ % 8) * 128:(k_ % 8 + 1) * 128]), reads=['psb'], writes=['CTm'])
            S.op('pool', lambda e: e.memset(CT, 0.0), writes=['CT'])
            for jj in range(4):
                for q in range(4):
                    S.op('dve', lambda e, jj=jj, q=q: e.tensor_copy(CT[:, q, jj, :, 32 * jj:32 * jj + 32], CTm[:, q, :, 32 * jj:32 * jj + 32]), reads=['CTm', 'CT'], writes=['CT'])
            S.flush()
            uin = [sb(es, "s5u%d" % i, [128, 4, 128], BF16) for i in range(2)]
            tmp = [[sb(es, "s5t%d_%d" % (i, s_), [128, 512], F32) for i in range(4)] for s_ in range(2)]
            vre2 = [sb(es, "vre%d" % s_, [128, 512], F32) for s_ in range(2)]
            vim2 = [sb(es, "vim%d" % s_, [128, 512], F32) for s_ in range(2)]
            wre2 = [sb(es, "wre%d" % s_, [128, 512], F32) for s_ in range(2)]
            wim2 = [sb(es, "wim%d" % s_, [128, 512], F32) for s_ in range(2)]
            Ap2 = [[sb(es, "Ap%d_%d" % (i, s_), [128, 512], BF16) for i in range(4)] for s_ in range(2)]
            car = sb(es, "car", [128, 6, 16], F32)
            yg2 = [sb(es, "yg%d" % s_, [128, 128], F32) for s_ in range(2)]
            t22 = [sb(es, "t2_%d" % s_, [128, 128], F32) for s_ in range(2)]
            sgm2 = [sb(es, "sgm%d" % s_, [128, 128], F32) for s_ in range(2)]
            gl = [sb(es, "gl%d" % i, [128, 4, 128], BF16) for i in range(2)]
            sgt = sb(es, "sgt", [128, 4, 128], F32)
            yst = [sb(es, "s5y%d" % i, [128, 4, 128], BF16) for i in range(2)]
            S.op('pool', lambda e: e.memset(car, 0.0), writes=['car'])
            for c in range(NCH):
                cs = slice(c * 128, (c + 1) * 128)
                u_, ku = uin[c % 2], 's5u%d' % (c % 2)
                gl_, kgl = gl[c % 2], 'gl%d' % (c % 2)
                ys, kys = yst[c % 2], 's5y%d' % (c % 2)
                S.dma('sp', u_, uT[:, :, cs].rearrange("k p t -> p k t"), key=ku, writes=[ku])
                def Q(qd):
                    s_ = qd % 2
                    j4 = slice(4 * qd, 4 * qd + 4)
                    return dict(qd=qd, s=s_, j4=j4, pR=s_ * 2, pI=s_ * 2 + 1, pY=4 + s_,
                                cosq=cosT[:, j4, :].rearrange("p j t -> p (j t)"), sinq=sinT[:, j4, :].rearrange("p j t -> p (j t)"),
                                Rq=Rtab[:, j4, :].rearrange("p j t -> p (j t)"),
                                tmp=tmp[s_], vre=vre2[s_], vim=vim2[s_], wre=wre2[s_], wim=wim2[s_], Ap=Ap2[s_],
                                yg=yg2[s_], t2=t22[s_], sgm=sgm2[s_], cc=slice(4 * s_, 4 * s_ + 4),
                                u=u_, ku=ku, gl=gl_, kgl=kgl,
                                k=lambda nm, s_=s_: '%s_%d' % (nm, s_))

                def st_bu(q):
                    for jj in range(4):
                        j = 4 * q['qd'] + jj
                        S.op('pe', lambda e, j=j, jj=jj: e.matmul(ps[q['pR']][:, jj * 128:(jj + 1) * 128], BT[:, j, 0, :], q['u'][:, q['qd'], :], start=True, stop=True), reads=['BT', q['ku']], writes=[PSK[q['pR']]])
                        S.op('pe', lambda e, j=j, jj=jj: e.matmul(ps[q['pI']][:, jj * 128:(jj + 1) * 128], BT[:, j, 1, :], q['u'][:, q['qd'], :], start=True, stop=True), reads=['BT', q['ku']], writes=[PSK[q['pI']]])

                def st_rot(q):
                    k, t_ = q['k'], q['tmp']
                    S.op('dve', lambda e: e.tensor_tensor(t_[0], ps[q['pR']], q['cosq'], ALU.mult), reads=[PSK[q['pR']], 'cosT'], writes=[k('s5t0')])
                    S.op('dve', lambda e: e.tensor_tensor(t_[1], ps[q['pI']], q['sinq'], ALU.mult), reads=[PSK[q['pI']], 'sinT'], writes=[k('s5t1')])
                    S.op('dve', lambda e: e.tensor_tensor(t_[2], ps[q['pI']], q['cosq'], ALU.mult), reads=[PSK[q['pI']], 'cosT'], writes=[k('s5t2')])
                    S.op('dve', lambda e: e.tensor_tensor(t_[3], ps[q['pR']], q['sinq'], ALU.mult), reads=[PSK[q['pR']], 'sinT'], writes=[k('s5t3')])

                def st_add(q):
                    k, t_ = q['k'], q['tmp']
                    S.op('pool', lambda e: e.tensor_tensor(q['vre'], t_[0], t_[1], ALU.add), reads=[k('s5t0'), k('s5t1')], writes=[k('vre')])
                    S.op('pool', lambda e: e.tensor_tensor(q['vim'], t_[2], t_[3], ALU.subtract), reads=[k('s5t2'), k('s5t3')], writes=[k('vim')])

                def st_scan(q):
                    k, j4 = q['k'], q['j4']
                    v3r = q['vre'].rearrange("p (j t) -> p j t", j=4)
                    v3i = q['vim'].rearrange("p (j t) -> p j t", j=4)
                    S.op('dve', lambda e: e.tensor_tensor(v3r[:, :, 0], v3r[:, :, 0], car[:, 0, j4], ALU.add), reads=[k('vre'), 'car'], writes=[k('vre')])
                    S.op('dve', lambda e: e.tensor_tensor(v3i[:, :, 0], v3i[:, :, 0], car[:, 1, j4], ALU.add), reads=[k('vim'), 'car'], writes=[k('vim')])
                    S.op('dve', lambda e: e.tensor_tensor_scan(q['wre'], q['Rq'], q['vre'], 0.0, ALU.mult, ALU.add), reads=['Rtab', k('vre')], writes=[k('wre')])
                    S.op('dve', lambda e: e.tensor_tensor_scan(q['wim'], q['Rq'], q['vim'], 0.0, ALU.mult, ALU.add), reads=['Rtab', k('vim')], writes=[k('wim')])

                def st_carry(q):
                    k, j4, cc = q['k'], q['j4'], q['cc']
                    w3r = q['wre'].rearrange("p (j t) -> p j t", j=4)
                    w3i = q['wim'].rearrange("p (j t) -> p j t", j=4)
                    c128, s128 = cs128[:, 0, j4], cs128[:, 1, j4]
                    S.op('dve', lambda e: e.tensor_tensor(car[:, 2, cc], w3r[:, :, 127], c128, ALU.mult), reads=[k('wre'), 'cs128'], writes=[k('car2')])
                    S.op('dve', lambda e: e.tensor_tensor(car[:, 3, cc], w3i[:, :, 127], s128, ALU.mult), reads=[k('wim'), 'cs128'], writes=[k('car3')])
                    S.op('dve', lambda e: e.tensor_tensor(car[:, 4, cc], w3r[:, :, 127], s128, ALU.mult), reads=[k('wre'), 'cs128'], writes=[k('car4')])
                    S.op('dve', lambda e: e.tensor_tensor(car[:, 5, cc], w3i[:, :, 127], c128, ALU.mult), reads=[k('wim'), 'cs128'], writes=[k('car5')])
                    S.op('dve', lambda e: e.tensor_tensor(car[:, 2, cc], car[:, 2, cc], car[:, 3, cc], ALU.subtract), reads=[k('car2'), k('car3')], writes=[k('car2')])
                    S.op('dve', lambda e: e.tensor_tensor(car[:, 4, cc], car[:, 4, cc], car[:, 5, cc], ALU.add), reads=[k('car4'), k('car5')], writes=[k('car4')])
                    S.op('dve', lambda e: e.tensor_tensor(car[:, 0, j4], car[:, 2, cc], rT[:, j4], ALU.mult), reads=[k('car2'), 'sm4', 'car'], writes=['car'])
                    S.op('dve', lambda e: e.tensor_tensor(car[:, 1, j4], car[:, 4, cc], rT[:, j4], ALU.mult), reads=[k('car4'), 'sm4', 'car'], writes=['car'])

                def st_prod(q):
                    k, A_ = q['k'], q['Ap']
                    S.op('pool', lambda e: e.tensor_tensor(A_[0], q['wre'], q['cosq'], ALU.mult), reads=[k('wre'), 'cosT'], writes=[k('Ap0')])
                    S.op('pool', lambda e: e.tensor_tensor(A_[1], q['wim'], q['sinq'], ALU.mult), reads=[k('wim'), 'sinT'], writes=[k('Ap1')])
                    S.op('dve', lambda e: e.tensor_tensor(A_[2], q['wre'], q['sinq'], ALU.mult), reads=[k('wre'), 'sinT'], writes=[k('Ap2')])
                    S.op('dve', lambda e: e.tensor_tensor(A_[3], q['wim'], q['cosq'], ALU.mult), reads=[k('wim'), 'cosT'], writes=[k('Ap3')])

                def st_y(q):
                    k, A_ = q['k'], q['Ap']
                    n_mm = 0
                    for jj in range(4):
                        for a_, v_ in ((0, 0), (1, 1), (2, 2), (3, 2)):
                            S.op('pe', lambda e, jj=jj, a_=a_, v_=v_, n_mm=n_mm: e.matmul(ps[q['pY']][:, 0:128], CT[:, q['qd'], jj, v_, :], A_[a_][:, jj * 128:(jj + 1) * 128],
                                                                                       start=(n_mm == 0), stop=(n_mm == 15)), reads=['CT', k('Ap%d' % a_)], writes=[PSK[q['pY']]])
                            n_mm += 1

                def st_gelu1(q):
                    k, yg, t2_ = q['k'], q['yg'], q['t2']
                    qd = q['qd']
                    S.op('dve', lambda e: e.scalar_tensor_tensor(yg, q['u'][:, qd, :], vecT[:, l, 68 + qd:69 + qd], ps[q['pY']][:, 0:128], ALU.mult, ALU.add), reads=[q['ku'], 'vecT', PSK[q['pY']]], writes=[k('yg')])
                    S.op('dve', lambda e: e.tensor_tensor(t2_, yg, yg, ALU.mult), reads=[k('yg')], writes=[k('t2_')])
                    S.op('dve', lambda e: e.tensor_scalar(t2_, t2_, 0.044715, 1.0, ALU.mult, ALU.add), reads=[k('t2_')], writes=[k('t2_')])
                    S.op('dve', lambda e: e.tensor_tensor(t2_, t2_, yg, ALU.mult), reads=[k('t2_'), k('yg')], writes=[k('t2_')])
                    S.op('act', lambda e: e.activation(q['sgm'], t2_, AF.Sigmoid, scale=1.5957691216057308), reads=[k('t2_')], writes=[k('sgm')])

                def st_gelu2(q):
                    k = q['k']
                    S.op('dve', lambda e: e.tensor_tensor(q['gl'][:, q['qd'], :], q['yg'], q['sgm'], ALU.mult), reads=[k('yg'), k('sgm')], writes=[q['kgl']])

                for qa in (0, 2):
                    A_, B_ = Q(qa), Q(qa + 1)
                    for st in (st_bu, st_rot):
                        st(A_)
                        st(B_)
                    st_add(A_)
                    st_add(B_)
                    st_scan(A_)
                    st_scan(B_)
                    st_prod(A_)
                    st_prod(B_)
                    st_y(A_)
                    st_y(B_)
                    st_carry(A_)
                    st_carry(B_)
                    st_gelu1(A_)
                    st_gelu1(B_)
                    st_gelu2(A_)
                    st_gelu2(B_)
                for ot_ in range(8):
                    pz = 2 + ot_ // 4 if False else (0 if ot_ < 4 else 1)
                    for kt in range(4):
                        S.op('pe', lambda e, ot_=ot_, kt=kt, gl_=gl_, pz=pz: e.matmul(ps[pz][:, (ot_ % 4) * 128:(ot_ % 4 + 1) * 128], gw[:, ot_ // 4, kt, (ot_ % 4) * 128:(ot_ % 4 + 1) * 128], gl_[:, kt, :],
                                                                                       start=(kt == 0), stop=(kt == 3)), reads=['gw', kgl], writes=[PSK[pz]])
                for o4 in range(4):
                    S.op('act', lambda e, o4=o4: e.activation(sgt[:, o4, :], ps[1][:, o4 * 128:(o4 + 1) * 128], AF.Sigmoid, bias=vecT[:, l, 76 + o4:77 + o4]), reads=['ps1', 'vecT'], writes=['sgt'])
                    S.op('dve', lambda e, o4=o4, ys=ys: e.scalar_tensor_tensor(ys[:, o4, :], ps[0][:, o4 * 128:(o4 + 1) * 128], vecT[:, l, 72 + o4:73 + o4], sgt[:, o4, :], ALU.add, ALU.mult),
                         reads=['ps0', 'vecT', 'sgt'], writes=[kys])
                S.dma('sp', yT[0:4, :, cs].rearrange("k p t -> p k t"), ys, key=kys, reads=[kys])
            S.flush()

    for l in range(DEPTH):
        phase_A(l)
        if stop_after == "A":
            return nc
        phase_s5(l)
        phase_mlstm(l)
        phase_fox(l)
        phase_pool(l)
        if stop_after == "mix":
            return nc
        phase_C(l)
        if stop_after == "C":
            return nc
    phase_Z()
    pes.close()
    return nc


def _in_maps(inputs, SEQ, DEPTH, n_cores):
    consts = host_consts()
    maps = []
    B = inputs["x"].shape[0]
    for c in range(n_cores):
        m = {"x": np.ascontiguousarray(inputs["x"][c % B], dtype=np.float32), "meta_tokens": np.asarray(inputs["meta_tokens"], np.float32)}
        for n, _ in PARAMS:
            m[n] = np.ascontiguousarray(inputs[n][:DEPTH], dtype=np.float32)
        m.update(consts)
        maps.append(m)
    return maps


def kernel(**inputs):
    SEQ, DEPTH = 4096, 2
    B = inputs["x"].shape[0]
    nc = build_program(SEQ, DEPTH)
    res = run_bass_kernel_spmd(nc, _in_maps(inputs, SEQ, DEPTH, 8), core_ids=list(range(8)))
    return np.stack([np.asarray(res.results[b]["out"], dtype=np.float32) for b in range(B)], axis=0)
```

```python
import numpy as np
import concourse.bass as bass
import concourse.mybir as mybir
from concourse.bass_utils import run_bass_kernel_spmd

F32 = mybir.dt.float32
BF16 = mybir.dt.bfloat16
AF = mybir.ActivationFunctionType
ALU = mybir.AluOpType
AX = mybir.AxisListType


class Sched:
    def __init__(self, nc):
        self.nc = nc
        self.engs = {'pe': nc.tensor, 'act': nc.scalar, 'dve': nc.vector, 'pool': nc.gpsimd, 'sp': nc.sync}
        self.ops = []
        self.last_w = {}
        self.readers = {}
        self.dma_cnt = {}
        self.esem = {e: nc.alloc_semaphore("s_" + e) for e in self.engs}
        self.dsem = {}
        self.ebase = {e: 0 for e in self.engs}
        self.dma_issued = {}

    def _deps(self, reads, writes):
        deps = set()
        for k in reads:
            if k in self.last_w:
                deps.add(self.last_w[k])
        for k in writes:
            if k in self.last_w:
                deps.add(self.last_w[k])
            deps.update(self.readers.get(k, ()))
        return deps

    def _record(self, oid, reads, writes):
        for k in writes:
            self.last_w[k] = oid
            self.readers[k] = []
        for k in reads:
            self.readers.setdefault(k, []).append(oid)

    @staticmethod
    def _bank(reads, writes):
        bk = lambda k: 'psb' if k.startswith('psb') else k
        w = [bk(k) for k in writes] + [bk(k) for k in reads if k.startswith('ps')]
        r = [k for k in reads if not k.startswith('ps')]
        return r, w

    def op(self, eng, fn, reads=(), writes=()):
        reads, writes = self._bank(list(reads), list(writes))
        deps = self._deps(reads, writes)
        oid = len(self.ops)
        self.ops.append(dict(eng=eng, fn=fn, deps=deps, dma=None))
        self._record(oid, reads, writes)
        return oid

    def dma(self, eng, out, in_, key, reads=(), writes=(), **kw):
        deps = self._deps(reads, writes)
        oid = len(self.ops)
        n = self.dma_cnt.get(key, 0) + 1
        self.dma_cnt[key] = n
        self.ops.append(dict(eng=eng, fn=lambda e: e.dma_start(out=out, in_=in_, **kw), deps=deps, dma=(key, n)))
        self._record(oid, reads, writes)
        return oid

    def flush(self):
        nc = self.nc
        ops = self.ops
        if not ops:
            return
        eng_list = {e: [] for e in self.engs}
        for oid, o in enumerate(ops):
            o['pos'] = len(eng_list[o['eng']])
            eng_list[o['eng']].append(oid)
        seen = {e: {} for e in self.engs}
        need_inc = set()
        for oid, o in enumerate(ops):
            e = o['eng']
            want = {}
            for d in o['deps']:
                do = ops[d]
                if do['dma'] is not None:
                    key, n = do['dma']
                    want[('d', key)] = max(want.get(('d', key), 0), n)
                else:
                    if do['eng'] == 'pe' and e == 'pe' and o['dma'] is None:
                        continue
                    want[('e', do['eng'])] = max(want.get(('e', do['eng']), -1), do['pos'])
            waits = []
            for k, v in want.items():
                if seen[e].get(k, -1) >= v:
                    continue
                seen[e][k] = v
                waits.append((k, v))
                if k[0] == 'e':
                    need_inc.add(eng_list[k[1]][v])
            o['waits'] = waits
        issued = self.dma_issued
        for oid, o in enumerate(ops):
            nw = []
            for k, v in o['waits']:
                if k[0] == 'd':
                    v = max(v, issued.get(k[1], 0))
                nw.append((k, v))
            o['waits'] = nw
            if o['dma'] is not None:
                issued[o['dma'][0]] = o['dma'][1]
        for k in self.dma_cnt:
            if k not in self.dsem:
                self.dsem[k] = nc.alloc_semaphore("d_%d" % len(self.dsem))
        inc_prefix = {}
        for e, lst in eng_list.items():
            c = self.ebase[e]
            for oid in lst:
                if oid in need_inc:
                    c += 1
                    inc_prefix[oid] = c
            self.ebase[e] = c
        finals = [(k, n) for k, n in self.dma_cnt.items()]

        def emit(e, eng):
            for oid in eng_list[e]:
                o = ops[oid]
                for k, v in o['waits']:
                    if k[0] == 'd':
                        eng.wait_ge(self.dsem[k[1]], 16 * v)
                    else:
                        eng.wait_ge(self.esem[k[1]], inc_prefix[eng_list[k[1]][v]])
                ins = o['fn'](eng)
                if o['dma'] is not None:
                    ins.then_inc(self.dsem[o['dma'][0]], 16)
                elif oid in need_inc:
                    ins.then_inc(self.esem[e], 1)
            if e == 'sp':
                for k, n in finals:
                    eng.wait_ge(self.dsem[k], 16 * n)

        with nc.Block() as block:
            @block.sync
            def _(eng):
                emit('sp', eng)

            @block.tensor
            def _(eng):
                emit('pe', eng)

            @block.scalar
            def _(eng):
                emit('act', eng)

            @block.vector
            def _(eng):
                emit('dve', eng)

            @block.gpsimd
            def _(eng):
                emit('pool', eng)
        nc.all_engine_barrier()
        self.ops = []
        self.last_w = {}
        self.readers = {}


D = 2048
KT = 16
INC = 4620
DFF = 8192
EPS = 1e-6
SEGS = [("s_u", 0), ("m_q", 512), ("m_k", 1024), ("m_v", 1536), ("m_o", 2048),
        ("f_q", 2568), ("f_k", 3080), ("f_v", 3592), ("p_u", 4108)]
PARAMS = [("g_pre_mix", [D]), ("g_post_mix", [D]), ("g_pre_ffn", [D]), ("g_post_ffn", [D]),
          ("w_in", [D, INC]), ("ml_gate_bias", [8]), ("fx_gate_bias", [4]),
          ("ssm_lam_re", [32, 64]), ("ssm_lam_im", [32, 64]), ("ssm_log_dt", [32]),
          ("ssm_b_re", [32, 64, 16]), ("ssm_b_im", [32, 64, 16]), ("ssm_c_re", [32, 16, 64]),
          ("ssm_c_im", [32, 16, 64]), ("ssm_d", [32, 16]), ("ssm_glu_w", [512, 1024]), ("ssm_glu_b", [1024]),
          ("ml_conv_w", [4, 1024]), ("ml_norm_g", [512]), ("pool_w", [4, 128, 128]), ("pool_scale", [512]),
          ("w_out", [D, D]), ("mlp_w1", [D, DFF]), ("mlp_w2", [DFF, D])]
TWO_PI = float(2.0 * np.pi)


def host_consts():
    tri = (np.arange(128)[:, None] <= np.arange(128)[None, :]).astype(np.float32)
    pad = np.zeros((128, 4), np.float32)
    pad[112:, 0] = 1.0
    pad[:112, 1] = -1e30
    pad[:, 2] = ((np.arange(128) // 16) % 2 == 0)
    pad[:, 3] = ((np.arange(128) // 16) % 2 == 1)
    corr = np.zeros((128, 4, 16), np.float32)
    for g, w in enumerate((2, 4, 8, 16)):
        corr[:, g, :] = w / np.minimum(np.arange(1, 17), w)
    iota = np.broadcast_to(np.arange(128, dtype=np.float32)[None, :], (128, 128)).copy()
    ident = np.eye(128, dtype=np.float32)
    return {"c_tri": tri, "c_pad": pad, "c_corr": corr, "c_iota": iota, "c_ident": ident}


def build_program(SEQ, DEPTH, debug=False, stop_after=None):
    nc = bass.Bass("TRN2", target_bir_lowering=False)
    LP = SEQ + 128
    NCH = LP // 128
    blocks = [(t0, min(512, LP - t0)) for t0 in range(0, LP, 512)]

    def din(name, shape):
        return nc.dram_tensor(name, list(shape), F32, kind="ExternalInput").ap()

    x = din("x", [SEQ, D])
    meta = din("meta_tokens", [16, D])
    P = {n: din(n, [DEPTH] + sh) for n, sh in PARAMS}
    c_tri = din("c_tri", [128, 128])
    c_pad = din("c_pad", [128, 4])
    c_corr = din("c_corr", [128, 4, 16])
    c_iota = din("c_iota", [128, 128])
    c_ident = din("c_ident", [128, 128])
    out = nc.dram_tensor("out", [SEQ, D], F32, kind="ExternalOutput").ap()
    dbg_kind = "ExternalOutput" if debug else "Internal"

    def scr(name, shape, dt, dbg=False):
        return nc.dram_tensor(name, list(shape), dt, kind=(dbg_kind if dbg else "Internal")).ap()

    hT = scr("hT", [KT, 128, LP], F32, True)
    uT = scr("uT", [4, 128, LP], BF16, True)
    mqkT = scr("mqkT", [8, 128, LP], BF16, True)
    mv = scr("mv", [LP, 512], BF16, True)
    mo = scr("mo", [LP, 512], BF16, True)
    gates = scr("gates", [LP, 12], F32, True)
    fqT = scr("fqT", [4, 128, LP], BF16, True)
    fkT = scr("fkT", [4, 128, LP], BF16, True)
    fv = scr("fv", [LP, 512], BF16, True)
    puT = scr("puT", [4, 128, LP], BF16, True)
    yT = scr("yT", [KT, 128, LP], BF16, True)
    s5scr = scr("s5scr", [8, 2048], F32)
    winb = scr("winb", [DEPTH, 9, 128, KT, 512], BF16)
    wgb = scr("wgb", [DEPTH, 128, KT, 12], BF16)
    woutb = scr("woutb", [DEPTH, 4, 128, KT, 512], BF16)
    w1b = scr("w1b", [DEPTH, 16, 128, KT, 512], BF16)
    w2b = scr("w2b", [DEPTH, 16, 128, KT, 512], BF16)
    glub = scr("glub", [DEPTH, 2, 128, 4, 512], BF16)
    poolwb = scr("poolwb", [DEPTH, 128, 4, 128], BF16)

    S = Sched(nc)
    ps = [nc.alloc_psum_tensor("ps%d" % i, [128, 512], F32).ap() for i in range(7)]
    psb = nc.alloc_psum_tensor("psb", [128, 1024], BF16).ap()
    PSK = ["ps%d" % i for i in range(7)]

    from contextlib import ExitStack

    uid = [0]

    def sb(es, name, shape, dt):
        uid[0] += 1
        return es.enter_context(nc.sbuf_tensor("%s_%d" % (name, uid[0]), list(shape), dt)).ap()

    pes = ExitStack()
    ident_f = sb(pes, "ident_f", [128, 128], F32)
    ident_b = sb(pes, "ident_b", [128, 128], BF16)
    ones_b = sb(pes, "ones_b", [128, 128], BF16)
    ones_f = sb(pes, "ones_f", [128, 128], F32)
    tri_f = sb(pes, "tri_f", [128, 128], F32)
    tri_b = sb(pes, "tri_b", [128, 128], BF16)
    padc = sb(pes, "padc", [128, 4], F32)
    corr = sb(pes, "corr", [128, 4, 16], F32)
    iota = sb(pes, "iota", [128, 128], F32)
    vecT = sb(pes, "vecT", [128, DEPTH, 112], F32)
    S.dma('sp', ident_f, c_ident, key='c', writes=['ident_f'])
    S.dma('sp', tri_f, c_tri, key='c', writes=['tri_f'])
    S.dma('sp', padc, c_pad, key='c', writes=['padc'])
    S.dma('sp', corr, c_corr, key='c', writes=['corr'])
    S.dma('sp', iota, c_iota, key='c', writes=['iota'])
    S.op('dve', lambda e: e.tensor_copy(ident_b, ident_f), reads=['ident_f'], writes=['ident_b'])
    S.op('dve', lambda e: e.tensor_copy(tri_b, tri_f), reads=['tri_f'], writes=['tri_b'])
    S.op('dve', lambda e: e.memset(ones_b, 1.0), writes=['ones_b'])
    S.op('dve', lambda e: e.memset(ones_f, 1.0), writes=['ones_f'])
    with ExitStack() as es:
        vs = sb(es, "vs", [112, 128], F32)
        for l in range(DEPTH):
            r = 0
            for nm, nrow in (("g_pre_mix", 16), ("g_post_mix", 16), ("g_pre_ffn", 16), ("g_post_ffn", 16),
                             ("pool_scale", 4), ("ssm_d", 4), ("ssm_glu_b", 8)):
                src = P[nm][l]
                if nm == "ssm_d":
                    src = src.rearrange("g h -> (g h)")
                S.dma('sp', vs[r:r + nrow, :], src.rearrange("(r c) -> r c", c=128), key='c', writes=['vs'])
                r += nrow
            S.dma('sp', vs[80:112, :], P["ml_conv_w"][l].rearrange("j (t c) -> (j t) c", c=128), key='c', writes=['vs'])
            S.op('pe', lambda e: e.transpose(ps[0][:, 0:112], vs, ident_f[0:112, 0:112]), reads=['vs', 'ident_f'], writes=['ps0'])
            S.op('dve', lambda e, l=l: e.tensor_copy(vecT[:, l, :], ps[0][:, 0:112]), reads=['ps0'], writes=['vecT'])
        def cast_panel(dst, src_rows, c0, nk, ncol=512):
            sv = src_rows.rearrange("(kt p) c -> p kt c", p=128)
            for k0 in range(0, nk, 4):
                k1 = min(nk, k0 + 4)
                S.dma('pool', dst[:, k0:k1, :], sv[:, k0:k1, c0:c0 + ncol], key='cast')
        for l in range(DEPTH):
            for i, (_, c0) in enumerate(SEGS):
                cast_panel(winb[l, i], P["w_in"][l], c0, KT)
            sv = P["w_in"][l].rearrange("(kt p) c -> p kt c", p=128)
            S.dma('pool', wgb[l][:, :, 0:8], sv[:, :, 2560:2568], key='cast')
            S.dma('pool', wgb[l][:, :, 8:12], sv[:, :, 4104:4108], key='cast')
            for i in range(4):
                cast_panel(woutb[l, i], P["w_out"][l], i * 512, KT)
            for i in range(16):
                cast_panel(w1b[l, i], P["mlp_w1"][l], i * 512, KT)
            for je in range(4):
                for kq in range(4):
                    cast_panel(w2b[l, je * 4 + kq], P["mlp_w2"][l][kq * 2048:(kq + 1) * 2048, :], je * 512, KT)
            for i in range(2):
                cast_panel(glub[l, i], P["ssm_glu_w"][l], i * 512, 4)
            S.dma('pool', poolwb[l], P["pool_w"][l].rearrange("g c d -> c g d"), key='cast')
        xin = [sb(es, "xin%d" % i, [128, D], F32) for i in range(2)]
        hst = [sb(es, "hst%d" % i, [128, KT, 128], F32) for i in range(2)]
        for c in range(NCH):
            xi, hs = xin[c % 2], hst[c % 2]
            kx, kh = "xin%d" % (c % 2), "hst%d" % (c % 2)
            if c == 0:
                S.op('pool', lambda e, xi=xi: e.memset(xi, 0.0), writes=[kx])
                S.dma('sp', xi[112:128, :], meta, key='xin', reads=[kx], writes=[kx])
            else:
                S.dma('sp', xi, x[(c - 1) * 128:c * 128, :], key='xin', writes=[kx])
            for q in range(4):
                pk = ps[q % 2]
                for j in range(4):
                    kt = q * 4 + j
                    S.op('pe', lambda e, pk=pk, j=j, kt=kt, xi=xi: e.transpose(pk[:, j * 128:(j + 1) * 128], xi[:, kt * 128:(kt + 1) * 128], ident_f),
                         reads=[kx, 'ident_f'], writes=[PSK[q % 2]])
                S.op('act' if q % 2 else 'dve',
                     (lambda e, pk=pk, q=q, hs=hs: e.activation(hs[:, q * 4:(q + 1) * 4, :], pk.rearrange("p (j t) -> p j t", j=4), AF.Identity)) if q % 2 else
                     (lambda e, pk=pk, q=q, hs=hs: e.tensor_copy(hs[:, q * 4:(q + 1) * 4, :], pk.rearrange("p (j t) -> p j t", j=4))),
                     reads=[PSK[q % 2]], writes=[kh])
            S.dma('sp', hT[:, :, c * 128:(c + 1) * 128].rearrange("k p t -> p k t"), hs, key='hst', reads=[kh], writes=['hT'])
        S.flush()
    if stop_after == "p0":
        return nc
    def evac(i, out_ap, in_ap, reads, writes):
        if i % 2:
            S.op('act', lambda e: e.activation(out_ap, in_ap, AF.Identity), reads=reads, writes=writes)
        else:
            S.op('dve', lambda e: e.tensor_copy(out_ap, in_ap), reads=reads, writes=writes)

    def sumsq_steps(src, ksrc, n, sq, rstd, dim, krstd='rstd', sq_eng='act', ksq='sq'):
        kf = ksrc if callable(ksrc) else (lambda kt: ksrc)
        steps = []
        for kt in range(KT):
            def st(kt=kt):
                sl = sq[kt % 2]
                if sq_eng == 'act':
                    S.op('act', lambda e: e.activation(sl[:, :n], src[:, kt, :n], AF.Square), reads=[kf(kt)], writes=['%s%d' % (ksq, kt % 2)])
                else:
                    S.op(sq_eng, lambda e: e.tensor_tensor(sl[:, :n], src[:, kt, :n], src[:, kt, :n], ALU.mult), reads=[kf(kt)], writes=['%s%d' % (ksq, kt % 2)])
                S.op('pe', lambda e: e.matmul(ps[6][:, :n], ones_b, sl[:, :n], start=(kt == 0), stop=(kt == KT - 1)),
                     reads=['%s%d' % (ksq, kt % 2), 'ones_b'], writes=['ps6'])
            steps.append(st)

        def fin():
            S.op('act', lambda e: e.activation(rstd[:, :n], ps[6][:, :n], AF.Sqrt, bias=EPS, scale=1.0 / dim), reads=['ps6'], writes=[krstd])
            S.op('dve', lambda e: e.reciprocal(rstd[:, :n], rstd[:, :n]), reads=[krstd], writes=[krstd])
        steps.append(fin)
        return steps

    def sumsq_rstd(src, ksrc, n, sq, rstd, dim):
        for st in sumsq_steps(src, ksrc, n, sq, rstd, dim):
            st()

    class PanelStream:
        def __init__(self, es, srcs, nb=3, name="wp"):
            self.srcs = srcs
            self.nb = nb
            self.bufs = [sb(es, "%s%d" % (name, i), [128, KT, 512], BF16) for i in range(nb)]
            self.keys = ["%s%d" % (name, i) for i in range(nb)]
            self.issued = 0
            self.used = 0

        def _issue(self):
            i = self.issued
            if i < len(self.srcs):
                S.dma('sp', self.bufs[i % self.nb], self.srcs[i], key=self.keys[i % self.nb], writes=[self.keys[i % self.nb]])
                self.issued += 1

        def next(self):
            while self.issued < min(len(self.srcs), self.used + self.nb - 0) and self.issued - self.used < self.nb:
                self._issue()
            i = self.used
            self.used += 1
            return self.bufs[i % self.nb], self.keys[i % self.nb]

        def prefetch(self):
            while self.issued < len(self.srcs) and self.issued - self.used < self.nb - 1:
                self._issue()

    def phase_A(l):
        with ExitStack() as es:
            hblk2 = [sb(es, "hblk%d" % i, [128, KT, 512], F32) for i in range(2)]
            xn2 = [sb(es, "xn%d" % i, [128, KT, 512], BF16) for i in range(2)]
            sq = [sb(es, "sq%d" % i, [128, 512], BF16) for i in range(2)]
            rstd2 = [sb(es, "rstd%d" % i, [128, 512], F32) for i in range(2)]
            wps = PanelStream(es, [winb[l, i] for _ in blocks for i in range(len(SEGS))], nb=3)
            wg = sb(es, "wg", [128, KT, 12], BF16)
            gbt = sb(es, "gbt", [128, 12], F32)
            ost = [sb(es, "ost%d" % i, [128, 4, 512], BF16) for i in range(2)]
            tst = [sb(es, "tst%d" % i, [128, 512], BF16) for i in range(2)]
            gst = [sb(es, "gst%d" % i, [128, 12], F32) for i in range(2)]
            S.dma('sp', wg, wgb[l], key='wg', writes=['wg'])
            S.dma('sp', gbt[:, 0:8], P["ml_gate_bias"][l:l + 1, :].partition_broadcast(128), key='wg', writes=['gbt'])
            S.dma('sp', gbt[:, 8:12], P["fx_gate_bias"][l:l + 1, :].partition_broadcast(128), key='wg', writes=['gbt'])
            fm_dest = {"s_u": (uT, 0), "m_q": (mqkT, 0), "m_k": (mqkT, 4), "f_q": (fqT, 0), "f_k": (fkT, 0), "p_u": (puT, 0)}
            tm_dest = {"m_v": mv, "m_o": mo, "f_v": fv}
            wcnt = 0
            ocnt = 0
            tcnt = 0
            gcnt = 0

            def norm_steps(bi, t0, n):
                s_ = bi % 2
                hb, xb_, rs_ = hblk2[s_], xn2[s_], rstd2[s_]
                kh, kx, kr = 'hblk%d' % s_, 'xn%d' % s_, 'rstd%d' % s_

                def ld():
                    for kt in range(KT):
                        S.dma('sp', hb[:, kt, :n], hT[kt, :, t0:t0 + n], key=kh, writes=[kh])
                steps = [ld] + sumsq_steps(hb, kh, n, sq, rs_, D, krstd=kr, sq_eng='dve')
                for kt in range(KT):
                    def st(kt=kt):
                        S.op('dve', lambda e: e.scalar_tensor_tensor(xb_[:, kt, :n], hb[:, kt, :n], vecT[:, l, kt:kt + 1], rs_[:, :n], ALU.mult, ALU.mult),
                             reads=[kh, kr, 'vecT'], writes=[kx])
                    steps.append(st)
                return steps

            def blk(bi, t0, n, bg):
                nonlocal wcnt, ocnt, tcnt, gcnt
                xn = xn2[bi % 2]
                kxn = 'xn%d' % (bi % 2)
                if bg:
                    bg[0]()
                    bg = bg[1:]
                per = (len(bg) + len(SEGS) - 1) // len(SEGS) if bg else 0
                for i, (nm, c0) in enumerate(SEGS):
                    w, kw = wps.next()
                    wps.prefetch()
                    if nm in fm_dest:
                        dst, toff = fm_dest[nm]
                        o = ost[ocnt % 2]
                        ko = 'ost%d' % (ocnt % 2)
                        ocnt += 1
                        for m in range(4):
                            for kt in range(KT):
                                S.op('pe', lambda e, w=w, m=m, kt=kt: e.matmul(ps[m][:, :n], w[:, kt, m * 128:(m + 1) * 128], xn[:, kt, :n], start=(kt == 0), stop=(kt == KT - 1)),
                                     reads=[kw, kxn], writes=[PSK[m]])
                            evac(1, o[:, m, :n], ps[m][:, :n], [PSK[m]], [ko])
                        for m in range(4):
                            S.dma('pool', dst[toff + m, :, t0:t0 + n], o[:, m, :n], key=ko, reads=[ko])
                    else:
                        dst = tm_dest[nm]
                        for i2 in range(n // 128):
                            pk = 4 + (i2 % 2)
                            ts_ = tst[tcnt % 2]
                            kts = 'tst%d' % (tcnt % 2)
                            tcnt += 1
                            for kt in range(KT):
                                S.op('pe', lambda e, w=w, kt=kt, i2=i2, pk=pk: e.matmul(ps[pk], xn[:, kt, i2 * 128:(i2 + 1) * 128], w[:, kt, :], start=(kt == 0), stop=(kt == KT - 1)),
                                     reads=[kw, kxn], writes=[PSK[pk]])
                            evac(1, ts_, ps[pk], [PSK[pk]], [kts])
                            S.dma('pool', dst[t0 + i2 * 128:t0 + (i2 + 1) * 128, :], ts_, key=kts, reads=[kts])
                    for st_ in bg[i * per:(i + 1) * per]:
                        st_()
                for st_ in bg[len(SEGS) * per:]:
                    st_()
                for i2 in range(n // 128):
                    g_ = gst[gcnt % 2]
                    kg = 'gst%d' % (gcnt % 2)
                    gcnt += 1
                    for kt in range(KT):
                        S.op('pe', lambda e, kt=kt, i2=i2: e.matmul(ps[6][:, 0:12], xn[:, kt, i2 * 128:(i2 + 1) * 128], wg[:, kt, :], start=(kt == 0), stop=(kt == KT - 1)),
                             reads=['wg', kxn], writes=['ps6'])
                    S.op('dve', lambda e, g_=g_: e.tensor_tensor(g_, ps[6][:, 0:12], gbt, ALU.add), reads=['ps6', 'gbt'], writes=[kg])
                    S.dma('pool', gates[t0 + i2 * 128:t0 + (i2 + 1) * 128, :], g_, key=kg, reads=[kg])

            pend = norm_steps(0, *blocks[0])
            for st_ in pend:
                st_()
            for bi, (t0_, n_) in enumerate(blocks):
                bg = norm_steps(bi + 1, *blocks[bi + 1]) if bi + 1 < len(blocks) else []
                blk(bi, t0_, n_, bg)
            S.flush()

    def phase_C(l):
        with ExitStack() as es:
            hblk = sb(es, "hblk", [128, KT, 512], F32)
            mix = sb(es, "mix", [128, KT, 512], F32)
            hn = sb(es, "hn", [128, KT, 512], BF16)
            big = sb(es, "big", [128, 32, 512], BF16)
            sq = [sb(es, "sq%d" % i, [128, 512], BF16) for i in range(2)]
            rstd = sb(es, "rstd", [128, 512], F32)
            rl = [sb(es, "rl%d" % i, [128, 512], F32) for i in range(2)]
            plist = []
            for _ in blocks:
                plist += [woutb[l, i] for i in range(4)]
                for half in range(2):
                    plist += [w1b[l, 8 * half + j] for j in range(8)]
                    plist += [w2b[l, je * 4 + kq] for je in range(4) for kq in (2 * half, 2 * half + 1)]
            wps = PanelStream(es, plist, nb=3)
            wcnt = 0
            rcnt = 0

            def post_norm_add(n, gcol):
                sumsq_rstd(mix, lambda kt: 'mx%d' % kt, n, sq, rstd, D)
                for kt in range(KT):
                    S.op('dve', lambda e, kt=kt: e.scalar_tensor_tensor(mix[:, kt, :n], mix[:, kt, :n], vecT[:, l, gcol + kt:gcol + kt + 1], rstd[:, :n], ALU.mult, ALU.mult),
                         reads=['mx%d' % kt, 'rstd', 'vecT'], writes=['mx%d' % kt])
                    S.op('pool' if kt % 3 else 'dve', lambda e, kt=kt: e.tensor_tensor(hblk[:, kt, :n], hblk[:, kt, :n], mix[:, kt, :n], ALU.add), reads=['mx%d' % kt, 'hb%d' % kt], writes=['hb%d' % kt])

            def load_y(t0, n):
                for kt in range(KT):
                    S.dma('sp', big[:, kt, :n], yT[kt, :, t0:t0 + n], key='ybl', writes=['big'])

            def blk(bi, t0, n):
                nonlocal wcnt, rcnt
                if bi == 0:
                    load_y(t0, n)
                for kt in range(KT):
                    S.dma('sp', hblk[:, kt, :n], hT[kt, :, t0:t0 + n], key='hblk', writes=['hb%d' % kt])
                for i in range(4):
                    w, kw = wps.next()
                    wps.prefetch()
                    for m in range(4):
                        for kt in range(KT):
                            S.op('pe', lambda e, w=w, m=m, kt=kt: e.matmul(ps[m][:, :n], w[:, kt, m * 128:(m + 1) * 128], big[:, kt, :n], start=(kt == 0), stop=(kt == KT - 1)),
                                 reads=[kw, 'big'], writes=[PSK[m]])
                        evac(m, mix[:, i * 4 + m, :n], ps[m][:, :n], [PSK[m]], ['mx%d' % (i * 4 + m)])
                post_norm_add(n, 16)
                if t0 == 0:
                    S.op('pool', lambda e: e.memset(hblk[:, :, 0:112], 0.0), writes=['hb%d' % k_ for k_ in range(KT)])
                sumsq_rstd(hblk, lambda kt: 'hb%d' % kt, n, sq, rstd, D)
                for kt in range(KT):
                    S.op('dve', lambda e, kt=kt: e.scalar_tensor_tensor(hn[:, kt, :n], hblk[:, kt, :n], vecT[:, l, 32 + kt:33 + kt], rstd[:, :n], ALU.mult, ALU.mult),
                         reads=['hb%d' % kt, 'rstd', 'vecT'], writes=['hn'])
                for half in range(2):
                    for jp in range(8):
                        w, kw = wps.next()
                        wps.prefetch()
                        for m in range(4):
                            for kt in range(KT):
                                S.op('pe', lambda e, w=w, m=m, kt=kt: e.matmul(ps[m][:, :n], w[:, kt, m * 128:(m + 1) * 128], hn[:, kt, :n], start=(kt == 0), stop=(kt == KT - 1)),
                                     reads=[kw, 'hn'], writes=[PSK[m]])
                            r_ = rl[rcnt % 2]
                            kr = 'rl%d' % (rcnt % 2)
                            rcnt += 1
                            S.op('act', lambda e, r_=r_, m=m: e.activation(r_[:, :n], ps[m][:, :n], AF.Relu), reads=[PSK[m]], writes=[kr])
                            S.op('dve', lambda e, r_=r_, jp=jp, m=m: e.tensor_tensor(big[:, jp * 4 + m, :n], r_[:, :n], r_[:, :n], ALU.mult), reads=[kr], writes=['big'])
                    for je in range(4):
                        for kq in range(2):
                            w, kw = wps.next()
                            wps.prefetch()
                            for m in range(4):
                                for kt in range(KT):
                                    S.op('pe', lambda e, w=w, m=m, kt=kt, kq=kq: e.matmul(ps[m][:, :n], w[:, kt, m * 128:(m + 1) * 128], big[:, kq * 16 + kt, :n],
                                                                                        start=(kq == 0 and kt == 0), stop=(kq == 1 and kt == KT - 1)),
                                         reads=[kw, 'big'], writes=[PSK[m]])
                        for m in range(4):
                            if half == 0:
                                evac(m, mix[:, je * 4 + m, :n], ps[m][:, :n], [PSK[m]], ['mx%d' % (je * 4 + m)])
                            else:
                                S.op('dve', lambda e, je=je, m=m: e.tensor_tensor(mix[:, je * 4 + m, :n], ps[m][:, :n], mix[:, je * 4 + m, :n], ALU.add),
                                     reads=[PSK[m], 'mx%d' % (je * 4 + m)], writes=['mx%d' % (je * 4 + m)])
                if bi + 1 < len(blocks):
                    load_y(*blocks[bi + 1])
                post_norm_add(n, 48)
                for kt in range(KT):
                    S.dma('sp', hT[kt, :, t0:t0 + n], hblk[:, kt, :n], key='hst', reads=['hb%d' % kt], writes=['hT'])

            for bi_, (t0_, n_) in enumerate(blocks):
                blk(bi_, t0_, n_)
            S.flush()

    def phase_Z():
        with ExitStack() as es:
            hs = [sb(es, "hs%d" % i, [128, KT, 128], F32) for i in range(2)]
            xo = [sb(es, "xo%d" % i, [128, D], F32) for i in range(2)]
            for c in range(1, NCH):
                h_, x_ = hs[c % 2], xo[c % 2]
                kh, kx = 'hs%d' % (c % 2), 'xo%d' % (c % 2)
                S.dma('sp', h_, hT[:, :, c * 128:(c + 1) * 128].rearrange("k p t -> p k t"), key=kh, writes=[kh])
                for q in range(4):
                    for j in range(4):
                        kt = q * 4 + j
                        S.op('pe', lambda e, q=q, j=j, kt=kt, h_=h_: e.transpose(ps[q][:, j * 128:(j + 1) * 128], h_[:, kt, :], ident_f), reads=[kh, 'ident_f'], writes=[PSK[q]])
                    evac(q, x_[:, q * 512:(q + 1) * 512], ps[q], [PSK[q]], [kx])
                S.dma('sp', out[(c - 1) * 128:c * 128, :], x_, key=kx, reads=[kx])
            S.flush()
    def phase_pool(l):
        with ExitStack() as es:
            ub = [sb(es, "ub%d" % i, [128, 16 + LP], F32) for i in range(3)]
            uin = sb(es, "uin", [128, LP], BF16)
            pooled = sb(es, "pooled", [128, LP], BF16)
            t16 = sb(es, "t16", [128, 16], F32)
            pw = sb(es, "pw", [128, 4, 128], BF16)
            ost = [sb(es, "post%d" % i, [128, 512], BF16) for i in range(2)]
            S.dma('sp', pw, poolwb[l], key='pw', writes=['pw'])
            for i in range(3):
                S.op('pool', lambda e, i=i: e.memset(ub[i][:, 0:16], 0.0), writes=['ub%d' % i])
            oc = 0
            for g, w in enumerate((2, 4, 8, 16)):
                S.dma('sp', uin, puT[g], key='uin', writes=['uin'])
                S.op('act', lambda e: e.activation(ub[0][:, 16:], uin, AF.Identity), reads=['uin'], writes=['ub0'])
                a, ka = ub[0], 'ub0'
                for s_ in range(g + 1):
                    m = 2 ** s_
                    d_, kd = ub[1 + s_ % 2], 'ub%d' % (1 + s_ % 2)
                    S.op('dve', lambda e, a=a, d_=d_, m=m: e.tensor_tensor(d_[:, 16:], a[:, 16:], a[:, 16 - m:16 - m + LP], ALU.add), reads=[ka], writes=[kd])
                    a, ka = d_, kd
                S.op('dve', lambda e, a=a, w=w: e.scalar_tensor_tensor(pooled, a[:, 16:], 1.0 / w, ub[0][:, 16:], ALU.mult, ALU.subtract), reads=[ka, 'ub0'], writes=['pooled'])
                S.op('dve', lambda e, a=a, g=g: e.tensor_tensor(t16, a[:, 16 + 112:16 + 128], corr[:, g, :], ALU.mult), reads=[ka, 'corr'], writes=['t16'])
                S.op('dve', lambda e, w=w: e.scalar_tensor_tensor(pooled[:, 112:128], t16, 1.0 / w, ub[0][:, 16 + 112:16 + 128], ALU.mult, ALU.subtract), reads=['t16', 'ub0'], writes=['pooled'])
                for bi, (t0, n) in enumerate(blocks):
                    pk = bi % 2
                    o, ko = ost[oc % 2], 'post%d' % (oc % 2)
                    oc += 1
                    S.op('pe', lambda e, g=g, t0=t0, n=n, pk=pk: e.matmul(ps[pk][:, :n], pw[:, g, :], pooled[:, t0:t0 + n], start=True, stop=True), reads=['pw', 'pooled'], writes=[PSK[pk]])
                    S.op('act', lambda e, o=o, n=n, pk=pk, g=g: e.activation(o[:, :n], ps[pk][:, :n], AF.Identity, scale=vecT[:, l, 64 + g:65 + g]), reads=[PSK[pk], 'vecT'], writes=[ko])
                    S.dma('sp', yT[12 + g, :, t0:t0 + n], o[:, :n], key=ko, reads=[ko])
            S.flush()

    def phase_fox(l):
        SC = float(128 ** -0.5)
        with ExitStack() as es:
            gts = sb(es, "gts", [128, NCH, 12], F32)
            e4 = sb(es, "e4", [128, 4, NCH], F32)
            lfh = sb(es, "lfh", [128, 4, NCH], F32)
            Fin = sb(es, "Fin", [128, 4, NCH], F32)
            tot = sb(es, "tot", [128, 4, NCH], F32)
            Fend = sb(es, "Fend", [128, 4, NCH], F32)
            Fg = sb(es, "Fg", [128, 4, NCH], F32)
            bias = sb(es, "bias", [128, NCH], F32)
            qTt = sb(es, "qTt", [128, LP], BF16)
            kTt = sb(es, "kTt", [128, LP], BF16)
            vtok = sb(es, "vtok", [128, NCH, 128], BF16)
            pT = [sb(es, "pT%d" % i, [128, 512], BF16) for i in range(2)]
            rden = sb(es, "rden", [128, 512], F32)
            ost = [sb(es, "fost%d" % i, [128, 512], BF16) for i in range(2)]
            for c0_ in range(0, NCH, 8):
                c1_ = min(NCH, c0_ + 8)
                S.dma('sp', gts[:, c0_:c1_, :], gates[c0_ * 128:c1_ * 128, :].rearrange("(c p) g -> p c g", p=128), key='gts', writes=['gts'])
            for h in range(4):
                S.op('act', lambda e, h=h: e.activation(e4[:, h, :], gts[:, :, 8 + h], AF.Exp, scale=-1.0), reads=['gts'], writes=['e4'])
            S.op('act', lambda e: e.activation(lfh, e4, AF.Ln, bias=1.0), reads=['e4'], writes=['lfh'])
            S.op('dve', lambda e: e.tensor_scalar(lfh, lfh, -1.0, None, ALU.mult), reads=['lfh'], writes=['lfh'])
            S.op('dve', lambda e: e.tensor_scalar(lfh[:, :, 0], lfh[:, :, 0], padc[:, 0:1], None, ALU.mult), reads=['lfh', 'padc'], writes=['lfh'])
            lff = lfh.rearrange("p h c -> p (h c)")
            S.op('pe', lambda e: e.matmul(ps[0][:, 0:4 * NCH], tri_f, lff, start=True, stop=True), reads=['tri_f', 'lfh'], writes=['ps0'])
            S.op('pe', lambda e: e.matmul(ps[1][:, 0:4 * NCH], ones_f, lff, start=True, stop=True), reads=['ones_f', 'lfh'], writes=['ps1'])
            S.op('dve', lambda e: e.tensor_copy(Fin.rearrange("p h c -> p (h c)"), ps[0][:, 0:4 * NCH]), reads=['ps0'], writes=['Fin'])
            S.op('dve', lambda e: e.tensor_copy(tot.rearrange("p h c -> p (h c)"), ps[1][:, 0:4 * NCH]), reads=['ps1'], writes=['tot'])
            for h in range(4):
                S.op('dve', lambda e, h=h: e.tensor_tensor_scan(Fend[:, h, :], ones_f[:, 0:NCH], tot[:, h, :], 0.0, ALU.mult, ALU.add), reads=['tot', 'ones_f'], writes=['Fend'])
            S.op('dve', lambda e: e.tensor_tensor(Fg, Fend, tot, ALU.subtract), reads=['Fend', 'tot'], writes=['Fg'])
            S.op('dve', lambda e: e.tensor_tensor(Fg, Fg, Fin, ALU.add), reads=['Fg', 'Fin'], writes=['Fg'])
            oc = 0
            pc = 0
            for h in range(4):
                S.dma('sp', qTt, fqT[h], key='fq', writes=['qTt'])
                S.dma('sp', kTt, fkT[h], key='fk', writes=['kTt'])
                for c0_ in range(0, NCH, 8):
                    c1_ = min(NCH, c0_ + 8)
                    S.dma('sp', vtok[:, c0_:c1_, :], fv[c0_ * 128:c1_ * 128, :].rearrange("(c p) (h d) -> p c h d", p=128, h=4)[:, :, h, :], key='fvv', writes=['vtok'])
                for qb, (t0, n) in enumerate(blocks):
                    c0 = t0 // 128
                    clast = c0 + n // 128 - 1
                    pO, pD = 2 + (qb % 2), 4 + (qb % 2)
                    S.op('dve', lambda e, h=h, clast=clast: e.tensor_scalar(bias[:, 0:clast + 1], Fg[:, h, 0:clast + 1], Fend[:, h, clast:clast + 1], -1.0, ALU.subtract, ALU.mult),
                         reads=['Fg', 'Fend'], writes=['bias'])
                    S.op('dve', lambda e: e.tensor_tensor(bias[:, 0:1], bias[:, 0:1], padc[:, 1:2], ALU.add), reads=['bias', 'padc'], writes=['bias'])
                    def emit_S(kc, pcv, t0=t0, n=n, c0=c0):
                        lo = max(0, kc - c0) * 128
                        pS = pcv % 2
                        S.op('pe', lambda e: e.matmul(ps[pS][:, lo:n], kTt[:, kc * 128:(kc + 1) * 128], qTt[:, t0 + lo:t0 + n], start=True, stop=True),
                             reads=['kTt', 'qTt'], writes=[PSK[pS]])

                    emit_S(0, pc)
                    for kc in range(clast + 1):
                        lo = max(0, kc - c0) * 128
                        pS = pc % 2
                        p_, kp = pT[pc % 2], 'pT%d' % (pc % 2)
                        pc += 1
                        S.op('act', lambda e, kc=kc, lo=lo, pS=pS, p_=p_, n=n: e.activation(p_[:, lo:n], ps[pS][:, lo:n], AF.Exp, bias=bias[:, kc:kc + 1], scale=SC),
                             reads=[PSK[pS], 'bias'], writes=[kp])
                        if kc >= c0:
                            S.op('pool', lambda e, p_=p_, lo=lo: e.tensor_tensor(p_[:, lo:lo + 128], p_[:, lo:lo + 128], tri_b, ALU.mult), reads=[kp, 'tri_b'], writes=[kp])
                        if kc + 1 <= clast:
                            emit_S(kc + 1, pc)
                        S.op('pe', lambda e, kc=kc, lo=lo, p_=p_, n=n, pO=pO, clast=clast: e.matmul(ps[pO][:, lo:n], vtok[:, kc, :], p_[:, lo:n], start=(kc == 0), stop=(kc == clast)),
                             reads=['vtok', kp], writes=[PSK[pO]])
                        S.op('pe', lambda e, kc=kc, lo=lo, p_=p_, n=n, pD=pD, clast=clast: e.matmul(ps[pD][:, lo:n], ones_b, p_[:, lo:n], start=(kc == 0), stop=(kc == clast)),
                             reads=['ones_b', kp], writes=[PSK[pD]])
                    o, ko = ost[oc % 2], 'fost%d' % (oc % 2)
                    oc += 1
                    S.op('dve', lambda e, n=n, pD=pD: e.reciprocal(rden[:, :n], ps[pD][:, :n]), reads=[PSK[pD]], writes=['rden'])
                    S.op('dve', lambda e, n=n, pO=pO, o=o: e.tensor_tensor(o[:, :n], ps[pO][:, :n], rden[:, :n], ALU.mult), reads=[PSK[pO], 'rden'], writes=[ko])
                    S.dma('sp', yT[8 + h, :, t0:t0 + n], o[:, :n], key=ko, reads=[ko])
            S.flush()
    def phase_mlstm(l):
        SC = float(128 ** -0.5)
        with ExitStack() as es:
            qc = sb(es, "qc", [128, 4, LP], BF16)
            kc_ = sb(es, "kc_", [128, 4, LP], BF16)
            cin = sb(es, "cin", [128, 3 + LP], BF16)
            acc = sb(es, "acc", [128, LP], F32)
            mlg = sb(es, "mlg", [128, 512], F32)
            vaug = [sb(es, "vaug%d" % i, [128, 4, 129], BF16) for i in range(2)]
            ot = [sb(es, "ot%d" % i, [128, 512], BF16) for i in range(2)]
            gt = [sb(es, "gt%d" % i, [128, 12], F32) for i in range(2)]
            sg = sb(es, "sg", [128, 512], F32)
            sm = sb(es, "sm", [128, 40], F32)
            Cf = sb(es, "Cf", [128, 4, 129], F32)
            Cb = sb(es, "Cb", [128, 4, 129], BF16)
            pt = [sb(es, "pt%d" % i, [128, 128], BF16) for i in range(2)]
            ktk = [sb(es, "ktk%d" % i, [128, 128], BF16) for i in range(2)]
            vw = [sb(es, "vw%d" % i, [128, 129], BF16) for i in range(2)]
            hh = [sb(es, "hh%d" % i, [128, 128], F32) for i in range(2)]
            junk = sb(es, "junk", [128, 128], F32)
            hs_ = [sb(es, "hsm%d" % i, [128, 8], F32) for i in range(2)]
            yo = [sb(es, "yo%d" % i, [128, 128], BF16) for i in range(2)]
            yst = [sb(es, "yst%d" % i, [128, 4, 128], BF16) for i in range(2)]
            S.dma('sp', mlg, P["ml_norm_g"][l:l + 1, :].partition_broadcast(128), key='mlg', writes=['mlg'])
            S.op('pool', lambda e: e.memset(cin[:, 0:3], 0.0), writes=['cin'])
            S.op('pool', lambda e: e.memset(Cf, 0.0), writes=['Cf'])
            S.op('pool', lambda e: e.memset(Cb, 0.0), writes=['Cb'])
            for i in range(2):
                S.op('pool', lambda e, i=i: e.memset(vaug[i], 1.0), writes=['vaug%d' % i])
            for j in range(8):
                S.dma('sp', cin[:, 3:], mqkT[j], key='cin', reads=['cin'], writes=['cin'])
                wc = lambda tap, j=j: vecT[:, l, 80 + tap * 8 + j:81 + tap * 8 + j]
                S.op('dve', lambda e, wc=wc: e.tensor_scalar(acc, cin[:, 3:3 + LP], wc(3), None, ALU.mult), reads=['cin', 'vecT'], writes=['acc'])
                for tap in range(3):
                    S.op('dve', lambda e, wc=wc, tap=tap: e.scalar_tensor_tensor(acc, cin[:, tap:tap + LP], wc(tap), acc, ALU.mult, ALU.add), reads=['cin', 'acc', 'vecT'], writes=['acc'])
                if j < 4:
                    S.op('act', lambda e: e.activation(acc, acc, AF.Silu), reads=['acc'], writes=['acc'])
                    S.op('dve', lambda e, j=j: e.tensor_scalar(qc[:, j, :], acc, SC, None, ALU.mult), reads=['acc'], writes=['qc'])
                else:
                    S.op('act', lambda e, j=j: e.activation(kc_[:, j - 4, :], acc, AF.Silu), reads=['acc'], writes=['kc_'])
            for c in range(NCH):
                cs = slice(c * 128, (c + 1) * 128)
                va, kva = vaug[c % 2], 'vaug%d' % (c % 2)
                o_, ko = ot[c % 2], 'ot%d' % (c % 2)
                g_, kg = gt[c % 2], 'gt%d' % (c % 2)
                ys, kys = yst[c % 2], 'yst%d' % (c % 2)
                S.dma('sp', va[:, :, 0:128], mv[cs, :].rearrange("p (h d) -> p h d", h=4), key=kva, writes=[kva])
                S.dma('sp', o_, mo[cs, :], key=ko, writes=[ko])
                S.dma('sp', g_, gates[cs, :], key=kg, writes=[kg])
                S.op('act', lambda e, g_=g_: e.activation(sm[:, 0:4], g_[:, 4:8], AF.Exp, scale=-1.0), reads=[kg], writes=['sm_lf'])
                S.op('act', lambda e: e.activation(sm[:, 0:4], sm[:, 0:4], AF.Ln, bias=1.0), reads=['sm_lf'], writes=['sm_lf'])
                if c == 0:
                    S.op('dve', lambda e: e.tensor_scalar(sm[:, 0:4], sm[:, 0:4], -1.0, padc[:, 0:1], ALU.mult, ALU.mult), reads=['sm_lf', 'padc'], writes=['sm_lf'])
                else:
                    S.op('dve', lambda e: e.tensor_scalar(sm[:, 0:4], sm[:, 0:4], -1.0, None, ALU.mult), reads=['sm_lf'], writes=['sm_lf'])
                S.op('pe', lambda e: e.matmul(ps[6][:, 0:4], tri_f, sm[:, 0:4], start=True, stop=True), reads=['tri_f', 'sm_lf'], writes=['ps6'])
                S.op('pe', lambda e: e.matmul(ps[6][:, 4:8], ones_f, sm[:, 0:4], start=True, stop=True), reads=['ones_f', 'sm_lf'], writes=['ps6'])
                S.op('dve', lambda e, g_=g_: e.tensor_tensor(sm[:, 4:8], g_[:, 0:4], ps[6][:, 0:4], ALU.subtract), reads=[kg, 'ps6'], writes=['sm_w'])
                S.op('act', lambda e: e.activation(sm[:, 4:8], sm[:, 4:8], AF.Exp), reads=['sm_w'], writes=['sm_w'])
                if c == 0:
                    S.op('dve', lambda e: e.tensor_scalar(sm[:, 4:8], sm[:, 4:8], padc[:, 0:1], None, ALU.mult), reads=['sm_w', 'padc'], writes=['sm_w'])
                S.op('act', lambda e: e.activation(sm[:, 8:16], ps[6][:, 0:8], AF.Exp), reads=['ps6'], writes=['sm_e'])
                S.op('act', lambda e, o_=o_: e.activation(sg, o_, AF.Sigmoid), reads=[ko], writes=['sg'])
                for h in range(4):
                    i2 = h % 2
                    kch = kc_[:, h, cs]
                    qch = qc[:, h, cs]
                    p_, kp = pt[i2], 'pt%d' % i2
                    S.op('pe', lambda e, kch=kch, qch=qch, i2=i2: e.matmul(ps[i2][:, 0:128], kch, qch, start=True, stop=True), reads=['kc_', 'qc'], writes=[PSK[i2]])
                    S.op('dve', lambda e, h=h, i2=i2, p_=p_: e.scalar_tensor_tensor(p_, ps[i2][:, 0:128], sm[:, 4 + h:5 + h], tri_f, ALU.mult, ALU.mult),
                         reads=[PSK[i2], 'sm_w', 'tri_f'], writes=[kp])
                    S.op('pe', lambda e, kch=kch, i2=i2: e.transpose(psb[:, i2 * 128:(i2 + 1) * 128], kch, ident_b), reads=['kc_', 'ident_b'], writes=['psb%d' % i2])
                    S.op('act', lambda e, i2=i2: e.activation(ktk[i2], psb[:, i2 * 128:(i2 + 1) * 128], AF.Identity), reads=['psb%d' % i2], writes=['ktk%d' % i2])
                    S.op('pe', lambda e, p_=p_, va=va, h=h, i2=i2: e.matmul(ps[2 + i2][:, 0:129], p_, va[:, h, :], start=True, stop=False), reads=[kp, kva], writes=[PSK[2 + i2]])
                    S.op('pe', lambda e, qch=qch, h=h, i2=i2: e.matmul(ps[2 + i2][:, 0:129], qch, Cb[:, h, :], start=False, stop=True), reads=['qc', 'Cb'], writes=[PSK[2 + i2]])
                    hs = hs_[i2]
                    khs = 'hsm%d' % i2
                    nd = ps[2 + i2]
                    S.op('dve', lambda e, nd=nd, hs=hs, h=h: e.tensor_scalar(hs[:, 0:1], nd[:, 128:129], sm[:, 8 + h:9 + h], None, ALU.mult), reads=[PSK[2 + i2], 'sm_e'], writes=[khs])
                    S.op('dve', lambda e, hs=hs: e.scalar_tensor_tensor(hs[:, 1:2], hs[:, 0:1], -1.0, hs[:, 0:1], ALU.mult, ALU.max), reads=[khs], writes=[khs + 'b'])
                    S.op('dve', lambda e, hs=hs: e.tensor_scalar(hs[:, 2:3], hs[:, 1:2], 1.0, None, ALU.max), reads=[khs + 'b'], writes=[khs + 'c'])
                    S.op('dve', lambda e, hs=hs: e.reciprocal(hs[:, 3:4], hs[:, 2:3]), reads=[khs + 'c'], writes=[khs + 'd'])
                    S.op('dve', lambda e, hs=hs, h=h: e.tensor_tensor(hs[:, 4:5], hs[:, 3:4], sm[:, 8 + h:9 + h], ALU.mult), reads=[khs + 'd', 'sm_e'], writes=[khs + 'e'])
                    hh_, khh = hh[i2], 'hh%d' % i2
                    S.op('dve', lambda e, nd=nd, hs=hs, hh_=hh_: e.tensor_scalar(hh_, nd[:, 0:128], hs[:, 4:5], None, ALU.mult), reads=[PSK[2 + i2], khs + 'e'], writes=[khh])
                    S.op('act', lambda e, hh_=hh_: e.activation(junk, hh_, AF.Square), reads=[khh], writes=['junk'])
                    S.op('dve', lambda e, hs=hs: e.reduce_sum(hs[:, 5:6], junk, AX.X), reads=['junk'], writes=[khs + 'f'])
                    S.op('act', lambda e, hs=hs: e.activation(hs[:, 6:7], hs[:, 5:6], AF.Sqrt, bias=EPS, scale=1.0 / 128), reads=[khs + 'f'], writes=[khs + 'g'])
                    S.op('dve', lambda e, hs=hs: e.reciprocal(hs[:, 7:8], hs[:, 6:7]), reads=[khs + 'g'], writes=[khs + 'h'])
                    S.op('dve', lambda e, hs=hs, hh_=hh_, h=h: e.scalar_tensor_tensor(hh_, hh_, hs[:, 7:8], mlg[:, h * 128:(h + 1) * 128], ALU.mult, ALU.mult), reads=[khh, khs + 'h', 'mlg'], writes=[khh])
                    y_, ky = yo[i2], 'yo%d' % i2
                    S.op('pool', lambda e, hh_=hh_, y_=y_, h=h: e.tensor_tensor(y_, hh_, sg[:, h * 128:(h + 1) * 128], ALU.mult), reads=[khh, 'sg'], writes=[ky])
                    S.op('pe', lambda e, y_=y_, i2=i2: e.transpose(psb[:, 256 + i2 * 128:256 + (i2 + 1) * 128], y_, ident_b), reads=[ky, 'ident_b'], writes=['psby%d' % i2])
                    S.op('act', lambda e, ys=ys, h=h, i2=i2: e.activation(ys[:, h, :], psb[:, 256 + i2 * 128:256 + (i2 + 1) * 128], AF.Identity), reads=['psby%d' % i2], writes=[kys])
                    v_, kv_ = vw[i2], 'vw%d' % i2
                    S.op('pool', lambda e, v_=v_, va=va, h=h: e.tensor_scalar(v_, va[:, h, :], sm[:, 4 + h:5 + h], None, ALU.mult), reads=[kva, 'sm_w'], writes=[kv_])
                    S.op('pe', lambda e, v_=v_, i2=i2: e.matmul(ps[4 + i2][:, 0:129], ktk[i2], v_, start=True, stop=True), reads=['ktk%d' % i2, kv_], writes=[PSK[4 + i2]])
                    S.op('dve', lambda e, h=h: e.tensor_scalar(Cf[:, h, :], Cf[:, h, :], sm[:, 12 + h:13 + h], None, ALU.mult), reads=['Cf', 'sm_e'], writes=['Cf'])
                    S.op('dve', lambda e, h=h, i2=i2: e.scalar_tensor_tensor(Cf[:, h, :], ps[4 + i2][:, 0:129], sm[:, 12 + h:13 + h], Cf[:, h, :], ALU.mult, ALU.add),
                         reads=[PSK[4 + i2], 'sm_e', 'Cf'], writes=['Cf'])
                    S.op('act', lambda e, h=h: e.activation(Cb[:, h, :], Cf[:, h, :], AF.Identity), reads=['Cf'], writes=['Cb'])
                S.dma('sp', yT[4:8, :, cs].rearrange("k p t -> p k t"), ys, key=kys, reads=[kys])
            S.flush()
    def phase_s5(l):
        with ExitStack() as es:
            lamr = sb(es, "lamr", [16, 128], F32)
            lami = sb(es, "lami", [16, 128], F32)
            ldt = sb(es, "ldt", [16, 2], F32)
            pm = sb(es, "pm", [16, 12, 128], F32)
            rr = sb(es, "rr", [128, 4, 2048], F32)
            ri = sb(es, "ri", [128, 2048], mybir.dt.int32)
            sm4 = sb(es, "sm4", [128, 4, 16], F32)
            cosT = sb(es, "cosT", [128, 16, 128], F32)
            sinT = sb(es, "sinT", [128, 16, 128], F32)
            Rtab = sb(es, "Rtab", [128, 16, 128], F32)
            cs128 = sb(es, "cs128", [128, 2, 16], F32)
            bnat = sb(es, "bnat", [128, 2, 16, 16], F32)
            bb = sb(es, "bb", [128, 4, 16, 16], F32)
            Xb = sb(es, "Xb", [128, 16, 2, 128], BF16)
            BT = sb(es, "BT", [128, 16, 2, 128], BF16)
            cnat = sb(es, "cnat", [128, 2, 4, 64], F32)
            Xc = sb(es, "Xc", [128, 4, 3, 128], BF16)
            CTm = sb(es, "CTm", [128, 4, 3, 128], BF16)
            CT = sb(es, "CT", [128, 4, 4, 3, 128], BF16)
            gw = sb(es, "gw", [128, 2, 4, 512], BF16)

            def sincos(ang, n, dsin, dcos, pp):
                for k_, (dst, off) in enumerate(((dsin, 0.0), (dcos, float(np.pi / 2)))):
                    a0, a1 = rr[:pp, 0, :n], rr[:pp, 1, :n]
                    S.op('dve', lambda e, a0=a0, off=off: e.tensor_scalar(a0, ang, off, None, ALU.add), reads=['ang'], writes=['rr0'])
                    S.op('dve', lambda e, a0=a0, a1=a1: e.tensor_scalar(a1, a0, 1.0 / TWO_PI, None, ALU.mult), reads=['rr0'], writes=['rr1'])
                    S.op('dve', lambda e, a1=a1: e.tensor_copy(ri[:pp, :n], a1), reads=['rr1'], writes=['ri'])
                    S.op('dve', lambda e, a1=a1: e.tensor_copy(a1, ri[:pp, :n]), reads=['ri'], writes=['rr1'])
                    S.op('dve', lambda e, a0=a0, a1=a1: e.scalar_tensor_tensor(a0, a1, -TWO_PI, a0, ALU.mult, ALU.add), reads=['rr0', 'rr1'], writes=['rr0'])
                    S.op('dve', lambda e, a0=a0, a1=a1: e.tensor_scalar(a1, a0, float(np.pi), -TWO_PI, ALU.is_gt, ALU.mult), reads=['rr0'], writes=['rr1'])
                    S.op('dve', lambda e, a0=a0, a1=a1: e.tensor_tensor(a0, a0, a1, ALU.add), reads=['rr0', 'rr1'], writes=['rr0'])
                    S.op('dve', lambda e, a0=a0, a1=a1: e.tensor_scalar(a1, a0, float(-np.pi), TWO_PI, ALU.is_lt, ALU.mult), reads=['rr0'], writes=['rr1'])
                    S.op('dve', lambda e, a0=a0, a1=a1: e.tensor_tensor(a0, a0, a1, ALU.add), reads=['rr0', 'rr1'], writes=['rr0'])
                    S.op('act', lambda e, a0=a0, dst=dst: e.activation(dst, a0, AF.Sin), reads=['rr0'], writes=['sc%d' % k_])

            S.dma('sp', lamr, P["ssm_lam_re"][l].rearrange("(j g) p -> j (g p)", g=2), key='s5c', writes=['lam'])
            S.dma('sp', lami, P["ssm_lam_im"][l].rearrange("(j g) p -> j (g p)", g=2), key='s5c', writes=['lam'])
            S.dma('sp', ldt, P["ssm_log_dt"][l].rearrange("(j g) -> j g", g=2), key='s5c', writes=['lam'])
            S.dma('sp', bnat[:, 0], P["ssm_b_re"][l].rearrange("(j g) p h -> (g p) j h", g=2), key='s5c', writes=['bnat'])
            S.dma('sp', bnat[:, 1], P["ssm_b_im"][l].rearrange("(j g) p h -> (g p) j h", g=2), key='s5c', writes=['bnat'])
            S.dma('sp', cnat[:, 0], P["ssm_c_re"][l].rearrange("(q g) h p -> (g h) q p", g=8), key='s5c', writes=['cnat'])
            S.dma('sp', cnat[:, 1], P["ssm_c_im"][l].rearrange("(q g) h p -> (g h) q p", g=8), key='s5c', writes=['cnat'])
            for i in range(2):
                S.dma('sp', gw[:, i], glub[l, i], key='s5c', writes=['gw'])
            S.op('act', lambda e: e.activation(ldt, ldt, AF.Exp), reads=['lam'], writes=['dt'])
            for g2 in range(2):
                sl = slice(g2 * 64, (g2 + 1) * 64)
                S.op('dve', lambda e, sl=sl, g2=g2: e.tensor_scalar(pm[:, 0, sl], lami[:, sl], ldt[:, g2:g2 + 1], None, ALU.mult), reads=['lam', 'dt'], writes=['ang'])
                S.op('dve', lambda e, sl=sl, g2=g2: e.tensor_scalar(pm[:, 1, sl], lamr[:, sl], ldt[:, g2:g2 + 1], None, ALU.mult), reads=['lam', 'dt'], writes=['pm1'])
            S.op('act', lambda e: e.activation(pm[:, 2, :], pm[:, 1, :], AF.Exp), reads=['pm1'], writes=['pm2'])
            ang = pm[:, 0, :]
            sincos(ang, 128, pm[:, 3, :], pm[:, 4, :], 16)
            TT = lambda o, a, b, op, r_, w_: S.op('dve', lambda e: e.tensor_tensor(o, a, b, op), reads=r_, writes=w_)
            pmk = ['pm%d' % i for i in range(12)]
            TT(pm[:, 5, :], pm[:, 2, :], pm[:, 4, :], ALU.mult, ['pm2', 'sc1'], ['pm5'])
            S.op('dve', lambda e: e.tensor_scalar(pm[:, 5, :], pm[:, 5, :], -1.0, None, ALU.add), reads=['pm5'], writes=['pm5'])
            TT(pm[:, 6, :], pm[:, 2, :], pm[:, 3, :], ALU.mult, ['pm2', 'sc0'], ['pm6'])
            TT(pm[:, 7, :], pm[:, 5, :], lamr, ALU.mult, ['pm5', 'lam'], ['pm7'])
            TT(pm[:, 9, :], pm[:, 6, :], lami, ALU.mult, ['pm6', 'lam'], ['pm9'])
            TT(pm[:, 7, :], pm[:, 7, :], pm[:, 9, :], ALU.add, ['pm7', 'pm9'], ['pm7'])
            TT(pm[:, 8, :], pm[:, 6, :], lamr, ALU.mult, ['pm6', 'lam'], ['pm8'])
            TT(pm[:, 9, :], pm[:, 5, :], lami, ALU.mult, ['pm5', 'lam'], ['pm9'])
            TT(pm[:, 8, :], pm[:, 8, :], pm[:, 9, :], ALU.subtract, ['pm8', 'pm9'], ['pm8'])
            TT(pm[:, 9, :], lamr, lamr, ALU.mult, ['lam'], ['pm9'])
            TT(pm[:, 10, :], lami, lami, ALU.mult, ['lam'], ['pm10'])
            TT(pm[:, 9, :], pm[:, 9, :], pm[:, 10, :], ALU.add, ['pm9', 'pm10'], ['pm9'])
            S.op('dve', lambda e: e.reciprocal(pm[:, 9, :], pm[:, 9, :]), reads=['pm9'], writes=['pm9'])
            TT(pm[:, 10, :], pm[:, 7, :], pm[:, 9, :], ALU.mult, ['pm7', 'pm9'], ['pm10'])
            TT(pm[:, 11, :], pm[:, 8, :], pm[:, 9, :], ALU.mult, ['pm8', 'pm9'], ['pm11'])
            for k_, row in enumerate((0, 2, 10, 11)):
                S.op('pe', lambda e, k_=k_, row=row: e.transpose(ps[0][:, k_ * 16:(k_ + 1) * 16], pm[:, row, :], ident_f[0:16, 0:16]),
                     reads=['ang', 'pm2', 'pm10', 'pm11', 'ident_f'], writes=['ps0'])
            S.op('dve', lambda e: e.tensor_copy(sm4.rearrange("p a b -> p (a b)"), ps[0][:, 0:64]), reads=['ps0'], writes=['sm4'])
            thT, rT, krT, kiT = sm4[:, 0, :], sm4[:, 1, :], sm4[:, 2, :], sm4[:, 3, :]
            angT = rr[:, 2, :]
            S.op('dve', lambda e: e.tensor_tensor(angT.rearrange("p (j t) -> p j t", j=16), iota.unsqueeze(1).to_broadcast([128, 16, 128]),
                                                  thT.unsqueeze(2).to_broadcast([128, 16, 128]), ALU.mult), reads=['iota', 'sm4'], writes=['ang'])
            ang = angT
            sincos(ang, 2048, sinT.rearrange("p j t -> p (j t)"), cosT.rearrange("p j t -> p (j t)"), 128)
            S.op('dve', lambda e: e.tensor_copy(sinT, sinT), reads=['sc0'], writes=['sinT'])
            S.op('dve', lambda e: e.tensor_copy(cosT, cosT), reads=['sc1'], writes=['cosT'])
            ang128 = rr[:, 3, 0:16]
            S.op('dve', lambda e: e.tensor_scalar(ang128, thT, 128.0, None, ALU.mult), reads=['sm4'], writes=['ang'])
            ang = ang128
            sincos(ang, 16, cs128[:, 1, :], cs128[:, 0, :], 128)
            S.op('dve', lambda e: e.tensor_copy(cs128, cs128), reads=['sc0', 'sc1'], writes=['cs128'])
            S.op('dve', lambda e: e.tensor_copy(Rtab, rT.unsqueeze(2).to_broadcast([128, 16, 128])), reads=['sm4'], writes=['Rtab'])
            S.op('dve', lambda e: e.memset(Rtab[:, :, 0:1], 0.0), reads=['Rtab'], writes=['Rtab'])
            krb = krT.unsqueeze(2).to_broadcast([128, 16, 16])
            kib = kiT.unsqueeze(2).to_broadcast([128, 16, 16])
            TT(bb[:, 0], bnat[:, 0], krb, ALU.mult, ['bnat', 'sm4'], ['bb0'])
            TT(bb[:, 1], bnat[:, 1], kib, ALU.mult, ['bnat', 'sm4'], ['bb1'])
            TT(bb[:, 0], bb[:, 0], bb[:, 1], ALU.subtract, ['bb0', 'bb1'], ['bb0'])
            TT(bb[:, 2], bnat[:, 1], krb, ALU.mult, ['bnat', 'sm4'], ['bb2'])
            TT(bb[:, 3], bnat[:, 0], kib, ALU.mult, ['bnat', 'sm4'], ['bb3'])
            TT(bb[:, 2], bb[:, 2], bb[:, 3], ALU.add, ['bb2', 'bb3'], ['bb2'])
            S.op('pool', lambda e: e.memset(Xb, 0.0), writes=['Xb'])
            for j in range(16):
                for g2 in range(2):
                    col0 = 16 * ((2 * j + g2) % 8)
                    rs_ = slice(g2 * 64, (g2 + 1) * 64)
                    for ri_, src in enumerate((0, 2)):
                        S.op('dve' if ri_ else 'pool', lambda e, j=j, rs_=rs_, col0=col0, ri_=ri_, src=src: e.tensor_copy(Xb[rs_, j, ri_, col0:col0 + 16], bb[rs_, src, j, :]),
                             reads=['bb0', 'bb2'], writes=['Xb'])
            for j0 in range(0, 16, 4):
                for jj in range(4):
                    for ri_ in range(2):
                        k_ = jj * 2 + ri_
                        S.op('pe', lambda e, j0=j0, jj=jj, ri_=ri_, k_=k_: e.transpose(psb[:, k_ * 128:(k_ + 1) * 128], Xb[:, j0 + jj, ri_, :], ident_b), reads=['Xb', 'ident_b'], writes=['psb'])
                S.op('dve', lambda e, j0=j0: e.tensor_copy(BT[:, j0:j0 + 4].rearrange("p j r c -> p (j r c)"), psb), reads=['psb'], writes=['BT'])
            for v_, (src, sgn) in enumerate(((0, 1.0), (0, -1.0), (1, -1.0))):
                for g2 in range(2):
                    S.op('dve', lambda e, v_=v_, src=src, sgn=sgn, g2=g2: e.tensor_scalar(Xc[:, :, v_, g2 * 64:(g2 + 1) * 64], cnat[:, src], padc[:, 2 + g2:3 + g2], sgn, ALU.mult, ALU.mult),
                         reads=['cnat', 'padc'], writes=['Xc'])
            for q in range(4):
                for v_ in range(3):
                    k_ = q * 3 + v_
                    S.op('pe', lambda e, q=q, v_=v_, k_=k_: e.transpose(psb[:, (k_ % 8) * 128:(k_ % 8 + 1) * 128], Xc[:, q, v_, :], ident_b), reads=['Xc', 'ident_b'], writes=['psb'])
                    S.op('dve', lambda e, q=q, v_=v_, k_=k_: e.tensor_copy(CTm[:, q, v_, :], psb[:, (k_ % 8) * 128:(k_ % 8 + 1) * 128]), reads=['psb'], writes=['CTm'])
            S.op('pool', lambda e: e.memset(CT, 0.0), writes=['CT'])
            for jj in range(4):
                for q in range(4):
                    S.op('dve', lambda e, jj=jj, q=q: e.tensor_copy(CT[:, q, jj, :, 32 * jj:32 * jj + 32], CTm[:, q, :, 32 * jj:32 * jj + 32]), reads=['CTm', 'CT'], writes=['CT'])
            S.flush()
            uin = [sb(es, "s5u%d" % i, [128, 4, 128], BF16) for i in range(2)]
            tmp = [[sb(es, "s5t%d_%d" % (i, s_), [128, 512], F32) for i in range(4)] for s_ in range(2)]
            vre2 = [sb(es, "vre%d" % s_, [128, 512], F32) for s_ in range(2)]
            vim2 = [sb(es, "vim%d" % s_, [128, 512], F32) for s_ in range(2)]
            wre2 = [sb(es, "wre%d" % s_, [128, 512], F32) for s_ in range(2)]
            wim2 = [sb(es, "wim%d" % s_, [128, 512], F32) for s_ in range(2)]
            Ap2 = [[sb(es, "Ap%d_%d" % (i, s_), [128, 512], BF16) for i in range(4)] for s_ in range(2)]
            car = sb(es, "car", [128, 6, 16], F32)
            yg2 = [sb(es, "yg%d" % s_, [128, 128], F32) for s_ in range(2)]
            t22 = [sb(es, "t2_%d" % s_, [128, 128], F32) for s_ in range(2)]
            sgm2 = [sb(es, "sgm%d" % s_, [128, 128], F32) for s_ in range(2)]
            gl = [sb(es, "gl%d" % i, [128, 4, 128], BF16) for i in range(2)]
            sgt = sb(es, "sgt", [128, 4, 128], F32)
            yst = [sb(es, "s5y%d" % i, [128, 4, 128], BF16) for i in range(2)]
            S.op('pool', lambda e: e.memset(car, 0.0), writes=['car'])
            for c in range(NCH):
                cs = slice(c * 128, (c + 1) * 128)
                u_, ku = uin[c % 2], 's5u%d' % (c % 2)
                gl_, kgl = gl[c % 2], 'gl%d' % (c % 2)
                ys, kys = yst[c % 2], 's5y%d' % (c % 2)
                S.dma('sp', u_, uT[:, :, cs].rearrange("k p t -> p k t"), key=ku, writes=[ku])
                def Q(qd):
                    s_ = qd % 2
                    j4 = slice(4 * qd, 4 * qd + 4)
                    return dict(qd=qd, s=s_, j4=j4, pR=s_ * 2, pI=s_ * 2 + 1, pY=4 + s_,
                                cosq=cosT[:, j4, :].rearrange("p j t -> p (j t)"), sinq=sinT[:, j4, :].rearrange("p j t -> p (j t)"),
                                Rq=Rtab[:, j4, :].rearrange("p j t -> p (j t)"),
                                tmp=tmp[s_], vre=vre2[s_], vim=vim2[s_], wre=wre2[s_], wim=wim2[s_], Ap=Ap2[s_],
                                yg=yg2[s_], t2=t22[s_], sgm=sgm2[s_], cc=slice(4 * s_, 4 * s_ + 4),
                                u=u_, ku=ku, gl=gl_, kgl=kgl,
                                k=lambda nm, s_=s_: '%s_%d' % (nm, s_))

                def st_bu(q):
                    for jj in range(4):
                        j = 4 * q['qd'] + jj
                        S.op('pe', lambda e, j=j, jj=jj: e.matmul(ps[q['pR']][:, jj * 128:(jj + 1) * 128], BT[:, j, 0, :], q['u'][:, q['qd'], :], start=True, stop=True), reads=['BT', q['ku']], writes=[PSK[q['pR']]])
                        S.op('pe', lambda e, j=j, jj=jj: e.matmul(ps[q['pI']][:, jj * 128:(jj + 1) * 128], BT[:, j, 1, :], q['u'][:, q['qd'], :], start=True, stop=True), reads=['BT', q['ku']], writes=[PSK[q['pI']]])

                def st_rot(q):
                    k, t_ = q['k'], q['tmp']
                    S.op('dve', lambda e: e.tensor_tensor(t_[0], ps[q['pR']], q['cosq'], ALU.mult), reads=[PSK[q['pR']], 'cosT'], writes=[k('s5t0')])
                    S.op('dve', lambda e: e.tensor_tensor(t_[1], ps[q['pI']], q['sinq'], ALU.mult), reads=[PSK[q['pI']], 'sinT'], writes=[k('s5t1')])
                    S.op('dve', lambda e: e.tensor_tensor(t_[2], ps[q['pI']], q['cosq'], ALU.mult), reads=[PSK[q['pI']], 'cosT'], writes=[k('s5t2')])
                    S.op('dve', lambda e: e.tensor_tensor(t_[3], ps[q['pR']], q['sinq'], ALU.mult), reads=[PSK[q['pR']], 'sinT'], writes=[k('s5t3')])

                def st_add(q):
                    k, t_ = q['k'], q['tmp']
                    S.op('pool', lambda e: e.tensor_tensor(q['vre'], t_[0], t_[1], ALU.add), reads=[k('s5t0'), k('s5t1')], writes=[k('vre')])
                    S.op('pool', lambda e: e.tensor_tensor(q['vim'], t_[2], t_[3], ALU.subtract), reads=[k('s5t2'), k('s5t3')], writes=[k('vim')])

                def st_scan(q):
                    k, j4 = q['k'], q['j4']
                    v3r = q['vre'].rearrange("p (j t) -> p j t", j=4)
                    v3i = q['vim'].rearrange("p (j t) -> p j t", j=4)
                    S.op('dve', lambda e: e.tensor_tensor(v3r[:, :, 0], v3r[:, :, 0], car[:, 0, j4], ALU.add), reads=[k('vre'), 'car'], writes=[k('vre')])
                    S.op('dve', lambda e: e.tensor_tensor(v3i[:, :, 0], v3i[:, :, 0], car[:, 1, j4], ALU.add), reads=[k('vim'), 'car'], writes=[k('vim')])
                    S.op('dve', lambda e: e.tensor_tensor_scan(q['wre'], q['Rq'], q['vre'], 0.0, ALU.mult, ALU.add), reads=['Rtab', k('vre')], writes=[k('wre')])
                    S.op('dve', lambda e: e.tensor_tensor_scan(q['wim'], q['Rq'], q['vim'], 0.0, ALU.mult, ALU.add), reads=['Rtab', k('vim')], writes=[k('wim')])

                def st_carry(q):
                    k, j4, cc = q['k'], q['j4'], q['cc']
                    w3r = q['wre'].rearrange("p (j t) -> p j t", j=4)
                    w3i = q['wim'].rearrange("p (j t) -> p j t", j=4)
                    c128, s128 = cs128[:, 0, j4], cs128[:, 1, j4]
                    S.op('dve', lambda e: e.tensor_tensor(car[:, 2, cc], w3r[:, :, 127], c128, ALU.mult), reads=[k('wre'), 'cs128'], writes=[k('car2')])
                    S.op('dve', lambda e: e.tensor_tensor(car[:, 3, cc], w3i[:, :, 127], s128, ALU.mult), reads=[k('wim'), 'cs128'], writes=[k('car3')])
                    S.op('dve', lambda e: e.tensor_tensor(car[:, 4, cc], w3r[:, :, 127], s128, ALU.mult), reads=[k('wre'), 'cs128'], writes=[k('car4')])
                    S.op('dve', lambda e: e.tensor_tensor(car[:, 5, cc], w3i[:, :, 127], c128, ALU.mult), reads=[k('wim'), 'cs128'], writes=[k('car5')])
                    S.op('dve', lambda e: e.tensor_tensor(car[:, 2, cc], car[:, 2, cc], car[:, 3, cc], ALU.subtract), reads=[k('car2'), k('car3')], writes=[k('car2')])
                    S.op('dve', lambda e: e.tensor_tensor(car[:, 4, cc], car[:, 4, cc], car[:, 5, cc], ALU.add), reads=[k('car4'), k('car5')], writes=[k('car4')])
                    S.op('dve', lambda e: e.tensor_tensor(car[:, 0, j4], car[:, 2, cc], rT[:, j4], ALU.mult), reads=[k('car2'), 'sm4', 'car'], writes=['car'])
                    S.op('dve', lambda e: e.tensor_tensor(car[:, 1, j4], car[:, 4, cc], rT[:, j4], ALU.mult), reads=[k('car4'), 'sm4', 'car'], writes=['car'])

                def st_prod(q):
                    k, A_ = q['k'], q['Ap']
                    S.op('pool', lambda e: e.tensor_tensor(A_[0], q['wre'], q['cosq'], ALU.mult), reads=[k('wre'), 'cosT'], writes=[k('Ap0')])
                    S.op('pool', lambda e: e.tensor_tensor(A_[1], q['wim'], q['sinq'], ALU.mult), reads=[k('wim'), 'sinT'], writes=[k('Ap1')])
                    S.op('dve', lambda e: e.tensor_tensor(A_[2], q['wre'], q['sinq'], ALU.mult), reads=[k('wre'), 'sinT'], writes=[k('Ap2')])
                    S.op('dve', lambda e: e.tensor_tensor(A_[3], q['wim'], q['cosq'], ALU.mult), reads=[k('wim'), 'cosT'], writes=[k('Ap3')])

                def st_y(q):
                    k, A_ = q['k'], q['Ap']
                    n_mm = 0
                    for jj in range(4):
                        for a_, v_ in ((0, 0), (1, 1), (2, 2), (3, 2)):
                            S.op('pe', lambda e, jj=jj, a_=a_, v_=v_, n_mm=n_mm: e.matmul(ps[q['pY']][:, 0:128], CT[:, q['qd'], jj, v_, :], A_[a_][:, jj * 128:(jj + 1) * 128],
                                                                                       start=(n_mm == 0), stop=(n_mm == 15)), reads=['CT', k('Ap%d' % a_)], writes=[PSK[q['pY']]])
                            n_mm += 1

                def st_gelu1(q):
                    k, yg, t2_ = q['k'], q['yg'], q['t2']
                    qd = q['qd']
                    S.op('dve', lambda e: e.scalar_tensor_tensor(yg, q['u'][:, qd, :], vecT[:, l, 68 + qd:69 + qd], ps[q['pY']][:, 0:128], ALU.mult, ALU.add), reads=[q['ku'], 'vecT', PSK[q['pY']]], writes=[k('yg')])
                    S.op('dve', lambda e: e.tensor_tensor(t2_, yg, yg, ALU.mult), reads=[k('yg')], writes=[k('t2_')])
                    S.op('dve', lambda e: e.tensor_scalar(t2_, t2_, 0.044715, 1.0, ALU.mult, ALU.add), reads=[k('t2_')], writes=[k('t2_')])
                    S.op('dve', lambda e: e.tensor_tensor(t2_, t2_, yg, ALU.mult), reads=[k('t2_'), k('yg')], writes=[k('t2_')])
                    S.op('act', lambda e: e.activation(q['sgm'], t2_, AF.Sigmoid, scale=1.5957691216057308), reads=[k('t2_')], writes=[k('sgm')])

                def st_gelu2(q):
                    k = q['k']
                    S.op('dve', lambda e: e.tensor_tensor(q['gl'][:, q['qd'], :], q['yg'], q['sgm'], ALU.mult), reads=[k('yg'), k('sgm')], writes=[q['kgl']])

                for qa in (0, 2):
                    A_, B_ = Q(qa), Q(qa + 1)
                    for st in (st_bu, st_rot):
                        st(A_)
                        st(B_)
                    st_add(A_)
                    st_add(B_)
                    st_scan(A_)
                    st_scan(B_)
                    st_prod(A_)
                    st_prod(B_)
                    st_y(A_)
                    st_y(B_)
                    st_carry(A_)
                    st_carry(B_)
                    st_gelu1(A_)
                    st_gelu1(B_)
                    st_gelu2(A_)
                    st_gelu2(B_)
                for ot_ in range(8):
                    pz = 2 + ot_ // 4 if False else (0 if ot_ < 4 else 1)
                    for kt in range(4):
                        S.op('pe', lambda e, ot_=ot_, kt=kt, gl_=gl_, pz=pz: e.matmul(ps[pz][:, (ot_ % 4) * 128:(ot_ % 4 + 1) * 128], gw[:, ot_ // 4, kt, (ot_ % 4) * 128:(ot_ % 4 + 1) * 128], gl_[:, kt, :],
                                                                                       start=(kt == 0), stop=(kt == 3)), reads=['gw', kgl], writes=[PSK[pz]])
                for o4 in range(4):
                    S.op('act', lambda e, o4=o4: e.activation(sgt[:, o4, :], ps[1][:, o4 * 128:(o4 + 1) * 128], AF.Sigmoid, bias=vecT[:, l, 76 + o4:77 + o4]), reads=['ps1', 'vecT'], writes=['sgt'])
                    S.op('dve', lambda e, o4=o4, ys=ys: e.scalar_tensor_tensor(ys[:, o4, :], ps[0][:, o4 * 128:(o4 + 1) * 128], vecT[:, l, 72 + o4:73 + o4], sgt[:, o4, :], ALU.add, ALU.mult),
                         reads=['ps0', 'vecT', 'sgt'], writes=[kys])
                S.dma('sp', yT[0:4, :, cs].rearrange("k p t -> p k t"), ys, key=kys, reads=[kys])
            S.flush()

    for l in range(DEPTH):
        phase_A(l)
        if stop_after == "A":
            return nc
        phase_s5(l)
        phase_mlstm(l)
        phase_fox(l)
        phase_pool(l)
        if stop_after == "mix":
            return nc
        phase_C(l)
        if stop_after == "C":
            return nc
    phase_Z()
    pes.close()
    return nc


def _in_maps(inputs, SEQ, DEPTH, n_cores):
    consts = host_consts()
    maps = []
    B = inputs["x"].shape[0]
    for c in range(n_cores):
        m = {"x": np.ascontiguousarray(inputs["x"][c % B], dtype=np.float32), "meta_tokens": np.asarray(inputs["meta_tokens"], np.float32)}
        for n, _ in PARAMS:
            m[n] = np.ascontiguousarray(inputs[n][:DEPTH], dtype=np.float32)
        m.update(consts)
        maps.append(m)
    return maps


def kernel(**inputs):
    SEQ, DEPTH = 4096, 2
    B = inputs["x"].shape[0]
    nc = build_program(SEQ, DEPTH)
    res = run_bass_kernel_spmd(nc, _in_maps(inputs, SEQ, DEPTH, 8), core_ids=list(range(8)))
    return np.stack([np.asarray(res.results[b]["out"], dtype=np.float32) for b in range(B)], axis=0)
```

```python
import numpy as np
import concourse.bass as bass
import concourse.mybir as mybir
from concourse.bass_utils import run_bass_kernel_spmd

F32 = mybir.dt.float32
BF16 = mybir.dt.bfloat16
AF = mybir.ActivationFunctionType
ALU = mybir.AluOpType
AX = mybir.AxisListType


class Sched:
    def __init__(self, nc):
        self.nc = nc
        self.engs = {'pe': nc.tensor, 'act': nc.scalar, 'dve': nc.vector, 'pool': nc.gpsimd, 'sp': nc.sync}
        self.ops = []
        self.last_w = {}
        self.readers = {}
        self.dma_cnt = {}
        self.esem = {e: nc.alloc_semaphore("s_" + e) for e in self.engs}
        self.dsem = {}
        self.ebase = {e: 0 for e in self.engs}
        self.dma_issued = {}

    def _deps(self, reads, writes):
        deps = set()
        for k in reads:
            if k in self.last_w:
                deps.add(self.last_w[k])
        for k in writes:
            if k in self.last_w:
                deps.add(self.last_w[k])
            deps.update(self.readers.get(k, ()))
        return deps

    def _record(self, oid, reads, writes):
        for k in writes:
            self.last_w[k] = oid
            self.readers[k] = []
        for k in reads:
            self.readers.setdefault(k, []).append(oid)

    @staticmethod
    def _bank(reads, writes):
        bk = lambda k: 'psb' if k.startswith('psb') else k
        w = [bk(k) for k in writes] + [bk(k) for k in reads if k.startswith('ps')]
        r = [k for k in reads if not k.startswith('ps')]
        return r, w

    def op(self, eng, fn, reads=(), writes=()):
        reads, writes = self._bank(list(reads), list(writes))
        deps = self._deps(reads, writes)
        oid = len(self.ops)
        self.ops.append(dict(eng=eng, fn=fn, deps=deps, dma=None))
        self._record(oid, reads, writes)
        return oid

    def dma(self, eng, out, in_, key, reads=(), writes=(), **kw):
        deps = self._deps(reads, writes)
        oid = len(self.ops)
        n = self.dma_cnt.get(key, 0) + 1
        self.dma_cnt[key] = n
        self.ops.append(dict(eng=eng, fn=lambda e: e.dma_start(out=out, in_=in_, **kw), deps=deps, dma=(key, n)))
        self._record(oid, reads, writes)
        return oid

    def flush(self):
        nc = self.nc
        ops = self.ops
        if not ops:
            return
        eng_list = {e: [] for e in self.engs}
        for oid, o in enumerate(ops):
            o['pos'] = len(eng_list[o['eng']])
            eng_list[o['eng']].append(oid)
        seen = {e: {} for e in self.engs}
        need_inc = set()
        for oid, o in enumerate(ops):
            e = o['eng']
            want = {}
            for d in o['deps']:
                do = ops[d]
                if do['dma'] is not None:
                    key, n = do['dma']
                    want[('d', key)] = max(want.get(('d', key), 0), n)
                else:
                    if do['eng'] == 'pe' and e == 'pe' and o['dma'] is None:
                        continue
                    want[('e', do['eng'])] = max(want.get(('e', do['eng']), -1), do['pos'])
            waits = []
            for k, v in want.items():
                if seen[e].get(k, -1) >= v:
                    continue
                seen[e][k] = v
                waits.append((k, v))
                if k[0] == 'e':
                    need_inc.add(eng_list[k[1]][v])
            o['waits'] = waits
        issued = self.dma_issued
        for oid, o in enumerate(ops):
            nw = []
            for k, v in o['waits']:
                if k[0] == 'd':
                    v = max(v, issued.get(k[1], 0))
                nw.append((k, v))
            o['waits'] = nw
            if o['dma'] is not None:
                issued[o['dma'][0]] = o['dma'][1]
        for k in self.dma_cnt:
            if k not in self.dsem:
                self.dsem[k] = nc.alloc_semaphore("d_%d" % len(self.dsem))
        inc_prefix = {}
        for e, lst in eng_list.items():
            c = self.ebase[e]
            for oid in lst:
                if oid in need_inc:
                    c += 1
                    inc_prefix[oid] = c
            self.ebase[e] = c
        finals = [(k, n) for k, n in self.dma_cnt.items()]

        def emit(e, eng):
            for oid in eng_list[e]:
                o = ops[oid]
                for k, v in o['waits']:
                    if k[0] == 'd':
                        eng.wait_ge(self.dsem[k[1]], 16 * v)
                    else:
                        eng.wait_ge(self.esem[k[1]], inc_prefix[eng_list[k[1]][v]])
                ins = o['fn'](eng)
                if o['dma'] is not None:
                    ins.then_inc(self.dsem[o['dma'][0]], 16)
                elif oid in need_inc:
                    ins.then_inc(self.esem[e], 1)
            if e == 'sp':
                for k, n in finals:
                    eng.wait_ge(self.dsem[k], 16 * n)

        with nc.Block() as block:
            @block.sync
            def _(eng):
                emit('sp', eng)

            @block.tensor
            def _(eng):
                emit('pe', eng)

            @block.scalar
            def _(eng):
                emit('act', eng)

            @block.vector
            def _(eng):
                emit('dve', eng)

            @block.gpsimd
            def _(eng):
                emit('pool', eng)
        nc.all_engine_barrier()
        self.ops = []
        self.last_w = {}
        self.readers = {}


D = 2048
KT = 16
INC = 4620
DFF = 8192
EPS = 1e-6
SEGS = [("s_u", 0), ("m_q", 512), ("m_k", 1024), ("m_v", 1536), ("m_o", 2048),
        ("f_q", 2568), ("f_k", 3080), ("f_v", 3592), ("p_u", 4108)]
PARAMS = [("g_pre_mix", [D]), ("g_post_mix", [D]), ("g_pre_ffn", [D]), ("g_post_ffn", [D]),
          ("w_in", [D, INC]), ("ml_gate_bias", [8]), ("fx_gate_bias", [4]),
          ("ssm_lam_re", [32, 64]), ("ssm_lam_im", [32, 64]), ("ssm_log_dt", [32]),
          ("ssm_b_re", [32, 64, 16]), ("ssm_b_im", [32, 64, 16]), ("ssm_c_re", [32, 16, 64]),
          ("ssm_c_im", [32, 16, 64]), ("ssm_d", [32, 16]), ("ssm_glu_w", [512, 1024]), ("ssm_glu_b", [1024]),
          ("ml_conv_w", [4, 1024]), ("ml_norm_g", [512]), ("pool_w", [4, 128, 128]), ("pool_scale", [512]),
          ("w_out", [D, D]), ("mlp_w1", [D, DFF]), ("mlp_w2", [DFF, D])]
TWO_PI = float(2.0 * np.pi)


def host_consts():
    tri = (np.arange(128)[:, None] <= np.arange(128)[None, :]).astype(np.float32)
    pad = np.zeros((128, 4), np.float32)
    pad[112:, 0] = 1.0
    pad[:112, 1] = -1e30
    pad[:, 2] = ((np.arange(128) // 16) % 2 == 0)
    pad[:, 3] = ((np.arange(128) // 16) % 2 == 1)
    corr = np.zeros((128, 4, 16), np.float32)
    for g, w in enumerate((2, 4, 8, 16)):
        corr[:, g, :] = w / np.minimum(np.arange(1, 17), w)
    iota = np.broadcast_to(np.arange(128, dtype=np.float32)[None, :], (128, 128)).copy()
    ident = np.eye(128, dtype=np.float32)
    return {"c_tri": tri, "c_pad": pad, "c_corr": corr, "c_iota": iota, "c_ident": ident}


def build_program(SEQ, DEPTH, debug=False, stop_after=None):
    nc = bass.Bass("TRN2", target_bir_lowering=False)
    LP = SEQ + 128
    NCH = LP // 128
    blocks = [(t0, min(512, LP - t0)) for t0 in range(0, LP, 512)]

    def din(name, shape):
        return nc.dram_tensor(name, list(shape), F32, kind="ExternalInput").ap()

    x = din("x", [SEQ, D])
    meta = din("meta_tokens", [16, D])
    P = {n: din(n, [DEPTH] + sh) for n, sh in PARAMS}
    c_tri = din("c_tri", [128, 128])
    c_pad = din("c_pad", [128, 4])
    c_corr = din("c_corr", [128, 4, 16])
    c_iota = din("c_iota", [128, 128])
    c_ident = din("c_ident", [128, 128])
    out = nc.dram_tensor("out", [SEQ, D], F32, kind="ExternalOutput").ap()
    dbg_kind = "ExternalOutput" if debug else "Internal"

    def scr(name, shape, dt, dbg=False):
        return nc.dram_tensor(name, list(shape), dt, kind=(dbg_kind if dbg else "Internal")).ap()

    hT = scr("hT", [KT, 128, LP], F32, True)
    uT = scr("uT", [4, 128, LP], BF16, True)
    mqkT = scr("mqkT", [8, 128, LP], BF16, True)
    mv = scr("mv", [LP, 512], BF16, True)
    mo = scr("mo", [LP, 512], BF16, True)
    gates = scr("gates", [LP, 12], F32, True)
    fqT = scr("fqT", [4, 128, LP], BF16, True)
    fkT = scr("fkT", [4, 128, LP], BF16, True)
    fv = scr("fv", [LP, 512], BF16, True)
    puT = scr("puT", [4, 128, LP], BF16, True)
    yT = scr("yT", [KT, 128, LP], BF16, True)
    s5scr = scr("s5scr", [8, 2048], F32)
    winb = scr("winb", [DEPTH, 9, 128, KT, 512], BF16)
    wgb = scr("wgb", [DEPTH, 128, KT, 12], BF16)
    woutb = scr("woutb", [DEPTH, 4, 128, KT, 512], BF16)
    w1b = scr("w1b", [DEPTH, 16, 128, KT, 512], BF16)
    w2b = scr("w2b", [DEPTH, 16, 128, KT, 512], BF16)
    glub = scr("glub", [DEPTH, 2, 128, 4, 512], BF16)
    poolwb = scr("poolwb", [DEPTH, 128, 4, 128], BF16)

    S = Sched(nc)
    ps = [nc.alloc_psum_tensor("ps%d" % i, [128, 512], F32).ap() for i in range(7)]
    psb = nc.alloc_psum_tensor("psb", [128, 1024], BF16).ap()
    PSK = ["ps%d" % i for i in range(7)]

    from contextlib import ExitStack

    uid = [0]

    def sb(es, name, shape, dt):
        uid[0] += 1
        return es.enter_context(nc.sbuf_tensor("%s_%d" % (name, uid[0]), list(shape), dt)).ap()

    pes = ExitStack()
    ident_f = sb(pes, "ident_f", [128, 128], F32)
    ident_b = sb(pes, "ident_b", [128, 128], BF16)
    ones_b = sb(pes, "ones_b", [128, 128], BF16)
    ones_f = sb(pes, "ones_f", [128, 128], F32)
    tri_f = sb(pes, "tri_f", [128, 128], F32)
    tri_b = sb(pes, "tri_b", [128, 128], BF16)
    padc = sb(pes, "padc", [128, 4], F32)
    corr = sb(pes, "corr", [128, 4, 16], F32)
    iota = sb(pes, "iota", [128, 128], F32)
    vecT = sb(pes, "vecT", [128, DEPTH, 112], F32)
    S.dma('sp', ident_f, c_ident, key='c', writes=['ident_f'])
    S.dma('sp', tri_f, c_tri, key='c', writes=['tri_f'])
    S.dma('sp', padc, c_pad, key='c', writes=['padc'])
    S.dma('sp', corr, c_corr, key='c', writes=['corr'])
    S.dma('sp', iota, c_iota, key='c', writes=['iota'])
    S.op('dve', lambda e: e.tensor_copy(ident_b, ident_f), reads=['ident_f'], writes=['ident_b'])
    S.op('dve', lambda e: e.tensor_copy(tri_b, tri_f), reads=['tri_f'], writes=['tri_b'])
    S.op('dve', lambda e: e.memset(ones_b, 1.0), writes=['ones_b'])
    S.op('dve', lambda e: e.memset(ones_f, 1.0), writes=['ones_f'])
    with ExitStack() as es:
        vs = sb(es, "vs", [112, 128], F32)
        for l in range(DEPTH):
            r = 0
            for nm, nrow in (("g_pre_mix", 16), ("g_post_mix", 16), ("g_pre_ffn", 16), ("g_post_ffn", 16),
                             ("pool_scale", 4), ("ssm_d", 4), ("ssm_glu_b", 8)):
                src = P[nm][l]
                if nm == "ssm_d":
                    src = src.rearrange("g h -> (g h)")
                S.dma('sp', vs[r:r + nrow, :], src.rearrange("(r c) -> r c", c=128), key='c', writes=['vs'])
                r += nrow
            S.dma('sp', vs[80:112, :], P["ml_conv_w"][l].rearrange("j (t c) -> (j t) c", c=128), key='c', writes=['vs'])
            S.op('pe', lambda e: e.transpose(ps[0][:, 0:112], vs, ident_f[0:112, 0:112]), reads=['vs', 'ident_f'], writes=['ps0'])
            S.op('dve', lambda e, l=l: e.tensor_copy(vecT[:, l, :], ps[0][:, 0:112]), reads=['ps0'], writes=['vecT'])
        def cast_panel(dst, src_rows, c0, nk, ncol=512):
            sv = src_rows.rearrange("(kt p) c -> p kt c", p=128)
            for k0 in range(0, nk, 4):
                k1 = min(nk, k0 + 4)
                S.dma('pool', dst[:, k0:k1, :], sv[:, k0:k1, c0:c0 + ncol], key='cast')
        for l in range(DEPTH):
            for i, (_, c0) in enumerate(SEGS):
                cast_panel(winb[l, i], P["w_in"][l], c0, KT)
            sv = P["w_in"][l].rearrange("(kt p) c -> p kt c", p=128)
            S.dma('pool', wgb[l][:, :, 0:8], sv[:, :, 2560:2568], key='cast')
            S.dma('pool', wgb[l][:, :, 8:12], sv[:, :, 4104:4108], key='cast')
            for i in range(4):
                cast_panel(woutb[l, i], P["w_out"][l], i * 512, KT)
            for i in range(16):
                cast_panel(w1b[l, i], P["mlp_w1"][l], i * 512, KT)
            for je in range(4):
                for kq in range(4):
                    cast_panel(w2b[l, je * 4 + kq], P["mlp_w2"][l][kq * 2048:(kq + 1) * 2048, :], je * 512, KT)
            for i in range(2):
                cast_panel(glub[l, i], P["ssm_glu_w"][l], i * 512, 4)
            S.dma('pool', poolwb[l], P["pool_w"][l].rearrange("g c d -> c g d"), key='cast')
        xin = [sb(es, "xin%d" % i, [128, D], F32) for i in range(2)]
        hst = [sb(es, "hst%d" % i, [128, KT, 128], F32) for i in range(2)]
        for c in range(NCH):
            xi, hs = xin[c % 2], hst[c % 2]
            kx, kh = "xin%d" % (c % 2), "hst%d" % (c % 2)
            if c == 0:
                S.op('pool', lambda e, xi=xi: e.memset(xi, 0.0), writes=[kx])
                S.dma('sp', xi[112:128, :], meta, key='xin', reads=[kx], writes=[kx])
            else:
                S.dma('sp', xi, x[(c - 1) * 128:c * 128, :], key='xin', writes=[kx])
            for q in range(4):
                pk = ps[q % 2]
                for j in range(4):
                    kt = q * 4 + j
                    S.op('pe', lambda e, pk=pk, j=j, kt=kt, xi=xi: e.transpose(pk[:, j * 128:(j + 1) * 128], xi[:, kt * 128:(kt + 1) * 128], ident_f),
                         reads=[kx, 'ident_f'], writes=[PSK[q % 2]])
                S.op('act' if q % 2 else 'dve',
                     (lambda e, pk=pk, q=q, hs=hs: e.activation(hs[:, q * 4:(q + 1) * 4, :], pk.rearrange("p (j t) -> p j t", j=4), AF.Identity)) if q % 2 else
                     (lambda e, pk=pk, q=q, hs=hs: e.tensor_copy(hs[:, q * 4:(q + 1) * 4, :], pk.rearrange("p (j t) -> p j t", j=4))),
                     reads=[PSK[q % 2]], writes=[kh])
            S.dma('sp', hT[:, :, c * 128:(c + 1) * 128].rearrange("k p t -> p k t"), hs, key='hst', reads=[kh], writes=['hT'])
        S.flush()
    if stop_after == "p0":
        return nc
    def evac(i, out_ap, in_ap, reads, writes):
        if i % 2:
            S.op('act', lambda e: e.activation(out_ap, in_ap, AF.Identity), reads=reads, writes=writes)
        else:
            S.op('dve', lambda e: e.tensor_copy(out_ap, in_ap), reads=reads, writes=writes)

    def sumsq_steps(src, ksrc, n, sq, rstd, dim, krstd='rstd', sq_eng='act', ksq='sq'):
        kf = ksrc if callable(ksrc) else (lambda kt: ksrc)
        steps = []
        for kt in range(KT):
            def st(kt=kt):
                sl = sq[kt % 2]
                if sq_eng == 'act':
                    S.op('act', lambda e: e.activation(sl[:, :n], src[:, kt, :n], AF.Square), reads=[kf(kt)], writes=['%s%d' % (ksq, kt % 2)])
                else:
                    S.op(sq_eng, lambda e: e.tensor_tensor(sl[:, :n], src[:, kt, :n], src[:, kt, :n], ALU.mult), reads=[kf(kt)], writes=['%s%d' % (ksq, kt % 2)])
                S.op('pe', lambda e: e.matmul(ps[6][:, :n], ones_b, sl[:, :n], start=(kt == 0), stop=(kt == KT - 1)),
                     reads=['%s%d' % (ksq, kt % 2), 'ones_b'], writes=['ps6'])
            steps.append(st)

        def fin():
            S.op('act', lambda e: e.activation(rstd[:, :n], ps[6][:, :n], AF.Sqrt, bias=EPS, scale=1.0 / dim), reads=['ps6'], writes=[krstd])
            S.op('dve', lambda e: e.reciprocal(rstd[:, :n], rstd[:, :n]), reads=[krstd], writes=[krstd])
        steps.append(fin)
        return steps

    def sumsq_rstd(src, ksrc, n, sq, rstd, dim):
        for st in sumsq_steps(src, ksrc, n, sq, rstd, dim):
            st()

    class PanelStream:
        def __init__(self, es, srcs, nb=3, name="wp"):
            self.srcs = srcs
            self.nb = nb
            self.bufs = [sb(es, "%s%d" % (name, i), [128, KT, 512], BF16) for i in range(nb)]
            self.keys = ["%s%d" % (name, i) for i in range(nb)]
            self.issued = 0
            self.used = 0

        def _issue(self):
            i = self.issued
            if i < len(self.srcs):
                S.dma('sp', self.bufs[i % self.nb], self.srcs[i], key=self.keys[i % self.nb], writes=[self.keys[i % self.nb]])
                self.issued += 1

        def next(self):
            while self.issued < min(len(self.srcs), self.used + self.nb - 0) and self.issued - self.used < self.nb:
                self._issue()
            i = self.used
            self.used += 1
            return self.bufs[i % self.nb], self.keys[i % self.nb]

        def prefetch(self):
            while self.issued < len(self.srcs) and self.issued - self.used < self.nb - 1:
                self._issue()

    def phase_A(l):
        with ExitStack() as es:
            hblk2 = [sb(es, "hblk%d" % i, [128, KT, 512], F32) for i in range(2)]
            xn2 = [sb(es, "xn%d" % i, [128, KT, 512], BF16) for i in range(2)]
            sq = [sb(es, "sq%d" % i, [128, 512], BF16) for i in range(2)]
            rstd2 = [sb(es, "rstd%d" % i, [128, 512], F32) for i in range(2)]
            wps = PanelStream(es, [winb[l, i] for _ in blocks for i in range(len(SEGS))], nb=3)
            wg = sb(es, "wg", [128, KT, 12], BF16)
            gbt = sb(es, "gbt", [128, 12], F32)
            ost = [sb(es, "ost%d" % i, [128, 4, 512], BF16) for i in range(2)]
            tst = [sb(es, "tst%d" % i, [128, 512], BF16) for i in range(2)]
            gst = [sb(es, "gst%d" % i, [128, 12], F32) for i in range(2)]
            S.dma('sp', wg, wgb[l], key='wg', writes=['wg'])
            S.dma('sp', gbt[:, 0:8], P["ml_gate_bias"][l:l + 1, :].partition_broadcast(128), key='wg', writes=['gbt'])
            S.dma('sp', gbt[:, 8:12], P["fx_gate_bias"][l:l + 1, :].partition_broadcast(128), key='wg', writes=['gbt'])
            fm_dest = {"s_u": (uT, 0), "m_q": (mqkT, 0), "m_k": (mqkT, 4), "f_q": (fqT, 0), "f_k": (fkT, 0), "p_u": (puT, 0)}
            tm_dest = {"m_v": mv, "m_o": mo, "f_v": fv}
            wcnt = 0
            ocnt = 0
            tcnt = 0
            gcnt = 0

            def norm_steps(bi, t0, n):
                s_ = bi % 2
                hb, xb_, rs_ = hblk2[s_], xn2[s_], rstd2[s_]
                kh, kx, kr = 'hblk%d' % s_, 'xn%d' % s_, 'rstd%d' % s_

                def ld():
                    for kt in range(KT):
                        S.dma('sp', hb[:, kt, :n], hT[kt, :, t0:t0 + n], key=kh, writes=[kh])
                steps = [ld] + sumsq_steps(hb, kh, n, sq, rs_, D, krstd=kr, sq_eng='dve')
                for kt in range(KT):
                    def st(kt=kt):
                        S.op('dve', lambda e: e.scalar_tensor_tensor(xb_[:, kt, :n], hb[:, kt, :n], vecT[:, l, kt:kt + 1], rs_[:, :n], ALU.mult, ALU.mult),
                             reads=[kh, kr, 'vecT'], writes=[kx])
                    steps.append(st)
                return steps

            def blk(bi, t0, n, bg):
                nonlocal wcnt, ocnt, tcnt, gcnt
                xn = xn2[bi % 2]
                kxn = 'xn%d' % (bi % 2)
                if bg:
                    bg[0]()
                    bg = bg[1:]
                per = (len(bg) + len(SEGS) - 1) // len(SEGS) if bg else 0
                for i, (nm, c0) in enumerate(SEGS):
                    w, kw = wps.next()
                    wps.prefetch()
                    if nm in fm_dest:
                        dst, toff = fm_dest[nm]
                        o = ost[ocnt % 2]
                        ko = 'ost%d' % (ocnt % 2)
                        ocnt += 1
                        for m in range(4):
                            for kt in range(KT):
                                S.op('pe', lambda e, w=w, m=m, kt=kt: e.matmul(ps[m][:, :n], w[:, kt, m * 128:(m + 1) * 128], xn[:, kt, :n], start=(kt == 0), stop=(kt == KT - 1)),
                                     reads=[kw, kxn], writes=[PSK[m]])
                            evac(1, o[:, m, :n], ps[m][:, :n], [PSK[m]], [ko])
                        for m in range(4):
                            S.dma('pool', dst[toff + m, :, t0:t0 + n], o[:, m, :n], key=ko, reads=[ko])
                    else:
                        dst = tm_dest[nm]
                        for i2 in range(n // 128):
                            pk = 4 + (i2 % 2)
                            ts_ = tst[tcnt % 2]
                            kts = 'tst%d' % (tcnt % 2)
                            tcnt += 1
                            for kt in range(KT):
                                S.op('pe', lambda e, w=w, kt=kt, i2=i2, pk=pk: e.matmul(ps[pk], xn[:, kt, i2 * 128:(i2 + 1) * 128], w[:, kt, :], start=(kt == 0), stop=(kt == KT - 1)),
                                     reads=[kw, kxn], writes=[PSK[pk]])
                            evac(1, ts_, ps[pk], [PSK[pk]], [kts])
                            S.dma('pool', dst[t0 + i2 * 128:t0 + (i2 + 1) * 128, :], ts_, key=kts, reads=[kts])
                    for st_ in bg[i * per:(i + 1) * per]:
                        st_()
                for st_ in bg[len(SEGS) * per:]:
                    st_()
                for i2 in range(n // 128):
                    g_ = gst[gcnt % 2]
                    kg = 'gst%d' % (gcnt % 2)
                    gcnt += 1
                    for kt in range(KT):
                        S.op('pe', lambda e, kt=kt, i2=i2: e.matmul(ps[6][:, 0:12], xn[:, kt, i2 * 128:(i2 + 1) * 128], wg[:, kt, :], start=(kt == 0), stop=(kt == KT - 1)),
                             reads=['wg', kxn], writes=['ps6'])
                    S.op('dve', lambda e, g_=g_: e.tensor_tensor(g_, ps[6][:, 0:12], gbt, ALU.add), reads=['ps6', 'gbt'], writes=[kg])
                    S.dma('pool', gates[t0 + i2 * 128:t0 + (i2 + 1) * 128, :], g_, key=kg, reads=[kg])

            pend = norm_steps(0, *blocks[0])
            for st_ in pend:
                st_()
            for bi, (t0_, n_) in enumerate(blocks):
                bg = norm_steps(bi + 1, *blocks[bi + 1]) if bi + 1 < len(blocks) else []
                blk(bi, t0_, n_, bg)
            S.flush()

    def phase_C(l):
        with ExitStack() as es:
            hblk = sb(es, "hblk", [128, KT, 512], F32)
            mix = sb(es, "mix", [128, KT, 512], F32)
            hn = sb(es, "hn", [128, KT, 512], BF16)
            big = sb(es, "big", [128, 32, 512], BF16)
            sq = [sb(es, "sq%d" % i, [128, 512], BF16) for i in range(2)]
            rstd = sb(es, "rstd", [128, 512], F32)
            rl = [sb(es, "rl%d" % i, [128, 512], F32) for i in range(2)]
            plist = []
            for _ in blocks:
                plist += [woutb[l, i] for i in range(4)]
                for half in range(2):
                    plist += [w1b[l, 8 * half + j] for j in range(8)]
                    plist += [w2b[l, je * 4 + kq] for je in range(4) for kq in (2 * half, 2 * half + 1)]
            wps = PanelStream(es, plist, nb=3)
            wcnt = 0
            rcnt = 0

            def post_norm_add(n, gcol):
                sumsq_rstd(mix, lambda kt: 'mx%d' % kt, n, sq, rstd, D)
                for kt in range(KT):
                    S.op('dve', lambda e, kt=kt: e.scalar_tensor_tensor(mix[:, kt, :n], mix[:, kt, :n], vecT[:, l, gcol + kt:gcol + kt + 1], rstd[:, :n], ALU.mult, ALU.mult),
                         reads=['mx%d' % kt, 'rstd', 'vecT'], writes=['mx%d' % kt])
                    S.op('pool' if kt % 3 else 'dve', lambda e, kt=kt: e.tensor_tensor(hblk[:, kt, :n], hblk[:, kt, :n], mix[:, kt, :n], ALU.add), reads=['mx%d' % kt, 'hb%d' % kt], writes=['hb%d' % kt])

            def load_y(t0, n):
                for kt in range(KT):
                    S.dma('sp', big[:, kt, :n], yT[kt, :, t0:t0 + n], key='ybl', writes=['big'])

            def blk(bi, t0, n):
                nonlocal wcnt, rcnt
                if bi == 0:
                    load_y(t0, n)
                for kt in range(KT):
                    S.dma('sp', hblk[:, kt, :n], hT[kt, :, t0:t0 + n], key='hblk', writes=['hb%d' % kt])
                for i in range(4):
                    w, kw = wps.next()
                    wps.prefetch()
                    for m in range(4):
                        for kt in range(KT):
                            S.op('pe', lambda e, w=w, m=m, kt=kt: e.matmul(ps[m][:, :n], w[:, kt, m * 128:(m + 1) * 128], big[:, kt, :n], start=(kt == 0), stop=(kt == KT - 1)),
                                 reads=[kw, 'big'], writes=[PSK[m]])
                        evac(m, mix[:, i * 4 + m, :n], ps[m][:, :n], [PSK[m]], ['mx%d' % (i * 4 + m)])
                post_norm_add(n, 16)
                if t0 == 0:
                    S.op('pool', lambda e: e.memset(hblk[:, :, 0:112], 0.0), writes=['hb%d' % k_ for k_ in range(KT)])
                sumsq_rstd(hblk, lambda kt: 'hb%d' % kt, n, sq, rstd, D)
                for kt in range(KT):
                    S.op('dve', lambda e, kt=kt: e.scalar_tensor_tensor(hn[:, kt, :n], hblk[:, kt, :n], vecT[:, l, 32 + kt:33 + kt], rstd[:, :n], ALU.mult, ALU.mult),
                         reads=['hb%d' % kt, 'rstd', 'vecT'], writes=['hn'])
                for half in range(2):
                    for jp in range(8):
                        w, kw = wps.next()
                        wps.prefetch()
                        for m in range(4):
                            for kt in range(KT):
                                S.op('pe', lambda e, w=w, m=m, kt=kt: e.matmul(ps[m][:, :n], w[:, kt, m * 128:(m + 1) * 128], hn[:, kt, :n], start=(kt == 0), stop=(kt == KT - 1)),
                                     reads=[kw, 'hn'], writes=[PSK[m]])
                            r_ = rl[rcnt % 2]
                            kr = 'rl%d' % (rcnt % 2)
                            rcnt += 1
                            S.op('act', lambda e, r_=r_, m=m: e.activation(r_[:, :n], ps[m][:, :n], AF.Relu), reads=[PSK[m]], writes=[kr])
                            S.op('dve', lambda e, r_=r_, jp=jp, m=m: e.tensor_tensor(big[:, jp * 4 + m, :n], r_[:, :n], r_[:, :n], ALU.mult), reads=[kr], writes=['big'])
                    for je in range(4):
                        for kq in range(2):
                            w, kw = wps.next()
                            wps.prefetch()
                            for m in range(4):
                                for kt in range(KT):
                                    S.op('pe', lambda e, w=w, m=m, kt=kt, kq=kq: e.matmul(ps[m][:, :n], w[:, kt, m * 128:(m + 1) * 128], big[:, kq * 16 + kt, :n],
                                                                                        start=(kq == 0 and kt == 0), stop=(kq == 1 and kt == KT - 1)),
                                         reads=[kw, 'big'], writes=[PSK[m]])
                        for m in range(4):
                            if half == 0:
                                evac(m, mix[:, je * 4 + m, :n], ps[m][:, :n], [PSK[m]], ['mx%d' % (je * 4 + m)])
                            else:
                                S.op('dve', lambda e, je=je, m=m: e.tensor_tensor(mix[:, je * 4 + m, :n], ps[m][:, :n], mix[:, je * 4 + m, :n], ALU.add),
                                     reads=[PSK[m], 'mx%d' % (je * 4 + m)], writes=['mx%d' % (je * 4 + m)])
                if bi + 1 < len(blocks):
                    load_y(*blocks[bi + 1])
                post_norm_add(n, 48)
                for kt in range(KT):
                    S.dma('pool', hT[kt, :, t0:t0 + n], hblk[:, kt, :n], key='hst', reads=['hb%d' % kt], writes=['hT'])

            for bi_, (t0_, n_) in enumerate(blocks):
                blk(bi_, t0_, n_)
            S.flush()

    def phase_Z():
        with ExitStack() as es:
            hs = [sb(es, "hs%d" % i, [128, KT, 128], F32) for i in range(2)]
            xo = [sb(es, "xo%d" % i, [128, D], F32) for i in range(2)]
            for c in range(1, NCH):
                h_, x_ = hs[c % 2], xo[c % 2]
                kh, kx = 'hs%d' % (c % 2), 'xo%d' % (c % 2)
                S.dma('sp', h_, hT[:, :, c * 128:(c + 1) * 128].rearrange("k p t -> p k t"), key=kh, writes=[kh])
                for q in range(4):
                    for j in range(4):
                        kt = q * 4 + j
                        S.op('pe', lambda e, q=q, j=j, kt=kt, h_=h_: e.transpose(ps[q][:, j * 128:(j + 1) * 128], h_[:, kt, :], ident_f), reads=[kh, 'ident_f'], writes=[PSK[q]])
                    evac(q, x_[:, q * 512:(q + 1) * 512], ps[q], [PSK[q]], [kx])
                S.dma('sp', out[(c - 1) * 128:c * 128, :], x_, key=kx, reads=[kx])
            S.flush()
    def phase_pool(l):
        with ExitStack() as es:
            ub = [sb(es, "ub%d" % i, [128, 16 + LP], F32) for i in range(3)]
            uin = sb(es, "uin", [128, LP], BF16)
            pooled = sb(es, "pooled", [128, LP], BF16)
            t16 = sb(es, "t16", [128, 16], F32)
            pw = sb(es, "pw", [128, 4, 128], BF16)
            ost = [sb(es, "post%d" % i, [128, 512], BF16) for i in range(2)]
            S.dma('sp', pw, poolwb[l], key='pw', writes=['pw'])
            for i in range(3):
                S.op('pool', lambda e, i=i: e.memset(ub[i][:, 0:16], 0.0), writes=['ub%d' % i])
            oc = 0
            for g, w in enumerate((2, 4, 8, 16)):
                S.dma('sp', uin, puT[g], key='uin', writes=['uin'])
                S.op('act', lambda e: e.activation(ub[0][:, 16:], uin, AF.Identity), reads=['uin'], writes=['ub0'])
                a, ka = ub[0], 'ub0'
                for s_ in range(g + 1):
                    m = 2 ** s_
                    d_, kd = ub[1 + s_ % 2], 'ub%d' % (1 + s_ % 2)
                    S.op('dve', lambda e, a=a, d_=d_, m=m: e.tensor_tensor(d_[:, 16:], a[:, 16:], a[:, 16 - m:16 - m + LP], ALU.add), reads=[ka], writes=[kd])
                    a, ka = d_, kd
                S.op('dve', lambda e, a=a, w=w: e.scalar_tensor_tensor(pooled, a[:, 16:], 1.0 / w, ub[0][:, 16:], ALU.mult, ALU.subtract), reads=[ka, 'ub0'], writes=['pooled'])
                S.op('dve', lambda e, a=a, g=g: e.tensor_tensor(t16, a[:, 16 + 112:16 + 128], corr[:, g, :], ALU.mult), reads=[ka, 'corr'], writes=['t16'])
                S.op('dve', lambda e, w=w: e.scalar_tensor_tensor(pooled[:, 112:128], t16, 1.0 / w, ub[0][:, 16 + 112:16 + 128], ALU.mult, ALU.subtract), reads=['t16', 'ub0'], writes=['pooled'])
                for bi, (t0, n) in enumerate(blocks):
                    pk = bi % 2
                    o, ko = ost[oc % 2], 'post%d' % (oc % 2)
                    oc += 1
                    S.op('pe', lambda e, g=g, t0=t0, n=n, pk=pk: e.matmul(ps[pk][:, :n], pw[:, g, :], pooled[:, t0:t0 + n], start=True, stop=True), reads=['pw', 'pooled'], writes=[PSK[pk]])
                    S.op('act', lambda e, o=o, n=n, pk=pk, g=g: e.activation(o[:, :n], ps[pk][:, :n], AF.Identity, scale=vecT[:, l, 64 + g:65 + g]), reads=[PSK[pk], 'vecT'], writes=[ko])
                    S.dma('sp', yT[12 + g, :, t0:t0 + n], o[:, :n], key=ko, reads=[ko])
            S.flush()

    def phase_fox(l):
        SC = float(128 ** -0.5)
        with ExitStack() as es:
            gts = sb(es, "gts", [128, NCH, 12], F32)
            e4 = sb(es, "e4", [128, 4, NCH], F32)
            lfh = sb(es, "lfh", [128, 4, NCH], F32)
            Fin = sb(es, "Fin", [128, 4, NCH], F32)
            tot = sb(es, "tot", [128, 4, NCH], F32)
            Fend = sb(es, "Fend", [128, 4, NCH], F32)
            Fg = sb(es, "Fg", [128, 4, NCH], F32)
            bias = sb(es, "bias", [128, NCH], F32)
            qTt = sb(es, "qTt", [128, LP], BF16)
            kTt = sb(es, "kTt", [128, LP], BF16)
            vtok = sb(es, "vtok", [128, NCH, 128], BF16)
            pT = [sb(es, "pT%d" % i, [128, 512], BF16) for i in range(2)]
            rden = sb(es, "rden", [128, 512], F32)
            ost = [sb(es, "fost%d" % i, [128, 512], BF16) for i in range(2)]
            for c0_ in range(0, NCH, 8):
                c1_ = min(NCH, c0_ + 8)
                S.dma('sp', gts[:, c0_:c1_, :], gates[c0_ * 128:c1_ * 128, :].rearrange("(c p) g -> p c g", p=128), key='gts', writes=['gts'])
            for h in range(4):
                S.op('act', lambda e, h=h: e.activation(e4[:, h, :], gts[:, :, 8 + h], AF.Exp, scale=-1.0), reads=['gts'], writes=['e4'])
            S.op('act', lambda e: e.activation(lfh, e4, AF.Ln, bias=1.0), reads=['e4'], writes=['lfh'])
            S.op('dve', lambda e: e.tensor_scalar(lfh, lfh, -1.0, None, ALU.mult), reads=['lfh'], writes=['lfh'])
            S.op('dve', lambda e: e.tensor_scalar(lfh[:, :, 0], lfh[:, :, 0], padc[:, 0:1], None, ALU.mult), reads=['lfh', 'padc'], writes=['lfh'])
            lff = lfh.rearrange("p h c -> p (h c)")
            S.op('pe', lambda e: e.matmul(ps[0][:, 0:4 * NCH], tri_f, lff, start=True, stop=True), reads=['tri_f', 'lfh'], writes=['ps0'])
            S.op('pe', lambda e: e.matmul(ps[1][:, 0:4 * NCH], ones_f, lff, start=True, stop=True), reads=['ones_f', 'lfh'], writes=['ps1'])
            S.op('dve', lambda e: e.tensor_copy(Fin.rearrange("p h c -> p (h c)"), ps[0][:, 0:4 * NCH]), reads=['ps0'], writes=['Fin'])
            S.op('dve', lambda e: e.tensor_copy(tot.rearrange("p h c -> p (h c)"), ps[1][:, 0:4 * NCH]), reads=['ps1'], writes=['tot'])
            for h in range(4):
                S.op('dve', lambda e, h=h: e.tensor_tensor_scan(Fend[:, h, :], ones_f[:, 0:NCH], tot[:, h, :], 0.0, ALU.mult, ALU.add), reads=['tot', 'ones_f'], writes=['Fend'])
            S.op('dve', lambda e: e.tensor_tensor(Fg, Fend, tot, ALU.subtract), reads=['Fend', 'tot'], writes=['Fg'])
            S.op('dve', lambda e: e.tensor_tensor(Fg, Fg, Fin, ALU.add), reads=['Fg', 'Fin'], writes=['Fg'])
            oc = 0
            pc = 0
            for h in range(4):
                S.dma('sp', qTt, fqT[h], key='fq', writes=['qTt'])
                S.dma('sp', kTt, fkT[h], key='fk', writes=['kTt'])
                for c0_ in range(0, NCH, 8):
                    c1_ = min(NCH, c0_ + 8)
                    S.dma('sp', vtok[:, c0_:c1_, :], fv[c0_ * 128:c1_ * 128, :].rearrange("(c p) (h d) -> p c h d", p=128, h=4)[:, :, h, :], key='fvv', writes=['vtok'])
                for qb, (t0, n) in enumerate(blocks):
                    c0 = t0 // 128
                    clast = c0 + n // 128 - 1
                    pO, pD = 2 + (qb % 2), 4 + (qb % 2)
                    S.op('dve', lambda e, h=h, clast=clast: e.tensor_scalar(bias[:, 0:clast + 1], Fg[:, h, 0:clast + 1], Fend[:, h, clast:clast + 1], -1.0, ALU.subtract, ALU.mult),
                         reads=['Fg', 'Fend'], writes=['bias'])
                    S.op('dve', lambda e: e.tensor_tensor(bias[:, 0:1], bias[:, 0:1], padc[:, 1:2], ALU.add), reads=['bias', 'padc'], writes=['bias'])
                    def emit_S(kc, pcv, t0=t0, n=n, c0=c0):
                        lo = max(0, kc - c0) * 128
                        pS = pcv % 2
                        S.op('pe', lambda e: e.matmul(ps[pS][:, lo:n], kTt[:, kc * 128:(kc + 1) * 128], qTt[:, t0 + lo:t0 + n], start=True, stop=True),
                             reads=['kTt', 'qTt'], writes=[PSK[pS]])

                    emit_S(0, pc)
                    for kc in range(clast + 1):
                        lo = max(0, kc - c0) * 128
                        pS = pc % 2
                        p_, kp = pT[pc % 2], 'pT%d' % (pc % 2)
                        pc += 1
                        S.op('act', lambda e, kc=kc, lo=lo, pS=pS, p_=p_, n=n: e.activation(p_[:, lo:n], ps[pS][:, lo:n], AF.Exp, bias=bias[:, kc:kc + 1], scale=SC),
                             reads=[PSK[pS], 'bias'], writes=[kp])
                        if kc >= c0:
                            S.op('pool', lambda e, p_=p_, lo=lo: e.tensor_tensor(p_[:, lo:lo + 128], p_[:, lo:lo + 128], tri_b, ALU.mult), reads=[kp, 'tri_b'], writes=[kp])
                        if kc + 1 <= clast:
                            emit_S(kc + 1, pc)
                        S.op('pe', lambda e, kc=kc, lo=lo, p_=p_, n=n, pO=pO, clast=clast: e.matmul(ps[pO][:, lo:n], vtok[:, kc, :], p_[:, lo:n], start=(kc == 0), stop=(kc == clast)),
                             reads=['vtok', kp], writes=[PSK[pO]])
                        S.op('pe', lambda e, kc=kc, lo=lo, p_=p_, n=n, pD=pD, clast=clast: e.matmul(ps[pD][:, lo:n], ones_b, p_[:, lo:n], start=(kc == 0), stop=(kc == clast)),
                             reads=['ones_b', kp], writes=[PSK[pD]])
                    o, ko = ost[oc % 2], 'fost%d' % (oc % 2)
                    oc += 1
                    S.op('dve', lambda e, n=n, pD=pD: e.reciprocal(rden[:, :n], ps[pD][:, :n]), reads=[PSK[pD]], writes=['rden'])
                    S.op('dve', lambda e, n=n, pO=pO, o=o: e.tensor_tensor(o[:, :n], ps[pO][:, :n], rden[:, :n], ALU.mult), reads=[PSK[pO], 'rden'], writes=[ko])
                    S.dma('act', yT[8 + h, :, t0:t0 + n], o[:, :n], key=ko, reads=[ko])
            S.flush()
    def phase_mlstm(l):
        SC = float(128 ** -0.5)
        with ExitStack() as es:
            qc = sb(es, "qc", [128, 4, LP], BF16)
            kc_ = sb(es, "kc_", [128, 4, LP], BF16)
            cin = sb(es, "cin", [128, 3 + LP], BF16)
            acc = sb(es, "acc", [128, LP], F32)
            mlg = sb(es, "mlg", [128, 512], F32)
            vaug = [sb(es, "vaug%d" % i, [128, 4, 129], BF16) for i in range(2)]
            ot = [sb(es, "ot%d" % i, [128, 512], BF16) for i in range(2)]
            gt = [sb(es, "gt%d" % i, [128, 12], F32) for i in range(2)]
            sg = sb(es, "sg", [128, 512], F32)
            sm = sb(es, "sm", [128, 40], F32)
            Cf = sb(es, "Cf", [128, 4, 129], F32)
            Cb = sb(es, "Cb", [128, 4, 129], BF16)
            pt = [sb(es, "pt%d" % i, [128, 128], BF16) for i in range(2)]
            ktk = [sb(es, "ktk%d" % i, [128, 128], BF16) for i in range(2)]
            vw = [sb(es, "vw%d" % i, [128, 129], BF16) for i in range(2)]
            hh = [sb(es, "hh%d" % i, [128, 128], F32) for i in range(2)]
            junk = sb(es, "junk", [128, 128], F32)
            hs_ = [sb(es, "hsm%d" % i, [128, 8], F32) for i in range(2)]
            yo = [sb(es, "yo%d" % i, [128, 128], BF16) for i in range(2)]
            yst = [sb(es, "yst%d" % i, [128, 4, 128], BF16) for i in range(2)]
            S.dma('sp', mlg, P["ml_norm_g"][l:l + 1, :].partition_broadcast(128), key='mlg', writes=['mlg'])
            S.op('pool', lambda e: e.memset(cin[:, 0:3], 0.0), writes=['cin'])
            S.op('pool', lambda e: e.memset(Cf, 0.0), writes=['Cf'])
            S.op('pool', lambda e: e.memset(Cb, 0.0), writes=['Cb'])
            for i in range(2):
                S.op('pool', lambda e, i=i: e.memset(vaug[i], 1.0), writes=['vaug%d' % i])
            for j in range(8):
                S.dma('sp', cin[:, 3:], mqkT[j], key='cin', reads=['cin'], writes=['cin'])
                wc = lambda tap, j=j: vecT[:, l, 80 + tap * 8 + j:81 + tap * 8 + j]
                S.op('dve', lambda e, wc=wc: e.tensor_scalar(acc, cin[:, 3:3 + LP], wc(3), None, ALU.mult), reads=['cin', 'vecT'], writes=['acc'])
                for tap in range(3):
                    S.op('dve', lambda e, wc=wc, tap=tap: e.scalar_tensor_tensor(acc, cin[:, tap:tap + LP], wc(tap), acc, ALU.mult, ALU.add), reads=['cin', 'acc', 'vecT'], writes=['acc'])
                if j < 4:
                    S.op('act', lambda e: e.activation(acc, acc, AF.Silu), reads=['acc'], writes=['acc'])
                    S.op('dve', lambda e, j=j: e.tensor_scalar(qc[:, j, :], acc, SC, None, ALU.mult), reads=['acc'], writes=['qc'])
                else:
                    S.op('act', lambda e, j=j: e.activation(kc_[:, j - 4, :], acc, AF.Silu), reads=['acc'], writes=['kc_'])
            for c in range(NCH):
                cs = slice(c * 128, (c + 1) * 128)
                va, kva = vaug[c % 2], 'vaug%d' % (c % 2)
                o_, ko = ot[c % 2], 'ot%d' % (c % 2)
                g_, kg = gt[c % 2], 'gt%d' % (c % 2)
                ys, kys = yst[c % 2], 'yst%d' % (c % 2)
                S.dma('sp', va[:, :, 0:128], mv[cs, :].rearrange("p (h d) -> p h d", h=4), key=kva, writes=[kva])
                S.dma('sp', o_, mo[cs, :], key=ko, writes=[ko])
                S.dma('sp', g_, gates[cs, :], key=kg, writes=[kg])
                S.op('act', lambda e, g_=g_: e.activation(sm[:, 0:4], g_[:, 4:8], AF.Exp, scale=-1.0), reads=[kg], writes=['sm_lf'])
                S.op('act', lambda e: e.activation(sm[:, 0:4], sm[:, 0:4], AF.Ln, bias=1.0), reads=['sm_lf'], writes=['sm_lf'])
                if c == 0:
                    S.op('dve', lambda e: e.tensor_scalar(sm[:, 0:4], sm[:, 0:4], -1.0, padc[:, 0:1], ALU.mult, ALU.mult), reads=['sm_lf', 'padc'], writes=['sm_lf'])
                else:
                    S.op('dve', lambda e: e.tensor_scalar(sm[:, 0:4], sm[:, 0:4], -1.0, None, ALU.mult), reads=['sm_lf'], writes=['sm_lf'])
                S.op('pe', lambda e: e.matmul(ps[6][:, 0:4], tri_f, sm[:, 0:4], start=True, stop=True), reads=['tri_f', 'sm_lf'], writes=['ps6'])
                S.op('pe', lambda e: e.matmul(ps[6][:, 4:8], ones_f, sm[:, 0:4], start=True, stop=True), reads=['ones_f', 'sm_lf'], writes=['ps6'])
                S.op('dve', lambda e, g_=g_: e.tensor_tensor(sm[:, 4:8], g_[:, 0:4], ps[6][:, 0:4], ALU.subtract), reads=[kg, 'ps6'], writes=['sm_w'])
                S.op('act', lambda e: e.activation(sm[:, 4:8], sm[:, 4:8], AF.Exp), reads=['sm_w'], writes=['sm_w'])
                if c == 0:
                    S.op('dve', lambda e: e.tensor_scalar(sm[:, 4:8], sm[:, 4:8], padc[:, 0:1], None, ALU.mult), reads=['sm_w', 'padc'], writes=['sm_w'])
                S.op('act', lambda e: e.activation(sm[:, 8:16], ps[6][:, 0:8], AF.Exp), reads=['ps6'], writes=['sm_e'])
                S.op('act', lambda e, o_=o_: e.activation(sg, o_, AF.Sigmoid), reads=[ko], writes=['sg'])
                for h in range(4):
                    i2 = h % 2
                    kch = kc_[:, h, cs]
                    qch = qc[:, h, cs]
                    p_, kp = pt[i2], 'pt%d' % i2
                    S.op('pe', lambda e, kch=kch, qch=qch, i2=i2: e.matmul(ps[i2][:, 0:128], kch, qch, start=True, stop=True), reads=['kc_', 'qc'], writes=[PSK[i2]])
                    S.op('dve', lambda e, h=h, i2=i2, p_=p_: e.scalar_tensor_tensor(p_, ps[i2][:, 0:128], sm[:, 4 + h:5 + h], tri_f, ALU.mult, ALU.mult),
                         reads=[PSK[i2], 'sm_w', 'tri_f'], writes=[kp])
                    S.op('pe', lambda e, kch=kch, i2=i2: e.transpose(psb[:, i2 * 128:(i2 + 1) * 128], kch, ident_b), reads=['kc_', 'ident_b'], writes=['psb%d' % i2])
                    S.op('act', lambda e, i2=i2: e.activation(ktk[i2], psb[:, i2 * 128:(i2 + 1) * 128], AF.Identity), reads=['psb%d' % i2], writes=['ktk%d' % i2])
                    S.op('pe', lambda e, p_=p_, va=va, h=h, i2=i2: e.matmul(ps[2 + i2][:, 0:129], p_, va[:, h, :], start=True, stop=False), reads=[kp, kva], writes=[PSK[2 + i2]])
                    S.op('pe', lambda e, qch=qch, h=h, i2=i2: e.matmul(ps[2 + i2][:, 0:129], qch, Cb[:, h, :], start=False, stop=True), reads=['qc', 'Cb'], writes=[PSK[2 + i2]])
                    hs = hs_[i2]
                    khs = 'hsm%d' % i2
                    nd = ps[2 + i2]
                    S.op('dve', lambda e, nd=nd, hs=hs, h=h: e.tensor_scalar(hs[:, 0:1], nd[:, 128:129], sm[:, 8 + h:9 + h], None, ALU.mult), reads=[PSK[2 + i2], 'sm_e'], writes=[khs])
                    S.op('dve', lambda e, hs=hs: e.scalar_tensor_tensor(hs[:, 1:2], hs[:, 0:1], -1.0, hs[:, 0:1], ALU.mult, ALU.max), reads=[khs], writes=[khs + 'b'])
                    S.op('dve', lambda e, hs=hs: e.tensor_scalar(hs[:, 2:3], hs[:, 1:2], 1.0, None, ALU.max), reads=[khs + 'b'], writes=[khs + 'c'])
                    S.op('dve', lambda e, hs=hs: e.reciprocal(hs[:, 3:4], hs[:, 2:3]), reads=[khs + 'c'], writes=[khs + 'd'])
                    S.op('dve', lambda e, hs=hs, h=h: e.tensor_tensor(hs[:, 4:5], hs[:, 3:4], sm[:, 8 + h:9 + h], ALU.mult), reads=[khs + 'd', 'sm_e'], writes=[khs + 'e'])
                    hh_, khh = hh[i2], 'hh%d' % i2
                    S.op('dve', lambda e, nd=nd, hs=hs, hh_=hh_: e.tensor_scalar(hh_, nd[:, 0:128], hs[:, 4:5], None, ALU.mult), reads=[PSK[2 + i2], khs + 'e'], writes=[khh])
                    S.op('act', lambda e, hh_=hh_: e.activation(junk, hh_, AF.Square), reads=[khh], writes=['junk'])
                    S.op('dve', lambda e, hs=hs: e.reduce_sum(hs[:, 5:6], junk, AX.X), reads=['junk'], writes=[khs + 'f'])
                    S.op('act', lambda e, hs=hs: e.activation(hs[:, 6:7], hs[:, 5:6], AF.Sqrt, bias=EPS, scale=1.0 / 128), reads=[khs + 'f'], writes=[khs + 'g'])
                    S.op('dve', lambda e, hs=hs: e.reciprocal(hs[:, 7:8], hs[:, 6:7]), reads=[khs + 'g'], writes=[khs + 'h'])
                    S.op('dve', lambda e, hs=hs, hh_=hh_, h=h: e.scalar_tensor_tensor(hh_, hh_, hs[:, 7:8], mlg[:, h * 128:(h + 1) * 128], ALU.mult, ALU.mult), reads=[khh, khs + 'h', 'mlg'], writes=[khh])
                    y_, ky = yo[i2], 'yo%d' % i2
                    S.op('pool', lambda e, hh_=hh_, y_=y_, h=h: e.tensor_tensor(y_, hh_, sg[:, h * 128:(h + 1) * 128], ALU.mult), reads=[khh, 'sg'], writes=[ky])
                    S.op('pe', lambda e, y_=y_, i2=i2: e.transpose(psb[:, 256 + i2 * 128:256 + (i2 + 1) * 128], y_, ident_b), reads=[ky, 'ident_b'], writes=['psby%d' % i2])
                    S.op('act', lambda e, ys=ys, h=h, i2=i2: e.activation(ys[:, h, :], psb[:, 256 + i2 * 128:256 + (i2 + 1) * 128], AF.Identity), reads=['psby%d' % i2], writes=[kys])
                    v_, kv_ = vw[i2], 'vw%d' % i2
                    S.op('pool', lambda e, v_=v_, va=va, h=h: e.tensor_scalar(v_, va[:, h, :], sm[:, 4 + h:5 + h], None, ALU.mult), reads=[kva, 'sm_w'], writes=[kv_])
                    S.op('pe', lambda e, v_=v_, i2=i2: e.matmul(ps[4 + i2][:, 0:129], ktk[i2], v_, start=True, stop=True), reads=['ktk%d' % i2, kv_], writes=[PSK[4 + i2]])
                    S.op('dve', lambda e, h=h: e.tensor_scalar(Cf[:, h, :], Cf[:, h, :], sm[:, 12 + h:13 + h], None, ALU.mult), reads=['Cf', 'sm_e'], writes=['Cf'])
                    S.op('dve', lambda e, h=h, i2=i2: e.scalar_tensor_tensor(Cf[:, h, :], ps[4 + i2][:, 0:129], sm[:, 12 + h:13 + h], Cf[:, h, :], ALU.mult, ALU.add),
                         reads=[PSK[4 + i2], 'sm_e', 'Cf'], writes=['Cf'])
                    S.op('act', lambda e, h=h: e.activation(Cb[:, h, :], Cf[:, h, :], AF.Identity), reads=['Cf'], writes=['Cb'])
                S.dma('act', yT[4:8, :, cs].rearrange("k p t -> p k t"), ys, key=kys, reads=[kys])
            S.flush()
    def phase_s5(l):
        with ExitStack() as es:
            lamr = sb(es, "lamr", [16, 128], F32)
            lami = sb(es, "lami", [16, 128], F32)
            ldt = sb(es, "ldt", [16, 2], F32)
            pm = sb(es, "pm", [16, 12, 128], F32)
            rr = sb(es, "rr", [128, 4, 2048], F32)
            ri = sb(es, "ri", [128, 2048], mybir.dt.int32)
            sm4 = sb(es, "sm4", [128, 4, 16], F32)
            cosT = sb(es, "cosT", [128, 16, 128], F32)
            sinT = sb(es, "sinT", [128, 16, 128], F32)
            Rtab = sb(es, "Rtab", [128, 16, 128], F32)
            cs128 = sb(es, "cs128", [128, 2, 16], F32)
            bnat = sb(es, "bnat", [128, 2, 16, 16], F32)
            bb = sb(es, "bb", [128, 4, 16, 16], F32)
            Xb = sb(es, "Xb", [128, 16, 2, 128], BF16)
            BT = sb(es, "BT", [128, 16, 2, 128], BF16)
            cnat = sb(es, "cnat", [128, 2, 4, 64], F32)
            Xc = sb(es, "Xc", [128, 4, 3, 128], BF16)
            CTm = sb(es, "CTm", [128, 4, 3, 128], BF16)
            CT = sb(es, "CT", [128, 4, 4, 3, 128], BF16)
            gw = sb(es, "gw", [128, 2, 4, 512], BF16)

            def sincos(ang, n, dsin, dcos, pp):
                for k_, (dst, off) in enumerate(((dsin, 0.0), (dcos, float(np.pi / 2)))):
                    a0, a1 = rr[:pp, 0, :n], rr[:pp, 1, :n]
                    S.op('dve', lambda e, a0=a0, off=off: e.tensor_scalar(a0, ang, off, None, ALU.add), reads=['ang'], writes=['rr0'])
                    S.op('dve', lambda e, a0=a0, a1=a1: e.tensor_scalar(a1, a0, 1.0 / TWO_PI, None, ALU.mult), reads=['rr0'], writes=['rr1'])
                    S.op('dve', lambda e, a1=a1: e.tensor_copy(ri[:pp, :n], a1), reads=['rr1'], writes=['ri'])
                    S.op('dve', lambda e, a1=a1: e.tensor_copy(a1, ri[:pp, :n]), reads=['ri'], writes=['rr1'])
                    S.op('dve', lambda e, a0=a0, a1=a1: e.scalar_tensor_tensor(a0, a1, -TWO_PI, a0, ALU.mult, ALU.add), reads=['rr0', 'rr1'], writes=['rr0'])
                    S.op('dve', lambda e, a0=a0, a1=a1: e.tensor_scalar(a1, a0, float(np.pi), -TWO_PI, ALU.is_gt, ALU.mult), reads=['rr0'], writes=['rr1'])
                    S.op('dve', lambda e, a0=a0, a1=a1: e.tensor_tensor(a0, a0, a1, ALU.add), reads=['rr0', 'rr1'], writes=['rr0'])
                    S.op('dve', lambda e, a0=a0, a1=a1: e.tensor_scalar(a1, a0, float(-np.pi), TWO_PI, ALU.is_lt, ALU.mult), reads=['rr0'], writes=['rr1'])
                    S.op('dve', lambda e, a0=a0, a1=a1: e.tensor_tensor(a0, a0, a1, ALU.add), reads=['rr0', 'rr1'], writes=['rr0'])
                    S.op('act', lambda e, a0=a0, dst=dst: e.activation(dst, a0, AF.Sin), reads=['rr0'], writes=['sc%d' % k_])

            S.dma('sp', lamr, P["ssm_lam_re"][l].rearrange("(j g) p -> j (g p)", g=2), key='s5c', writes=['lam'])
            S.dma('sp', lami, P["ssm_lam_im"][l].rearrange("(j g) p -> j (g p)", g=2), key='s5c', writes=['lam'])
            S.dma('sp', ldt, P["ssm_log_dt"][l].rearrange("(j g) -> j g", g=2), key='s5c', writes=['lam'])
            S.dma('sp', bnat[:, 0], P["ssm_b_re"][l].rearrange("(j g) p h -> (g p) j h", g=2), key='s5c', writes=['bnat'])
            S.dma('sp', bnat[:, 1], P["ssm_b_im"][l].rearrange("(j g) p h -> (g p) j h", g=2), key='s5c', writes=['bnat'])
            S.dma('sp', cnat[:, 0], P["ssm_c_re"][l].rearrange("(q g) h p -> (g h) q p", g=8), key='s5c', writes=['cnat'])
            S.dma('sp', cnat[:, 1], P["ssm_c_im"][l].rearrange("(q g) h p -> (g h) q p", g=8), key='s5c', writes=['cnat'])
            for i in range(2):
                S.dma('sp', gw[:, i], glub[l, i], key='s5c', writes=['gw'])
            S.op('act', lambda e: e.activation(ldt, ldt, AF.Exp), reads=['lam'], writes=['dt'])
            for g2 in range(2):
                sl = slice(g2 * 64, (g2 + 1) * 64)
                S.op('dve', lambda e, sl=sl, g2=g2: e.tensor_scalar(pm[:, 0, sl], lami[:, sl], ldt[:, g2:g2 + 1], None, ALU.mult), reads=['lam', 'dt'], writes=['ang'])
                S.op('dve', lambda e, sl=sl, g2=g2: e.tensor_scalar(pm[:, 1, sl], lamr[:, sl], ldt[:, g2:g2 + 1], None, ALU.mult), reads=['lam', 'dt'], writes=['pm1'])
            S.op('act', lambda e: e.activation(pm[:, 2, :], pm[:, 1, :], AF.Exp), reads=['pm1'], writes=['pm2'])
            ang = pm[:, 0, :]
            sincos(ang, 128, pm[:, 3, :], pm[:, 4, :], 16)
            TT = lambda o, a, b, op, r_, w_: S.op('dve', lambda e: e.tensor_tensor(o, a, b, op), reads=r_, writes=w_)
            pmk = ['pm%d' % i for i in range(12)]
            TT(pm[:, 5, :], pm[:, 2, :], pm[:, 4, :], ALU.mult, ['pm2', 'sc1'], ['pm5'])
            S.op('dve', lambda e: e.tensor_scalar(pm[:, 5, :], pm[:, 5, :], -1.0, None, ALU.add), reads=['pm5'], writes=['pm5'])
            TT(pm[:, 6, :], pm[:, 2, :], pm[:, 3, :], ALU.mult, ['pm2', 'sc0'], ['pm6'])
            TT(pm[:, 7, :], pm[:, 5, :], lamr, ALU.mult, ['pm5', 'lam'], ['pm7'])
            TT(pm[:, 9, :], pm[:, 6, :], lami, ALU.mult, ['pm6', 'lam'], ['pm9'])
            TT(pm[:, 7, :], pm[:, 7, :], pm[:, 9, :], ALU.add, ['pm7', 'pm9'], ['pm7'])
            TT(pm[:, 8, :], pm[:, 6, :], lamr, ALU.mult, ['pm6', 'lam'], ['pm8'])
            TT(pm[:, 9, :], pm[:, 5, :], lami, ALU.mult, ['pm5', 'lam'], ['pm9'])
            TT(pm[:, 8, :], pm[:, 8, :], pm[:, 9, :], ALU.subtract, ['pm8', 'pm9'], ['pm8'])
            TT(pm[:, 9, :], lamr, lamr, ALU.mult, ['lam'], ['pm9'])
            TT(pm[:, 10, :], lami, lami, ALU.mult, ['lam'], ['pm10'])
            TT(pm[:, 9, :], pm[:, 9, :], pm[:, 10, :], ALU.add, ['pm9', 'pm10'], ['pm9'])
            S.op('dve', lambda e: e.reciprocal(pm[:, 9, :], pm[:, 9, :]), reads=['pm9'], writes=['pm9'])
            TT(pm[:, 10, :], pm[:, 7, :], pm[:, 9, :], ALU.mult, ['pm7', 'pm9'], ['pm10'])
            TT(pm[:, 11, :], pm[:, 8, :], pm[:, 9, :], ALU.mult, ['pm8', 'pm9'], ['pm11'])
            for k_, row in enumerate((0, 2, 10, 11)):
                S.op('pe', lambda e, k_=k_, row=row: e.transpose(ps[0][:, k_ * 16:(k_ + 1) * 16], pm[:, row, :], ident_f[0:16, 0:16]),
                     reads=['ang', 'pm2', 'pm10', 'pm11', 'ident_f'], writes=['ps0'])
            S.op('dve', lambda e: e.tensor_copy(sm4.rearrange("p a b -> p (a b)"), ps[0][:, 0:64]), reads=['ps0'], writes=['sm4'])
            thT, rT, krT, kiT = sm4[:, 0, :], sm4[:, 1, :], sm4[:, 2, :], sm4[:, 3, :]
            angT = rr[:, 2, :]
            S.op('dve', lambda e: e.tensor_tensor(angT.rearrange("p (j t) -> p j t", j=16), iota.unsqueeze(1).to_broadcast([128, 16, 128]),
                                                  thT.unsqueeze(2).to_broadcast([128, 16, 128]), ALU.mult), reads=['iota', 'sm4'], writes=['ang'])
            ang = angT
            sincos(ang, 2048, sinT.rearrange("p j t -> p (j t)"), cosT.rearrange("p j t -> p (j t)"), 128)
            S.op('dve', lambda e: e.tensor_copy(sinT, sinT), reads=['sc0'], writes=['sinT'])
            S.op('dve', lambda e: e.tensor_copy(cosT, cosT), reads=['sc1'], writes=['cosT'])
            ang128 = rr[:, 3, 0:16]
            S.op('dve', lambda e: e.tensor_scalar(ang128, thT, 128.0, None, ALU.mult), reads=['sm4'], writes=['ang'])
            ang = ang128
            sincos(ang, 16, cs128[:, 1, :], cs128[:, 0, :], 128)
            S.op('dve', lambda e: e.tensor_copy(cs128, cs128), reads=['sc0', 'sc1'], writes=['cs128'])
            S.op('dve', lambda e: e.tensor_copy(Rtab, rT.unsqueeze(2).to_broadcast([128, 16, 128])), reads=['sm4'], writes=['Rtab'])
            S.op('dve', lambda e: e.memset(Rtab[:, :, 0:1], 0.0), reads=['Rtab'], writes=['Rtab'])
            krb = krT.unsqueeze(2).to_broadcast([128, 16, 16])
            kib = kiT.unsqueeze(2).to_broadcast([128, 16, 16])
            TT(bb[:, 0], bnat[:, 0], krb, ALU.mult, ['bnat', 'sm4'], ['bb0'])
            TT(bb[:, 1], bnat[:, 1], kib, ALU.mult, ['bnat', 'sm4'], ['bb1'])
            TT(bb[:, 0], bb[:, 0], bb[:, 1], ALU.subtract, ['bb0', 'bb1'], ['bb0'])
            TT(bb[:, 2], bnat[:, 1], krb, ALU.mult, ['bnat', 'sm4'], ['bb2'])
            TT(bb[:, 3], bnat[:, 0], kib, ALU.mult, ['bnat', 'sm4'], ['bb3'])
            TT(bb[:, 2], bb[:, 2], bb[:, 3], ALU.add, ['bb2', 'bb3'], ['bb2'])
            S.op('pool', lambda e: e.memset(Xb, 0.0), writes=['Xb'])
            for j in range(16):
                for g2 in range(2):
                    col0 = 16 * ((2 * j + g2) % 8)
                    rs_ = slice(g2 * 64, (g2 + 1) * 64)
                    for ri_, src in enumerate((0, 2)):
                        S.op('dve' if ri_ else 'pool', lambda e, j=j, rs_=rs_, col0=col0, ri_=ri_, src=src: e.tensor_copy(Xb[rs_, j, ri_, col0:col0 + 16], bb[rs_, src, j, :]),
                             reads=['bb0', 'bb2'], writes=['Xb'])
            for j0 in range(0, 16, 4):
                for jj in range(4):
                    for ri_ in range(2):
                        k_ = jj * 2 + ri_
                        S.op('pe', lambda e, j0=j0, jj=jj, ri_=ri_, k_=k_: e.transpose(psb[:, k_ * 128:(k_ + 1) * 128], Xb[:, j0 + jj, ri_, :], ident_b), reads=['Xb', 'ident_b'], writes=['psb'])
                S.op('dve', lambda e, j0=j0: e.tensor_copy(BT[:, j0:j0 + 4].rearrange("p j r c -> p (j r c)"), psb), reads=['psb'], writes=['BT'])
            for v_, (src, sgn) in enumerate(((0, 1.0), (0, -1.0), (1, -1.0))):
                for g2 in range(2):
                    S.op('dve', lambda e, v_=v_, src=src, sgn=sgn, g2=g2: e.tensor_scalar(Xc[:, :, v_, g2 * 64:(g2 + 1) * 64], cnat[:, src], padc[:, 2 + g2:3 + g2], sgn, ALU.mult, ALU.mult),
                         reads=['cnat', 'padc'], writes=['Xc'])
            for q in range(4):
                for v_ in range(3):
                    k_ = q * 3 + v_
                    S.op('pe', lambda e, q=q, v_=v_, k_=k_: e.transpose(psb[:, (k_ % 8) * 128:(k_ % 8 + 1) * 128], Xc[:, q, v_, :], ident_b), reads=['Xc', 'ident_b'], writes=['psb'])
                    S.op('dve', lambda e, q=q, v_=v_, k_=k_: e.tensor_copy(CTm[:, q, v_, :], psb[:, (k_ % 8) * 128:(k_ % 8 + 1) * 128]), reads=['psb'], writes=['CTm'])
            S.op('pool', lambda e: e.memset(CT, 0.0), writes=['CT'])
            for jj in range(4):
                for q in range(4):
                    S.op('dve', lambda e, jj=jj, q=q: e.tensor_copy(CT[:, q, jj, :, 32 * jj:32 * jj + 32], CTm[:, q, :, 32 * jj:32 * jj + 32]), reads=['CTm', 'CT'], writes=['CT'])
            S.flush()
            uin = [sb(es, "s5u%d" % i, [128, 4, 128], BF16) for i in range(2)]
            tmp = [[sb(es, "s5t%d_%d" % (i, s_), [128, 512], F32) for i in range(4)] for s_ in range(2)]
            vre2 = [sb(es, "vre%d" % s_, [128, 512], F32) for s_ in range(2)]
            vim2 = [sb(es, "vim%d" % s_, [128, 512], F32) for s_ in range(2)]
            wre2 = [sb(es, "wre%d" % s_, [128, 512], F32) for s_ in range(2)]
            wim2 = [sb(es, "wim%d" % s_, [128, 512], F32) for s_ in range(2)]
            Ap2 = [[sb(es, "Ap%d_%d" % (i, s_), [128, 512], BF16) for i in range(4)] for s_ in range(2)]
            car = sb(es, "car", [128, 6, 16], F32)
            yg2 = [sb(es, "yg%d" % s_, [128, 128], F32) for s_ in range(2)]
            t22 = [sb(es, "t2_%d" % s_, [128, 128], F32) for s_ in range(2)]
            sgm2 = [sb(es, "sgm%d" % s_, [128, 128], F32) for s_ in range(2)]
            gl = [sb(es, "gl%d" % i, [128, 4, 128], BF16) for i in range(2)]
            sgt = sb(es, "sgt", [128, 4, 128], F32)
            yst = [sb(es, "s5y%d" % i, [128, 4, 128], BF16) for i in range(2)]
            S.op('pool', lambda e: e.memset(car, 0.0), writes=['car'])
            for c in range(NCH):
                cs = slice(c * 128, (c + 1) * 128)
                u_, ku = uin[c % 2], 's5u%d' % (c % 2)
                gl_, kgl = gl[c % 2], 'gl%d' % (c % 2)
                ys, kys = yst[c % 2], 's5y%d' % (c % 2)
                S.dma('sp', u_, uT[:, :, cs].rearrange("k p t -> p k t"), key=ku, writes=[ku])
                def Q(qd):
                    s_ = qd % 2
                    j4 = slice(4 * qd, 4 * qd + 4)
                    return dict(qd=qd, s=s_, j4=j4, pR=s_ * 2, pI=s_ * 2 + 1, pY=4 + s_,
                                cosq=cosT[:, j4, :].rearrange("p j t -> p (j t)"), sinq=sinT[:, j4, :].rearrange("p j t -> p (j t)"),
                                Rq=Rtab[:, j4, :].rearrange("p j t -> p (j t)"),
                                tmp=tmp[s_], vre=vre2[s_], vim=vim2[s_], wre=wre2[s_], wim=wim2[s_], Ap=Ap2[s_],
                                yg=yg2[s_], t2=t22[s_], sgm=sgm2[s_], cc=slice(4 * s_, 4 * s_ + 4),
                                u=u_, ku=ku, gl=gl_, kgl=kgl,
                                k=lambda nm, s_=s_: '%s_%d' % (nm, s_))

                def st_bu(q):
                    for jj in range(4):
                        j = 4 * q['qd'] + jj
                        S.op('pe', lambda e, j=j, jj=jj: e.matmul(ps[q['pR']][:, jj * 128:(jj + 1) * 128], BT[:, j, 0, :], q['u'][:, q['qd'], :], start=True, stop=True), reads=['BT', q['ku']], writes=[PSK[q['pR']]])
                        S.op('pe', lambda e, j=j, jj=jj: e.matmul(ps[q['pI']][:, jj * 128:(jj + 1) * 128], BT[:, j, 1, :], q['u'][:, q['qd'], :], start=True, stop=True), reads=['BT', q['ku']], writes=[PSK[q['pI']]])

                def st_rot(q):
                    k, t_ = q['k'], q['tmp']
                    S.op('dve', lambda e: e.tensor_tensor(t_[0], ps[q['pR']], q['cosq'], ALU.mult), reads=[PSK[q['pR']], 'cosT'], writes=[k('s5t0')])
                    S.op('dve', lambda e: e.tensor_tensor(t_[1], ps[q['pI']], q['sinq'], ALU.mult), reads=[PSK[q['pI']], 'sinT'], writes=[k('s5t1')])
                    S.op('dve', lambda e: e.tensor_tensor(t_[2], ps[q['pI']], q['cosq'], ALU.mult), reads=[PSK[q['pI']], 'cosT'], writes=[k('s5t2')])
                    S.op('dve', lambda e: e.tensor_tensor(t_[3], ps[q['pR']], q['sinq'], ALU.mult), reads=[PSK[q['pR']], 'sinT'], writes=[k('s5t3')])

                def st_add(q):
                    k, t_ = q['k'], q['tmp']
                    S.op('pool', lambda e: e.tensor_tensor(q['vre'], t_[0], t_[1], ALU.add), reads=[k('s5t0'), k('s5t1')], writes=[k('vre')])
                    S.op('pool', lambda e: e.tensor_tensor(q['vim'], t_[2], t_[3], ALU.subtract), reads=[k('s5t2'), k('s5t3')], writes=[k('vim')])

                def st_scan(q):
                    k, j4 = q['k'], q['j4']
                    v3r = q['vre'].rearrange("p (j t) -> p j t", j=4)
                    v3i = q['vim'].rearrange("p (j t) -> p j t", j=4)
                    S.op('dve', lambda e: e.tensor_tensor(v3r[:, :, 0], v3r[:, :, 0], car[:, 0, j4], ALU.add), reads=[k('vre'), 'car'], writes=[k('vre')])
                    S.op('dve', lambda e: e.tensor_tensor(v3i[:, :, 0], v3i[:, :, 0], car[:, 1, j4], ALU.add), reads=[k('vim'), 'car'], writes=[k('vim')])
                    S.op('dve', lambda e: e.tensor_tensor_scan(q['wre'], q['Rq'], q['vre'], 0.0, ALU.mult, ALU.add), reads=['Rtab', k('vre')], writes=[k('wre')])
                    S.op('dve', lambda e: e.tensor_tensor_scan(q['wim'], q['Rq'], q['vim'], 0.0, ALU.mult, ALU.add), reads=['Rtab', k('vim')], writes=[k('wim')])

                def st_carry(q):
                    k, j4, cc = q['k'], q['j4'], q['cc']
                    w3r = q['wre'].rearrange("p (j t) -> p j t", j=4)
                    w3i = q['wim'].rearrange("p (j t) -> p j t", j=4)
                    c128, s128 = cs128[:, 0, j4], cs128[:, 1, j4]
                    S.op('dve', lambda e: e.tensor_tensor(car[:, 2, cc], w3r[:, :, 127], c128, ALU.mult), reads=[k('wre'), 'cs128'], writes=[k('car2')])
                    S.op('dve', lambda e: e.tensor_tensor(car[:, 3, cc], w3i[:, :, 127], s128, ALU.mult), reads=[k('wim'), 'cs128'], writes=[k('car3')])
                    S.op('dve', lambda e: e.tensor_tensor(car[:, 4, cc], w3r[:, :, 127], s128, ALU.mult), reads=[k('wre'), 'cs128'], writes=[k('car4')])
                    S.op('dve', lambda e: e.tensor_tensor(car[:, 5, cc], w3i[:, :, 127], c128, ALU.mult), reads=[k('wim'), 'cs128'], writes=[k('car5')])
                    S.op('dve', lambda e: e.tensor_tensor(car[:, 2, cc], car[:, 2, cc], car[:, 3, cc], ALU.subtract), reads=[k('car2'), k('car3')], writes=[k('car2')])
                    S.op('dve', lambda e: e.tensor_tensor(car[:, 4, cc], car[:, 4, cc], car[:, 5, cc], ALU.add), reads=[k('car4'), k('car5')], writes=[k('car4')])
                    S.op('dve', lambda e: e.tensor_tensor(car[:, 0, j4], car[:, 2, cc], rT[:, j4], ALU.mult), reads=[k('car2'), 'sm4', 'car'], writes=['car'])
                    S.op('dve', lambda e: e.tensor_tensor(car[:, 1, j4], car[:, 4, cc], rT[:, j4], ALU.mult), reads=[k('car4'), 'sm4', 'car'], writes=['car'])

                def st_prod(q):
                    k, A_ = q['k'], q['Ap']
                    S.op('pool', lambda e: e.tensor_tensor(A_[0], q['wre'], q['cosq'], ALU.mult), reads=[k('wre'), 'cosT'], writes=[k('Ap0')])
                    S.op('pool', lambda e: e.tensor_tensor(A_[1], q['wim'], q['sinq'], ALU.mult), reads=[k('wim'), 'sinT'], writes=[k('Ap1')])
                    S.op('dve', lambda e: e.tensor_tensor(A_[2], q['wre'], q['sinq'], ALU.mult), reads=[k('wre'), 'sinT'], writes=[k('Ap2')])
                    S.op('dve', lambda e: e.tensor_tensor(A_[3], q['wim'], q['cosq'], ALU.mult), reads=[k('wim'), 'cosT'], writes=[k('Ap3')])

                def st_y(q):
                    k, A_ = q['k'], q['Ap']
                    n_mm = 0
                    for jj in range(4):
                        for a_, v_ in ((0, 0), (1, 1), (2, 2), (3, 2)):
                            S.op('pe', lambda e, jj=jj, a_=a_, v_=v_, n_mm=n_mm: e.matmul(ps[q['pY']][:, 0:128], CT[:, q['qd'], jj, v_, :], A_[a_][:, jj * 128:(jj + 1) * 128],
                                                                                       start=(n_mm == 0), stop=(n_mm == 15)), reads=['CT', k('Ap%d' % a_)], writes=[PSK[q['pY']]])
                            n_mm += 1

                def st_gelu1(q):
                    k, yg, t2_ = q['k'], q['yg'], q['t2']
                    qd = q['qd']
                    S.op('dve', lambda e: e.scalar_tensor_tensor(yg, q['u'][:, qd, :], vecT[:, l, 68 + qd:69 + qd], ps[q['pY']][:, 0:128], ALU.mult, ALU.add), reads=[q['ku'], 'vecT', PSK[q['pY']]], writes=[k('yg')])
                    S.op('dve', lambda e: e.tensor_tensor(t2_, yg, yg, ALU.mult), reads=[k('yg')], writes=[k('t2_')])
                    S.op('dve', lambda e: e.tensor_scalar(t2_, t2_, 0.044715, 1.0, ALU.mult, ALU.add), reads=[k('t2_')], writes=[k('t2_')])
                    S.op('dve', lambda e: e.tensor_tensor(t2_, t2_, yg, ALU.mult), reads=[k('t2_'), k('yg')], writes=[k('t2_')])
                    S.op('act', lambda e: e.activation(q['sgm'], t2_, AF.Sigmoid, scale=1.5957691216057308), reads=[k('t2_')], writes=[k('sgm')])

                def st_gelu2(q):
                    k = q['k']
                    S.op('dve', lambda e: e.tensor_tensor(q['gl'][:, q['qd'], :], q['yg'], q['sgm'], ALU.mult), reads=[k('yg'), k('sgm')], writes=[q['kgl']])

                for qa in (0, 2):
                    A_, B_ = Q(qa), Q(qa + 1)
                    for st in (st_bu, st_rot):
                        st(A_)
                        st(B_)
                    st_add(A_)
                    st_add(B_)
                    st_scan(A_)
                    st_scan(B_)
                    st_prod(A_)
                    st_prod(B_)
                    st_y(A_)
                    st_y(B_)
                    st_carry(A_)
                    st_carry(B_)
                    st_gelu1(A_)
                    st_gelu1(B_)
                    st_gelu2(A_)
                    st_gelu2(B_)
                for ot_ in range(8):
                    pz = 2 + ot_ // 4 if False else (0 if ot_ < 4 else 1)
                    for kt in range(4):
                        S.op('pe', lambda e, ot_=ot_, kt=kt, gl_=gl_, pz=pz: e.matmul(ps[pz][:, (ot_ % 4) * 128:(ot_ % 4 + 1) * 128], gw[:, ot_ // 4, kt, (ot_ % 4) * 128:(ot_ % 4 + 1) * 128], gl_[:, kt, :],
                                                                                       start=(kt == 0), stop=(kt == 3)), reads=['gw', kgl], writes=[PSK[pz]])
                for o4 in range(4):
                    S.op('act', lambda e, o4=o4: e.activation(sgt[:, o4, :], ps[1][:, o4 * 128:(o4 + 1) * 128], AF.Sigmoid, bias=vecT[:, l, 76 + o4:77 + o4]), reads=['ps1', 'vecT'], writes=['sgt'])
                    S.op('dve', lambda e, o4=o4, ys=ys: e.scalar_tensor_tensor(ys[:, o4, :], ps[0][:, o4 * 128:(o4 + 1) * 128], vecT[:, l, 72 + o4:73 + o4], sgt[:, o4, :], ALU.add, ALU.mult),
                         reads=['ps0', 'vecT', 'sgt'], writes=[kys])
                S.dma('act', yT[0:4, :, cs].rearrange("k p t -> p k t"), ys, key=kys, reads=[kys])
            S.flush()

    for l in range(DEPTH):
        phase_A(l)
        if stop_after == "A":
            return nc
        phase_s5(l)
        phase_mlstm(l)
        phase_fox(l)
        phase_pool(l)
        if stop_after == "mix":
            return nc
        phase_C(l)
        if stop_after == "C":
            return nc
    phase_Z()
    pes.close()
    return nc


def _in_maps(inputs, SEQ, DEPTH, n_cores):
    consts = host_consts()
    maps = []
    B = inputs["x"].shape[0]
    for c in range(n_cores):
        m = {"x": np.ascontiguousarray(inputs["x"][c % B], dtype=np.float32), "meta_tokens": np.asarray(inputs["meta_tokens"], np.float32)}
        for n, _ in PARAMS:
            m[n] = np.ascontiguousarray(inputs[n][:DEPTH], dtype=np.float32)
        m.update(consts)
        maps.append(m)
    return maps


def kernel(**inputs):
    SEQ, DEPTH = 4096, 2
    B = inputs["x"].shape[0]
    nc = build_program(SEQ, DEPTH)
    res = run_bass_kernel_spmd(nc, _in_maps(inputs, SEQ, DEPTH, 8), core_ids=list(range(8)))
    return np.stack([np.asarray(res.results[b]["out"], dtype=np.float32) for b in range(B)], axis=0)
```

```python
import numpy as np
import concourse.bass as bass
import concourse.mybir as mybir
from concourse.bass_utils import run_bass_kernel_spmd

F32 = mybir.dt.float32
BF16 = mybir.dt.bfloat16
AF = mybir.ActivationFunctionType
ALU = mybir.AluOpType
AX = mybir.AxisListType


class Sched:
    def __init__(self, nc):
        self.nc = nc
        self.engs = {'pe': nc.tensor, 'act': nc.scalar, 'dve': nc.vector, 'pool': nc.gpsimd, 'sp': nc.sync}
        self.ops = []
        self.last_w = {}
        self.readers = {}
        self.dma_cnt = {}
        self.esem = {e: nc.alloc_semaphore("s_" + e) for e in self.engs}
        self.dsem = {}
        self.ebase = {e: 0 for e in self.engs}
        self.dma_issued = {}

    def _deps(self, reads, writes):
        deps = set()
        for k in reads:
            if k in self.last_w:
                deps.add(self.last_w[k])
        for k in writes:
            if k in self.last_w:
                deps.add(self.last_w[k])
            deps.update(self.readers.get(k, ()))
        return deps

    def _record(self, oid, reads, writes):
        for k in writes:
            self.last_w[k] = oid
            self.readers[k] = []
        for k in reads:
            self.readers.setdefault(k, []).append(oid)

    @staticmethod
    def _bank(reads, writes):
        bk = lambda k: 'psb' if k.startswith('psb') else k
        w = [bk(k) for k in writes] + [bk(k) for k in reads if k.startswith('ps')]
        r = [k for k in reads if not k.startswith('ps')]
        return r, w

    def op(self, eng, fn, reads=(), writes=()):
        reads, writes = self._bank(list(reads), list(writes))
        deps = self._deps(reads, writes)
        oid = len(self.ops)
        self.ops.append(dict(eng=eng, fn=fn, deps=deps, dma=None))
        self._record(oid, reads, writes)
        return oid

    def dma(self, eng, out, in_, key, reads=(), writes=(), **kw):
        deps = self._deps(reads, writes)
        oid = len(self.ops)
        n = self.dma_cnt.get(key, 0) + 1
        self.dma_cnt[key] = n
        self.ops.append(dict(eng=eng, fn=lambda e: e.dma_start(out=out, in_=in_, **kw), deps=deps, dma=(key, n)))
        self._record(oid, reads, writes)
        return oid

    def flush(self):
        nc = self.nc
        ops = self.ops
        if not ops:
            return
        eng_list = {e: [] for e in self.engs}
        for oid, o in enumerate(ops):
            o['pos'] = len(eng_list[o['eng']])
            eng_list[o['eng']].append(oid)
        seen = {e: {} for e in self.engs}
        need_inc = set()
        for oid, o in enumerate(ops):
            e = o['eng']
            want = {}
            for d in o['deps']:
                do = ops[d]
                if do['dma'] is not None:
                    key, n = do['dma']
                    want[('d', key)] = max(want.get(('d', key), 0), n)
                else:
                    if do['eng'] == 'pe' and e == 'pe' and o['dma'] is None:
                        continue
                    want[('e', do['eng'])] = max(want.get(('e', do['eng']), -1), do['pos'])
            waits = []
            for k, v in want.items():
                if seen[e].get(k, -1) >= v:
                    continue
                seen[e][k] = v
                waits.append((k, v))
                if k[0] == 'e':
                    need_inc.add(eng_list[k[1]][v])
            o['waits'] = waits
        issued = self.dma_issued
        for oid, o in enumerate(ops):
            nw = []
            for k, v in o['waits']:
                if k[0] == 'd':
                    v = max(v, issued.get(k[1], 0))
                nw.append((k, v))
            o['waits'] = nw
            if o['dma'] is not None:
                issued[o['dma'][0]] = o['dma'][1]
        for k in self.dma_cnt:
            if k not in self.dsem:
                self.dsem[k] = nc.alloc_semaphore("d_%d" % len(self.dsem))
        inc_prefix = {}
        for e, lst in eng_list.items():
            c = self.ebase[e]
            for oid in lst:
                if oid in need_inc:
                    c += 1
                    inc_prefix[oid] = c
            self.ebase[e] = c
        finals = [(k, n) for k, n in self.dma_cnt.items()]

        def emit(e, eng):
            for oid in eng_list[e]:
                o = ops[oid]
                for k, v in o['waits']:
                    if k[0] == 'd':
                        eng.wait_ge(self.dsem[k[1]], 16 * v)
                    else:
                        eng.wait_ge(self.esem[k[1]], inc_prefix[eng_list[k[1]][v]])
                ins = o['fn'](eng)
                if o['dma'] is not None:
                    ins.then_inc(self.dsem[o['dma'][0]], 16)
                elif oid in need_inc:
                    ins.then_inc(self.esem[e], 1)
            if e == 'sp':
                for k, n in finals:
                    eng.wait_ge(self.dsem[k], 16 * n)

        with nc.Block() as block:
            @block.sync
            def _(eng):
                emit('sp', eng)

            @block.tensor
            def _(eng):
                emit('pe', eng)

            @block.scalar
            def _(eng):
                emit('act', eng)

            @block.vector
            def _(eng):
                emit('dve', eng)

            @block.gpsimd
            def _(eng):
                emit('pool', eng)
        nc.all_engine_barrier()
        self.ops = []
        self.last_w = {}
        self.readers = {}


D = 2048
KT = 16
INC = 4620
DFF = 8192
EPS = 1e-6
SEGS = [("s_u", 0), ("m_q", 512), ("m_k", 1024), ("m_v", 1536), ("m_o", 2048),
        ("f_q", 2568), ("f_k", 3080), ("f_v", 3592), ("p_u", 4108)]
PARAMS = [("g_pre_mix", [D]), ("g_post_mix", [D]), ("g_pre_ffn", [D]), ("g_post_ffn", [D]),
          ("w_in", [D, INC]), ("ml_gate_bias", [8]), ("fx_gate_bias", [4]),
          ("ssm_lam_re", [32, 64]), ("ssm_lam_im", [32, 64]), ("ssm_log_dt", [32]),
          ("ssm_b_re", [32, 64, 16]), ("ssm_b_im", [32, 64, 16]), ("ssm_c_re", [32, 16, 64]),
          ("ssm_c_im", [32, 16, 64]), ("ssm_d", [32, 16]), ("ssm_glu_w", [512, 1024]), ("ssm_glu_b", [1024]),
          ("ml_conv_w", [4, 1024]), ("ml_norm_g", [512]), ("pool_w", [4, 128, 128]), ("pool_scale", [512]),
          ("w_out", [D, D]), ("mlp_w1", [D, DFF]), ("mlp_w2", [DFF, D])]
TWO_PI = float(2.0 * np.pi)


def host_consts():
    tri = (np.arange(128)[:, None] <= np.arange(128)[None, :]).astype(np.float32)
    pad = np.zeros((128, 4), np.float32)
    pad[112:, 0] = 1.0
    pad[:112, 1] = -1e30
    pad[:, 2] = ((np.arange(128) // 16) % 2 == 0)
    pad[:, 3] = ((np.arange(128) // 16) % 2 == 1)
    corr = np.zeros((128, 4, 16), np.float32)
    for g, w in enumerate((2, 4, 8, 16)):
        corr[:, g, :] = w / np.minimum(np.arange(1, 17), w)
    iota = np.broadcast_to(np.arange(128, dtype=np.float32)[None, :], (128, 128)).copy()
    ident = np.eye(128, dtype=np.float32)
    return {"c_tri": tri, "c_pad": pad, "c_corr": corr, "c_iota": iota, "c_ident": ident}


def build_program(SEQ, DEPTH, debug=False, stop_after=None):
    nc = bass.Bass("TRN2", target_bir_lowering=False)
    LP = SEQ + 128
    NCH = LP // 128
    blocks = [(t0, min(512, LP - t0)) for t0 in range(0, LP, 512)]

    def din(name, shape):
        return nc.dram_tensor(name, list(shape), F32, kind="ExternalInput").ap()

    x = din("x", [SEQ, D])
    meta = din("meta_tokens", [16, D])
    P = {n: din(n, [DEPTH] + sh) for n, sh in PARAMS}
    c_tri = din("c_tri", [128, 128])
    c_pad = din("c_pad", [128, 4])
    c_corr = din("c_corr", [128, 4, 16])
    c_iota = din("c_iota", [128, 128])
    c_ident = din("c_ident", [128, 128])
    out = nc.dram_tensor("out", [SEQ, D], F32, kind="ExternalOutput").ap()
    dbg_kind = "ExternalOutput" if debug else "Internal"

    def scr(name, shape, dt, dbg=False):
        return nc.dram_tensor(name, list(shape), dt, kind=(dbg_kind if dbg else "Internal")).ap()

    hT = scr("hT", [KT, 128, LP], F32, True)
    uT = scr("uT", [4, 128, LP], BF16, True)
    mqkT = scr("mqkT", [8, 128, LP], BF16, True)
    mv = scr("mv", [LP, 512], BF16, True)
    mo = scr("mo", [LP, 512], BF16, True)
    gates = scr("gates", [LP, 12], F32, True)
    fqT = scr("fqT", [4, 128, LP], BF16, True)
    fkT = scr("fkT", [4, 128, LP], BF16, True)
    fv = scr("fv", [LP, 512], BF16, True)
    puT = scr("puT", [4, 128, LP], BF16, True)
    yT = scr("yT", [KT, 128, LP], BF16, True)
    s5scr = scr("s5scr", [8, 2048], F32)
    winb = scr("winb", [DEPTH, 9, 128, KT, 512], BF16)
    wgb = scr("wgb", [DEPTH, 128, KT, 12], BF16)
    woutb = scr("woutb", [DEPTH, 4, 128, KT, 512], BF16)
    w1b = scr("w1b", [DEPTH, 16, 128, KT, 512], BF16)
    w2b = scr("w2b", [DEPTH, 16, 128, KT, 512], BF16)
    glub = scr("glub", [DEPTH, 2, 128, 4, 512], BF16)
    poolwb = scr("poolwb", [DEPTH, 128, 4, 128], BF16)

    S = Sched(nc)
    ps = [nc.alloc_psum_tensor("ps%d" % i, [128, 512], F32).ap() for i in range(7)]
    psb = nc.alloc_psum_tensor("psb", [128, 1024], BF16).ap()
    PSK = ["ps%d" % i for i in range(7)]

    from contextlib import ExitStack

    uid = [0]

    def sb(es, name, shape, dt):
        uid[0] += 1
        return es.enter_context(nc.sbuf_tensor("%s_%d" % (name, uid[0]), list(shape), dt)).ap()

    pes = ExitStack()
    ident_f = sb(pes, "ident_f", [128, 128], F32)
    ident_b = sb(pes, "ident_b", [128, 128], BF16)
    ones_b = sb(pes, "ones_b", [128, 128], BF16)
    ones_f = sb(pes, "ones_f", [128, 128], F32)
    tri_f = sb(pes, "tri_f", [128, 128], F32)
    tri_b = sb(pes, "tri_b", [128, 128], BF16)
    padc = sb(pes, "padc", [128, 4], F32)
    corr = sb(pes, "corr", [128, 4, 16], F32)
    iota = sb(pes, "iota", [128, 128], F32)
    vecT = sb(pes, "vecT", [128, DEPTH, 112], F32)
    S.dma('sp', ident_f, c_ident, key='c', writes=['ident_f'])
    S.dma('sp', tri_f, c_tri, key='c', writes=['tri_f'])
    S.dma('sp', padc, c_pad, key='c', writes=['padc'])
    S.dma('sp', corr, c_corr, key='c', writes=['corr'])
    S.dma('sp', iota, c_iota, key='c', writes=['iota'])
    S.op('dve', lambda e: e.tensor_copy(ident_b, ident_f), reads=['ident_f'], writes=['ident_b'])
    S.op('dve', lambda e: e.tensor_copy(tri_b, tri_f), reads=['tri_f'], writes=['tri_b'])
    S.op('dve', lambda e: e.memset(ones_b, 1.0), writes=['ones_b'])
    S.op('dve', lambda e: e.memset(ones_f, 1.0), writes=['ones_f'])
    with ExitStack() as es:
        vs = sb(es, "vs", [112, 128], F32)
        for l in range(DEPTH):
            r = 0
            for nm, nrow in (("g_pre_mix", 16), ("g_post_mix", 16), ("g_pre_ffn", 16), ("g_post_ffn", 16),
                             ("pool_scale", 4), ("ssm_d", 4), ("ssm_glu_b", 8)):
                src = P[nm][l]
                if nm == "ssm_d":
                    src = src.rearrange("g h -> (g h)")
                S.dma('sp', vs[r:r + nrow, :], src.rearrange("(r c) -> r c", c=128), key='c', writes=['vs'])
                r += nrow
            S.dma('sp', vs[80:112, :], P["ml_conv_w"][l].rearrange("j (t c) -> (j t) c", c=128), key='c', writes=['vs'])
            S.op('pe', lambda e: e.transpose(ps[0][:, 0:112], vs, ident_f[0:112, 0:112]), reads=['vs', 'ident_f'], writes=['ps0'])
            S.op('dve', lambda e, l=l: e.tensor_copy(vecT[:, l, :], ps[0][:, 0:112]), reads=['ps0'], writes=['vecT'])
        def cast_panel(dst, src_rows, c0, nk, ncol=512):
            sv = src_rows.rearrange("(kt p) c -> p kt c", p=128)
            for k0 in range(0, nk, 4):
                k1 = min(nk, k0 + 4)
                S.dma('pool', dst[:, k0:k1, :], sv[:, k0:k1, c0:c0 + ncol], key='cast')
        def cast_win(l):
            for i, (_, c0) in enumerate(SEGS):
                cast_panel(winb[l, i], P["w_in"][l], c0, KT)
            sv = P["w_in"][l].rearrange("(kt p) c -> p kt c", p=128)
            S.dma('pool', wgb[l][:, :, 0:8], sv[:, :, 2560:2568], key='cast')
            S.dma('pool', wgb[l][:, :, 8:12], sv[:, :, 4104:4108], key='cast')

        def cast_rest(l):
            for i in range(4):
                cast_panel(woutb[l, i], P["w_out"][l], i * 512, KT)
            for i in range(16):
                cast_panel(w1b[l, i], P["mlp_w1"][l], i * 512, KT)
            for je in range(4):
                for kq in range(4):
                    cast_panel(w2b[l, je * 4 + kq], P["mlp_w2"][l][kq * 2048:(kq + 1) * 2048, :], je * 512, KT)
            for i in range(2):
                cast_panel(glub[l, i], P["ssm_glu_w"][l], i * 512, 4)
            S.dma('pool', poolwb[l], P["pool_w"][l].rearrange("g c d -> c g d"), key='cast')

        cast_win(0)
        xin = [sb(es, "xin%d" % i, [128, D], F32) for i in range(2)]
        hst = [sb(es, "hst%d" % i, [128, KT, 128], F32) for i in range(2)]
        for c in range(NCH):
            xi, hs = xin[c % 2], hst[c % 2]
            kx, kh = "xin%d" % (c % 2), "hst%d" % (c % 2)
            if c == 0:
                S.op('pool', lambda e, xi=xi: e.memset(xi, 0.0), writes=[kx])
                S.dma('sp', xi[112:128, :], meta, key='xin', reads=[kx], writes=[kx])
            else:
                S.dma('sp', xi, x[(c - 1) * 128:c * 128, :], key='xin', writes=[kx])
            for q in range(4):
                pk = ps[q % 2]
                for j in range(4):
                    kt = q * 4 + j
                    S.op('pe', lambda e, pk=pk, j=j, kt=kt, xi=xi: e.transpose(pk[:, j * 128:(j + 1) * 128], xi[:, kt * 128:(kt + 1) * 128], ident_f),
                         reads=[kx, 'ident_f'], writes=[PSK[q % 2]])
                S.op('act' if q % 2 else 'dve',
                     (lambda e, pk=pk, q=q, hs=hs: e.activation(hs[:, q * 4:(q + 1) * 4, :], pk.rearrange("p (j t) -> p j t", j=4), AF.Identity)) if q % 2 else
                     (lambda e, pk=pk, q=q, hs=hs: e.tensor_copy(hs[:, q * 4:(q + 1) * 4, :], pk.rearrange("p (j t) -> p j t", j=4))),
                     reads=[PSK[q % 2]], writes=[kh])
            S.dma('sp', hT[:, :, c * 128:(c + 1) * 128].rearrange("k p t -> p k t"), hs, key='hst', reads=[kh], writes=['hT'])
        S.flush()
    if stop_after == "p0":
        return nc
    def evac(i, out_ap, in_ap, reads, writes):
        if i % 2:
            S.op('act', lambda e: e.activation(out_ap, in_ap, AF.Identity), reads=reads, writes=writes)
        else:
            S.op('dve', lambda e: e.tensor_copy(out_ap, in_ap), reads=reads, writes=writes)

    def sumsq_steps(src, ksrc, n, sq, rstd, dim, krstd='rstd', sq_eng='act', ksq='sq'):
        kf = ksrc if callable(ksrc) else (lambda kt: ksrc)
        steps = []
        for kt in range(KT):
            def st(kt=kt):
                sl = sq[kt % 2]
                if sq_eng == 'act':
                    S.op('act', lambda e: e.activation(sl[:, :n], src[:, kt, :n], AF.Square), reads=[kf(kt)], writes=['%s%d' % (ksq, kt % 2)])
                else:
                    S.op(sq_eng, lambda e: e.tensor_tensor(sl[:, :n], src[:, kt, :n], src[:, kt, :n], ALU.mult), reads=[kf(kt)], writes=['%s%d' % (ksq, kt % 2)])
                S.op('pe', lambda e: e.matmul(ps[6][:, :n], ones_b, sl[:, :n], start=(kt == 0), stop=(kt == KT - 1)),
                     reads=['%s%d' % (ksq, kt % 2), 'ones_b'], writes=['ps6'])
            steps.append(st)

        def fin():
            S.op('act', lambda e: e.activation(rstd[:, :n], ps[6][:, :n], AF.Sqrt, bias=EPS, scale=1.0 / dim), reads=['ps6'], writes=[krstd])
            S.op('dve', lambda e: e.reciprocal(rstd[:, :n], rstd[:, :n]), reads=[krstd], writes=[krstd])
        steps.append(fin)
        return steps

    def sumsq_rstd(src, ksrc, n, sq, rstd, dim):
        for st in sumsq_steps(src, ksrc, n, sq, rstd, dim):
            st()

    class PanelStream:
        def __init__(self, es, srcs, nb=3, name="wp"):
            self.srcs = srcs
            self.nb = nb
            self.bufs = [sb(es, "%s%d" % (name, i), [128, KT, 512], BF16) for i in range(nb)]
            self.keys = ["%s%d" % (name, i) for i in range(nb)]
            self.issued = 0
            self.used = 0

        def _issue(self):
            i = self.issued
            if i < len(self.srcs):
                S.dma('sp', self.bufs[i % self.nb], self.srcs[i], key=self.keys[i % self.nb], writes=[self.keys[i % self.nb]])
                self.issued += 1

        def next(self):
            while self.issued < min(len(self.srcs), self.used + self.nb - 0) and self.issued - self.used < self.nb:
                self._issue()
            i = self.used
            self.used += 1
            return self.bufs[i % self.nb], self.keys[i % self.nb]

        def prefetch(self):
            while self.issued < len(self.srcs) and self.issued - self.used < self.nb - 1:
                self._issue()

    def phase_A(l):
        with ExitStack() as es:
            if l == 0:
                cast_rest(0)
                for l2 in range(1, DEPTH):
                    cast_win(l2)
                    cast_rest(l2)
            hblk2 = [sb(es, "hblk%d" % i, [128, KT, 512], F32) for i in range(2)]
            xn2 = [sb(es, "xn%d" % i, [128, KT, 512], BF16) for i in range(2)]
            sq = [sb(es, "sq%d" % i, [128, 512], BF16) for i in range(2)]
            rstd2 = [sb(es, "rstd%d" % i, [128, 512], F32) for i in range(2)]
            wps = PanelStream(es, [winb[l, i] for _ in blocks for i in range(len(SEGS))], nb=3)
            wg = sb(es, "wg", [128, KT, 12], BF16)
            gbt = sb(es, "gbt", [128, 12], F32)
            ost = [sb(es, "ost%d" % i, [128, 4, 512], BF16) for i in range(2)]
            tst = [sb(es, "tst%d" % i, [128, 512], BF16) for i in range(2)]
            gst = [sb(es, "gst%d" % i, [128, 12], F32) for i in range(2)]
            S.dma('sp', wg, wgb[l], key='wg', writes=['wg'])
            S.dma('sp', gbt[:, 0:8], P["ml_gate_bias"][l:l + 1, :].partition_broadcast(128), key='wg', writes=['gbt'])
            S.dma('sp', gbt[:, 8:12], P["fx_gate_bias"][l:l + 1, :].partition_broadcast(128), key='wg', writes=['gbt'])
            fm_dest = {"s_u": (uT, 0), "m_q": (mqkT, 0), "m_k": (mqkT, 4), "f_q": (fqT, 0), "f_k": (fkT, 0), "p_u": (puT, 0)}
            tm_dest = {"m_v": mv, "m_o": mo, "f_v": fv}
            wcnt = 0
            ocnt = 0
            tcnt = 0
            gcnt = 0

            def norm_steps(bi, t0, n):
                s_ = bi % 2
                hb, xb_, rs_ = hblk2[s_], xn2[s_], rstd2[s_]
                kh, kx, kr = 'hblk%d' % s_, 'xn%d' % s_, 'rstd%d' % s_

                def ld():
                    for kt in range(KT):
                        S.dma('sp', hb[:, kt, :n], hT[kt, :, t0:t0 + n], key=kh, writes=[kh])
                steps = [ld] + sumsq_steps(hb, kh, n, sq, rs_, D, krstd=kr, sq_eng='dve')
                for kt in range(KT):
                    def st(kt=kt):
                        S.op('dve', lambda e: e.scalar_tensor_tensor(xb_[:, kt, :n], hb[:, kt, :n], vecT[:, l, kt:kt + 1], rs_[:, :n], ALU.mult, ALU.mult),
                             reads=[kh, kr, 'vecT'], writes=[kx])
                    steps.append(st)
                return steps

            def blk(bi, t0, n, bg):
                nonlocal wcnt, ocnt, tcnt, gcnt
                xn = xn2[bi % 2]
                kxn = 'xn%d' % (bi % 2)
                if bg:
                    bg[0]()
                    bg = bg[1:]
                per = (len(bg) + len(SEGS) - 1) // len(SEGS) if bg else 0
                for i, (nm, c0) in enumerate(SEGS):
                    w, kw = wps.next()
                    wps.prefetch()
                    if nm in fm_dest:
                        dst, toff = fm_dest[nm]
                        o = ost[ocnt % 2]
                        ko = 'ost%d' % (ocnt % 2)
                        ocnt += 1
                        for m in range(4):
                            for kt in range(KT):
                                S.op('pe', lambda e, w=w, m=m, kt=kt: e.matmul(ps[m][:, :n], w[:, kt, m * 128:(m + 1) * 128], xn[:, kt, :n], start=(kt == 0), stop=(kt == KT - 1)),
                                     reads=[kw, kxn], writes=[PSK[m]])
                            evac(1, o[:, m, :n], ps[m][:, :n], [PSK[m]], [ko])
                        for m in range(4):
                            S.dma('act', dst[toff + m, :, t0:t0 + n], o[:, m, :n], key=ko, reads=[ko])
                    else:
                        dst = tm_dest[nm]
                        for i2 in range(n // 128):
                            pk = 4 + (i2 % 2)
                            ts_ = tst[tcnt % 2]
                            kts = 'tst%d' % (tcnt % 2)
                            tcnt += 1
                            for kt in range(KT):
                                S.op('pe', lambda e, w=w, kt=kt, i2=i2, pk=pk: e.matmul(ps[pk], xn[:, kt, i2 * 128:(i2 + 1) * 128], w[:, kt, :], start=(kt == 0), stop=(kt == KT - 1)),
                                     reads=[kw, kxn], writes=[PSK[pk]])
                            evac(1, ts_, ps[pk], [PSK[pk]], [kts])
                            S.dma('act', dst[t0 + i2 * 128:t0 + (i2 + 1) * 128, :], ts_, key=kts, reads=[kts])
                    for st_ in bg[i * per:(i + 1) * per]:
                        st_()
                for st_ in bg[len(SEGS) * per:]:
                    st_()
                for i2 in range(n // 128):
                    g_ = gst[gcnt % 2]
                    kg = 'gst%d' % (gcnt % 2)
                    gcnt += 1
                    for kt in range(KT):
                        S.op('pe', lambda e, kt=kt, i2=i2: e.matmul(ps[6][:, 0:12], xn[:, kt, i2 * 128:(i2 + 1) * 128], wg[:, kt, :], start=(kt == 0), stop=(kt == KT - 1)),
                             reads=['wg', kxn], writes=['ps6'])
                    S.op('dve', lambda e, g_=g_: e.tensor_tensor(g_, ps[6][:, 0:12], gbt, ALU.add), reads=['ps6', 'gbt'], writes=[kg])
                    S.dma('act', gates[t0 + i2 * 128:t0 + (i2 + 1) * 128, :], g_, key=kg, reads=[kg])

            pend = norm_steps(0, *blocks[0])
            for st_ in pend:
                st_()
            for bi, (t0_, n_) in enumerate(blocks):
                bg = norm_steps(bi + 1, *blocks[bi + 1]) if bi + 1 < len(blocks) else []
                blk(bi, t0_, n_, bg)
            S.flush()

    def phase_C(l):
        with ExitStack() as es:
            hblk = sb(es, "hblk", [128, KT, 512], F32)
            mix = sb(es, "mix", [128, KT, 512], F32)
            hn = sb(es, "hn", [128, KT, 512], BF16)
            big = sb(es, "big", [128, 32, 512], BF16)
            sq = [sb(es, "sq%d" % i, [128, 512], BF16) for i in range(2)]
            rstd = sb(es, "rstd", [128, 512], F32)
            rl = [sb(es, "rl%d" % i, [128, 512], F32) for i in range(2)]
            plist = []
            for _ in blocks:
                plist += [woutb[l, i] for i in range(4)]
                for half in range(2):
                    plist += [w1b[l, 8 * half + j] for j in range(8)]
                    plist += [w2b[l, je * 4 + kq] for je in range(4) for kq in (2 * half, 2 * half + 1)]
            wps = PanelStream(es, plist, nb=3)
            wcnt = 0
            rcnt = 0

            def post_norm_add(n, gcol):
                sumsq_rstd(mix, lambda kt: 'mx%d' % kt, n, sq, rstd, D)
                for kt in range(KT):
                    S.op('dve', lambda e, kt=kt: e.scalar_tensor_tensor(mix[:, kt, :n], mix[:, kt, :n], vecT[:, l, gcol + kt:gcol + kt + 1], rstd[:, :n], ALU.mult, ALU.mult),
                         reads=['mx%d' % kt, 'rstd', 'vecT'], writes=['mx%d' % kt])
                    S.op('pool' if kt % 3 else 'dve', lambda e, kt=kt: e.tensor_tensor(hblk[:, kt, :n], hblk[:, kt, :n], mix[:, kt, :n], ALU.add), reads=['mx%d' % kt, 'hb%d' % kt], writes=['hb%d' % kt])

            def load_y(t0, n):
                for kt in range(KT):
                    S.dma('sp', big[:, kt, :n], yT[kt, :, t0:t0 + n], key='ybl', writes=['big'])

            def blk(bi, t0, n):
                nonlocal wcnt, rcnt
                if bi == 0:
                    load_y(t0, n)
                for kt in range(KT):
                    S.dma('sp', hblk[:, kt, :n], hT[kt, :, t0:t0 + n], key='hblk', writes=['hb%d' % kt])
                for i in range(4):
                    w, kw = wps.next()
                    wps.prefetch()
                    for m in range(4):
                        for kt in range(KT):
                            S.op('pe', lambda e, w=w, m=m, kt=kt: e.matmul(ps[m][:, :n], w[:, kt, m * 128:(m + 1) * 128], big[:, kt, :n], start=(kt == 0), stop=(kt == KT - 1)),
                                 reads=[kw, 'big'], writes=[PSK[m]])
                        evac(m, mix[:, i * 4 + m, :n], ps[m][:, :n], [PSK[m]], ['mx%d' % (i * 4 + m)])
                post_norm_add(n, 16)
                if t0 == 0:
                    S.op('pool', lambda e: e.memset(hblk[:, :, 0:112], 0.0), writes=['hb%d' % k_ for k_ in range(KT)])
                sumsq_rstd(hblk, lambda kt: 'hb%d' % kt, n, sq, rstd, D)
                for kt in range(KT):
                    S.op('dve', lambda e, kt=kt: e.scalar_tensor_tensor(hn[:, kt, :n], hblk[:, kt, :n], vecT[:, l, 32 + kt:33 + kt], rstd[:, :n], ALU.mult, ALU.mult),
                         reads=['hb%d' % kt, 'rstd', 'vecT'], writes=['hn'])
                for half in range(2):
                    for jp in range(8):
                        w, kw = wps.next()
                        wps.prefetch()
                        for m in range(4):
                            for kt in range(KT):
                                S.op('pe', lambda e, w=w, m=m, kt=kt: e.matmul(ps[m][:, :n], w[:, kt, m * 128:(m + 1) * 128], hn[:, kt, :n], start=(kt == 0), stop=(kt == KT - 1)),
                                     reads=[kw, 'hn'], writes=[PSK[m]])
                            r_ = rl[rcnt % 2]
                            kr = 'rl%d' % (rcnt % 2)
                            rcnt += 1
                            S.op('act', lambda e, r_=r_, m=m: e.activation(r_[:, :n], ps[m][:, :n], AF.Relu), reads=[PSK[m]], writes=[kr])
                            S.op('dve', lambda e, r_=r_, jp=jp, m=m: e.tensor_tensor(big[:, jp * 4 + m, :n], r_[:, :n], r_[:, :n], ALU.mult), reads=[kr], writes=['big'])
                    for je in range(4):
                        for kq in range(2):
                            w, kw = wps.next()
                            wps.prefetch()
                            for m in range(4):
                                for kt in range(KT):
                                    S.op('pe', lambda e, w=w, m=m, kt=kt, kq=kq: e.matmul(ps[m][:, :n], w[:, kt, m * 128:(m + 1) * 128], big[:, kq * 16 + kt, :n],
                                                                                        start=(kq == 0 and kt == 0), stop=(kq == 1 and kt == KT - 1)),
                                         reads=[kw, 'big'], writes=[PSK[m]])
                        for m in range(4):
                            if half == 0:
                                evac(m, mix[:, je * 4 + m, :n], ps[m][:, :n], [PSK[m]], ['mx%d' % (je * 4 + m)])
                            else:
                                S.op('dve', lambda e, je=je, m=m: e.tensor_tensor(mix[:, je * 4 + m, :n], ps[m][:, :n], mix[:, je * 4 + m, :n], ALU.add),
                                     reads=[PSK[m], 'mx%d' % (je * 4 + m)], writes=['mx%d' % (je * 4 + m)])
                if bi + 1 < len(blocks):
                    load_y(*blocks[bi + 1])
                post_norm_add(n, 48)
                for kt in range(KT):
                    S.dma('pool', hT[kt, :, t0:t0 + n], hblk[:, kt, :n], key='hst', reads=['hb%d' % kt], writes=['hT'])

            for bi_, (t0_, n_) in enumerate(blocks):
                blk(bi_, t0_, n_)
            S.flush()

    def phase_Z():
        with ExitStack() as es:
            hs = [sb(es, "hs%d" % i, [128, KT, 128], F32) for i in range(2)]
            xo = [sb(es, "xo%d" % i, [128, D], F32) for i in range(2)]
            for c in range(1, NCH):
                h_, x_ = hs[c % 2], xo[c % 2]
                kh, kx = 'hs%d' % (c % 2), 'xo%d' % (c % 2)
                S.dma('sp', h_, hT[:, :, c * 128:(c + 1) * 128].rearrange("k p t -> p k t"), key=kh, writes=[kh])
                for q in range(4):
                    for j in range(4):
                        kt = q * 4 + j
                        S.op('pe', lambda e, q=q, j=j, kt=kt, h_=h_: e.transpose(ps[q][:, j * 128:(j + 1) * 128], h_[:, kt, :], ident_f), reads=[kh, 'ident_f'], writes=[PSK[q]])
                    evac(q, x_[:, q * 512:(q + 1) * 512], ps[q], [PSK[q]], [kx])
                S.dma('sp', out[(c - 1) * 128:c * 128, :], x_, key=kx, reads=[kx])
            S.flush()
    def phase_pool(l):
        with ExitStack() as es:
            ub = [sb(es, "ub%d" % i, [128, 16 + LP], F32) for i in range(3)]
            uin = sb(es, "uin", [128, LP], BF16)
            pooled = sb(es, "pooled", [128, LP], BF16)
            t16 = sb(es, "t16", [128, 16], F32)
            pw = sb(es, "pw", [128, 4, 128], BF16)
            ost = [sb(es, "post%d" % i, [128, 512], BF16) for i in range(2)]
            S.dma('sp', pw, poolwb[l], key='pw', writes=['pw'])
            for i in range(3):
                S.op('pool', lambda e, i=i: e.memset(ub[i][:, 0:16], 0.0), writes=['ub%d' % i])
            oc = 0
            for g, w in enumerate((2, 4, 8, 16)):
                S.dma('sp', uin, puT[g], key='uin', writes=['uin'])
                S.op('act', lambda e: e.activation(ub[0][:, 16:], uin, AF.Identity), reads=['uin'], writes=['ub0'])
                a, ka = ub[0], 'ub0'
                for s_ in range(g + 1):
                    m = 2 ** s_
                    d_, kd = ub[1 + s_ % 2], 'ub%d' % (1 + s_ % 2)
                    S.op('dve', lambda e, a=a, d_=d_, m=m: e.tensor_tensor(d_[:, 16:], a[:, 16:], a[:, 16 - m:16 - m + LP], ALU.add), reads=[ka], writes=[kd])
                    a, ka = d_, kd
                S.op('dve', lambda e, a=a, w=w: e.scalar_tensor_tensor(pooled, a[:, 16:], 1.0 / w, ub[0][:, 16:], ALU.mult, ALU.subtract), reads=[ka, 'ub0'], writes=['pooled'])
                S.op('dve', lambda e, a=a, g=g: e.tensor_tensor(t16, a[:, 16 + 112:16 + 128], corr[:, g, :], ALU.mult), reads=[ka, 'corr'], writes=['t16'])
                S.op('dve', lambda e, w=w: e.scalar_tensor_tensor(pooled[:, 112:128], t16, 1.0 / w, ub[0][:, 16 + 112:16 + 128], ALU.mult, ALU.subtract), reads=['t16', 'ub0'], writes=['pooled'])
                for bi, (t0, n) in enumerate(blocks):
                    pk = bi % 2
                    o, ko = ost[oc % 2], 'post%d' % (oc % 2)
                    oc += 1
                    S.op('pe', lambda e, g=g, t0=t0, n=n, pk=pk: e.matmul(ps[pk][:, :n], pw[:, g, :], pooled[:, t0:t0 + n], start=True, stop=True), reads=['pw', 'pooled'], writes=[PSK[pk]])
                    S.op('act', lambda e, o=o, n=n, pk=pk, g=g: e.activation(o[:, :n], ps[pk][:, :n], AF.Identity, scale=vecT[:, l, 64 + g:65 + g]), reads=[PSK[pk], 'vecT'], writes=[ko])
                    S.dma('sp', yT[12 + g, :, t0:t0 + n], o[:, :n], key=ko, reads=[ko])
            S.flush()

    def phase_fox(l):
        SC = float(128 ** -0.5)
        with ExitStack() as es:
            gts = sb(es, "gts", [128, NCH, 12], F32)
            e4 = sb(es, "e4", [128, 4, NCH], F32)
            lfh = sb(es, "lfh", [128, 4, NCH], F32)
            Fin = sb(es, "Fin", [128, 4, NCH], F32)
            tot = sb(es, "tot", [128, 4, NCH], F32)
            Fend = sb(es, "Fend", [128, 4, NCH], F32)
            Fg = sb(es, "Fg", [128, 4, NCH], F32)
            bias = sb(es, "bias", [128, NCH], F32)
            qTt = sb(es, "qTt", [128, LP], BF16)
            kTt = sb(es, "kTt", [128, LP], BF16)
            vtok = sb(es, "vtok", [128, NCH, 128], BF16)
            pT = [sb(es, "pT%d" % i, [128, 512], BF16) for i in range(2)]
            rden = sb(es, "rden", [128, 512], F32)
            ost = [sb(es, "fost%d" % i, [128, 512], BF16) for i in range(2)]
            for c0_ in range(0, NCH, 8):
                c1_ = min(NCH, c0_ + 8)
                S.dma('sp', gts[:, c0_:c1_, :], gates[c0_ * 128:c1_ * 128, :].rearrange("(c p) g -> p c g", p=128), key='gts', writes=['gts'])
            for h in range(4):
                S.op('act', lambda e, h=h: e.activation(e4[:, h, :], gts[:, :, 8 + h], AF.Exp, scale=-1.0), reads=['gts'], writes=['e4'])
            S.op('act', lambda e: e.activation(lfh, e4, AF.Ln, bias=1.0), reads=['e4'], writes=['lfh'])
            S.op('dve', lambda e: e.tensor_scalar(lfh, lfh, -1.0, None, ALU.mult), reads=['lfh'], writes=['lfh'])
            S.op('dve', lambda e: e.tensor_scalar(lfh[:, :, 0], lfh[:, :, 0], padc[:, 0:1], None, ALU.mult), reads=['lfh', 'padc'], writes=['lfh'])
            lff = lfh.rearrange("p h c -> p (h c)")
            S.op('pe', lambda e: e.matmul(ps[0][:, 0:4 * NCH], tri_f, lff, start=True, stop=True), reads=['tri_f', 'lfh'], writes=['ps0'])
            S.op('pe', lambda e: e.matmul(ps[1][:, 0:4 * NCH], ones_f, lff, start=True, stop=True), reads=['ones_f', 'lfh'], writes=['ps1'])
            S.op('dve', lambda e: e.tensor_copy(Fin.rearrange("p h c -> p (h c)"), ps[0][:, 0:4 * NCH]), reads=['ps0'], writes=['Fin'])
            S.op('dve', lambda e: e.tensor_copy(tot.rearrange("p h c -> p (h c)"), ps[1][:, 0:4 * NCH]), reads=['ps1'], writes=['tot'])
            for h in range(4):
                S.op('dve', lambda e, h=h: e.tensor_tensor_scan(Fend[:, h, :], ones_f[:, 0:NCH], tot[:, h, :], 0.0, ALU.mult, ALU.add), reads=['tot', 'ones_f'], writes=['Fend'])
            S.op('dve', lambda e: e.tensor_tensor(Fg, Fend, tot, ALU.subtract), reads=['Fend', 'tot'], writes=['Fg'])
            S.op('dve', lambda e: e.tensor_tensor(Fg, Fg, Fin, ALU.add), reads=['Fg', 'Fin'], writes=['Fg'])
            oc = 0
            pc = 0
            for h in range(4):
                S.dma('sp', qTt, fqT[h], key='fq', writes=['qTt'])
                S.dma('sp', kTt, fkT[h], key='fk', writes=['kTt'])
                for c0_ in range(0, NCH, 8):
                    c1_ = min(NCH, c0_ + 8)
                    S.dma('sp', vtok[:, c0_:c1_, :], fv[c0_ * 128:c1_ * 128, :].rearrange("(c p) (h d) -> p c h d", p=128, h=4)[:, :, h, :], key='fvv', writes=['vtok'])
                for qb, (t0, n) in enumerate(blocks):
                    c0 = t0 // 128
                    clast = c0 + n // 128 - 1
                    pO, pD = 2 + (qb % 2), 4 + (qb % 2)
                    S.op('dve', lambda e, h=h, clast=clast: e.tensor_scalar(bias[:, 0:clast + 1], Fg[:, h, 0:clast + 1], Fend[:, h, clast:clast + 1], -1.0, ALU.subtract, ALU.mult),
                         reads=['Fg', 'Fend'], writes=['bias'])
                    S.op('dve', lambda e: e.tensor_tensor(bias[:, 0:1], bias[:, 0:1], padc[:, 1:2], ALU.add), reads=['bias', 'padc'], writes=['bias'])
                    def emit_S(kc, pcv, t0=t0, n=n, c0=c0):
                        lo = max(0, kc - c0) * 128
                        pS = pcv % 2
                        S.op('pe', lambda e: e.matmul(ps[pS][:, lo:n], kTt[:, kc * 128:(kc + 1) * 128], qTt[:, t0 + lo:t0 + n], start=True, stop=True),
                             reads=['kTt', 'qTt'], writes=[PSK[pS]])

                    emit_S(0, pc)
                    for kc in range(clast + 1):
                        lo = max(0, kc - c0) * 128
                        pS = pc % 2
                        p_, kp = pT[pc % 2], 'pT%d' % (pc % 2)
                        pc += 1
                        S.op('act', lambda e, kc=kc, lo=lo, pS=pS, p_=p_, n=n: e.activation(p_[:, lo:n], ps[pS][:, lo:n], AF.Exp, bias=bias[:, kc:kc + 1], scale=SC),
                             reads=[PSK[pS], 'bias'], writes=[kp])
                        if kc >= c0:
                            S.op('pool', lambda e, p_=p_, lo=lo: e.tensor_tensor(p_[:, lo:lo + 128], p_[:, lo:lo + 128], tri_b, ALU.mult), reads=[kp, 'tri_b'], writes=[kp])
                        if kc + 1 <= clast:
                            emit_S(kc + 1, pc)
                        S.op('pe', lambda e, kc=kc, lo=lo, p_=p_, n=n, pO=pO, clast=clast: e.matmul(ps[pO][:, lo:n], vtok[:, kc, :], p_[:, lo:n], start=(kc == 0), stop=(kc == clast)),
                             reads=['vtok', kp], writes=[PSK[pO]])
                        S.op('pe', lambda e, kc=kc, lo=lo, p_=p_, n=n, pD=pD, clast=clast: e.matmul(ps[pD][:, lo:n], ones_b, p_[:, lo:n], start=(kc == 0), stop=(kc == clast)),
                             reads=['ones_b', kp], writes=[PSK[pD]])
                    o, ko = ost[oc % 2], 'fost%d' % (oc % 2)
                    oc += 1
                    S.op('dve', lambda e, n=n, pD=pD: e.reciprocal(rden[:, :n], ps[pD][:, :n]), reads=[PSK[pD]], writes=['rden'])
                    S.op('dve', lambda e, n=n, pO=pO, o=o: e.tensor_tensor(o[:, :n], ps[pO][:, :n], rden[:, :n], ALU.mult), reads=[PSK[pO], 'rden'], writes=[ko])
                    S.dma('act', yT[8 + h, :, t0:t0 + n], o[:, :n], key=ko, reads=[ko])
            S.flush()
    def phase_mlstm(l):
        SC = float(128 ** -0.5)
        with ExitStack() as es:
            qc = sb(es, "qc", [128, 4, LP], BF16)
            kc_ = sb(es, "kc_", [128, 4, LP], BF16)
            cin = sb(es, "cin", [128, 3 + LP], BF16)
            acc = sb(es, "acc", [128, LP], F32)
            mlg = sb(es, "mlg", [128, 512], F32)
            vaug = [sb(es, "vaug%d" % i, [128, 4, 129], BF16) for i in range(2)]
            ot = [sb(es, "ot%d" % i, [128, 512], BF16) for i in range(2)]
            gt = [sb(es, "gt%d" % i, [128, 12], F32) for i in range(2)]
            sg = sb(es, "sg", [128, 512], F32)
            sm = sb(es, "sm", [128, 40], F32)
            Cf = sb(es, "Cf", [128, 4, 129], F32)
            Cb = sb(es, "Cb", [128, 4, 129], BF16)
            pt = [sb(es, "pt%d" % i, [128, 128], BF16) for i in range(2)]
            ktk = [sb(es, "ktk%d" % i, [128, 128], BF16) for i in range(2)]
            vw = [sb(es, "vw%d" % i, [128, 129], BF16) for i in range(2)]
            hh = [sb(es, "hh%d" % i, [128, 128], F32) for i in range(2)]
            junk = sb(es, "junk", [128, 128], F32)
            hs_ = [sb(es, "hsm%d" % i, [128, 8], F32) for i in range(2)]
            yo = [sb(es, "yo%d" % i, [128, 128], BF16) for i in range(2)]
            yst = [sb(es, "yst%d" % i, [128, 4, 128], BF16) for i in range(2)]
            S.dma('sp', mlg, P["ml_norm_g"][l:l + 1, :].partition_broadcast(128), key='mlg', writes=['mlg'])
            S.op('pool', lambda e: e.memset(cin[:, 0:3], 0.0), writes=['cin'])
            S.op('pool', lambda e: e.memset(Cf, 0.0), writes=['Cf'])
            S.op('pool', lambda e: e.memset(Cb, 0.0), writes=['Cb'])
            for i in range(2):
                S.op('pool', lambda e, i=i: e.memset(vaug[i], 1.0), writes=['vaug%d' % i])
            for j in range(8):
                S.dma('sp', cin[:, 3:], mqkT[j], key='cin', reads=['cin'], writes=['cin'])
                wc = lambda tap, j=j: vecT[:, l, 80 + tap * 8 + j:81 + tap * 8 + j]
                S.op('dve', lambda e, wc=wc: e.tensor_scalar(acc, cin[:, 3:3 + LP], wc(3), None, ALU.mult), reads=['cin', 'vecT'], writes=['acc'])
                for tap in range(3):
                    S.op('dve', lambda e, wc=wc, tap=tap: e.scalar_tensor_tensor(acc, cin[:, tap:tap + LP], wc(tap), acc, ALU.mult, ALU.add), reads=['cin', 'acc', 'vecT'], writes=['acc'])
                if j < 4:
                    S.op('act', lambda e: e.activation(acc, acc, AF.Silu), reads=['acc'], writes=['acc'])
                    S.op('dve', lambda e, j=j: e.tensor_scalar(qc[:, j, :], acc, SC, None, ALU.mult), reads=['acc'], writes=['qc'])
                else:
                    S.op('act', lambda e, j=j: e.activation(kc_[:, j - 4, :], acc, AF.Silu), reads=['acc'], writes=['kc_'])
            for c in range(NCH):
                cs = slice(c * 128, (c + 1) * 128)
                va, kva = vaug[c % 2], 'vaug%d' % (c % 2)
                o_, ko = ot[c % 2], 'ot%d' % (c % 2)
                g_, kg = gt[c % 2], 'gt%d' % (c % 2)
                ys, kys = yst[c % 2], 'yst%d' % (c % 2)
                S.dma('sp', va[:, :, 0:128], mv[cs, :].rearrange("p (h d) -> p h d", h=4), key=kva, writes=[kva])
                S.dma('sp', o_, mo[cs, :], key=ko, writes=[ko])
                S.dma('sp', g_, gates[cs, :], key=kg, writes=[kg])
                S.op('act', lambda e, g_=g_: e.activation(sm[:, 0:4], g_[:, 4:8], AF.Exp, scale=-1.0), reads=[kg], writes=['sm_lf'])
                S.op('act', lambda e: e.activation(sm[:, 0:4], sm[:, 0:4], AF.Ln, bias=1.0), reads=['sm_lf'], writes=['sm_lf'])
                if c == 0:
                    S.op('dve', lambda e: e.tensor_scalar(sm[:, 0:4], sm[:, 0:4], -1.0, padc[:, 0:1], ALU.mult, ALU.mult), reads=['sm_lf', 'padc'], writes=['sm_lf'])
                else:
                    S.op('dve', lambda e: e.tensor_scalar(sm[:, 0:4], sm[:, 0:4], -1.0, None, ALU.mult), reads=['sm_lf'], writes=['sm_lf'])
                S.op('pe', lambda e: e.matmul(ps[6][:, 0:4], tri_f, sm[:, 0:4], start=True, stop=True), reads=['tri_f', 'sm_lf'], writes=['ps6'])
                S.op('pe', lambda e: e.matmul(ps[6][:, 4:8], ones_f, sm[:, 0:4], start=True, stop=True), reads=['ones_f', 'sm_lf'], writes=['ps6'])
                S.op('dve', lambda e, g_=g_: e.tensor_tensor(sm[:, 4:8], g_[:, 0:4], ps[6][:, 0:4], ALU.subtract), reads=[kg, 'ps6'], writes=['sm_w'])
                S.op('act', lambda e: e.activation(sm[:, 4:8], sm[:, 4:8], AF.Exp), reads=['sm_w'], writes=['sm_w'])
                if c == 0:
                    S.op('dve', lambda e: e.tensor_scalar(sm[:, 4:8], sm[:, 4:8], padc[:, 0:1], None, ALU.mult), reads=['sm_w', 'padc'], writes=['sm_w'])
                S.op('act', lambda e: e.activation(sm[:, 8:16], ps[6][:, 0:8], AF.Exp), reads=['ps6'], writes=['sm_e'])
                S.op('act', lambda e, o_=o_: e.activation(sg, o_, AF.Sigmoid), reads=[ko], writes=['sg'])
                for h in range(4):
                    i2 = h % 2
                    kch = kc_[:, h, cs]
                    qch = qc[:, h, cs]
                    p_, kp = pt[i2], 'pt%d' % i2
                    S.op('pe', lambda e, kch=kch, qch=qch, i2=i2: e.matmul(ps[i2][:, 0:128], kch, qch, start=True, stop=True), reads=['kc_', 'qc'], writes=[PSK[i2]])
                    S.op('dve', lambda e, h=h, i2=i2, p_=p_: e.scalar_tensor_tensor(p_, ps[i2][:, 0:128], sm[:, 4 + h:5 + h], tri_f, ALU.mult, ALU.mult),
                         reads=[PSK[i2], 'sm_w', 'tri_f'], writes=[kp])
                    S.op('pe', lambda e, kch=kch, i2=i2: e.transpose(psb[:, i2 * 128:(i2 + 1) * 128], kch, ident_b), reads=['kc_', 'ident_b'], writes=['psb%d' % i2])
                    S.op('act', lambda e, i2=i2: e.activation(ktk[i2], psb[:, i2 * 128:(i2 + 1) * 128], AF.Identity), reads=['psb%d' % i2], writes=['ktk%d' % i2])
                    S.op('pe', lambda e, p_=p_, va=va, h=h, i2=i2: e.matmul(ps[2 + i2][:, 0:129], p_, va[:, h, :], start=True, stop=False), reads=[kp, kva], writes=[PSK[2 + i2]])
                    S.op('pe', lambda e, qch=qch, h=h, i2=i2: e.matmul(ps[2 + i2][:, 0:129], qch, Cb[:, h, :], start=False, stop=True), reads=['qc', 'Cb'], writes=[PSK[2 + i2]])
                    hs = hs_[i2]
                    khs = 'hsm%d' % i2
                    nd = ps[2 + i2]
                    S.op('dve', lambda e, nd=nd, hs=hs, h=h: e.tensor_scalar(hs[:, 0:1], nd[:, 128:129], sm[:, 8 + h:9 + h], None, ALU.mult), reads=[PSK[2 + i2], 'sm_e'], writes=[khs])
                    S.op('dve', lambda e, hs=hs: e.scalar_tensor_tensor(hs[:, 1:2], hs[:, 0:1], -1.0, hs[:, 0:1], ALU.mult, ALU.max), reads=[khs], writes=[khs + 'b'])
                    S.op('dve', lambda e, hs=hs: e.tensor_scalar(hs[:, 2:3], hs[:, 1:2], 1.0, None, ALU.max), reads=[khs + 'b'], writes=[khs + 'c'])
                    S.op('dve', lambda e, hs=hs: e.reciprocal(hs[:, 3:4], hs[:, 2:3]), reads=[khs + 'c'], writes=[khs + 'd'])
                    S.op('dve', lambda e, hs=hs, h=h: e.tensor_tensor(hs[:, 4:5], hs[:, 3:4], sm[:, 8 + h:9 + h], ALU.mult), reads=[khs + 'd', 'sm_e'], writes=[khs + 'e'])
                    hh_, khh = hh[i2], 'hh%d' % i2
                    S.op('dve', lambda e, nd=nd, hs=hs, hh_=hh_: e.tensor_scalar(hh_, nd[:, 0:128], hs[:, 4:5], None, ALU.mult), reads=[PSK[2 + i2], khs + 'e'], writes=[khh])
                    S.op('act', lambda e, hh_=hh_: e.activation(junk, hh_, AF.Square), reads=[khh], writes=['junk'])
                    S.op('dve', lambda e, hs=hs: e.reduce_sum(hs[:, 5:6], junk, AX.X), reads=['junk'], writes=[khs + 'f'])
                    S.op('act', lambda e, hs=hs: e.activation(hs[:, 6:7], hs[:, 5:6], AF.Sqrt, bias=EPS, scale=1.0 / 128), reads=[khs + 'f'], writes=[khs + 'g'])
                    S.op('dve', lambda e, hs=hs: e.reciprocal(hs[:, 7:8], hs[:, 6:7]), reads=[khs + 'g'], writes=[khs + 'h'])
                    S.op('dve', lambda e, hs=hs, hh_=hh_, h=h: e.scalar_tensor_tensor(hh_, hh_, hs[:, 7:8], mlg[:, h * 128:(h + 1) * 128], ALU.mult, ALU.mult), reads=[khh, khs + 'h', 'mlg'], writes=[khh])
                    y_, ky = yo[i2], 'yo%d' % i2
                    S.op('pool', lambda e, hh_=hh_, y_=y_, h=h: e.tensor_tensor(y_, hh_, sg[:, h * 128:(h + 1) * 128], ALU.mult), reads=[khh, 'sg'], writes=[ky])
                    S.op('pe', lambda e, y_=y_, i2=i2: e.transpose(psb[:, 256 + i2 * 128:256 + (i2 + 1) * 128], y_, ident_b), reads=[ky, 'ident_b'], writes=['psby%d' % i2])
                    S.op('act', lambda e, ys=ys, h=h, i2=i2: e.activation(ys[:, h, :], psb[:, 256 + i2 * 128:256 + (i2 + 1) * 128], AF.Identity), reads=['psby%d' % i2], writes=[kys])
                    v_, kv_ = vw[i2], 'vw%d' % i2
                    S.op('pool', lambda e, v_=v_, va=va, h=h: e.tensor_scalar(v_, va[:, h, :], sm[:, 4 + h:5 + h], None, ALU.mult), reads=[kva, 'sm_w'], writes=[kv_])
                    S.op('pe', lambda e, v_=v_, i2=i2: e.matmul(ps[4 + i2][:, 0:129], ktk[i2], v_, start=True, stop=True), reads=['ktk%d' % i2, kv_], writes=[PSK[4 + i2]])
                    S.op('dve', lambda e, h=h: e.tensor_scalar(Cf[:, h, :], Cf[:, h, :], sm[:, 12 + h:13 + h], None, ALU.mult), reads=['Cf', 'sm_e'], writes=['Cf'])
                    S.op('dve', lambda e, h=h, i2=i2: e.scalar_tensor_tensor(Cf[:, h, :], ps[4 + i2][:, 0:129], sm[:, 12 + h:13 + h], Cf[:, h, :], ALU.mult, ALU.add),
                         reads=[PSK[4 + i2], 'sm_e', 'Cf'], writes=['Cf'])
                    S.op('act', lambda e, h=h: e.activation(Cb[:, h, :], Cf[:, h, :], AF.Identity), reads=['Cf'], writes=['Cb'])
                S.dma('act', yT[4:8, :, cs].rearrange("k p t -> p k t"), ys, key=kys, reads=[kys])
            S.flush()
    def phase_s5(l):
        with ExitStack() as es:
            lamr = sb(es, "lamr", [16, 128], F32)
            lami = sb(es, "lami", [16, 128], F32)
            ldt = sb(es, "ldt", [16, 2], F32)
            pm = sb(es, "pm", [16, 12, 128], F32)
            rr = sb(es, "rr", [128, 4, 2048], F32)
            ri = sb(es, "ri", [128, 2048], mybir.dt.int32)
            sm4 = sb(es, "sm4", [128, 4, 16], F32)
            cosT = sb(es, "cosT", [128, 16, 128], F32)
            sinT = sb(es, "sinT", [128, 16, 128], F32)
            Rtab = sb(es, "Rtab", [128, 16, 128], F32)
            cs128 = sb(es, "cs128", [128, 2, 16], F32)
            bnat = sb(es, "bnat", [128, 2, 16, 16], F32)
            bb = sb(es, "bb", [128, 4, 16, 16], F32)
            Xb = sb(es, "Xb", [128, 16, 2, 128], BF16)
            BT = sb(es, "BT", [128, 16, 2, 128], BF16)
            cnat = sb(es, "cnat", [128, 2, 4, 64], F32)
            Xc = sb(es, "Xc", [128, 4, 3, 128], BF16)
            CTm = sb(es, "CTm", [128, 4, 3, 128], BF16)
            CT = sb(es, "CT", [128, 4, 4, 3, 128], BF16)
            gw = sb(es, "gw", [128, 2, 4, 512], BF16)

            def sincos(ang, n, dsin, dcos, pp):
                for k_, (dst, off) in enumerate(((dsin, 0.0), (dcos, float(np.pi / 2)))):
                    a0, a1 = rr[:pp, 0, :n], rr[:pp, 1, :n]
                    S.op('dve', lambda e, a0=a0, off=off: e.tensor_scalar(a0, ang, off, None, ALU.add), reads=['ang'], writes=['rr0'])
                    S.op('dve', lambda e, a0=a0, a1=a1: e.tensor_scalar(a1, a0, 1.0 / TWO_PI, None, ALU.mult), reads=['rr0'], writes=['rr1'])
                    S.op('dve', lambda e, a1=a1: e.tensor_copy(ri[:pp, :n], a1), reads=['rr1'], writes=['ri'])
                    S.op('dve', lambda e, a1=a1: e.tensor_copy(a1, ri[:pp, :n]), reads=['ri'], writes=['rr1'])
                    S.op('dve', lambda e, a0=a0, a1=a1: e.scalar_tensor_tensor(a0, a1, -TWO_PI, a0, ALU.mult, ALU.add), reads=['rr0', 'rr1'], writes=['rr0'])
                    S.op('dve', lambda e, a0=a0, a1=a1: e.tensor_scalar(a1, a0, float(np.pi), -TWO_PI, ALU.is_gt, ALU.mult), reads=['rr0'], writes=['rr1'])
                    S.op('dve', lambda e, a0=a0, a1=a1: e.tensor_tensor(a0, a0, a1, ALU.add), reads=['rr0', 'rr1'], writes=['rr0'])
                    S.op('dve', lambda e, a0=a0, a1=a1: e.tensor_scalar(a1, a0, float(-np.pi), TWO_PI, ALU.is_lt, ALU.mult), reads=['rr0'], writes=['rr1'])
                    S.op('dve', lambda e, a0=a0, a1=a1: e.tensor_tensor(a0, a0, a1, ALU.add), reads=['rr0', 'rr1'], writes=['rr0'])
                    S.op('act', lambda e, a0=a0, dst=dst: e.activation(dst, a0, AF.Sin), reads=['rr0'], writes=['sc%d' % k_])

            S.dma('sp', lamr, P["ssm_lam_re"][l].rearrange("(j g) p -> j (g p)", g=2), key='s5c', writes=['lam'])
            S.dma('sp', lami, P["ssm_lam_im"][l].rearrange("(j g) p -> j (g p)", g=2), key='s5c', writes=['lam'])
            S.dma('sp', ldt, P["ssm_log_dt"][l].rearrange("(j g) -> j g", g=2), key='s5c', writes=['lam'])
            S.dma('sp', bnat[:, 0], P["ssm_b_re"][l].rearrange("(j g) p h -> (g p) j h", g=2), key='s5c', writes=['bnat'])
            S.dma('sp', bnat[:, 1], P["ssm_b_im"][l].rearrange("(j g) p h -> (g p) j h", g=2), key='s5c', writes=['bnat'])
            S.dma('sp', cnat[:, 0], P["ssm_c_re"][l].rearrange("(q g) h p -> (g h) q p", g=8), key='s5c', writes=['cnat'])
            S.dma('sp', cnat[:, 1], P["ssm_c_im"][l].rearrange("(q g) h p -> (g h) q p", g=8), key='s5c', writes=['cnat'])
            for i in range(2):
                S.dma('sp', gw[:, i], glub[l, i], key='s5c', writes=['gw'])
            S.op('act', lambda e: e.activation(ldt, ldt, AF.Exp), reads=['lam'], writes=['dt'])
            for g2 in range(2):
                sl = slice(g2 * 64, (g2 + 1) * 64)
                S.op('dve', lambda e, sl=sl, g2=g2: e.tensor_scalar(pm[:, 0, sl], lami[:, sl], ldt[:, g2:g2 + 1], None, ALU.mult), reads=['lam', 'dt'], writes=['ang'])
                S.op('dve', lambda e, sl=sl, g2=g2: e.tensor_scalar(pm[:, 1, sl], lamr[:, sl], ldt[:, g2:g2 + 1], None, ALU.mult), reads=['lam', 'dt'], writes=['pm1'])
            S.op('act', lambda e: e.activation(pm[:, 2, :], pm[:, 1, :], AF.Exp), reads=['pm1'], writes=['pm2'])
            ang = pm[:, 0, :]
            sincos(ang, 128, pm[:, 3, :], pm[:, 4, :], 16)
            TT = lambda o, a, b, op, r_, w_: S.op('dve', lambda e: e.tensor_tensor(o, a, b, op), reads=r_, writes=w_)
            pmk = ['pm%d' % i for i in range(12)]
            TT(pm[:, 5, :], pm[:, 2, :], pm[:, 4, :], ALU.mult, ['pm2', 'sc1'], ['pm5'])
            S.op('dve', lambda e: e.tensor_scalar(pm[:, 5, :], pm[:, 5, :], -1.0, None, ALU.add), reads=['pm5'], writes=['pm5'])
            TT(pm[:, 6, :], pm[:, 2, :], pm[:, 3, :], ALU.mult, ['pm2', 'sc0'], ['pm6'])
            TT(pm[:, 7, :], pm[:, 5, :], lamr, ALU.mult, ['pm5', 'lam'], ['pm7'])
            TT(pm[:, 9, :], pm[:, 6, :], lami, ALU.mult, ['pm6', 'lam'], ['pm9'])
            TT(pm[:, 7, :], pm[:, 7, :], pm[:, 9, :], ALU.add, ['pm7', 'pm9'], ['pm7'])
            TT(pm[:, 8, :], pm[:, 6, :], lamr, ALU.mult, ['pm6', 'lam'], ['pm8'])
            TT(pm[:, 9, :], pm[:, 5, :], lami, ALU.mult, ['pm5', 'lam'], ['pm9'])
            TT(pm[:, 8, :], pm[:, 8, :], pm[:, 9, :], ALU.subtract, ['pm8', 'pm9'], ['pm8'])
            TT(pm[:, 9, :], lamr, lamr, ALU.mult, ['lam'], ['pm9'])
            TT(pm[:, 10, :], lami, lami, ALU.mult, ['lam'], ['pm10'])
            TT(pm[:, 9, :], pm[:, 9, :], pm[:, 10, :], ALU.add, ['pm9', 'pm10'], ['pm9'])
            S.op('dve', lambda e: e.reciprocal(pm[:, 9, :], pm[:, 9, :]), reads=['pm9'], writes=['pm9'])
            TT(pm[:, 10, :], pm[:, 7, :], pm[:, 9, :], ALU.mult, ['pm7', 'pm9'], ['pm10'])
            TT(pm[:, 11, :], pm[:, 8, :], pm[:, 9, :], ALU.mult, ['pm8', 'pm9'], ['pm11'])
            for k_, row in enumerate((0, 2, 10, 11)):
                S.op('pe', lambda e, k_=k_, row=row: e.transpose(ps[0][:, k_ * 16:(k_ + 1) * 16], pm[:, row, :], ident_f[0:16, 0:16]),
                     reads=['ang', 'pm2', 'pm10', 'pm11', 'ident_f'], writes=['ps0'])
            S.op('dve', lambda e: e.tensor_copy(sm4.rearrange("p a b -> p (a b)"), ps[0][:, 0:64]), reads=['ps0'], writes=['sm4'])
            thT, rT, krT, kiT = sm4[:, 0, :], sm4[:, 1, :], sm4[:, 2, :], sm4[:, 3, :]
            angT = rr[:, 2, :]
            S.op('dve', lambda e: e.tensor_tensor(angT.rearrange("p (j t) -> p j t", j=16), iota.unsqueeze(1).to_broadcast([128, 16, 128]),
                                                  thT.unsqueeze(2).to_broadcast([128, 16, 128]), ALU.mult), reads=['iota', 'sm4'], writes=['ang'])
            ang = angT
            sincos(ang, 2048, sinT.rearrange("p j t -> p (j t)"), cosT.rearrange("p j t -> p (j t)"), 128)
            S.op('dve', lambda e: e.tensor_copy(sinT, sinT), reads=['sc0'], writes=['sinT'])
            S.op('dve', lambda e: e.tensor_copy(cosT, cosT), reads=['sc1'], writes=['cosT'])
            ang128 = rr[:, 3, 0:16]
            S.op('dve', lambda e: e.tensor_scalar(ang128, thT, 128.0, None, ALU.mult), reads=['sm4'], writes=['ang'])
            ang = ang128
            sincos(ang, 16, cs128[:, 1, :], cs128[:, 0, :], 128)
            S.op('dve', lambda e: e.tensor_copy(cs128, cs128), reads=['sc0', 'sc1'], writes=['cs128'])
            S.op('dve', lambda e: e.tensor_copy(Rtab, rT.unsqueeze(2).to_broadcast([128, 16, 128])), reads=['sm4'], writes=['Rtab'])
            S.op('dve', lambda e: e.memset(Rtab[:, :, 0:1], 0.0), reads=['Rtab'], writes=['Rtab'])
            krb = krT.unsqueeze(2).to_broadcast([128, 16, 16])
            kib = kiT.unsqueeze(2).to_broadcast([128, 16, 16])
            TT(bb[:, 0], bnat[:, 0], krb, ALU.mult, ['bnat', 'sm4'], ['bb0'])
            TT(bb[:, 1], bnat[:, 1], kib, ALU.mult, ['bnat', 'sm4'], ['bb1'])
            TT(bb[:, 0], bb[:, 0], bb[:, 1], ALU.subtract, ['bb0', 'bb1'], ['bb0'])
            TT(bb[:, 2], bnat[:, 1], krb, ALU.mult, ['bnat', 'sm4'], ['bb2'])
            TT(bb[:, 3], bnat[:, 0], kib, ALU.mult, ['bnat', 'sm4'], ['bb3'])
            TT(bb[:, 2], bb[:, 2], bb[:, 3], ALU.add, ['bb2', 'bb3'], ['bb2'])
            S.op('pool', lambda e: e.memset(Xb, 0.0), writes=['Xb'])
            for j in range(16):
                for g2 in range(2):
                    col0 = 16 * ((2 * j + g2) % 8)
                    rs_ = slice(g2 * 64, (g2 + 1) * 64)
                    for ri_, src in enumerate((0, 2)):
                        S.op('dve' if ri_ else 'pool', lambda e, j=j, rs_=rs_, col0=col0, ri_=ri_, src=src: e.tensor_copy(Xb[rs_, j, ri_, col0:col0 + 16], bb[rs_, src, j, :]),
                             reads=['bb0', 'bb2'], writes=['Xb'])
            for j0 in range(0, 16, 4):
                for jj in range(4):
                    for ri_ in range(2):
                        k_ = jj * 2 + ri_
                        S.op('pe', lambda e, j0=j0, jj=jj, ri_=ri_, k_=k_: e.transpose(psb[:, k_ * 128:(k_ + 1) * 128], Xb[:, j0 + jj, ri_, :], ident_b), reads=['Xb', 'ident_b'], writes=['psb'])
                S.op('dve', lambda e, j0=j0: e.tensor_copy(BT[:, j0:j0 + 4].rearrange("p j r c -> p (j r c)"), psb), reads=['psb'], writes=['BT'])
            for v_, (src, sgn) in enumerate(((0, 1.0), (0, -1.0), (1, -1.0))):
                for g2 in range(2):
                    S.op('dve', lambda e, v_=v_, src=src, sgn=sgn, g2=g2: e.tensor_scalar(Xc[:, :, v_, g2 * 64:(g2 + 1) * 64], cnat[:, src], padc[:, 2 + g2:3 + g2], sgn, ALU.mult, ALU.mult),
                         reads=['cnat', 'padc'], writes=['Xc'])
            for q in range(4):
                for v_ in range(3):
                    k_ = q * 3 + v_
                    S.op('pe', lambda e, q=q, v_=v_, k_=k_: e.transpose(psb[:, (k_ % 8) * 128:(k_ % 8 + 1) * 128], Xc[:, q, v_, :], ident_b), reads=['Xc', 'ident_b'], writes=['psb'])
                    S.op('dve', lambda e, q=q, v_=v_, k_=k_: e.tensor_copy(CTm[:, q, v_, :], psb[:, (k_ % 8) * 128:(k_ % 8 + 1) * 128]), reads=['psb'], writes=['CTm'])
            S.op('pool', lambda e: e.memset(CT, 0.0), writes=['CT'])
            for jj in range(4):
                for q in range(4):
                    S.op('dve', lambda e, jj=jj, q=q: e.tensor_copy(CT[:, q, jj, :, 32 * jj:32 * jj + 32], CTm[:, q, :, 32 * jj:32 * jj + 32]), reads=['CTm', 'CT'], writes=['CT'])
            S.flush()
            uin = [sb(es, "s5u%d" % i, [128, 4, 128], BF16) for i in range(2)]
            tmp = [[sb(es, "s5t%d_%d" % (i, s_), [128, 512], F32) for i in range(4)] for s_ in range(2)]
            vre2 = [sb(es, "vre%d" % s_, [128, 512], F32) for s_ in range(2)]
            vim2 = [sb(es, "vim%d" % s_, [128, 512], F32) for s_ in range(2)]
            wre2 = [sb(es, "wre%d" % s_, [128, 512], F32) for s_ in range(2)]
            wim2 = [sb(es, "wim%d" % s_, [128, 512], F32) for s_ in range(2)]
            Ap2 = [[sb(es, "Ap%d_%d" % (i, s_), [128, 512], BF16) for i in range(4)] for s_ in range(2)]
            car = sb(es, "car", [128, 6, 16], F32)
            yg2 = [sb(es, "yg%d" % s_, [128, 128], F32) for s_ in range(2)]
            t22 = [sb(es, "t2_%d" % s_, [128, 128], F32) for s_ in range(2)]
            sgm2 = [sb(es, "sgm%d" % s_, [128, 128], F32) for s_ in range(2)]
            gl = [sb(es, "gl%d" % i, [128, 4, 128], BF16) for i in range(2)]
            sgt = sb(es, "sgt", [128, 4, 128], F32)
            yst = [sb(es, "s5y%d" % i, [128, 4, 128], BF16) for i in range(2)]
            S.op('pool', lambda e: e.memset(car, 0.0), writes=['car'])
            for c in range(NCH):
                cs = slice(c * 128, (c + 1) * 128)
                u_, ku = uin[c % 2], 's5u%d' % (c % 2)
                gl_, kgl = gl[c % 2], 'gl%d' % (c % 2)
                ys, kys = yst[c % 2], 's5y%d' % (c % 2)
                S.dma('sp', u_, uT[:, :, cs].rearrange("k p t -> p k t"), key=ku, writes=[ku])
                def Q(qd):
                    s_ = qd % 2
                    j4 = slice(4 * qd, 4 * qd + 4)
                    return dict(qd=qd, s=s_, j4=j4, pR=s_ * 2, pI=s_ * 2 + 1, pY=4 + s_,
                                cosq=cosT[:, j4, :].rearrange("p j t -> p (j t)"), sinq=sinT[:, j4, :].rearrange("p j t -> p (j t)"),
                                Rq=Rtab[:, j4, :].rearrange("p j t -> p (j t)"),
                                tmp=tmp[s_], vre=vre2[s_], vim=vim2[s_], wre=wre2[s_], wim=wim2[s_], Ap=Ap2[s_],
                                yg=yg2[s_], t2=t22[s_], sgm=sgm2[s_], cc=slice(4 * s_, 4 * s_ + 4),
                                u=u_, ku=ku, gl=gl_, kgl=kgl,
                                k=lambda nm, s_=s_: '%s_%d' % (nm, s_))

                def st_bu(q):
                    for jj in range(4):
                        j = 4 * q['qd'] + jj
                        S.op('pe', lambda e, j=j, jj=jj: e.matmul(ps[q['pR']][:, jj * 128:(jj + 1) * 128], BT[:, j, 0, :], q['u'][:, q['qd'], :], start=True, stop=True), reads=['BT', q['ku']], writes=[PSK[q['pR']]])
                        S.op('pe', lambda e, j=j, jj=jj: e.matmul(ps[q['pI']][:, jj * 128:(jj + 1) * 128], BT[:, j, 1, :], q['u'][:, q['qd'], :], start=True, stop=True), reads=['BT', q['ku']], writes=[PSK[q['pI']]])

                def st_rot(q):
                    k, t_ = q['k'], q['tmp']
                    S.op('dve', lambda e: e.tensor_tensor(t_[0], ps[q['pR']], q['cosq'], ALU.mult), reads=[PSK[q['pR']], 'cosT'], writes=[k('s5t0')])
                    S.op('dve', lambda e: e.tensor_tensor(t_[1], ps[q['pI']], q['sinq'], ALU.mult), reads=[PSK[q['pI']], 'sinT'], writes=[k('s5t1')])
                    S.op('dve', lambda e: e.tensor_tensor(t_[2], ps[q['pI']], q['cosq'], ALU.mult), reads=[PSK[q['pI']], 'cosT'], writes=[k('s5t2')])
                    S.op('dve', lambda e: e.tensor_tensor(t_[3], ps[q['pR']], q['sinq'], ALU.mult), reads=[PSK[q['pR']], 'sinT'], writes=[k('s5t3')])

                def st_add(q):
                    k, t_ = q['k'], q['tmp']
                    S.op('pool', lambda e: e.tensor_tensor(q['vre'], t_[0], t_[1], ALU.add), reads=[k('s5t0'), k('s5t1')], writes=[k('vre')])
                    S.op('pool', lambda e: e.tensor_tensor(q['vim'], t_[2], t_[3], ALU.subtract), reads=[k('s5t2'), k('s5t3')], writes=[k('vim')])

                def st_scan(q):
                    k, j4 = q['k'], q['j4']
                    v3r = q['vre'].rearrange("p (j t) -> p j t", j=4)
                    v3i = q['vim'].rearrange("p (j t) -> p j t", j=4)
                    S.op('dve', lambda e: e.tensor_tensor(v3r[:, :, 0], v3r[:, :, 0], car[:, 0, j4], ALU.add), reads=[k('vre'), 'car'], writes=[k('vre')])
                    S.op('dve', lambda e: e.tensor_tensor(v3i[:, :, 0], v3i[:, :, 0], car[:, 1, j4], ALU.add), reads=[k('vim'), 'car'], writes=[k('vim')])
                    S.op('dve', lambda e: e.tensor_tensor_scan(q['wre'], q['Rq'], q['vre'], 0.0, ALU.mult, ALU.add), reads=['Rtab', k('vre')], writes=[k('wre')])
                    S.op('dve', lambda e: e.tensor_tensor_scan(q['wim'], q['Rq'], q['vim'], 0.0, ALU.mult, ALU.add), reads=['Rtab', k('vim')], writes=[k('wim')])

                def st_carry(q):
                    k, j4, cc = q['k'], q['j4'], q['cc']
                    w3r = q['wre'].rearrange("p (j t) -> p j t", j=4)
                    w3i = q['wim'].rearrange("p (j t) -> p j t", j=4)
                    c128, s128 = cs128[:, 0, j4], cs128[:, 1, j4]
                    S.op('pool', lambda e: e.tensor_tensor(car[:, 2, cc], w3r[:, :, 127], c128, ALU.mult), reads=[k('wre'), 'cs128'], writes=[k('car2')])
                    S.op('pool', lambda e: e.tensor_tensor(car[:, 3, cc], w3i[:, :, 127], s128, ALU.mult), reads=[k('wim'), 'cs128'], writes=[k('car3')])
                    S.op('pool', lambda e: e.tensor_tensor(car[:, 4, cc], w3r[:, :, 127], s128, ALU.mult), reads=[k('wre'), 'cs128'], writes=[k('car4')])
                    S.op('pool', lambda e: e.tensor_tensor(car[:, 5, cc], w3i[:, :, 127], c128, ALU.mult), reads=[k('wim'), 'cs128'], writes=[k('car5')])
                    S.op('pool', lambda e: e.tensor_tensor(car[:, 2, cc], car[:, 2, cc], car[:, 3, cc], ALU.subtract), reads=[k('car2'), k('car3')], writes=[k('car2')])
                    S.op('pool', lambda e: e.tensor_tensor(car[:, 4, cc], car[:, 4, cc], car[:, 5, cc], ALU.add), reads=[k('car4'), k('car5')], writes=[k('car4')])
                    S.op('pool', lambda e: e.tensor_tensor(car[:, 0, j4], car[:, 2, cc], rT[:, j4], ALU.mult), reads=[k('car2'), 'sm4', 'car'], writes=['car'])
                    S.op('pool', lambda e: e.tensor_tensor(car[:, 1, j4], car[:, 4, cc], rT[:, j4], ALU.mult), reads=[k('car4'), 'sm4', 'car'], writes=['car'])

                def st_prod(q):
                    k, A_ = q['k'], q['Ap']
                    S.op('pool', lambda e: e.tensor_tensor(A_[0], q['wre'], q['cosq'], ALU.mult), reads=[k('wre'), 'cosT'], writes=[k('Ap0')])
                    S.op('pool', lambda e: e.tensor_tensor(A_[1], q['wim'], q['sinq'], ALU.mult), reads=[k('wim'), 'sinT'], writes=[k('Ap1')])
                    S.op('dve', lambda e: e.tensor_tensor(A_[2], q['wre'], q['sinq'], ALU.mult), reads=[k('wre'), 'sinT'], writes=[k('Ap2')])
                    S.op('dve', lambda e: e.tensor_tensor(A_[3], q['wim'], q['cosq'], ALU.mult), reads=[k('wim'), 'cosT'], writes=[k('Ap3')])

                def st_y(q):
                    k, A_ = q['k'], q['Ap']
                    n_mm = 0
                    for jj in range(4):
                        for a_, v_ in ((0, 0), (1, 1), (2, 2), (3, 2)):
                            S.op('pe', lambda e, jj=jj, a_=a_, v_=v_, n_mm=n_mm: e.matmul(ps[q['pY']][:, 0:128], CT[:, q['qd'], jj, v_, :], A_[a_][:, jj * 128:(jj + 1) * 128],
                                                                                       start=(n_mm == 0), stop=(n_mm == 15)), reads=['CT', k('Ap%d' % a_)], writes=[PSK[q['pY']]])
                            n_mm += 1

                def st_gelu1(q):
                    k, yg, t2_ = q['k'], q['yg'], q['t2']
                    qd = q['qd']
                    S.op('dve', lambda e: e.scalar_tensor_tensor(yg, q['u'][:, qd, :], vecT[:, l, 68 + qd:69 + qd], ps[q['pY']][:, 0:128], ALU.mult, ALU.add), reads=[q['ku'], 'vecT', PSK[q['pY']]], writes=[k('yg')])
                    S.op('dve', lambda e: e.tensor_tensor(t2_, yg, yg, ALU.mult), reads=[k('yg')], writes=[k('t2_')])
                    S.op('dve', lambda e: e.tensor_scalar(t2_, t2_, 0.044715, 1.0, ALU.mult, ALU.add), reads=[k('t2_')], writes=[k('t2_')])
                    S.op('dve', lambda e: e.tensor_tensor(t2_, t2_, yg, ALU.mult), reads=[k('t2_'), k('yg')], writes=[k('t2_')])
                    S.op('act', lambda e: e.activation(q['sgm'], t2_, AF.Sigmoid, scale=1.5957691216057308), reads=[k('t2_')], writes=[k('sgm')])

                def st_gelu2(q):
                    k = q['k']
                    S.op('dve', lambda e: e.tensor_tensor(q['gl'][:, q['qd'], :], q['yg'], q['sgm'], ALU.mult), reads=[k('yg'), k('sgm')], writes=[q['kgl']])

                for qa in (0, 2):
                    A_, B_ = Q(qa), Q(qa + 1)
                    for st in (st_bu, st_rot):
                        st(A_)
                        st(B_)
                    st_add(A_)
                    st_add(B_)
                    st_scan(A_)
                    st_scan(B_)
                    st_prod(A_)
                    st_prod(B_)
                    st_y(A_)
                    st_y(B_)
                    st_carry(A_)
                    st_carry(B_)
                    st_gelu1(A_)
                    st_gelu1(B_)
                    st_gelu2(A_)
                    st_gelu2(B_)
                for ot_ in range(8):
                    pz = 2 + ot_ // 4 if False else (0 if ot_ < 4 else 1)
                    for kt in range(4):
                        S.op('pe', lambda e, ot_=ot_, kt=kt, gl_=gl_, pz=pz: e.matmul(ps[pz][:, (ot_ % 4) * 128:(ot_ % 4 + 1) * 128], gw[:, ot_ // 4, kt, (ot_ % 4) * 128:(ot_ % 4 + 1) * 128], gl_[:, kt, :],
                                                                                       start=(kt == 0), stop=(kt == 3)), reads=['gw', kgl], writes=[PSK[pz]])
                for o4 in range(4):
                    S.op('act', lambda e, o4=o4: e.activation(sgt[:, o4, :], ps[1][:, o4 * 128:(o4 + 1) * 128], AF.Sigmoid, bias=vecT[:, l, 76 + o4:77 + o4]), reads=['ps1', 'vecT'], writes=['sgt'])
                    S.op('dve', lambda e, o4=o4, ys=ys: e.scalar_tensor_tensor(ys[:, o4, :], ps[0][:, o4 * 128:(o4 + 1) * 128], vecT[:, l, 72 + o4:73 + o4], sgt[:, o4, :], ALU.add, ALU.mult),
                         reads=['ps0', 'vecT', 'sgt'], writes=[kys])
                S.dma('act', yT[0:4, :, cs].rearrange("k p t -> p k t"), ys, key=kys, reads=[kys])
            S.flush()

    for l in range(DEPTH):
        phase_A(l)
        if stop_after == "A":
            return nc
        phase_s5(l)
        phase_mlstm(l)
        phase_fox(l)
        phase_pool(l)
        if stop_after == "mix":
            return nc
        phase_C(l)
        if stop_after == "C":
            return nc
    phase_Z()
    pes.close()
    return nc


def _in_maps(inputs, SEQ, DEPTH, n_cores):
    consts = host_consts()
    maps = []
    B = inputs["x"].shape[0]
    for c in range(n_cores):
        m = {"x": np.ascontiguousarray(inputs["x"][c % B], dtype=np.float32), "meta_tokens": np.asarray(inputs["meta_tokens"], np.float32)}
        for n, _ in PARAMS:
            m[n] = np.ascontiguousarray(inputs[n][:DEPTH], dtype=np.float32)
        m.update(consts)
        maps.append(m)
    return maps


def kernel(**inputs):
    SEQ, DEPTH = 4096, 2
    B = inputs["x"].shape[0]
    nc = build_program(SEQ, DEPTH)
    res = run_bass_kernel_spmd(nc, _in_maps(inputs, SEQ, DEPTH, 8), core_ids=list(range(8)))
    return np.stack([np.asarray(res.results[b]["out"], dtype=np.float32) for b in range(B)], axis=0)
```

```python
import numpy as np
import concourse.bass as bass
import concourse.mybir as mybir
from concourse.bass_utils import run_bass_kernel_spmd

F32 = mybir.dt.float32
BF16 = mybir.dt.bfloat16
AF = mybir.ActivationFunctionType
ALU = mybir.AluOpType
AX = mybir.AxisListType


class Sched:
    def __init__(self, nc):
        self.nc = nc
        self.engs = {'pe': nc.tensor, 'act': nc.scalar, 'dve': nc.vector, 'pool': nc.gpsimd, 'sp': nc.sync}
        self.ops = []
        self.last_w = {}
        self.readers = {}
        self.dma_cnt = {}
        self.esem = {e: nc.alloc_semaphore("s_" + e) for e in self.engs}
        self.dsem = {}
        self.ebase = {e: 0 for e in self.engs}
        self.dma_issued = {}

    def _deps(self, reads, writes):
        deps = set()
        for k in reads:
            if k in self.last_w:
                deps.add(self.last_w[k])
        for k in writes:
            if k in self.last_w:
                deps.add(self.last_w[k])
            deps.update(self.readers.get(k, ()))
        return deps

    def _record(self, oid, reads, writes):
        for k in writes:
            self.last_w[k] = oid
            self.readers[k] = []
        for k in reads:
            self.readers.setdefault(k, []).append(oid)

    @staticmethod
    def _bank(reads, writes):
        bk = lambda k: 'psb' if k.startswith('psb') else k
        w = [bk(k) for k in writes] + [bk(k) for k in reads if k.startswith('ps')]
        r = [k for k in reads if not k.startswith('ps')]
        return r, w

    def op(self, eng, fn, reads=(), writes=()):
        reads, writes = self._bank(list(reads), list(writes))
        deps = self._deps(reads, writes)
        oid = len(self.ops)
        self.ops.append(dict(eng=eng, fn=fn, deps=deps, dma=None))
        self._record(oid, reads, writes)
        return oid

    def dma(self, eng, out, in_, key, reads=(), writes=(), **kw):
        deps = self._deps(reads, writes)
        oid = len(self.ops)
        n = self.dma_cnt.get(key, 0) + 1
        self.dma_cnt[key] = n
        self.ops.append(dict(eng=eng, fn=lambda e: e.dma_start(out=out, in_=in_, **kw), deps=deps, dma=(key, n)))
        self._record(oid, reads, writes)
        return oid

    def flush(self):
        nc = self.nc
        ops = self.ops
        if not ops:
            return
        eng_list = {e: [] for e in self.engs}
        for oid, o in enumerate(ops):
            o['pos'] = len(eng_list[o['eng']])
            eng_list[o['eng']].append(oid)
        seen = {e: {} for e in self.engs}
        need_inc = set()
        for oid, o in enumerate(ops):
            e = o['eng']
            want = {}
            for d in o['deps']:
                do = ops[d]
                if do['dma'] is not None:
                    key, n = do['dma']
                    want[('d', key)] = max(want.get(('d', key), 0), n)
                else:
                    if do['eng'] == 'pe' and e == 'pe' and o['dma'] is None:
                        continue
                    want[('e', do['eng'])] = max(want.get(('e', do['eng']), -1), do['pos'])
            waits = []
            for k, v in want.items():
                if seen[e].get(k, -1) >= v:
                    continue
                seen[e][k] = v
                waits.append((k, v))
                if k[0] == 'e':
                    need_inc.add(eng_list[k[1]][v])
            o['waits'] = waits
        issued = self.dma_issued
        for oid, o in enumerate(ops):
            nw = []
            for k, v in o['waits']:
                if k[0] == 'd':
                    v = max(v, issued.get(k[1], 0))
                nw.append((k, v))
            o['waits'] = nw
            if o['dma'] is not None:
                issued[o['dma'][0]] = o['dma'][1]
        for k in self.dma_cnt:
            if k not in self.dsem:
                self.dsem[k] = nc.alloc_semaphore("d_%d" % len(self.dsem))
        inc_prefix = {}
        for e, lst in eng_list.items():
            c = self.ebase[e]
            for oid in lst:
                if oid in need_inc:
                    c += 1
                    inc_prefix[oid] = c
            self.ebase[e] = c
        finals = [(k, n) for k, n in self.dma_cnt.items()]

        def emit(e, eng):
            for oid in eng_list[e]:
                o = ops[oid]
                for k, v in o['waits']:
                    if k[0] == 'd':
                        eng.wait_ge(self.dsem[k[1]], 16 * v)
                    else:
                        eng.wait_ge(self.esem[k[1]], inc_prefix[eng_list[k[1]][v]])
                ins = o['fn'](eng)
                if o['dma'] is not None:
                    ins.then_inc(self.dsem[o['dma'][0]], 16)
                elif oid in need_inc:
                    ins.then_inc(self.esem[e], 1)
            if e == 'sp':
                for k, n in finals:
                    eng.wait_ge(self.dsem[k], 16 * n)

        with nc.Block() as block:
            @block.sync
            def _(eng):
                emit('sp', eng)

            @block.tensor
            def _(eng):
                emit('pe', eng)

            @block.scalar
            def _(eng):
                emit('act', eng)

            @block.vector
            def _(eng):
                emit('dve', eng)

            @block.gpsimd
            def _(eng):
                emit('pool', eng)
        nc.all_engine_barrier()
        self.ops = []
        self.last_w = {}
        self.readers = {}


D = 2048
KT = 16
INC = 4620
DFF = 8192
EPS = 1e-6
SEGS = [("s_u", 0), ("m_q", 512), ("m_k", 1024), ("m_v", 1536), ("m_o", 2048),
        ("f_q", 2568), ("f_k", 3080), ("f_v", 3592), ("p_u", 4108)]
PARAMS = [("g_pre_mix", [D]), ("g_post_mix", [D]), ("g_pre_ffn", [D]), ("g_post_ffn", [D]),
          ("w_in", [D, INC]), ("ml_gate_bias", [8]), ("fx_gate_bias", [4]),
          ("ssm_lam_re", [32, 64]), ("ssm_lam_im", [32, 64]), ("ssm_log_dt", [32]),
          ("ssm_b_re", [32, 64, 16]), ("ssm_b_im", [32, 64, 16]), ("ssm_c_re", [32, 16, 64]),
          ("ssm_c_im", [32, 16, 64]), ("ssm_d", [32, 16]), ("ssm_glu_w", [512, 1024]), ("ssm_glu_b", [1024]),
          ("ml_conv_w", [4, 1024]), ("ml_norm_g", [512]), ("pool_w", [4, 128, 128]), ("pool_scale", [512]),
          ("w_out", [D, D]), ("mlp_w1", [D, DFF]), ("mlp_w2", [DFF, D])]
TWO_PI = float(2.0 * np.pi)


def host_consts():
    tri = (np.arange(128)[:, None] <= np.arange(128)[None, :]).astype(np.float32)
    pad = np.zeros((128, 4), np.float32)
    pad[112:, 0] = 1.0
    pad[:112, 1] = -1e30
    pad[:, 2] = ((np.arange(128) // 16) % 2 == 0)
    pad[:, 3] = ((np.arange(128) // 16) % 2 == 1)
    corr = np.zeros((128, 4, 16), np.float32)
    for g, w in enumerate((2, 4, 8, 16)):
        corr[:, g, :] = w / np.minimum(np.arange(1, 17), w)
    iota = np.broadcast_to(np.arange(128, dtype=np.float32)[None, :], (128, 128)).copy()
    ident = np.eye(128, dtype=np.float32)
    return {"c_tri": tri, "c_pad": pad, "c_corr": corr, "c_iota": iota, "c_ident": ident}


def build_program(SEQ, DEPTH, debug=False, stop_after=None):
    nc = bass.Bass("TRN2", target_bir_lowering=False)
    LP = SEQ + 128
    NCH = LP // 128
    blocks = [(t0, min(512, LP - t0)) for t0 in range(0, LP, 512)]

    def din(name, shape):
        return nc.dram_tensor(name, list(shape), F32, kind="ExternalInput").ap()

    x = din("x", [SEQ, D])
    meta = din("meta_tokens", [16, D])
    P = {n: din(n, [DEPTH] + sh) for n, sh in PARAMS}
    c_tri = din("c_tri", [128, 128])
    c_pad = din("c_pad", [128, 4])
    c_corr = din("c_corr", [128, 4, 16])
    c_iota = din("c_iota", [128, 128])
    c_ident = din("c_ident", [128, 128])
    out = nc.dram_tensor("out", [SEQ, D], F32, kind="ExternalOutput").ap()
    dbg_kind = "ExternalOutput" if debug else "Internal"

    def scr(name, shape, dt, dbg=False):
        return nc.dram_tensor(name, list(shape), dt, kind=(dbg_kind if dbg else "Internal")).ap()

    hT = scr("hT", [KT, 128, LP], F32, True)
    uT = scr("uT", [4, 128, LP], BF16, True)
    mqkT = scr("mqkT", [8, 128, LP], BF16, True)
    mv = scr("mv", [LP, 512], BF16, True)
    mo = scr("mo", [LP, 512], BF16, True)
    gates = scr("gates", [LP, 12], F32, True)
    fqT = scr("fqT", [4, 128, LP], BF16, True)
    fkT = scr("fkT", [4, 128, LP], BF16, True)
    fv = scr("fv", [LP, 512], BF16, True)
    puT = scr("puT", [4, 128, LP], BF16, True)
    yT = scr("yT", [KT, 128, LP], BF16, True)
    s5scr = scr("s5scr", [8, 2048], F32)
    winb = scr("winb", [DEPTH, 9, 128, KT, 512], BF16)
    wgb = scr("wgb", [DEPTH, 128, KT, 12], BF16)
    woutb = scr("woutb", [DEPTH, 4, 128, KT, 512], BF16)
    w1b = scr("w1b", [DEPTH, 16, 128, KT, 512], BF16)
    w2b = scr("w2b", [DEPTH, 16, 128, KT, 512], BF16)
    glub = scr("glub", [DEPTH, 2, 128, 4, 512], BF16)
    poolwb = scr("poolwb", [DEPTH, 128, 4, 128], BF16)

    S = Sched(nc)
    ps = [nc.alloc_psum_tensor("ps%d" % i, [128, 512], F32).ap() for i in range(7)]
    psb = nc.alloc_psum_tensor("psb", [128, 1024], BF16).ap()
    PSK = ["ps%d" % i for i in range(7)]

    from contextlib import ExitStack

    uid = [0]

    def sb(es, name, shape, dt):
        uid[0] += 1
        return es.enter_context(nc.sbuf_tensor("%s_%d" % (name, uid[0]), list(shape), dt)).ap()

    pes = ExitStack()
    ident_f = sb(pes, "ident_f", [128, 128], F32)
    ident_b = sb(pes, "ident_b", [128, 128], BF16)
    ones_b = sb(pes, "ones_b", [128, 128], BF16)
    ones_f = sb(pes, "ones_f", [128, 128], F32)
    tri_f = sb(pes, "tri_f", [128, 128], F32)
    tri_b = sb(pes, "tri_b", [128, 128], BF16)
    padc = sb(pes, "padc", [128, 4], F32)
    corr = sb(pes, "corr", [128, 4, 16], F32)
    iota = sb(pes, "iota", [128, 128], F32)
    vecT = sb(pes, "vecT", [128, DEPTH, 112], F32)
    S.dma('sp', ident_f, c_ident, key='c', writes=['ident_f'])
    S.dma('sp', tri_f, c_tri, key='c', writes=['tri_f'])
    S.dma('sp', padc, c_pad, key='c', writes=['padc'])
    S.dma('sp', corr, c_corr, key='c', writes=['corr'])
    S.dma('sp', iota, c_iota, key='c', writes=['iota'])
    S.op('dve', lambda e: e.tensor_copy(ident_b, ident_f), reads=['ident_f'], writes=['ident_b'])
    S.op('dve', lambda e: e.tensor_copy(tri_b, tri_f), reads=['tri_f'], writes=['tri_b'])
    S.op('dve', lambda e: e.memset(ones_b, 1.0), writes=['ones_b'])
    S.op('dve', lambda e: e.memset(ones_f, 1.0), writes=['ones_f'])
    with ExitStack() as es:
        vs = sb(es, "vs", [112, 128], F32)
        for l in range(DEPTH):
            r = 0
            for nm, nrow in (("g_pre_mix", 16), ("g_post_mix", 16), ("g_pre_ffn", 16), ("g_post_ffn", 16),
                             ("pool_scale", 4), ("ssm_d", 4), ("ssm_glu_b", 8)):
                src = P[nm][l]
                if nm == "ssm_d":
                    src = src.rearrange("g h -> (g h)")
                S.dma('sp', vs[r:r + nrow, :], src.rearrange("(r c) -> r c", c=128), key='c', writes=['vs'])
                r += nrow
            S.dma('sp', vs[80:112, :], P["ml_conv_w"][l].rearrange("j (t c) -> (j t) c", c=128), key='c', writes=['vs'])
            S.op('pe', lambda e: e.transpose(ps[0][:, 0:112], vs, ident_f[0:112, 0:112]), reads=['vs', 'ident_f'], writes=['ps0'])
            S.op('dve', lambda e, l=l: e.tensor_copy(vecT[:, l, :], ps[0][:, 0:112]), reads=['ps0'], writes=['vecT'])
        def cast_panel(dst, src_rows, c0, nk, ncol=512):
            sv = src_rows.rearrange("(kt p) c -> p kt c", p=128)
            for k0 in range(0, nk, 4):
                k1 = min(nk, k0 + 4)
                S.dma('pool', dst[:, k0:k1, :], sv[:, k0:k1, c0:c0 + ncol], key='cast')
        def cast_win(l):
            for i, (_, c0) in enumerate(SEGS):
                cast_panel(winb[l, i], P["w_in"][l], c0, KT)
            sv = P["w_in"][l].rearrange("(kt p) c -> p kt c", p=128)
            S.dma('pool', wgb[l][:, :, 0:8], sv[:, :, 2560:2568], key='cast')
            S.dma('pool', wgb[l][:, :, 8:12], sv[:, :, 4104:4108], key='cast')

        def cast_rest(l):
            for i in range(4):
                cast_panel(woutb[l, i], P["w_out"][l], i * 512, KT)
            for i in range(16):
                cast_panel(w1b[l, i], P["mlp_w1"][l], i * 512, KT)
            for je in range(4):
                for kq in range(4):
                    cast_panel(w2b[l, je * 4 + kq], P["mlp_w2"][l][kq * 2048:(kq + 1) * 2048, :], je * 512, KT)
            for i in range(2):
                cast_panel(glub[l, i], P["ssm_glu_w"][l], i * 512, 4)
            S.dma('pool', poolwb[l], P["pool_w"][l].rearrange("g c d -> c g d"), key='cast')

        cast_win(0)
        xin = [sb(es, "xin%d" % i, [128, D], F32) for i in range(2)]
        hst = [sb(es, "hst%d" % i, [128, KT, 128], F32) for i in range(2)]
        for c in range(NCH):
            xi, hs = xin[c % 2], hst[c % 2]
            kx, kh = "xin%d" % (c % 2), "hst%d" % (c % 2)
            if c == 0:
                S.op('pool', lambda e, xi=xi: e.memset(xi, 0.0), writes=[kx])
                S.dma('sp', xi[112:128, :], meta, key='xin', reads=[kx], writes=[kx])
            else:
                S.dma('sp', xi, x[(c - 1) * 128:c * 128, :], key='xin', writes=[kx])
            for q in range(4):
                pk = ps[q % 2]
                for j in range(4):
                    kt = q * 4 + j
                    S.op('pe', lambda e, pk=pk, j=j, kt=kt, xi=xi: e.transpose(pk[:, j * 128:(j + 1) * 128], xi[:, kt * 128:(kt + 1) * 128], ident_f),
                         reads=[kx, 'ident_f'], writes=[PSK[q % 2]])
                S.op('act' if q % 2 else 'dve',
                     (lambda e, pk=pk, q=q, hs=hs: e.activation(hs[:, q * 4:(q + 1) * 4, :], pk.rearrange("p (j t) -> p j t", j=4), AF.Identity)) if q % 2 else
                     (lambda e, pk=pk, q=q, hs=hs: e.tensor_copy(hs[:, q * 4:(q + 1) * 4, :], pk.rearrange("p (j t) -> p j t", j=4))),
                     reads=[PSK[q % 2]], writes=[kh])
            S.dma('sp', hT[:, :, c * 128:(c + 1) * 128].rearrange("k p t -> p k t"), hs, key='hst', reads=[kh], writes=['hT'])
        S.flush()
    if stop_after == "p0":
        return nc
    def evac(i, out_ap, in_ap, reads, writes):
        if i % 2:
            S.op('act', lambda e: e.activation(out_ap, in_ap, AF.Identity), reads=reads, writes=writes)
        else:
            S.op('dve', lambda e: e.tensor_copy(out_ap, in_ap), reads=reads, writes=writes)

    def sumsq_steps(src, ksrc, n, sq, rstd, dim, krstd='rstd', sq_eng='act', ksq='sq'):
        kf = ksrc if callable(ksrc) else (lambda kt: ksrc)
        steps = []
        for kt in range(KT):
            def st(kt=kt):
                sl = sq[kt % 2]
                if sq_eng == 'act':
                    S.op('act', lambda e: e.activation(sl[:, :n], src[:, kt, :n], AF.Square), reads=[kf(kt)], writes=['%s%d' % (ksq, kt % 2)])
                else:
                    S.op(sq_eng, lambda e: e.tensor_tensor(sl[:, :n], src[:, kt, :n], src[:, kt, :n], ALU.mult), reads=[kf(kt)], writes=['%s%d' % (ksq, kt % 2)])
                S.op('pe', lambda e: e.matmul(ps[6][:, :n], ones_b, sl[:, :n], start=(kt == 0), stop=(kt == KT - 1)),
                     reads=['%s%d' % (ksq, kt % 2), 'ones_b'], writes=['ps6'])
            steps.append(st)

        def fin():
            S.op('act', lambda e: e.activation(rstd[:, :n], ps[6][:, :n], AF.Sqrt, bias=EPS, scale=1.0 / dim), reads=['ps6'], writes=[krstd])
            S.op('dve', lambda e: e.reciprocal(rstd[:, :n], rstd[:, :n]), reads=[krstd], writes=[krstd])
        steps.append(fin)
        return steps

    def sumsq_rstd(src, ksrc, n, sq, rstd, dim):
        for st in sumsq_steps(src, ksrc, n, sq, rstd, dim):
            st()

    class PanelStream:
        def __init__(self, es, srcs, nb=3, name="wp"):
            self.srcs = srcs
            self.nb = nb
            self.bufs = [sb(es, "%s%d" % (name, i), [128, KT, 512], BF16) for i in range(nb)]
            self.keys = ["%s%d" % (name, i) for i in range(nb)]
            self.issued = 0
            self.used = 0

        def _issue(self):
            i = self.issued
            if i < len(self.srcs):
                S.dma('sp', self.bufs[i % self.nb], self.srcs[i], key=self.keys[i % self.nb], writes=[self.keys[i % self.nb]])
                self.issued += 1

        def next(self):
            while self.issued < min(len(self.srcs), self.used + self.nb - 0) and self.issued - self.used < self.nb:
                self._issue()
            i = self.used
            self.used += 1
            return self.bufs[i % self.nb], self.keys[i % self.nb]

        def prefetch(self):
            while self.issued < len(self.srcs) and self.issued - self.used < self.nb - 1:
                self._issue()

    def phase_A(l):
        with ExitStack() as es:
            if l == 0:
                cast_rest(0)
                for l2 in range(1, DEPTH):
                    cast_win(l2)
                    cast_rest(l2)
            hblk2 = [sb(es, "hblk%d" % i, [128, KT, 512], F32) for i in range(2)]
            xn2 = [sb(es, "xn%d" % i, [128, KT, 512], BF16) for i in range(2)]
            sq = [sb(es, "sq%d" % i, [128, 512], BF16) for i in range(2)]
            rstd2 = [sb(es, "rstd%d" % i, [128, 512], F32) for i in range(2)]
            wps = PanelStream(es, [winb[l, i] for _ in blocks for i in range(len(SEGS))], nb=3)
            wg = sb(es, "wg", [128, KT, 12], BF16)
            gbt = sb(es, "gbt", [128, 12], F32)
            ost = [sb(es, "ost%d" % i, [128, 4, 512], BF16) for i in range(2)]
            tst = [sb(es, "tst%d" % i, [128, 512], BF16) for i in range(2)]
            gst = [sb(es, "gst%d" % i, [128, 12], F32) for i in range(2)]
            S.dma('sp', wg, wgb[l], key='wg', writes=['wg'])
            S.dma('sp', gbt[:, 0:8], P["ml_gate_bias"][l:l + 1, :].partition_broadcast(128), key='wg', writes=['gbt'])
            S.dma('sp', gbt[:, 8:12], P["fx_gate_bias"][l:l + 1, :].partition_broadcast(128), key='wg', writes=['gbt'])
            fm_dest = {"s_u": (uT, 0), "m_q": (mqkT, 0), "m_k": (mqkT, 4), "f_q": (fqT, 0), "f_k": (fkT, 0), "p_u": (puT, 0)}
            tm_dest = {"m_v": mv, "m_o": mo, "f_v": fv}
            wcnt = 0
            ocnt = 0
            tcnt = 0
            gcnt = 0

            def norm_steps(bi, t0, n):
                s_ = bi % 2
                hb, xb_, rs_ = hblk2[s_], xn2[s_], rstd2[s_]
                kh, kx, kr = 'hblk%d' % s_, 'xn%d' % s_, 'rstd%d' % s_

                def ld():
                    for kt in range(KT):
                        S.dma('sp', hb[:, kt, :n], hT[kt, :, t0:t0 + n], key=kh, writes=[kh])
                steps = [ld] + sumsq_steps(hb, kh, n, sq, rs_, D, krstd=kr, sq_eng='dve')
                for kt in range(KT):
                    def st(kt=kt):
                        S.op('dve', lambda e: e.scalar_tensor_tensor(xb_[:, kt, :n], hb[:, kt, :n], vecT[:, l, kt:kt + 1], rs_[:, :n], ALU.mult, ALU.mult),
                             reads=[kh, kr, 'vecT'], writes=[kx])
                    steps.append(st)
                return steps

            def blk(bi, t0, n, bg):
                nonlocal wcnt, ocnt, tcnt, gcnt
                xn = xn2[bi % 2]
                kxn = 'xn%d' % (bi % 2)
                if bg:
                    bg[0]()
                    bg = bg[1:]
                per = (len(bg) + len(SEGS) - 1) // len(SEGS) if bg else 0
                for i, (nm, c0) in enumerate(SEGS):
                    w, kw = wps.next()
                    wps.prefetch()
                    if nm in fm_dest:
                        dst, toff = fm_dest[nm]
                        o = ost[ocnt % 2]
                        ko = 'ost%d' % (ocnt % 2)
                        ocnt += 1
                        for m in range(4):
                            for kt in range(KT):
                                S.op('pe', lambda e, w=w, m=m, kt=kt: e.matmul(ps[m][:, :n], w[:, kt, m * 128:(m + 1) * 128], xn[:, kt, :n], start=(kt == 0), stop=(kt == KT - 1)),
                                     reads=[kw, kxn], writes=[PSK[m]])
                            evac(1, o[:, m, :n], ps[m][:, :n], [PSK[m]], [ko])
                        for m in range(4):
                            S.dma('act', dst[toff + m, :, t0:t0 + n], o[:, m, :n], key=ko, reads=[ko])
                    else:
                        dst = tm_dest[nm]
                        for i2 in range(n // 128):
                            pk = 4 + (i2 % 2)
                            ts_ = tst[tcnt % 2]
                            kts = 'tst%d' % (tcnt % 2)
                            tcnt += 1
                            for kt in range(KT):
                                S.op('pe', lambda e, w=w, kt=kt, i2=i2, pk=pk: e.matmul(ps[pk], xn[:, kt, i2 * 128:(i2 + 1) * 128], w[:, kt, :], start=(kt == 0), stop=(kt == KT - 1)),
                                     reads=[kw, kxn], writes=[PSK[pk]])
                            evac(1, ts_, ps[pk], [PSK[pk]], [kts])
                            S.dma('act', dst[t0 + i2 * 128:t0 + (i2 + 1) * 128, :], ts_, key=kts, reads=[kts])
                    for st_ in bg[i * per:(i + 1) * per]:
                        st_()
                for st_ in bg[len(SEGS) * per:]:
                    st_()
                for i2 in range(n // 128):
                    g_ = gst[gcnt % 2]
                    kg = 'gst%d' % (gcnt % 2)
                    gcnt += 1
                    for kt in range(KT):
                        S.op('pe', lambda e, kt=kt, i2=i2: e.matmul(ps[6][:, 0:12], xn[:, kt, i2 * 128:(i2 + 1) * 128], wg[:, kt, :], start=(kt == 0), stop=(kt == KT - 1)),
                             reads=['wg', kxn], writes=['ps6'])
                    S.op('dve', lambda e, g_=g_: e.tensor_tensor(g_, ps[6][:, 0:12], gbt, ALU.add), reads=['ps6', 'gbt'], writes=[kg])
                    S.dma('act', gates[t0 + i2 * 128:t0 + (i2 + 1) * 128, :], g_, key=kg, reads=[kg])

            pend = norm_steps(0, *blocks[0])
            for st_ in pend:
                st_()
            for bi, (t0_, n_) in enumerate(blocks):
                bg = norm_steps(bi + 1, *blocks[bi + 1]) if bi + 1 < len(blocks) else []
                blk(bi, t0_, n_, bg)
            S.flush()

    def phase_C(l):
        with ExitStack() as es:
            hblk = sb(es, "hblk", [128, KT, 512], F32)
            mix = sb(es, "mix", [128, KT, 512], F32)
            hn = sb(es, "hn", [128, KT, 512], BF16)
            big = sb(es, "big", [128, 32, 512], BF16)
            sq = [sb(es, "sq%d" % i, [128, 512], BF16) for i in range(2)]
            rstd = sb(es, "rstd", [128, 512], F32)
            rl = [sb(es, "rl%d" % i, [128, 512], F32) for i in range(2)]
            plist = []
            for _ in blocks:
                plist += [woutb[l, i] for i in range(4)]
                for half in range(2):
                    plist += [w1b[l, 8 * half + j] for j in range(8)]
                    plist += [w2b[l, je * 4 + kq] for je in range(4) for kq in (2 * half, 2 * half + 1)]
            wps = PanelStream(es, plist, nb=3)
            wcnt = 0
            rcnt = 0
            last = (l == DEPTH - 1)
            xo = [sb(es, "xo%d" % i, [128, D], F32) for i in range(2)] if last else None
            xcnt = 0

            def post_norm_add(n, gcol):
                sumsq_rstd(mix, lambda kt: 'mx%d' % kt, n, sq, rstd, D)
                for kt in range(KT):
                    S.op('dve', lambda e, kt=kt: e.scalar_tensor_tensor(mix[:, kt, :n], mix[:, kt, :n], vecT[:, l, gcol + kt:gcol + kt + 1], rstd[:, :n], ALU.mult, ALU.mult),
                         reads=['mx%d' % kt, 'rstd', 'vecT'], writes=['mx%d' % kt])
                    S.op('pool' if kt % 3 else 'dve', lambda e, kt=kt: e.tensor_tensor(hblk[:, kt, :n], hblk[:, kt, :n], mix[:, kt, :n], ALU.add), reads=['mx%d' % kt, 'hb%d' % kt], writes=['hb%d' % kt])

            def load_y(t0, n):
                for kt in range(KT):
                    S.dma('sp', big[:, kt, :n], yT[kt, :, t0:t0 + n], key='ybl', writes=['big'])

            def blk(bi, t0, n):
                nonlocal wcnt, rcnt, xcnt
                if bi == 0:
                    load_y(t0, n)
                for kt in range(KT):
                    S.dma('sp', hblk[:, kt, :n], hT[kt, :, t0:t0 + n], key='hblk', writes=['hb%d' % kt])
                for i in range(4):
                    w, kw = wps.next()
                    wps.prefetch()
                    for m in range(4):
                        for kt in range(KT):
                            S.op('pe', lambda e, w=w, m=m, kt=kt: e.matmul(ps[m][:, :n], w[:, kt, m * 128:(m + 1) * 128], big[:, kt, :n], start=(kt == 0), stop=(kt == KT - 1)),
                                 reads=[kw, 'big'], writes=[PSK[m]])
                        evac(m, mix[:, i * 4 + m, :n], ps[m][:, :n], [PSK[m]], ['mx%d' % (i * 4 + m)])
                post_norm_add(n, 16)
                if t0 == 0:
                    S.op('pool', lambda e: e.memset(hblk[:, :, 0:112], 0.0), writes=['hb%d' % k_ for k_ in range(KT)])
                sumsq_rstd(hblk, lambda kt: 'hb%d' % kt, n, sq, rstd, D)
                for kt in range(KT):
                    S.op('dve', lambda e, kt=kt: e.scalar_tensor_tensor(hn[:, kt, :n], hblk[:, kt, :n], vecT[:, l, 32 + kt:33 + kt], rstd[:, :n], ALU.mult, ALU.mult),
                         reads=['hb%d' % kt, 'rstd', 'vecT'], writes=['hn'])
                for half in range(2):
                    for jp in range(8):
                        w, kw = wps.next()
                        wps.prefetch()
                        for m in range(4):
                            for kt in range(KT):
                                S.op('pe', lambda e, w=w, m=m, kt=kt: e.matmul(ps[m][:, :n], w[:, kt, m * 128:(m + 1) * 128], hn[:, kt, :n], start=(kt == 0), stop=(kt == KT - 1)),
                                     reads=[kw, 'hn'], writes=[PSK[m]])
                            r_ = rl[rcnt % 2]
                            kr = 'rl%d' % (rcnt % 2)
                            rcnt += 1
                            S.op('act', lambda e, r_=r_, m=m: e.activation(r_[:, :n], ps[m][:, :n], AF.Relu), reads=[PSK[m]], writes=[kr])
                            S.op('dve', lambda e, r_=r_, jp=jp, m=m: e.tensor_tensor(big[:, jp * 4 + m, :n], r_[:, :n], r_[:, :n], ALU.mult), reads=[kr], writes=['big'])
                    for je in range(4):
                        for kq in range(2):
                            w, kw = wps.next()
                            wps.prefetch()
                            for m in range(4):
                                for kt in range(KT):
                                    S.op('pe', lambda e, w=w, m=m, kt=kt, kq=kq: e.matmul(ps[m][:, :n], w[:, kt, m * 128:(m + 1) * 128], big[:, kq * 16 + kt, :n],
                                                                                        start=(kq == 0 and kt == 0), stop=(kq == 1 and kt == KT - 1)),
                                         reads=[kw, 'big'], writes=[PSK[m]])
                        for m in range(4):
                            if half == 0:
                                evac(m, mix[:, je * 4 + m, :n], ps[m][:, :n], [PSK[m]], ['mx%d' % (je * 4 + m)])
                            else:
                                S.op('dve', lambda e, je=je, m=m: e.tensor_tensor(mix[:, je * 4 + m, :n], ps[m][:, :n], mix[:, je * 4 + m, :n], ALU.add),
                                     reads=[PSK[m], 'mx%d' % (je * 4 + m)], writes=['mx%d' % (je * 4 + m)])
                if bi + 1 < len(blocks):
                    load_y(*blocks[bi + 1])
                post_norm_add(n, 48)
                if last:
                    for i2 in range(n // 128):
                        cglob = t0 // 128 + i2
                        if cglob == 0:
                            continue
                        x_, kx = xo[xcnt % 2], 'xo%d' % (xcnt % 2)
                        xcnt += 1
                        for q in range(4):
                            for j in range(4):
                                kt = q * 4 + j
                                S.op('pe', lambda e, q=q, j=j, kt=kt, i2=i2: e.transpose(ps[q][:, j * 128:(j + 1) * 128], hblk[:, kt, i2 * 128:(i2 + 1) * 128], ident_f),
                                     reads=['hb%d' % kt, 'ident_f'], writes=[PSK[q]])
                            evac(q, x_[:, q * 512:(q + 1) * 512], ps[q], [PSK[q]], [kx])
                        S.dma('pool', out[(cglob - 1) * 128:cglob * 128, :], x_, key=kx, reads=[kx])
                    return
                for kt in range(KT):
                    S.dma('pool', hT[kt, :, t0:t0 + n], hblk[:, kt, :n], key='hst', reads=['hb%d' % kt], writes=['hT'])

            for bi_, (t0_, n_) in enumerate(blocks):
                blk(bi_, t0_, n_)
            S.flush()

    def phase_Z():
        with ExitStack() as es:
            hs = [sb(es, "hs%d" % i, [128, KT, 128], F32) for i in range(2)]
            xo = [sb(es, "xo%d" % i, [128, D], F32) for i in range(2)]
            for c in range(1, NCH):
                h_, x_ = hs[c % 2], xo[c % 2]
                kh, kx = 'hs%d' % (c % 2), 'xo%d' % (c % 2)
                S.dma('sp', h_, hT[:, :, c * 128:(c + 1) * 128].rearrange("k p t -> p k t"), key=kh, writes=[kh])
                for q in range(4):
                    for j in range(4):
                        kt = q * 4 + j
                        S.op('pe', lambda e, q=q, j=j, kt=kt, h_=h_: e.transpose(ps[q][:, j * 128:(j + 1) * 128], h_[:, kt, :], ident_f), reads=[kh, 'ident_f'], writes=[PSK[q]])
                    evac(q, x_[:, q * 512:(q + 1) * 512], ps[q], [PSK[q]], [kx])
                S.dma('sp', out[(c - 1) * 128:c * 128, :], x_, key=kx, reads=[kx])
            S.flush()
    def phase_pool(l):
        with ExitStack() as es:
            ub = [sb(es, "ub%d" % i, [128, 16 + LP], F32) for i in range(3)]
            uin = sb(es, "uin", [128, LP], BF16)
            pooled = sb(es, "pooled", [128, LP], BF16)
            t16 = sb(es, "t16", [128, 16], F32)
            pw = sb(es, "pw", [128, 4, 128], BF16)
            ost = [sb(es, "post%d" % i, [128, 512], BF16) for i in range(2)]
            S.dma('sp', pw, poolwb[l], key='pw', writes=['pw'])
            for i in range(3):
                S.op('pool', lambda e, i=i: e.memset(ub[i][:, 0:16], 0.0), writes=['ub%d' % i])
            oc = 0
            for g, w in enumerate((2, 4, 8, 16)):
                S.dma('sp', uin, puT[g], key='uin', writes=['uin'])
                S.op('act', lambda e: e.activation(ub[0][:, 16:], uin, AF.Identity), reads=['uin'], writes=['ub0'])
                a, ka = ub[0], 'ub0'
                for s_ in range(g + 1):
                    m = 2 ** s_
                    d_, kd = ub[1 + s_ % 2], 'ub%d' % (1 + s_ % 2)
                    S.op('dve', lambda e, a=a, d_=d_, m=m: e.tensor_tensor(d_[:, 16:], a[:, 16:], a[:, 16 - m:16 - m + LP], ALU.add), reads=[ka], writes=[kd])
                    a, ka = d_, kd
                S.op('dve', lambda e, a=a, w=w: e.scalar_tensor_tensor(pooled, a[:, 16:], 1.0 / w, ub[0][:, 16:], ALU.mult, ALU.subtract), reads=[ka, 'ub0'], writes=['pooled'])
                S.op('dve', lambda e, a=a, g=g: e.tensor_tensor(t16, a[:, 16 + 112:16 + 128], corr[:, g, :], ALU.mult), reads=[ka, 'corr'], writes=['t16'])
                S.op('dve', lambda e, w=w: e.scalar_tensor_tensor(pooled[:, 112:128], t16, 1.0 / w, ub[0][:, 16 + 112:16 + 128], ALU.mult, ALU.subtract), reads=['t16', 'ub0'], writes=['pooled'])
                for bi, (t0, n) in enumerate(blocks):
                    pk = bi % 2
                    o, ko = ost[oc % 2], 'post%d' % (oc % 2)
                    oc += 1
                    S.op('pe', lambda e, g=g, t0=t0, n=n, pk=pk: e.matmul(ps[pk][:, :n], pw[:, g, :], pooled[:, t0:t0 + n], start=True, stop=True), reads=['pw', 'pooled'], writes=[PSK[pk]])
                    S.op('act', lambda e, o=o, n=n, pk=pk, g=g: e.activation(o[:, :n], ps[pk][:, :n], AF.Identity, scale=vecT[:, l, 64 + g:65 + g]), reads=[PSK[pk], 'vecT'], writes=[ko])
                    S.dma('sp', yT[12 + g, :, t0:t0 + n], o[:, :n], key=ko, reads=[ko])
            S.flush()

    def phase_fox(l):
        SC = float(128 ** -0.5)
        with ExitStack() as es:
            gts = sb(es, "gts", [128, NCH, 12], F32)
            e4 = sb(es, "e4", [128, 4, NCH], F32)
            lfh = sb(es, "lfh", [128, 4, NCH], F32)
            Fin = sb(es, "Fin", [128, 4, NCH], F32)
            tot = sb(es, "tot", [128, 4, NCH], F32)
            Fend = sb(es, "Fend", [128, 4, NCH], F32)
            Fg = sb(es, "Fg", [128, 4, NCH], F32)
            bias = sb(es, "bias", [128, NCH], F32)
            qTt = sb(es, "qTt", [128, LP], BF16)
            kTt = sb(es, "kTt", [128, LP], BF16)
            vtok = sb(es, "vtok", [128, NCH, 128], BF16)
            pT = [sb(es, "pT%d" % i, [128, 512], BF16) for i in range(2)]
            rden = sb(es, "rden", [128, 512], F32)
            ost = [sb(es, "fost%d" % i, [128, 512], BF16) for i in range(2)]
            for c0_ in range(0, NCH, 8):
                c1_ = min(NCH, c0_ + 8)
                S.dma('sp', gts[:, c0_:c1_, :], gates[c0_ * 128:c1_ * 128, :].rearrange("(c p) g -> p c g", p=128), key='gts', writes=['gts'])
            for h in range(4):
                S.op('act', lambda e, h=h: e.activation(e4[:, h, :], gts[:, :, 8 + h], AF.Exp, scale=-1.0), reads=['gts'], writes=['e4'])
            S.op('act', lambda e: e.activation(lfh, e4, AF.Ln, bias=1.0), reads=['e4'], writes=['lfh'])
            S.op('dve', lambda e: e.tensor_scalar(lfh, lfh, -1.0, None, ALU.mult), reads=['lfh'], writes=['lfh'])
            S.op('dve', lambda e: e.tensor_scalar(lfh[:, :, 0], lfh[:, :, 0], padc[:, 0:1], None, ALU.mult), reads=['lfh', 'padc'], writes=['lfh'])
            lff = lfh.rearrange("p h c -> p (h c)")
            S.op('pe', lambda e: e.matmul(ps[0][:, 0:4 * NCH], tri_f, lff, start=True, stop=True), reads=['tri_f', 'lfh'], writes=['ps0'])
            S.op('pe', lambda e: e.matmul(ps[1][:, 0:4 * NCH], ones_f, lff, start=True, stop=True), reads=['ones_f', 'lfh'], writes=['ps1'])
            S.op('dve', lambda e: e.tensor_copy(Fin.rearrange("p h c -> p (h c)"), ps[0][:, 0:4 * NCH]), reads=['ps0'], writes=['Fin'])
            S.op('dve', lambda e: e.tensor_copy(tot.rearrange("p h c -> p (h c)"), ps[1][:, 0:4 * NCH]), reads=['ps1'], writes=['tot'])
            for h in range(4):
                S.op('dve', lambda e, h=h: e.tensor_tensor_scan(Fend[:, h, :], ones_f[:, 0:NCH], tot[:, h, :], 0.0, ALU.mult, ALU.add), reads=['tot', 'ones_f'], writes=['Fend'])
            S.op('dve', lambda e: e.tensor_tensor(Fg, Fend, tot, ALU.subtract), reads=['Fend', 'tot'], writes=['Fg'])
            S.op('dve', lambda e: e.tensor_tensor(Fg, Fg, Fin, ALU.add), reads=['Fg', 'Fin'], writes=['Fg'])
            oc = 0
            pc = 0
            for h in range(4):
                S.dma('sp', qTt, fqT[h], key='fq', writes=['qTt'])
                S.dma('sp', kTt, fkT[h], key='fk', writes=['kTt'])
                for c0_ in range(0, NCH, 8):
                    c1_ = min(NCH, c0_ + 8)
                    S.dma('sp', vtok[:, c0_:c1_, :], fv[c0_ * 128:c1_ * 128, :].rearrange("(c p) (h d) -> p c h d", p=128, h=4)[:, :, h, :], key='fvv', writes=['vtok'])
                for qb, (t0, n) in enumerate(blocks):
                    c0 = t0 // 128
                    clast = c0 + n // 128 - 1
                    pO, pD = 2 + (qb % 2), 4 + (qb % 2)
                    S.op('dve', lambda e, h=h, clast=clast: e.tensor_scalar(bias[:, 0:clast + 1], Fg[:, h, 0:clast + 1], Fend[:, h, clast:clast + 1], -1.0, ALU.subtract, ALU.mult),
                         reads=['Fg', 'Fend'], writes=['bias'])
                    S.op('dve', lambda e: e.tensor_tensor(bias[:, 0:1], bias[:, 0:1], padc[:, 1:2], ALU.add), reads=['bias', 'padc'], writes=['bias'])
                    def emit_S(kc, pcv, t0=t0, n=n, c0=c0):
                        lo = max(0, kc - c0) * 128
                        pS = pcv % 2
                        S.op('pe', lambda e: e.matmul(ps[pS][:, lo:n], kTt[:, kc * 128:(kc + 1) * 128], qTt[:, t0 + lo:t0 + n], start=True, stop=True),
                             reads=['kTt', 'qTt'], writes=[PSK[pS]])

                    emit_S(0, pc)
                    for kc in range(clast + 1):
                        lo = max(0, kc - c0) * 128
                        pS = pc % 2
                        p_, kp = pT[pc % 2], 'pT%d' % (pc % 2)
                        pc += 1
                        S.op('act', lambda e, kc=kc, lo=lo, pS=pS, p_=p_, n=n: e.activation(p_[:, lo:n], ps[pS][:, lo:n], AF.Exp, bias=bias[:, kc:kc + 1], scale=SC),
                             reads=[PSK[pS], 'bias'], writes=[kp])
                        if kc >= c0:
                            S.op('pool', lambda e, p_=p_, lo=lo: e.tensor_tensor(p_[:, lo:lo + 128], p_[:, lo:lo + 128], tri_b, ALU.mult), reads=[kp, 'tri_b'], writes=[kp])
                        if kc + 1 <= clast:
                            emit_S(kc + 1, pc)
                        S.op('pe', lambda e, kc=kc, lo=lo, p_=p_, n=n, pO=pO, clast=clast: e.matmul(ps[pO][:, lo:n], vtok[:, kc, :], p_[:, lo:n], start=(kc == 0), stop=(kc == clast)),
                             reads=['vtok', kp], writes=[PSK[pO]])
                        S.op('pe', lambda e, kc=kc, lo=lo, p_=p_, n=n, pD=pD, clast=clast: e.matmul(ps[pD][:, lo:n], ones_b, p_[:, lo:n], start=(kc == 0), stop=(kc == clast)),
                             reads=['ones_b', kp], writes=[PSK[pD]])
                    o, ko = ost[oc % 2], 'fost%d' % (oc % 2)
                    oc += 1
                    S.op('dve', lambda e, n=n, pD=pD: e.reciprocal(rden[:, :n], ps[pD][:, :n]), reads=[PSK[pD]], writes=['rden'])
                    S.op('dve', lambda e, n=n, pO=pO, o=o: e.tensor_tensor(o[:, :n], ps[pO][:, :n], rden[:, :n], ALU.mult), reads=[PSK[pO], 'rden'], writes=[ko])
                    S.dma('act', yT[8 + h, :, t0:t0 + n], o[:, :n], key=ko, reads=[ko])
            S.flush()
    def phase_mlstm(l):
        SC = float(128 ** -0.5)
        with ExitStack() as es:
            qc = sb(es, "qc", [128, 4, LP], BF16)
            kc_ = sb(es, "kc_", [128, 4, LP], BF16)
            cin = sb(es, "cin", [128, 3 + LP], BF16)
            acc = sb(es, "acc", [128, LP], F32)
            mlg = sb(es, "mlg", [128, 512], F32)
            vaug = [sb(es, "vaug%d" % i, [128, 4, 129], BF16) for i in range(2)]
            ot = [sb(es, "ot%d" % i, [128, 512], BF16) for i in range(2)]
            gt = [sb(es, "gt%d" % i, [128, 12], F32) for i in range(2)]
            sg = sb(es, "sg", [128, 512], F32)
            sm = sb(es, "sm", [128, 40], F32)
            Cf = sb(es, "Cf", [128, 4, 129], F32)
            Cb = sb(es, "Cb", [128, 4, 129], BF16)
            pt = [sb(es, "pt%d" % i, [128, 128], BF16) for i in range(2)]
            ktk = [sb(es, "ktk%d" % i, [128, 128], BF16) for i in range(2)]
            vw = [sb(es, "vw%d" % i, [128, 129], BF16) for i in range(2)]
            hh = [sb(es, "hh%d" % i, [128, 128], F32) for i in range(2)]
            junk = sb(es, "junk", [128, 128], F32)
            hs_ = [sb(es, "hsm%d" % i, [128, 8], F32) for i in range(2)]
            yo = [sb(es, "yo%d" % i, [128, 128], BF16) for i in range(2)]
            yst = [sb(es, "yst%d" % i, [128, 4, 128], BF16) for i in range(2)]
            S.dma('sp', mlg, P["ml_norm_g"][l:l + 1, :].partition_broadcast(128), key='mlg', writes=['mlg'])
            S.op('pool', lambda e: e.memset(cin[:, 0:3], 0.0), writes=['cin'])
            S.op('pool', lambda e: e.memset(Cf, 0.0), writes=['Cf'])
            S.op('pool', lambda e: e.memset(Cb, 0.0), writes=['Cb'])
            for i in range(2):
                S.op('pool', lambda e, i=i: e.memset(vaug[i], 1.0), writes=['vaug%d' % i])
            for j in range(8):
                S.dma('sp', cin[:, 3:], mqkT[j], key='cin', reads=['cin'], writes=['cin'])
                wc = lambda tap, j=j: vecT[:, l, 80 + tap * 8 + j:81 + tap * 8 + j]
                S.op('dve', lambda e, wc=wc: e.tensor_scalar(acc, cin[:, 3:3 + LP], wc(3), None, ALU.mult), reads=['cin', 'vecT'], writes=['acc'])
                for tap in range(3):
                    S.op('dve', lambda e, wc=wc, tap=tap: e.scalar_tensor_tensor(acc, cin[:, tap:tap + LP], wc(tap), acc, ALU.mult, ALU.add), reads=['cin', 'acc', 'vecT'], writes=['acc'])
                if j < 4:
                    S.op('act', lambda e: e.activation(acc, acc, AF.Silu), reads=['acc'], writes=['acc'])
                    S.op('dve', lambda e, j=j: e.tensor_scalar(qc[:, j, :], acc, SC, None, ALU.mult), reads=['acc'], writes=['qc'])
                else:
                    S.op('act', lambda e, j=j: e.activation(kc_[:, j - 4, :], acc, AF.Silu), reads=['acc'], writes=['kc_'])
            for c in range(NCH):
                cs = slice(c * 128, (c + 1) * 128)
                va, kva = vaug[c % 2], 'vaug%d' % (c % 2)
                o_, ko = ot[c % 2], 'ot%d' % (c % 2)
                g_, kg = gt[c % 2], 'gt%d' % (c % 2)
                ys, kys = yst[c % 2], 'yst%d' % (c % 2)
                S.dma('sp', va[:, :, 0:128], mv[cs, :].rearrange("p (h d) -> p h d", h=4), key=kva, writes=[kva])
                S.dma('sp', o_, mo[cs, :], key=ko, writes=[ko])
                S.dma('sp', g_, gates[cs, :], key=kg, writes=[kg])
                S.op('act', lambda e, g_=g_: e.activation(sm[:, 0:4], g_[:, 4:8], AF.Exp, scale=-1.0), reads=[kg], writes=['sm_lf'])
                S.op('act', lambda e: e.activation(sm[:, 0:4], sm[:, 0:4], AF.Ln, bias=1.0), reads=['sm_lf'], writes=['sm_lf'])
                if c == 0:
                    S.op('dve', lambda e: e.tensor_scalar(sm[:, 0:4], sm[:, 0:4], -1.0, padc[:, 0:1], ALU.mult, ALU.mult), reads=['sm_lf', 'padc'], writes=['sm_lf'])
                else:
                    S.op('dve', lambda e: e.tensor_scalar(sm[:, 0:4], sm[:, 0:4], -1.0, None, ALU.mult), reads=['sm_lf'], writes=['sm_lf'])
                S.op('pe', lambda e: e.matmul(ps[6][:, 0:4], tri_f, sm[:, 0:4], start=True, stop=True), reads=['tri_f', 'sm_lf'], writes=['ps6'])
                S.op('pe', lambda e: e.matmul(ps[6][:, 4:8], ones_f, sm[:, 0:4], start=True, stop=True), reads=['ones_f', 'sm_lf'], writes=['ps6'])
                S.op('dve', lambda e, g_=g_: e.tensor_tensor(sm[:, 4:8], g_[:, 0:4], ps[6][:, 0:4], ALU.subtract), reads=[kg, 'ps6'], writes=['sm_w'])
                S.op('act', lambda e: e.activation(sm[:, 4:8], sm[:, 4:8], AF.Exp), reads=['sm_w'], writes=['sm_w'])
                if c == 0:
                    S.op('dve', lambda e: e.tensor_scalar(sm[:, 4:8], sm[:, 4:8], padc[:, 0:1], None, ALU.mult), reads=['sm_w', 'padc'], writes=['sm_w'])
                S.op('act', lambda e: e.activation(sm[:, 8:16], ps[6][:, 0:8], AF.Exp), reads=['ps6'], writes=['sm_e'])
                S.op('act', lambda e, o_=o_: e.activation(sg, o_, AF.Sigmoid), reads=[ko], writes=['sg'])
                for h in range(4):
                    i2 = h % 2
                    kch = kc_[:, h, cs]
                    qch = qc[:, h, cs]
                    p_, kp = pt[i2], 'pt%d' % i2
                    S.op('pe', lambda e, kch=kch, qch=qch, i2=i2: e.matmul(ps[i2][:, 0:128], kch, qch, start=True, stop=True), reads=['kc_', 'qc'], writes=[PSK[i2]])
                    S.op('dve', lambda e, h=h, i2=i2, p_=p_: e.scalar_tensor_tensor(p_, ps[i2][:, 0:128], sm[:, 4 + h:5 + h], tri_f, ALU.mult, ALU.mult),
                         reads=[PSK[i2], 'sm_w', 'tri_f'], writes=[kp])
                    S.op('pe', lambda e, kch=kch, i2=i2: e.transpose(psb[:, i2 * 128:(i2 + 1) * 128], kch, ident_b), reads=['kc_', 'ident_b'], writes=['psb%d' % i2])
                    S.op('act', lambda e, i2=i2: e.activation(ktk[i2], psb[:, i2 * 128:(i2 + 1) * 128], AF.Identity), reads=['psb%d' % i2], writes=['ktk%d' % i2])
                    S.op('pe', lambda e, p_=p_, va=va, h=h, i2=i2: e.matmul(ps[2 + i2][:, 0:129], p_, va[:, h, :], start=True, stop=False), reads=[kp, kva], writes=[PSK[2 + i2]])
                    S.op('pe', lambda e, qch=qch, h=h, i2=i2: e.matmul(ps[2 + i2][:, 0:129], qch, Cb[:, h, :], start=False, stop=True), reads=['qc', 'Cb'], writes=[PSK[2 + i2]])
                    hs = hs_[i2]
                    khs = 'hsm%d' % i2
                    nd = ps[2 + i2]
                    S.op('dve', lambda e, nd=nd, hs=hs, h=h: e.tensor_scalar(hs[:, 0:1], nd[:, 128:129], sm[:, 8 + h:9 + h], None, ALU.mult), reads=[PSK[2 + i2], 'sm_e'], writes=[khs])
                    S.op('dve', lambda e, hs=hs: e.scalar_tensor_tensor(hs[:, 1:2], hs[:, 0:1], -1.0, hs[:, 0:1], ALU.mult, ALU.max), reads=[khs], writes=[khs + 'b'])
                    S.op('dve', lambda e, hs=hs: e.tensor_scalar(hs[:, 2:3], hs[:, 1:2], 1.0, None, ALU.max), reads=[khs + 'b'], writes=[khs + 'c'])
                    S.op('dve', lambda e, hs=hs: e.reciprocal(hs[:, 3:4], hs[:, 2:3]), reads=[khs + 'c'], writes=[khs + 'd'])
                    S.op('dve', lambda e, hs=hs, h=h: e.tensor_tensor(hs[:, 4:5], hs[:, 3:4], sm[:, 8 + h:9 + h], ALU.mult), reads=[khs + 'd', 'sm_e'], writes=[khs + 'e'])
                    hh_, khh = hh[i2], 'hh%d' % i2
                    S.op('dve', lambda e, nd=nd, hs=hs, hh_=hh_: e.tensor_scalar(hh_, nd[:, 0:128], hs[:, 4:5], None, ALU.mult), reads=[PSK[2 + i2], khs + 'e'], writes=[khh])
                    S.op('act', lambda e, hh_=hh_: e.activation(junk, hh_, AF.Square), reads=[khh], writes=['junk'])
                    S.op('dve', lambda e, hs=hs: e.reduce_sum(hs[:, 5:6], junk, AX.X), reads=['junk'], writes=[khs + 'f'])
                    S.op('act', lambda e, hs=hs: e.activation(hs[:, 6:7], hs[:, 5:6], AF.Sqrt, bias=EPS, scale=1.0 / 128), reads=[khs + 'f'], writes=[khs + 'g'])
                    S.op('dve', lambda e, hs=hs: e.reciprocal(hs[:, 7:8], hs[:, 6:7]), reads=[khs + 'g'], writes=[khs + 'h'])
                    S.op('dve', lambda e, hs=hs, hh_=hh_, h=h: e.scalar_tensor_tensor(hh_, hh_, hs[:, 7:8], mlg[:, h * 128:(h + 1) * 128], ALU.mult, ALU.mult), reads=[khh, khs + 'h', 'mlg'], writes=[khh])
                    y_, ky = yo[i2], 'yo%d' % i2
                    S.op('pool', lambda e, hh_=hh_, y_=y_, h=h: e.tensor_tensor(y_, hh_, sg[:, h * 128:(h + 1) * 128], ALU.mult), reads=[khh, 'sg'], writes=[ky])
                    S.op('pe', lambda e, y_=y_, i2=i2: e.transpose(psb[:, 256 + i2 * 128:256 + (i2 + 1) * 128], y_, ident_b), reads=[ky, 'ident_b'], writes=['psby%d' % i2])
                    S.op('act', lambda e, ys=ys, h=h, i2=i2: e.activation(ys[:, h, :], psb[:, 256 + i2 * 128:256 + (i2 + 1) * 128], AF.Identity), reads=['psby%d' % i2], writes=[kys])
                    v_, kv_ = vw[i2], 'vw%d' % i2
                    S.op('pool', lambda e, v_=v_, va=va, h=h: e.tensor_scalar(v_, va[:, h, :], sm[:, 4 + h:5 + h], None, ALU.mult), reads=[kva, 'sm_w'], writes=[kv_])
                    S.op('pe', lambda e, v_=v_, i2=i2: e.matmul(ps[4 + i2][:, 0:129], ktk[i2], v_, start=True, stop=True), reads=['ktk%d' % i2, kv_], writes=[PSK[4 + i2]])
                    S.op('dve', lambda e, h=h: e.tensor_scalar(Cf[:, h, :], Cf[:, h, :], sm[:, 12 + h:13 + h], None, ALU.mult), reads=['Cf', 'sm_e'], writes=['Cf'])
                    S.op('dve', lambda e, h=h, i2=i2: e.scalar_tensor_tensor(Cf[:, h, :], ps[4 + i2][:, 0:129], sm[:, 12 + h:13 + h], Cf[:, h, :], ALU.mult, ALU.add),
                         reads=[PSK[4 + i2], 'sm_e', 'Cf'], writes=['Cf'])
                    S.op('act', lambda e, h=h: e.activation(Cb[:, h, :], Cf[:, h, :], AF.Identity), reads=['Cf'], writes=['Cb'])
                S.dma('act', yT[4:8, :, cs].rearrange("k p t -> p k t"), ys, key=kys, reads=[kys])
            S.flush()
    def phase_s5(l):
        with ExitStack() as es:
            lamr = sb(es, "lamr", [16, 128], F32)
            lami = sb(es, "lami", [16, 128], F32)
            ldt = sb(es, "ldt", [16, 2], F32)
            pm = sb(es, "pm", [16, 12, 128], F32)
            rr = sb(es, "rr", [128, 4, 2048], F32)
            ri = sb(es, "ri", [128, 2048], mybir.dt.int32)
            sm4 = sb(es, "sm4", [128, 4, 16], F32)
            cosT = sb(es, "cosT", [128, 16, 128], F32)
            sinT = sb(es, "sinT", [128, 16, 128], F32)
            Rtab = sb(es, "Rtab", [128, 16, 128], F32)
            cs128 = sb(es, "cs128", [128, 2, 16], F32)
            bnat = sb(es, "bnat", [128, 2, 16, 16], F32)
            bb = sb(es, "bb", [128, 4, 16, 16], F32)
            Xb = sb(es, "Xb", [128, 16, 2, 128], BF16)
            BT = sb(es, "BT", [128, 16, 2, 128], BF16)
            cnat = sb(es, "cnat", [128, 2, 4, 64], F32)
            Xc = sb(es, "Xc", [128, 4, 3, 128], BF16)
            CTm = sb(es, "CTm", [128, 4, 3, 128], BF16)
            CT = sb(es, "CT", [128, 4, 4, 3, 128], BF16)
            gw = sb(es, "gw", [128, 2, 4, 512], BF16)

            def sincos(ang, n, dsin, dcos, pp):
                for k_, (dst, off) in enumerate(((dsin, 0.0), (dcos, float(np.pi / 2)))):
                    a0, a1 = rr[:pp, 0, :n], rr[:pp, 1, :n]
                    S.op('dve', lambda e, a0=a0, off=off: e.tensor_scalar(a0, ang, off, None, ALU.add), reads=['ang'], writes=['rr0'])
                    S.op('dve', lambda e, a0=a0, a1=a1: e.tensor_scalar(a1, a0, 1.0 / TWO_PI, None, ALU.mult), reads=['rr0'], writes=['rr1'])
                    S.op('dve', lambda e, a1=a1: e.tensor_copy(ri[:pp, :n], a1), reads=['rr1'], writes=['ri'])
                    S.op('dve', lambda e, a1=a1: e.tensor_copy(a1, ri[:pp, :n]), reads=['ri'], writes=['rr1'])
                    S.op('dve', lambda e, a0=a0, a1=a1: e.scalar_tensor_tensor(a0, a1, -TWO_PI, a0, ALU.mult, ALU.add), reads=['rr0', 'rr1'], writes=['rr0'])
                    S.op('dve', lambda e, a0=a0, a1=a1: e.tensor_scalar(a1, a0, float(np.pi), -TWO_PI, ALU.is_gt, ALU.mult), reads=['rr0'], writes=['rr1'])
                    S.op('dve', lambda e, a0=a0, a1=a1: e.tensor_tensor(a0, a0, a1, ALU.add), reads=['rr0', 'rr1'], writes=['rr0'])
                    S.op('dve', lambda e, a0=a0, a1=a1: e.tensor_scalar(a1, a0, float(-np.pi), TWO_PI, ALU.is_lt, ALU.mult), reads=['rr0'], writes=['rr1'])
                    S.op('dve', lambda e, a0=a0, a1=a1: e.tensor_tensor(a0, a0, a1, ALU.add), reads=['rr0', 'rr1'], writes=['rr0'])
                    S.op('act', lambda e, a0=a0, dst=dst: e.activation(dst, a0, AF.Sin), reads=['rr0'], writes=['sc%d' % k_])

            S.dma('sp', lamr, P["ssm_lam_re"][l].rearrange("(j g) p -> j (g p)", g=2), key='s5c', writes=['lam'])
            S.dma('sp', lami, P["ssm_lam_im"][l].rearrange("(j g) p -> j (g p)", g=2), key='s5c', writes=['lam'])
            S.dma('sp', ldt, P["ssm_log_dt"][l].rearrange("(j g) -> j g", g=2), key='s5c', writes=['lam'])
            S.dma('sp', bnat[:, 0], P["ssm_b_re"][l].rearrange("(j g) p h -> (g p) j h", g=2), key='s5c', writes=['bnat'])
            S.dma('sp', bnat[:, 1], P["ssm_b_im"][l].rearrange("(j g) p h -> (g p) j h", g=2), key='s5c', writes=['bnat'])
            S.dma('sp', cnat[:, 0], P["ssm_c_re"][l].rearrange("(q g) h p -> (g h) q p", g=8), key='s5c', writes=['cnat'])
            S.dma('sp', cnat[:, 1], P["ssm_c_im"][l].rearrange("(q g) h p -> (g h) q p", g=8), key='s5c', writes=['cnat'])
            for i in range(2):
                S.dma('sp', gw[:, i], glub[l, i], key='s5c', writes=['gw'])
            S.op('act', lambda e: e.activation(ldt, ldt, AF.Exp), reads=['lam'], writes=['dt'])
            for g2 in range(2):
                sl = slice(g2 * 64, (g2 + 1) * 64)
                S.op('dve', lambda e, sl=sl, g2=g2: e.tensor_scalar(pm[:, 0, sl], lami[:, sl], ldt[:, g2:g2 + 1], None, ALU.mult), reads=['lam', 'dt'], writes=['ang'])
                S.op('dve', lambda e, sl=sl, g2=g2: e.tensor_scalar(pm[:, 1, sl], lamr[:, sl], ldt[:, g2:g2 + 1], None, ALU.mult), reads=['lam', 'dt'], writes=['pm1'])
            S.op('act', lambda e: e.activation(pm[:, 2, :], pm[:, 1, :], AF.Exp), reads=['pm1'], writes=['pm2'])
            ang = pm[:, 0, :]
            sincos(ang, 128, pm[:, 3, :], pm[:, 4, :], 16)
            TT = lambda o, a, b, op, r_, w_: S.op('dve', lambda e: e.tensor_tensor(o, a, b, op), reads=r_, writes=w_)
            pmk = ['pm%d' % i for i in range(12)]
            TT(pm[:, 5, :], pm[:, 2, :], pm[:, 4, :], ALU.mult, ['pm2', 'sc1'], ['pm5'])
            S.op('dve', lambda e: e.tensor_scalar(pm[:, 5, :], pm[:, 5, :], -1.0, None, ALU.add), reads=['pm5'], writes=['pm5'])
            TT(pm[:, 6, :], pm[:, 2, :], pm[:, 3, :], ALU.mult, ['pm2', 'sc0'], ['pm6'])
            TT(pm[:, 7, :], pm[:, 5, :], lamr, ALU.mult, ['pm5', 'lam'], ['pm7'])
            TT(pm[:, 9, :], pm[:, 6, :], lami, ALU.mult, ['pm6', 'lam'], ['pm9'])
            TT(pm[:, 7, :], pm[:, 7, :], pm[:, 9, :], ALU.add, ['pm7', 'pm9'], ['pm7'])
            TT(pm[:, 8, :], pm[:, 6, :], lamr, ALU.mult, ['pm6', 'lam'], ['pm8'])
            TT(pm[:, 9, :], pm[:, 5, :], lami, ALU.mult, ['pm5', 'lam'], ['pm9'])
            TT(pm[:, 8, :], pm[:, 8, :], pm[:, 9, :], ALU.subtract, ['pm8', 'pm9'], ['pm8'])
            TT(pm[:, 9, :], lamr, lamr, ALU.mult, ['lam'], ['pm9'])
            TT(pm[:, 10, :], lami, lami, ALU.mult, ['lam'], ['pm10'])
            TT(pm[:, 9, :], pm[:, 9, :], pm[:, 10, :], ALU.add, ['pm9', 'pm10'], ['pm9'])
            S.op('dve', lambda e: e.reciprocal(pm[:, 9, :], pm[:, 9, :]), reads=['pm9'], writes=['pm9'])
            TT(pm[:, 10, :], pm[:, 7, :], pm[:, 9, :], ALU.mult, ['pm7', 'pm9'], ['pm10'])
            TT(pm[:, 11, :], pm[:, 8, :], pm[:, 9, :], ALU.mult, ['pm8', 'pm9'], ['pm11'])
            for k_, row in enumerate((0, 2, 10, 11)):
                S.op('pe', lambda e, k_=k_, row=row: e.transpose(ps[0][:, k_ * 16:(k_ + 1) * 16], pm[:, row, :], ident_f[0:16, 0:16]),
                     reads=['ang', 'pm2', 'pm10', 'pm11', 'ident_f'], writes=['ps0'])
            S.op('dve', lambda e: e.tensor_copy(sm4.rearrange("p a b -> p (a b)"), ps[0][:, 0:64]), reads=['ps0'], writes=['sm4'])
            thT, rT, krT, kiT = sm4[:, 0, :], sm4[:, 1, :], sm4[:, 2, :], sm4[:, 3, :]
            angT = rr[:, 2, :]
            S.op('dve', lambda e: e.tensor_tensor(angT.rearrange("p (j t) -> p j t", j=16), iota.unsqueeze(1).to_broadcast([128, 16, 128]),
                                                  thT.unsqueeze(2).to_broadcast([128, 16, 128]), ALU.mult), reads=['iota', 'sm4'], writes=['ang'])
            ang = angT
            sincos(ang, 2048, sinT.rearrange("p j t -> p (j t)"), cosT.rearrange("p j t -> p (j t)"), 128)
            S.op('dve', lambda e: e.tensor_copy(sinT, sinT), reads=['sc0'], writes=['sinT'])
            S.op('dve', lambda e: e.tensor_copy(cosT, cosT), reads=['sc1'], writes=['cosT'])
            ang128 = rr[:, 3, 0:16]
            S.op('dve', lambda e: e.tensor_scalar(ang128, thT, 128.0, None, ALU.mult), reads=['sm4'], writes=['ang'])
            ang = ang128
            sincos(ang, 16, cs128[:, 1, :], cs128[:, 0, :], 128)
            S.op('dve', lambda e: e.tensor_copy(cs128, cs128), reads=['sc0', 'sc1'], writes=['cs128'])
            S.op('dve', lambda e: e.tensor_copy(Rtab, rT.unsqueeze(2).to_broadcast([128, 16, 128])), reads=['sm4'], writes=['Rtab'])
            S.op('dve', lambda e: e.memset(Rtab[:, :, 0:1], 0.0), reads=['Rtab'], writes=['Rtab'])
            krb = krT.unsqueeze(2).to_broadcast([128, 16, 16])
            kib = kiT.unsqueeze(2).to_broadcast([128, 16, 16])
            TT(bb[:, 0], bnat[:, 0], krb, ALU.mult, ['bnat', 'sm4'], ['bb0'])
            TT(bb[:, 1], bnat[:, 1], kib, ALU.mult, ['bnat', 'sm4'], ['bb1'])
            TT(bb[:, 0], bb[:, 0], bb[:, 1], ALU.subtract, ['bb0', 'bb1'], ['bb0'])
            TT(bb[:, 2], bnat[:, 1], krb, ALU.mult, ['bnat', 'sm4'], ['bb2'])
            TT(bb[:, 3], bnat[:, 0], kib, ALU.mult, ['bnat', 'sm4'], ['bb3'])
            TT(bb[:, 2], bb[:, 2], bb[:, 3], ALU.add, ['bb2', 'bb3'], ['bb2'])
            S.op('pool', lambda e: e.memset(Xb, 0.0), writes=['Xb'])
            for j in range(16):
                for g2 in range(2):
                    col0 = 16 * ((2 * j + g2) % 8)
                    rs_ = slice(g2 * 64, (g2 + 1) * 64)
                    for ri_, src in enumerate((0, 2)):
                        S.op('dve' if ri_ else 'pool', lambda e, j=j, rs_=rs_, col0=col0, ri_=ri_, src=src: e.tensor_copy(Xb[rs_, j, ri_, col0:col0 + 16], bb[rs_, src, j, :]),
                             reads=['bb0', 'bb2'], writes=['Xb'])
            for j0 in range(0, 16, 4):
                for jj in range(4):
                    for ri_ in range(2):
                        k_ = jj * 2 + ri_
                        S.op('pe', lambda e, j0=j0, jj=jj, ri_=ri_, k_=k_: e.transpose(psb[:, k_ * 128:(k_ + 1) * 128], Xb[:, j0 + jj, ri_, :], ident_b), reads=['Xb', 'ident_b'], writes=['psb'])
                S.op('dve', lambda e, j0=j0: e.tensor_copy(BT[:, j0:j0 + 4].rearrange("p j r c -> p (j r c)"), psb), reads=['psb'], writes=['BT'])
            for v_, (src, sgn) in enumerate(((0, 1.0), (0, -1.0), (1, -1.0))):
                for g2 in range(2):
                    S.op('dve', lambda e, v_=v_, src=src, sgn=sgn, g2=g2: e.tensor_scalar(Xc[:, :, v_, g2 * 64:(g2 + 1) * 64], cnat[:, src], padc[:, 2 + g2:3 + g2], sgn, ALU.mult, ALU.mult),
                         reads=['cnat', 'padc'], writes=['Xc'])
            for q in range(4):
                for v_ in range(3):
                    k_ = q * 3 + v_
                    S.op('pe', lambda e, q=q, v_=v_, k_=k_: e.transpose(psb[:, (k_ % 8) * 128:(k_ % 8 + 1) * 128], Xc[:, q, v_, :], ident_b), reads=['Xc', 'ident_b'], writes=['psb'])
                    S.op('dve', lambda e, q=q, v_=v_, k_=k_: e.tensor_copy(CTm[:, q, v_, :], psb[:, (k_ % 8) * 128:(k_ % 8 + 1) * 128]), reads=['psb'], writes=['CTm'])
            S.op('pool', lambda e: e.memset(CT, 0.0), writes=['CT'])
            for jj in range(4):
                for q in range(4):
                    S.op('dve', lambda e, jj=jj, q=q: e.tensor_copy(CT[:, q, jj, :, 32 * jj:32 * jj + 32], CTm[:, q, :, 32 * jj:32 * jj + 32]), reads=['CTm', 'CT'], writes=['CT'])
            S.flush()
            uin = [sb(es, "s5u%d" % i, [128, 4, 128], BF16) for i in range(2)]
            tmp = [[sb(es, "s5t%d_%d" % (i, s_), [128, 512], F32) for i in range(4)] for s_ in range(2)]
            vre2 = [sb(es, "vre%d" % s_, [128, 512], F32) for s_ in range(2)]
            vim2 = [sb(es, "vim%d" % s_, [128, 512], F32) for s_ in range(2)]
            wre2 = [sb(es, "wre%d" % s_, [128, 512], F32) for s_ in range(2)]
            wim2 = [sb(es, "wim%d" % s_, [128, 512], F32) for s_ in range(2)]
            Ap2 = [[sb(es, "Ap%d_%d" % (i, s_), [128, 512], BF16) for i in range(4)] for s_ in range(2)]
            car = sb(es, "car", [128, 6, 16], F32)
            yg2 = [sb(es, "yg%d" % s_, [128, 128], F32) for s_ in range(2)]
            t22 = [sb(es, "t2_%d" % s_, [128, 128], F32) for s_ in range(2)]
            sgm2 = [sb(es, "sgm%d" % s_, [128, 128], F32) for s_ in range(2)]
            gl = [sb(es, "gl%d" % i, [128, 4, 128], BF16) for i in range(2)]
            sgt = sb(es, "sgt", [128, 4, 128], F32)
            yst = [sb(es, "s5y%d" % i, [128, 4, 128], BF16) for i in range(2)]
            S.op('pool', lambda e: e.memset(car, 0.0), writes=['car'])
            for c in range(NCH):
                cs = slice(c * 128, (c + 1) * 128)
                u_, ku = uin[c % 2], 's5u%d' % (c % 2)
                gl_, kgl = gl[c % 2], 'gl%d' % (c % 2)
                ys, kys = yst[c % 2], 's5y%d' % (c % 2)
                S.dma('sp', u_, uT[:, :, cs].rearrange("k p t -> p k t"), key=ku, writes=[ku])
                def Q(qd):
                    s_ = qd % 2
                    j4 = slice(4 * qd, 4 * qd + 4)
                    return dict(qd=qd, s=s_, j4=j4, pR=s_ * 2, pI=s_ * 2 + 1, pY=4 + s_,
                                cosq=cosT[:, j4, :].rearrange("p j t -> p (j t)"), sinq=sinT[:, j4, :].rearrange("p j t -> p (j t)"),
                                Rq=Rtab[:, j4, :].rearrange("p j t -> p (j t)"),
                                tmp=tmp[s_], vre=vre2[s_], vim=vim2[s_], wre=wre2[s_], wim=wim2[s_], Ap=Ap2[s_],
                                yg=yg2[s_], t2=t22[s_], sgm=sgm2[s_], cc=slice(4 * s_, 4 * s_ + 4),
                                u=u_, ku=ku, gl=gl_, kgl=kgl,
                                k=lambda nm, s_=s_: '%s_%d' % (nm, s_))

                def st_bu(q):
                    for jj in range(4):
                        j = 4 * q['qd'] + jj
                        S.op('pe', lambda e, j=j, jj=jj: e.matmul(ps[q['pR']][:, jj * 128:(jj + 1) * 128], BT[:, j, 0, :], q['u'][:, q['qd'], :], start=True, stop=True), reads=['BT', q['ku']], writes=[PSK[q['pR']]])
                        S.op('pe', lambda e, j=j, jj=jj: e.matmul(ps[q['pI']][:, jj * 128:(jj + 1) * 128], BT[:, j, 1, :], q['u'][:, q['qd'], :], start=True, stop=True), reads=['BT', q['ku']], writes=[PSK[q['pI']]])

                def st_rot(q):
                    k, t_ = q['k'], q['tmp']
                    S.op('dve', lambda e: e.tensor_tensor(t_[0], ps[q['pR']], q['cosq'], ALU.mult), reads=[PSK[q['pR']], 'cosT'], writes=[k('s5t0')])
                    S.op('dve', lambda e: e.tensor_tensor(t_[1], ps[q['pI']], q['sinq'], ALU.mult), reads=[PSK[q['pI']], 'sinT'], writes=[k('s5t1')])
                    S.op('dve', lambda e: e.tensor_tensor(t_[2], ps[q['pI']], q['cosq'], ALU.mult), reads=[PSK[q['pI']], 'cosT'], writes=[k('s5t2')])
                    S.op('dve', lambda e: e.tensor_tensor(t_[3], ps[q['pR']], q['sinq'], ALU.mult), reads=[PSK[q['pR']], 'sinT'], writes=[k('s5t3')])

                def st_add(q):
                    k, t_ = q['k'], q['tmp']
                    S.op('pool', lambda e: e.tensor_tensor(q['vre'], t_[0], t_[1], ALU.add), reads=[k('s5t0'), k('s5t1')], writes=[k('vre')])
                    S.op('pool', lambda e: e.tensor_tensor(q['vim'], t_[2], t_[3], ALU.subtract), reads=[k('s5t2'), k('s5t3')], writes=[k('vim')])

                def st_scan(q):
                    k, j4 = q['k'], q['j4']
                    v3r = q['vre'].rearrange("p (j t) -> p j t", j=4)
                    v3i = q['vim'].rearrange("p (j t) -> p j t", j=4)
                    S.op('dve', lambda e: e.tensor_tensor(v3r[:, :, 0], v3r[:, :, 0], car[:, 0, j4], ALU.add), reads=[k('vre'), 'car'], writes=[k('vre')])
                    S.op('dve', lambda e: e.tensor_tensor(v3i[:, :, 0], v3i[:, :, 0], car[:, 1, j4], ALU.add), reads=[k('vim'), 'car'], writes=[k('vim')])
                    S.op('dve', lambda e: e.tensor_tensor_scan(q['wre'], q['Rq'], q['vre'], 0.0, ALU.mult, ALU.add), reads=['Rtab', k('vre')], writes=[k('wre')])
                    S.op('dve', lambda e: e.tensor_tensor_scan(q['wim'], q['Rq'], q['vim'], 0.0, ALU.mult, ALU.add), reads=['Rtab', k('vim')], writes=[k('wim')])

                def st_carry(q):
                    k, j4, cc = q['k'], q['j4'], q['cc']
                    w3r = q['wre'].rearrange("p (j t) -> p j t", j=4)
                    w3i = q['wim'].rearrange("p (j t) -> p j t", j=4)
                    c128, s128 = cs128[:, 0, j4], cs128[:, 1, j4]
                    S.op('pool', lambda e: e.tensor_tensor(car[:, 2, cc], w3r[:, :, 127], c128, ALU.mult), reads=[k('wre'), 'cs128'], writes=[k('car2')])
                    S.op('pool', lambda e: e.tensor_tensor(car[:, 3, cc], w3i[:, :, 127], s128, ALU.mult), reads=[k('wim'), 'cs128'], writes=[k('car3')])
                    S.op('pool', lambda e: e.tensor_tensor(car[:, 4, cc], w3r[:, :, 127], s128, ALU.mult), reads=[k('wre'), 'cs128'], writes=[k('car4')])
                    S.op('pool', lambda e: e.tensor_tensor(car[:, 5, cc], w3i[:, :, 127], c128, ALU.mult), reads=[k('wim'), 'cs128'], writes=[k('car5')])
                    S.op('pool', lambda e: e.tensor_tensor(car[:, 2, cc], car[:, 2, cc], car[:, 3, cc], ALU.subtract), reads=[k('car2'), k('car3')], writes=[k('car2')])
                    S.op('pool', lambda e: e.tensor_tensor(car[:, 4, cc], car[:, 4, cc], car[:, 5, cc], ALU.add), reads=[k('car4'), k('car5')], writes=[k('car4')])
                    S.op('pool', lambda e: e.tensor_tensor(car[:, 0, j4], car[:, 2, cc], rT[:, j4], ALU.mult), reads=[k('car2'), 'sm4', 'car'], writes=['car'])
                    S.op('pool', lambda e: e.tensor_tensor(car[:, 1, j4], car[:, 4, cc], rT[:, j4], ALU.mult), reads=[k('car4'), 'sm4', 'car'], writes=['car'])

                def st_prod(q):
                    k, A_ = q['k'], q['Ap']
                    S.op('pool', lambda e: e.tensor_tensor(A_[0], q['wre'], q['cosq'], ALU.mult), reads=[k('wre'), 'cosT'], writes=[k('Ap0')])
                    S.op('pool', lambda e: e.tensor_tensor(A_[1], q['wim'], q['sinq'], ALU.mult), reads=[k('wim'), 'sinT'], writes=[k('Ap1')])
                    S.op('dve', lambda e: e.tensor_tensor(A_[2], q['wre'], q['sinq'], ALU.mult), reads=[k('wre'), 'sinT'], writes=[k('Ap2')])
                    S.op('dve', lambda e: e.tensor_tensor(A_[3], q['wim'], q['cosq'], ALU.mult), reads=[k('wim'), 'cosT'], writes=[k('Ap3')])

                def st_y(q):
                    k, A_ = q['k'], q['Ap']
                    n_mm = 0
                    for jj in range(4):
                        for a_, v_ in ((0, 0), (1, 1), (2, 2), (3, 2)):
                            S.op('pe', lambda e, jj=jj, a_=a_, v_=v_, n_mm=n_mm: e.matmul(ps[q['pY']][:, 0:128], CT[:, q['qd'], jj, v_, :], A_[a_][:, jj * 128:(jj + 1) * 128],
                                                                                       start=(n_mm == 0), stop=(n_mm == 15)), reads=['CT', k('Ap%d' % a_)], writes=[PSK[q['pY']]])
                            n_mm += 1

                def st_gelu1(q):
                    k, yg, t2_ = q['k'], q['yg'], q['t2']
                    qd = q['qd']
                    S.op('dve', lambda e: e.scalar_tensor_tensor(yg, q['u'][:, qd, :], vecT[:, l, 68 + qd:69 + qd], ps[q['pY']][:, 0:128], ALU.mult, ALU.add), reads=[q['ku'], 'vecT', PSK[q['pY']]], writes=[k('yg')])
                    S.op('dve', lambda e: e.tensor_tensor(t2_, yg, yg, ALU.mult), reads=[k('yg')], writes=[k('t2_')])
                    S.op('dve', lambda e: e.tensor_scalar(t2_, t2_, 0.044715, 1.0, ALU.mult, ALU.add), reads=[k('t2_')], writes=[k('t2_')])
                    S.op('dve', lambda e: e.tensor_tensor(t2_, t2_, yg, ALU.mult), reads=[k('t2_'), k('yg')], writes=[k('t2_')])
                    S.op('act', lambda e: e.activation(q['sgm'], t2_, AF.Sigmoid, scale=1.5957691216057308), reads=[k('t2_')], writes=[k('sgm')])

                def st_gelu2(q):
                    k = q['k']
                    S.op('dve', lambda e: e.tensor_tensor(q['gl'][:, q['qd'], :], q['yg'], q['sgm'], ALU.mult), reads=[k('yg'), k('sgm')], writes=[q['kgl']])

                for qa in (0, 2):
                    A_, B_ = Q(qa), Q(qa + 1)
                    for st in (st_bu, st_rot):
                        st(A_)
                        st(B_)
                    st_add(A_)
                    st_add(B_)
                    st_scan(A_)
                    st_scan(B_)
                    st_prod(A_)
                    st_prod(B_)
                    st_y(A_)
                    st_y(B_)
                    st_carry(A_)
                    st_carry(B_)
                    st_gelu1(A_)
                    st_gelu1(B_)
                    st_gelu2(A_)
                    st_gelu2(B_)
                for ot_ in range(8):
                    pz = 2 + ot_ // 4 if False else (0 if ot_ < 4 else 1)
                    for kt in range(4):
                        S.op('pe', lambda e, ot_=ot_, kt=kt, gl_=gl_, pz=pz: e.matmul(ps[pz][:, (ot_ % 4) * 128:(ot_ % 4 + 1) * 128], gw[:, ot_ // 4, kt, (ot_ % 4) * 128:(ot_ % 4 + 1) * 128], gl_[:, kt, :],
                                                                                       start=(kt == 0), stop=(kt == 3)), reads=['gw', kgl], writes=[PSK[pz]])
                for o4 in range(4):
                    S.op('act', lambda e, o4=o4: e.activation(sgt[:, o4, :], ps[1][:, o4 * 128:(o4 + 1) * 128], AF.Sigmoid, bias=vecT[:, l, 76 + o4:77 + o4]), reads=['ps1', 'vecT'], writes=['sgt'])
                    S.op('dve', lambda e, o4=o4, ys=ys: e.scalar_tensor_tensor(ys[:, o4, :], ps[0][:, o4 * 128:(o4 + 1) * 128], vecT[:, l, 72 + o4:73 + o4], sgt[:, o4, :], ALU.add, ALU.mult),
                         reads=['ps0', 'vecT', 'sgt'], writes=[kys])
                S.dma('act', yT[0:4, :, cs].rearrange("k p t -> p k t"), ys, key=kys, reads=[kys])
            S.flush()

    for l in range(DEPTH):
        phase_A(l)
        if stop_after == "A":
            return nc
        phase_s5(l)
        phase_mlstm(l)
        phase_fox(l)
        phase_pool(l)
        if stop_after == "mix":
            return nc
        phase_C(l)
        if stop_after == "C":
            return nc
    pes.close()
    return nc


def _in_maps(inputs, SEQ, DEPTH, n_cores):
    consts = host_consts()
    maps = []
    B = inputs["x"].shape[0]
    for c in range(n_cores):
        m = {"x": np.ascontiguousarray(inputs["x"][c % B], dtype=np.float32), "meta_tokens": np.asarray(inputs["meta_tokens"], np.float32)}
        for n, _ in PARAMS:
            m[n] = np.ascontiguousarray(inputs[n][:DEPTH], dtype=np.float32)
        m.update(consts)
        maps.append(m)
    return maps


def kernel(**inputs):
    SEQ, DEPTH = 4096, 2
    B = inputs["x"].shape[0]
    nc = build_program(SEQ, DEPTH)
    res = run_bass_kernel_spmd(nc, _in_maps(inputs, SEQ, DEPTH, 8), core_ids=list(range(8)))
    return np.stack([np.asarray(res.results[b]["out"], dtype=np.float32) for b in range(B)], axis=0)
```
